# Optimizing a Trainium2 kernel written in Bass

```python
import jax, jax.numpy as jnp
from jax import lax
import numpy as np

D_MODEL = 1024
BATCH = 8
SEQ = 2048
DEPTH = 2

PLE_DIM = 256
BRANCH_WIDTH = D_MODEL // 2
N_BRANCH = 3
FOX_HEAD_DIM = 64
FOX_HEADS = BRANCH_WIDTH // FOX_HEAD_DIM
FOX_WIDTH = FOX_HEADS * FOX_HEAD_DIM
FOX_BLOCK = 128
SC_WIDTH = BRANCH_WIDTH
SC_KERNEL = 3
DN_HEAD_DIM = 128
DN_HEADS = BRANCH_WIDTH // DN_HEAD_DIM
DN_WIDTH = DN_HEADS * DN_HEAD_DIM
DN_CONV = 4
DN_CHUNK = 64
D_FF = 128 * ((8 * D_MODEL // 3 + 127) // 128)
FFN_CONV = 3
EPS = 1e-6

IN_SIZES = (3 * FOX_WIDTH, FOX_HEADS, 3 * SC_WIDTH, 3 * DN_WIDTH, DN_HEADS, DN_HEADS, DN_WIDTH, N_BRANCH * D_MODEL)
IN_WIDTH = sum(IN_SIZES)

kernel_name = 'hybrid_fox_shortconv_gdn_parallel_block'


def split_cols(t, sizes):
    offs = []
    acc = 0
    for s in sizes[:-1]:
        acc += s
        offs.append(acc)
    return jnp.split(t, offs, axis=-1)


def rmsnorm(x, gain):
    xf = x.astype(jnp.float32)
    y = xf * lax.rsqrt(jnp.mean(xf * xf, axis=-1, keepdims=True) + EPS)
    return (y * gain.astype(jnp.float32)).astype(x.dtype)


def l2norm(x):
    xf = x.astype(jnp.float32)
    return xf * lax.rsqrt(jnp.sum(xf * xf, axis=-1, keepdims=True) + EPS)


def causal_dwconv(x, w):
    k_width, chans = w.shape
    return lax.conv_general_dilated(x, w[:, None, :].astype(x.dtype), window_strides=(1,),
                                    padding=[(k_width - 1, 0)],
                                    dimension_numbers=('NWC', 'WIO', 'NWC'),
                                    feature_group_count=chans)


def forgetting_attention(q, k, v, f_logit, b_f, q_gain, k_gain):
    seq = q.shape[1]
    dh = q.shape[-1]
    q = rmsnorm(q, q_gain).transpose(0, 2, 1, 3)
    k = rmsnorm(k, k_gain).transpose(0, 2, 1, 3)
    v = v.transpose(0, 2, 1, 3)
    log_f = jax.nn.log_sigmoid(f_logit.astype(jnp.float32) + b_f.astype(jnp.float32))
    cum_f = jnp.cumsum(log_f, axis=1).transpose(0, 2, 1)
    scale = dh ** -0.5
    outs = []
    for start in range(0, seq, FOX_BLOCK):
        end = start + FOX_BLOCK
        s = jnp.einsum('bhqd,bhkd->bhqk', q[:, :, start:end], k[:, :, :end],
                       preferred_element_type=jnp.float32) * scale
        s = s + cum_f[:, :, start:end, None] - cum_f[:, :, None, :end]
        causal = jnp.arange(start, end)[:, None] >= jnp.arange(end)[None, :]
        s = jnp.where(causal, s, -jnp.inf)
        pr = jax.nn.softmax(s, axis=-1).astype(v.dtype)
        outs.append(jnp.einsum('bhqk,bhkd->bqhd', pr, v[:, :, :end]))
    return jnp.concatenate(outs, axis=1)


def gated_delta_rule(q, k, v, g, beta):
    bsz, seq, heads, dk = q.shape
    dv = v.shape[-1]
    c = DN_CHUNK
    n_chunks = seq // c

    def to_chunks(t):
        return t.astype(jnp.float32).reshape(bsz, n_chunks, c, heads, -1).transpose(1, 0, 3, 2, 4)

    qc = to_chunks(q) * dk ** -0.5
    kc = to_chunks(k)
    vc = to_chunks(v)
    gc = to_chunks(g[..., None])[..., 0]
    bc = to_chunks(beta[..., None])[..., 0]
    gcum = jnp.cumsum(gc, axis=-1)
    incl = jnp.tril(jnp.ones((c, c), dtype=bool))
    strict = jnp.tril(jnp.ones((c, c), dtype=bool), k=-1)
    decay = jnp.exp(jnp.where(incl, gcum[..., :, None] - gcum[..., None, :], -jnp.inf))
    kb = kc * bc[..., None]
    a_mat = jnp.where(strict, jnp.einsum('nbhid,nbhjd->nbhij', kb, kc) * decay, 0.0) \
        + jnp.eye(c, dtype=jnp.float32)
    rhs = jnp.concatenate([vc * bc[..., None], kb * jnp.exp(gcum)[..., None]], axis=-1)
    sol = lax.linalg.triangular_solve(a_mat, rhs, left_side=True, lower=True, unit_diagonal=True)
    u_val, k_cum = sol[..., :dv], sol[..., dv:]
    qk = jnp.where(incl, jnp.einsum('nbhid,nbhjd->nbhij', qc, kc) * decay, 0.0)
    q_dec = qc * jnp.exp(gcum)[..., None]
    k_dec = kc * jnp.exp(gcum[..., -1:] - gcum)[..., None]
    g_tot = jnp.exp(gcum[..., -1])

    def step(state, xs):
        u_i, kcum_i, qk_i, qdec_i, kdec_i, gtot_i = xs
        v_new = u_i - jnp.einsum('bhck,bhkv->bhcv', kcum_i, state)
        out = jnp.einsum('bhck,bhkv->bhcv', qdec_i, state) + jnp.einsum('bhij,bhjv->bhiv', qk_i, v_new)
        state = state * gtot_i[..., None, None] + jnp.einsum('bhck,bhcv->bhkv', kdec_i, v_new)
        return state, out

    state0 = jnp.zeros((bsz, heads, dk, dv), jnp.float32)
    _, out = lax.scan(step, state0, (u_val, k_cum, qk, q_dec, k_dec, g_tot))
    return out.transpose(1, 0, 3, 2, 4).reshape(bsz, seq, heads, dv)


def hybrid_layer(x, p_i, g_mix, w_in, b_fox_f, fox_q_gain, fox_k_gain, sc_conv_w, dn_conv_w,
                 dn_a_log, dn_dt_bias, dn_norm_gain, w_branch, w_o, g_ffn, w_up, ffn_conv_w,
                 w_down, g_ple, w_ple_gate, w_ple):
    bsz, seq, _ = x.shape
    h = rmsnorm(x, g_mix)
    proj = h @ w_in
    fox_qkv, fox_f, sc_bcv, dn_qkv, dn_b, dn_a, dn_z, br_gate = split_cols(proj, IN_SIZES)

    fq, fk, fv = [t.reshape(bsz, seq, FOX_HEADS, FOX_HEAD_DIM) for t in jnp.split(fox_qkv, 3, axis=-1)]
    y_fox = forgetting_attention(fq, fk, fv, fox_f, b_fox_f, fox_q_gain, fox_k_gain)
    y_fox = y_fox.reshape(bsz, seq, FOX_WIDTH)

    sb, sc, sv = jnp.split(sc_bcv, 3, axis=-1)
    y_sc = sb * causal_dwconv(sc * sv, sc_conv_w)

    dn_qkv = jax.nn.silu(causal_dwconv(dn_qkv, dn_conv_w))
    dq, dk_, dv_ = [t.reshape(bsz, seq, DN_HEADS, DN_HEAD_DIM) for t in jnp.split(dn_qkv, 3, axis=-1)]
    beta = jax.nn.sigmoid(dn_b.astype(jnp.float32))
    g = -jnp.exp(dn_a_log.astype(jnp.float32)) * jax.nn.softplus(dn_a.astype(jnp.float32) + dn_dt_bias.astype(jnp.float32))
    o_dn = gated_delta_rule(l2norm(dq), l2norm(dk_), dv_, g, beta).astype(x.dtype)
    z = dn_z.reshape(bsz, seq, DN_HEADS, DN_HEAD_DIM)
    y_dn = (rmsnorm(o_dn, dn_norm_gain) * jax.nn.silu(z)).reshape(bsz, seq, DN_WIDTH)

    ys = jnp.stack([y_fox, y_sc, y_dn], axis=2)
    gates = jax.nn.sigmoid(br_gate).reshape(bsz, seq, N_BRANCH, D_MODEL)
    merged = jnp.sum(jnp.einsum('bsnc,ncd->bsnd', ys, w_branch) * gates, axis=2)
    x = x + merged @ w_o

    u = causal_dwconv(rmsnorm(x, g_ffn) @ w_up, ffn_conv_w)
    u_gate, u_val = jnp.split(u, 2, axis=-1)
    x = x + (jax.nn.silu(u_gate) * u_val) @ w_down

    x = x + jax.nn.sigmoid(rmsnorm(x, g_ple) @ w_ple_gate) * (p_i.astype(x.dtype) @ w_ple)
    return x


def setup_inputs(seed: int = 0) -> dict:
    key = jax.random.key(seed)
    ks = jax.random.split(key, 24)
    f32 = jnp.float32

    def nrm(k, shape, scale):
        return jax.random.normal(k, shape, f32) * scale

    def gain(k, shape):
        return 1.0 + 0.02 * jax.random.normal(k, shape, f32)

    x = nrm(ks[0], (BATCH, SEQ, D_MODEL), 1.0)
    p = nrm(ks[1], (DEPTH, BATCH, SEQ, PLE_DIM), 1.0)
    g_mix = gain(ks[2], (DEPTH, D_MODEL))
    w_in = nrm(ks[3], (DEPTH, D_MODEL, IN_WIDTH), D_MODEL ** -0.5)
    b_fox_f = jnp.linspace(1.0, 5.0, FOX_HEADS, dtype=f32)[None, :] + nrm(ks[4], (DEPTH, FOX_HEADS), 0.1)
    fox_q_gain = gain(ks[5], (DEPTH, FOX_HEAD_DIM))
    fox_k_gain = gain(ks[6], (DEPTH, FOX_HEAD_DIM))
    sc_conv_w = nrm(ks[7], (DEPTH, SC_KERNEL, SC_WIDTH), SC_KERNEL ** -0.5)
    dn_conv_w = nrm(ks[8], (DEPTH, DN_CONV, 3 * DN_WIDTH), DN_CONV ** -0.5)
    dn_a_log = jnp.log(jax.random.uniform(ks[9], (DEPTH, DN_HEADS), f32, 1.0, 16.0))
    dt = jnp.exp(jax.random.uniform(ks[10], (DEPTH, DN_HEADS), f32, float(np.log(1e-3)), float(np.log(1e-1))))
    dn_dt_bias = dt + jnp.log(-jnp.expm1(-dt))
    dn_norm_gain = gain(ks[11], (DEPTH, DN_HEAD_DIM))
    w_branch = nrm(ks[12], (DEPTH, N_BRANCH, BRANCH_WIDTH, D_MODEL), BRANCH_WIDTH ** -0.5)
    w_o = nrm(ks[13], (DEPTH, D_MODEL, D_MODEL), D_MODEL ** -0.5)
    g_ffn = gain(ks[14], (DEPTH, D_MODEL))
    w_up = nrm(ks[15], (DEPTH, D_MODEL, 2 * D_FF), D_MODEL ** -0.5)
    ffn_conv_w = nrm(ks[16], (DEPTH, FFN_CONV, 2 * D_FF), FFN_CONV ** -0.5)
    w_down = nrm(ks[17], (DEPTH, D_FF, D_MODEL), D_FF ** -0.5)
    g_ple = gain(ks[18], (DEPTH, D_MODEL))
    w_ple_gate = nrm(ks[19], (DEPTH, D_MODEL, D_MODEL), D_MODEL ** -0.5)
    w_ple = nrm(ks[20], (DEPTH, PLE_DIM, D_MODEL), PLE_DIM ** -0.5)
    return {'x': x, 'p': p, 'g_mix': g_mix, 'w_in': w_in, 'b_fox_f': b_fox_f,
            'fox_q_gain': fox_q_gain, 'fox_k_gain': fox_k_gain, 'sc_conv_w': sc_conv_w,
            'dn_conv_w': dn_conv_w, 'dn_a_log': dn_a_log, 'dn_dt_bias': dn_dt_bias,
            'dn_norm_gain': dn_norm_gain, 'w_branch': w_branch, 'w_o': w_o, 'g_ffn': g_ffn,
            'w_up': w_up, 'ffn_conv_w': ffn_conv_w, 'w_down': w_down, 'g_ple': g_ple,
            'w_ple_gate': w_ple_gate, 'w_ple': w_ple}


def reference(x, p, g_mix, w_in, b_fox_f, fox_q_gain, fox_k_gain, sc_conv_w, dn_conv_w,
              dn_a_log, dn_dt_bias, dn_norm_gain, w_branch, w_o, g_ffn, w_up, ffn_conv_w,
              w_down, g_ple, w_ple_gate, w_ple):
    for i in range(DEPTH):
        x = hybrid_layer(x, p[i], g_mix[i], w_in[i], b_fox_f[i], fox_q_gain[i], fox_k_gain[i],
                         sc_conv_w[i], dn_conv_w[i], dn_a_log[i], dn_dt_bias[i], dn_norm_gain[i],
                         w_branch[i], w_o[i], g_ffn[i], w_up[i], ffn_conv_w[i], w_down[i],
                         g_ple[i], w_ple_gate[i], w_ple[i])
    return x
```

```python
import numpy as np
import concourse.bass as bass
import concourse.mybir as mybir

F32 = mybir.dt.float32
BF16 = mybir.dt.bfloat16
ALU = mybir.AluOpType
AF = mybir.ActivationFunctionType

ENGS = ("pe", "act", "dve", "pool", "sp")


def _prod(xs):
    r = 1
    for v in xs:
        r *= int(v)
    return r


def region(ap):
    t = ap.tensor
    es = mybir.dt.size(ap.dtype)
    off = int(ap.offset)
    pat = ap.ap
    if str(ap.space) == "DRAM":
        ext = 0
        for st, cnt in pat:
            ext += (cnt - 1) * abs(st)
        return (t.name, 0, 1, off * es, (off + ext + 1) * es)
    shape = list(t.shape)
    F = _prod(shape[1:])
    p0 = off // F
    f0 = off % F
    pstep, pcnt = pat[0]
    npart = pcnt if pstep != 0 else 1
    ext = 0
    for st, cnt in pat[1:]:
        ext += (cnt - 1) * abs(st)
    return (t.name, p0, p0 + npart, f0 * es, (f0 + ext + 1) * es)


def _untracked(ap):
    return str(ap.space) == "DRAM" and not ap.tensor.name.startswith("scr_")


def _overlap(a, b):
    return a[1] < b[2] and b[1] < a[2] and a[3] < b[4] and b[3] < a[4]


def _covers(a, b):
    return a[1] <= b[1] and a[2] >= b[2] and a[3] <= b[3] and a[4] >= b[4]


class Instr:
    __slots__ = ("eng", "fn", "deps", "signal", "is_dma", "key", "val", "idx")

    def __init__(self, eng, fn, is_dma, key):
        self.eng = eng
        self.fn = fn
        self.deps = set()
        self.signal = False
        self.is_dma = is_dma
        self.key = key
        self.val = 0


class Prog:
    def __init__(self, nc):
        self.nc = nc
        self.instrs = []
        self.wr = {}
        self.rd = {}
        self.final_keys = set()
        self.last_dma = {}

    def add(self, eng, fn, reads=(), writes=(), dma_key=None):
        ins = Instr(eng, fn, dma_key is not None, dma_key if dma_key is not None else eng)
        idx = len(self.instrs)
        ins.idx = idx
        self.instrs.append(ins)
        deps = ins.deps
        for ap in reads:
            if ap is None or isinstance(ap, (int, float)):
                continue
            if _untracked(ap):
                continue
            r = region(ap)
            for (w, wi) in self.wr.get(r[0], ()):
                if _overlap(w, r):
                    deps.add(wi)
        for ap in writes:
            if _untracked(ap):
                continue
            r = region(ap)
            wl = self.wr.setdefault(r[0], [])
            rl = self.rd.setdefault(r[0], [])
            for (w, wi) in wl:
                if _overlap(w, r):
                    deps.add(wi)
            for (q, qi) in rl:
                if _overlap(q, r):
                    deps.add(qi)
            self.wr[r[0]] = [(w, wi) for (w, wi) in wl if not _covers(r, w)]
            self.rd[r[0]] = [(q, qi) for (q, qi) in rl if not _covers(r, q)]
            self.wr[r[0]].append((r, idx))
        for ap in reads:
            if ap is None or isinstance(ap, (int, float)):
                continue
            if _untracked(ap):
                continue
            r = region(ap)
            rl = self.rd.setdefault(r[0], [])
            if dma_key is None:
                rl[:] = [(q, qi) for (q, qi) in rl
                         if not (q == r and self.instrs[qi].eng == eng and not self.instrs[qi].is_dma)]
            rl.append((r, idx))
        if dma_key is not None:
            if dma_key in self.last_dma:
                deps.add(self.last_dma[dma_key])
            self.last_dma[dma_key] = idx
        deps.discard(idx)
        return ins

    def mm(self, out, lhsT, rhs, start=True, stop=True):
        rd = [lhsT, rhs]
        return self.add("pe", lambda e: e.matmul(out, lhsT, rhs, start=start, stop=stop),
                        reads=rd, writes=[out])

    def transpose(self, out, in_, ident):
        return self.add("pe", lambda e: e.transpose(out, in_, ident),
                        reads=[in_, ident], writes=[out])

    def act(self, out, in_, func, bias=None, scale=1.0, accum_out=None, eng="act"):
        kw = {}
        if bias is not None:
            kw["bias"] = bias
        if accum_out is not None:
            kw["accum_out"] = accum_out
        rd = [in_, bias if not isinstance(bias, (int, float)) else None,
              scale if not isinstance(scale, (int, float)) else None]
        wr = [out] + ([accum_out] if accum_out is not None else [])
        return self.add(eng, lambda e: e.activation(out, in_, func, scale=scale, **kw),
                        reads=rd, writes=wr)

    def tt(self, eng, out, in0, in1, op):
        return self.add(eng, lambda e: e.tensor_tensor(out, in0, in1, op),
                        reads=[in0, in1], writes=[out])

    def ts(self, eng, out, in0, s1, s2=None, op0=ALU.mult, op1=None, accum_out=None):
        kw = {}
        if op1 is not None:
            kw["op1"] = op1
        if accum_out is not None:
            kw["accum_out"] = accum_out
        rd = [in0, s1 if not isinstance(s1, (int, float)) else None,
              s2 if not isinstance(s2, (int, float)) else None]
        wr = [out] + ([accum_out] if accum_out is not None else [])
        return self.add(eng, lambda e: e.tensor_scalar(out, in0, s1, s2, op0, **kw),
                        reads=rd, writes=wr)

    def stt(self, eng, out, in0, scalar, in1, op0, op1):
        rd = [in0, in1, scalar if not isinstance(scalar, (int, float)) else None]
        return self.add(eng, lambda e: e.scalar_tensor_tensor(out, in0, scalar, in1, op0, op1),
                        reads=rd, writes=[out])

    def copy(self, eng, out, in_):
        if eng == "act":
            return self.add(eng, lambda e: e.copy(out, in_), reads=[in_], writes=[out])
        return self.add(eng, lambda e: e.tensor_copy(out, in_), reads=[in_], writes=[out])

    def memset(self, eng, out, val):
        return self.add(eng, lambda e: e.memset(out, val), reads=[], writes=[out])

    def scan(self, out, d0, d1, initial, op0, op1):
        rd = [d0, d1, initial if not isinstance(initial, (int, float)) else None]
        return self.add("dve", lambda e: e.tensor_tensor_scan(out, d0, d1, initial, op0, op1),
                        reads=rd, writes=[out])

    def dma(self, eng, out, in_, key, final=False):
        if final:
            self.final_keys.add(key)
        return self.add(eng, lambda e: e.dma_start(out, in_), reads=[in_], writes=[out], dma_key=key)

    def emit(self):
        nc = self.nc
        instrs = self.instrs
        for ins in instrs:
            if ins.eng == "pe" and not ins.is_dma:
                ins.deps = {d for d in ins.deps if not (instrs[d].eng == "pe" and not instrs[d].is_dma)}
            for d in ins.deps:
                instrs[d].signal = True
        keys = []
        for ins in instrs:
            if ins.key not in keys:
                keys.append(ins.key)
        for ins in instrs:
            if ins.is_dma:
                ins.signal = True
        cnt = {k: 0 for k in keys}
        for ins in instrs:
            if ins.signal:
                cnt[ins.key] += 16 if ins.is_dma else 1
            ins.val = cnt[ins.key]
        used = [k for k in keys if cnt[k] > 0]
        self.sem_totals = {k: cnt[k] for k in used}
        import contextlib
        with contextlib.ExitStack() as st:
            sems = {k: st.enter_context(nc.semaphore("s_" + str(k))) for k in used}
            block = st.enter_context(nc.Block())
            per_eng = {e: [i for i in instrs if i.eng == e] for e in ENGS}
            nwaits = [0]

            def run(engobj, lst, is_last_sp=False):
                clock = {}
                for ins in lst:
                    need = {}
                    for d in ins.deps:
                        di = instrs[d]
                        if di.val > need.get(di.key, 0):
                            need[di.key] = di.val
                    for k, v in need.items():
                        if clock.get(k, 0) < v:
                            engobj.wait_ge(sems[k], v)
                            clock[k] = v
                            nwaits[0] += 1
                    bi = ins.fn(engobj)
                    if ins.signal:
                        bi.then_inc(sems[ins.key], 16 if ins.is_dma else 1)
                if is_last_sp:
                    for k in sorted(self.final_keys, key=str):
                        if k in sems:
                            engobj.wait_ge(sems[k], cnt[k])

            @block.tensor
            def _(e):
                run(e, per_eng["pe"])

            @block.scalar
            def _(e):
                run(e, per_eng["act"])

            @block.vector
            def _(e):
                run(e, per_eng["dve"])

            @block.gpsimd
            def _(e):
                run(e, per_eng["pool"])

            @block.sync
            def _(e):
                run(e, per_eng["sp"], True)
            self.nwaits = nwaits[0]

from concourse.bass_utils import run_bass_kernel_spmd
import ml_dtypes

S = 2048
D = 1024
DEPTH = 2
NCH = D // 128
NTT = S // 512
DFF = 2816
NFF = DFF // 128
EPS = 1e-6
GRP = 4
NPRM = 224
NCST = 1536

C_FQ, C_FK, C_FV, C_FF = 0, 512, 1024, 1536
C_SB, C_SC, C_SV = 1544, 2056, 2568
C_DQ, C_DK, C_DV = 3080, 3592, 4104
C_DB, C_DA, C_DZ, C_G = 4616, 4620, 4624, 5136

P_GMIX, P_GFFN, P_GPLE = 0, 8, 16
P_FQG, P_FKG, P_BF, P_ALOG, P_DTB, P_DNG = 24, 25, 26, 27, 28, 29
P_SCW, P_DNW, P_FFW = 30, 42, 90

K_ID, K_M01, K_ONE, K_SEL, K_MPA, K_MPB = 0, 128, 384, 512, 1024, 1088


def _ffn_groups():
    gs = []
    j = 0
    while j < NFF:
        n = min(GRP, NFF - j)
        gs.append((j, n))
        j += n
    return gs


def wtile_index():
    idx = {}
    names = ["small"]
    names += [f"foxv{c}" for c in range(4)]
    names += [f"foxqk{h}" for h in range(8)]
    for j in range(4):
        names += [f"scb{j}", f"scc{j}", f"scv{j}"]
    for h in range(4):
        names += [f"dnq{h}", f"dnk{h}", f"dnv{h}", f"dnz{h}"]
    for dc in range(8):
        names += [f"g0_{dc}", f"g1_{dc}", f"g2_{dc}", f"brA{dc}", f"brB{dc}"]
    names += [f"wo{dc}" for dc in range(8)]
    for j in range(NFF):
        names += [f"upg{j}", f"upv{j}"]
    for gi, (j0, n) in enumerate(_ffn_groups()):
        names += [f"dn{gi}_{q}" for q in range(4)]
    for dc in range(8):
        names += [f"pg{dc}", f"pl{dc}"]
    for i, n in enumerate(names):
        idx[n] = i
    return idx


WIDX = wtile_index()
NT = len(WIDX)


def pack_weights(inp, l):
    W = np.zeros((NT, 128, 1024), np.float32)
    w_in = inp["w_in"][l]

    def kc(cols):
        n = cols.shape[1]
        return cols.reshape(8, 128, n).transpose(1, 0, 2).reshape(128, 8 * n)

    def put(name, arr):
        W[WIDX[name], :, :arr.shape[1]] = arr

    sm = np.zeros((1024, 96), np.float32)
    sm[:, 0:8] = w_in[:, C_FF:C_FF + 8]
    sm[:, 32:36] = w_in[:, C_DA:C_DA + 4]
    sm[:, 64:68] = w_in[:, C_DB:C_DB + 4]
    put("small", kc(sm))
    for c in range(4):
        put(f"foxv{c}", kc(w_in[:, C_FV + c * 128:C_FV + (c + 1) * 128]))
    for h in range(8):
        qk = np.concatenate([w_in[:, C_FQ + h * 64:C_FQ + (h + 1) * 64],
                             w_in[:, C_FK + h * 64:C_FK + (h + 1) * 64]], axis=1)
        put(f"foxqk{h}", kc(qk))
    for j in range(4):
        put(f"scb{j}", kc(w_in[:, C_SB + j * 128:C_SB + (j + 1) * 128]))
        put(f"scc{j}", kc(w_in[:, C_SC + j * 128:C_SC + (j + 1) * 128]))
        put(f"scv{j}", kc(w_in[:, C_SV + j * 128:C_SV + (j + 1) * 128]))
    for h in range(4):
        put(f"dnq{h}", kc(w_in[:, C_DQ + h * 128:C_DQ + (h + 1) * 128]))
        put(f"dnk{h}", kc(w_in[:, C_DK + h * 128:C_DK + (h + 1) * 128]))
        put(f"dnv{h}", kc(w_in[:, C_DV + h * 128:C_DV + (h + 1) * 128]))
        put(f"dnz{h}", kc(w_in[:, C_DZ + h * 128:C_DZ + (h + 1) * 128]))
    wb = inp["w_branch"][l]
    for dc in range(8):
        for b in range(3):
            put(f"g{b}_{dc}", kc(w_in[:, C_G + b * 1024 + dc * 128:C_G + b * 1024 + (dc + 1) * 128]))

        def br(b):
            a = wb[b][:, dc * 128:(dc + 1) * 128]
            return a.reshape(4, 128, 128).transpose(1, 0, 2).reshape(128, 512)
        put(f"brA{dc}", np.concatenate([br(0), br(1)], axis=1))
        put(f"brB{dc}", br(2))
        put(f"wo{dc}", kc(inp["w_o"][l][:, dc * 128:(dc + 1) * 128]))
        put(f"pg{dc}", kc(inp["w_ple_gate"][l][:, dc * 128:(dc + 1) * 128]))
        a = inp["w_ple"][l][:, dc * 128:(dc + 1) * 128]
        put(f"pl{dc}", a.reshape(2, 128, 128).transpose(1, 0, 2).reshape(128, 256))
    w_up = inp["w_up"][l]
    for j in range(NFF):
        put(f"upg{j}", kc(w_up[:, j * 128:(j + 1) * 128]))
        put(f"upv{j}", kc(w_up[:, DFF + j * 128:DFF + (j + 1) * 128]))
    w_dn = inp["w_down"][l]
    for gi, (j0, n) in enumerate(_ffn_groups()):
        for q in range(4):
            a = w_dn[j0 * 128:(j0 + n) * 128, q * 256:(q + 1) * 256]
            put(f"dn{gi}_{q}", a.reshape(n, 128, 256).transpose(1, 0, 2).reshape(128, n * 256))
    return W


def pack_params(inp, l):
    Pm = np.zeros((128, NPRM), np.float32)
    Pm[:, P_GMIX:P_GMIX + 8] = inp["g_mix"][l].reshape(8, 128).T
    Pm[:, P_GFFN:P_GFFN + 8] = inp["g_ffn"][l].reshape(8, 128).T
    Pm[:, P_GPLE:P_GPLE + 8] = inp["g_ple"][l].reshape(8, 128).T
    Pm[0:64, P_FQG] = inp["fox_q_gain"][l]
    Pm[0:64, P_FKG] = inp["fox_k_gain"][l]
    Pm[0:8, P_BF] = inp["b_fox_f"][l]
    Pm[32:36, P_ALOG] = inp["dn_a_log"][l]
    Pm[32:36, P_DTB] = inp["dn_dt_bias"][l]
    Pm[:, P_DNG] = inp["dn_norm_gain"][l]
    Pm[:, P_SCW:P_SCW + 12] = inp["sc_conv_w"][l].reshape(3, 4, 128).transpose(2, 1, 0).reshape(128, 12)
    Pm[:, P_DNW:P_DNW + 48] = inp["dn_conv_w"][l].reshape(4, 12, 128).transpose(2, 1, 0).reshape(128, 48)
    Pm[:, P_FFW:P_FFW + 132] = inp["ffn_conv_w"][l].reshape(3, 44, 128).transpose(2, 1, 0).reshape(128, 132)
    return Pm


def make_consts():
    C = np.zeros((128, NCST), np.float32)
    r = np.arange(128)[:, None]
    c = np.arange(128)[None, :]
    C[:, K_ID:K_ID + 128] = (r == c)
    C[:, K_M01:K_M01 + 128] = (c >= r)
    C[:, K_M01 + 128:K_M01 + 256] = (c > r)
    C[:, K_ONE:K_ONE + 128] = 1.0
    for h in range(4):
        C[32 + h, K_SEL + h * 128:K_SEL + (h + 1) * 128] = 1.0
    r6 = np.arange(64)[:, None]
    c6 = np.arange(64)[None, :]
    C[0:64, K_MPA:K_MPA + 64] = np.where(r6 > c6, 0.0, 1.0e4)
    C[0:64, K_MPB:K_MPB + 64] = np.where(c6 >= r6, 0.0, 1.0e4)
    return C


def build(depth=DEPTH, dbg=None, stop_after=None):
    nc = bass.Bass("TRN2", target_bir_lowering=False)
    x_in = nc.dram_tensor("x", [S, D], F32, kind="ExternalInput").ap()
    p_in = nc.dram_tensor("p", [DEPTH, S, 256], F32, kind="ExternalInput").ap()
    wst = nc.dram_tensor("wst", [DEPTH, NT, 128, 1024], F32, kind="ExternalInput").ap()
    prm_in = nc.dram_tensor("prm", [DEPTH, 128, NPRM], F32, kind="ExternalInput").ap()
    cst_in = nc.dram_tensor("cst", [128, NCST], F32, kind="ExternalInput").ap()
    out = nc.dram_tensor("out", [S, D], F32, kind="ExternalOutput").ap()
    scrx = nc.dram_tensor("scr_x", [128, NCH, S], F32).ap()
    dbg_outs = {}

    TOT = 211456
    SB = nc.alloc_sbuf_tensor("SB", [128, TOT // 4], F32)
    PS = nc.alloc_psum_tensor("PS", [128, 8 * 512], F32)
    P = Prog(nc)

    def V(off, shape, dt=F32, p0=0):
        es = mybir.dt.size(dt)
        n = _prod(shape[1:])
        assert off % 4 == 0 and off + n * es <= TOT, (off, shape)
        nw = (n * es + 3) // 4
        v = SB[p0:p0 + shape[0], off // 4: off // 4 + nw]
        if dt != F32:
            v = v.bitcast(dt)
        v = v[:, 0:n]
        if len(shape) == 3:
            v = v.rearrange("p (a b) -> p a b", a=shape[1])
        elif len(shape) == 4:
            v = v.rearrange("p (a b c) -> p a b c", a=shape[1], b=shape[2])
        return v

    def bank(b, parts=128, dt=F32, p0=0):
        v = PS[p0:p0 + parts, b * 512:(b + 1) * 512]
        if dt != F32:
            v = v.bitcast(dt)
        return v

    KB = 1024
    O_XT = 0
    O_HT = 64 * KB
    O_YT = 96 * KB
    O_WS = 144 * KB
    O_WB = 152 * KB
    O_CST = 158 * KB
    O_CBF = 164 * KB
    O_PRM = 165 * KB
    O_MISC = 166 * KB
    O_SMT = 167 * KB
    O_SCR = 175 * KB
    SCR_END = TOT

    xT = V(O_XT, [128, NCH, S], F32)
    hT = V(O_HT, [128, NCH, S], BF16)
    yT = V(O_YT, [128, 12, S], BF16)
    wstage = [V(O_WS + i * 4 * KB, [128, 1024], F32) for i in range(2)]
    wbf = [V(O_WB + i * 2 * KB, [128, 1024], BF16) for i in range(3)]
    cst = V(O_CST, [128, NCST], F32)
    cbf = V(O_CBF, [128, 512], BF16)
    prm = V(O_PRM, [128, NPRM], F32)
    misc = V(O_MISC, [128, 256], F32)
    smT = V(O_SMT, [128, S], F32)

    ident_f = cst[:, K_ID:K_ID + 128]
    ident_b = cbf[:, 0:128]
    mask01_b = cbf[:, 128:256]
    ones_b = cbf[:, 384:512]
    sel_f = cst[:, K_SEL:K_SEL + 512]
    mposA = cst[0:64, K_MPA:K_MPA + 64]
    mposB = cst[0:64, K_MPB:K_MPB + 64]
    epsc = misc[:, 0:1]
    onec = misc[:, 1:2]
    qgs = misc[:, 2:3]
    negb = misc[:, 3:4]
    negA = misc[:, 4:5]

    state = {"ws": 0, "wb": 0, "ps": 0}

    def wload(l, name, X):
        si = state["ws"] % 2
        bi = state["wb"] % 3
        state["ws"] += 1
        state["wb"] += 1
        P.dma("sp", wstage[si][:, 0:X], wst[l, WIDX[name], :, 0:X], f"ws{si}")
        P.copy("pool", wbf[bi][:, 0:X], wstage[si][:, 0:X])
        return wbf[bi][:, 0:X]

    def psb(n=4, base=0):
        k = ("ps", base, n)
        state[k] = state.get(k, 0) + 1
        return base + (state[k] - 1) % n

    def dump(name, ap, shape):
        if dbg is None or name not in dbg:
            return
        t = nc.dram_tensor("dbg_" + name, list(shape), ap.dtype, kind="ExternalOutput").ap()
        dbg_outs[name] = t
        P.dma("sp", t, ap, "dbgout", final=True)

    P.dma("sp", cst[:], cst_in, "cstin")
    P.copy("dve", cbf[:], cst[:, 0:512])
    P.memset("dve", epsc, EPS)
    P.memset("dve", onec, 1.0)

    def rmsnorm_to_hT(gcol0, o_scr):
        sq = [V(o_scr + i * KB, [128, 512], BF16) for i in range(2)]
        lnt = V(o_scr + 2 * KB, [128, 512], F32)
        rstd = V(o_scr + 4 * KB, [128, 512], F32)
        for tt in range(NTT):
            ts_ = slice(tt * 512, (tt + 1) * 512)
            pb = bank(psb(2, 6))
            for c in range(NCH):
                s_ = sq[c % 2]
                P.act(s_, xT[:, c, ts_], AF.Square)
                P.mm(pb, ones_b, s_, start=(c == 0), stop=(c == NCH - 1))
            P.act(lnt, pb, AF.Ln, bias=epsc, scale=1.0 / D)
            P.act(rstd, lnt, AF.Exp, scale=-0.5)
            for c in range(NCH):
                P.stt("dve", hT[:, c, ts_], xT[:, c, ts_], prm[:, gcol0 + c:gcol0 + c + 1], rstd,
                      ALU.mult, ALU.mult)

    def proj_fm(wt, M, m0, consumer, kchunks=NCH, rhs_of=None, tts=range(NTT)):
        w3 = wt.rearrange("p (k c) -> p k c", k=kchunks)
        for tt in tts:
            ts_ = slice(tt * 512, (tt + 1) * 512)
            pb = bank(psb(4, 0), M)
            for k in range(kchunks):
                rhs = hT[:, k, ts_] if rhs_of is None else rhs_of(k, ts_)
                P.mm(pb, w3[:, k, m0:m0 + M], rhs, start=(k == 0), stop=(k == kchunks - 1))
            consumer(tt, ts_, pb)

    def layer(l):
        P.dma("sp", prm[:], prm_in[l], "prmin")
        P.ts("dve", qgs[0:64], prm[0:64, P_FQG:P_FQG + 1], 0.125, None, op0=ALU.mult)
        P.ts("dve", negb[0:8], prm[0:8, P_BF:P_BF + 1], -1.0, None, op0=ALU.mult)
        P.act(negA[32:36], prm[32:36, P_ALOG:P_ALOG + 1], AF.Exp)
        P.ts("dve", negA[32:36], negA[32:36], -1.0, None, op0=ALU.mult)

        rmsnorm_to_hT(P_GMIX, O_SCR)
        dump(f"h{l}", hT[:, :, :], [128, NCH, S])
        for c in range(NCH):
            P.dma("sp", scrx[:, c, :], xT[:, c, :], f"spill{c}")
        if stop_after == "norm1":
            return

        wt = wload(l, "small", 8 * 96)

        def small_cons(tt, ts_, pb):
            P.act(smT[0:8, ts_], pb[0:8, :], AF.Exp, bias=negb[0:8], scale=-1.0)
            P.act(smT[32:36, ts_], pb[32:36, :], AF.Exp, bias=prm[32:36, P_DTB:P_DTB + 1], scale=1.0)
            P.act(smT[64:68, ts_], pb[64:68, :], AF.Exp, scale=-1.0)
        proj_fm(wt, 96, 0, small_cons)
        P.act(smT[0:8, :], smT[0:8, :], AF.Ln, bias=onec[0:8], scale=1.0)
        P.act(smT[32:36, :], smT[32:36, :], AF.Ln, bias=onec[32:36], scale=1.0)
        P.ts("dve", smT[64:68, :], smT[64:68, :], 1.0, None, op0=ALU.add)
        P.add("dve", lambda e: e.reciprocal(smT[64:68, :], smT[64:68, :]), reads=[smT[64:68, :]],
              writes=[smT[64:68, :]])
        aux = V(O_SCR + 16 * KB, [128, S], F32)
        P.scan(aux[0:8, :], onec[0:8].to_broadcast([8, S]), smT[0:8, :], 0.0, ALU.mult, ALU.subtract)
        cqf = aux[0:8, :]
        P.ts("dve", smT[32:36, :], smT[32:36, :], negA[32:36], None, op0=ALU.mult)
        dump(f"dn_g{l}", smT[32:36, :], [4, S])
        dump(f"dn_beta{l}", smT[64:68, :], [4, S])
        P.scan(aux[32:36, :], onec[32:36].to_broadcast([4, S]), smT[32:36, :], 0.0, ALU.mult, ALU.add)
        a3 = aux[32:36, :].rearrange("p (n c) -> p n c", c=64)
        g3 = smT[32:36, :].rearrange("p (n c) -> p n c", c=64)
        gl = V(O_SCR, [128, 32], F32)
        P.copy("dve", gl[32:36, :], a3[:, :, 63])
        P.copy("dve", g3[:, 0, :], a3[:, 0, :])
        P.tt("dve", g3[:, 1:32, :], a3[:, 1:32, :], gl[32:36, 0:31].unsqueeze(2).to_broadcast([4, 31, 64]),
             ALU.subtract)
        dump(f"cq{l}", cqf, [8, S])
        dump(f"gcum{l}", smT[32:36, :], [4, S])

        o = O_XT
        cqs = V(o, [8, 3, S], BF16); o += 12 * KB
        vext = V(o, [128, 16, 8, 65], BF16); o += 17 * KB
        qa = [V(o + i * 8 * KB, [128, S], BF16) for i in range(2)]
        ka = [V(o + i * 8 * KB + 4 * KB, [128, S], BF16) for i in range(2)]
        o += 16 * KB
        ytok = V(o, [128, 16, 128], BF16); o += 4 * KB
        ptile = [V(o + i * KB, [128, 512], BF16) for i in range(3)]; o += 3 * KB
        sqh = [V(o + i * KB, [64, 512], BF16) for i in range(2)]; o += 2 * KB
        lnh = V(o, [64, 512], F32); o += 2 * KB
        rsh = V(o, [64, 512], F32); o += 2 * KB
        rcp = V(o, [128, 4], F32); o += 128
        assert o <= 64 * KB
        cr = V(O_SCR + 8 * KB, [8, S], F32)
        P.copy("dve", cqs[:, 0, :], cqf)
        P.tt("dve", cr, cqf, cqs[:, 0, :], ALU.subtract)
        P.copy("dve", cqs[:, 1, :], cr)
        P.tt("dve", cr, cr, cqs[:, 1, :], ALU.subtract)
        P.copy("dve", cqs[:, 2, :], cr)
        P.memset("pool", vext[:, :, :, 64:65], 1.0)
        for ct in range(4):
            wt = wload(l, f"foxv{ct}", 1024)
            w3 = wt.rearrange("p (k c) -> p k c", k=NCH)
            for g4 in range(4):
                pb = bank(psb(4, 0))
                for t4 in range(4):
                    tb = g4 * 4 + t4
                    for k in range(NCH):
                        P.mm(pb[:, t4 * 128:(t4 + 1) * 128], hT[:, k, tb * 128:(tb + 1) * 128], w3[:, k, :],
                             start=(k == 0), stop=(k == NCH - 1))
                P.copy("act", vext[:, g4 * 4:(g4 + 1) * 4, 2 * ct:2 * ct + 2, 0:64],
                       pb.rearrange("p (a b c) -> p a b c", a=4, b=2))
        for h in range(8):
            wt = wload(l, f"foxqk{h}", 1024)
            qa_h, ka_h = qa[h % 2], ka[h % 2]
            P.memset("pool", qa_h[64:70, :], -1.0)
            P.memset("pool", ka_h[64:70, :], 1.0)
            P.dma("sp", qa_h[64:67, :], cqs[h:h + 1, :, :], f"augq{h % 2}")
            P.dma("sp", ka_h[67:70, :], cqs[h:h + 1, :, :], f"augk{h % 2}")
            for which, dst, m0, gcol in ((0, qa_h, 0, qgs), (1, ka_h, 64, prm[:, P_FKG:P_FKG + 1])):
                def qk_cons(tt, ts_, pb, dst=dst, gcol=gcol):
                    s_ = sqh[tt % 2]
                    P.act(s_, pb, AF.Square)
                    p2 = bank(psb(2, 6), 64)
                    P.mm(p2, ones_b[0:64, 0:64], s_)
                    P.act(lnh, p2, AF.Ln, bias=epsc[0:64], scale=1.0 / 64)
                    P.act(rsh, lnh, AF.Exp, scale=-0.5)
                    P.stt("dve", dst[0:64, ts_], pb, gcol[0:64], rsh, ALU.mult, ALU.mult)
                proj_fm(wt, 64, m0, qk_cons)
            for qt in range(4):
                ob = bank(4 + qt % 2)
                o4 = ob.rearrange("p (a b) -> p a b", a=4)
                for kb in range(4 * qt + 4):
                    n0 = max(kb * 128, qt * 512)
                    ncols = (qt + 1) * 512 - n0
                    pb = bank(psb(4, 0))
                    P.mm(pb[:, 0:ncols], ka_h[0:70, kb * 128:(kb + 1) * 128], qa_h[0:70, n0:n0 + ncols])
                    pt = ptile[state["ps"] % 3]
                    state["ps"] += 1
                    P.act(pt[:, 0:ncols], pb[:, 0:ncols], AF.Exp)
                    if kb * 128 >= qt * 512:
                        P.tt("pool", pt[:, 0:128], pt[:, 0:128], mask01_b, ALU.mult)
                    for qb in range(n0 // 128, 4 * qt + 4):
                        c0 = qb * 128 - n0
                        P.mm(o4[:, qb - 4 * qt, 0:65], pt[:, c0:c0 + 128], vext[:, kb, h, :],
                             start=(kb == 0 and qb == 4 * qt), stop=(kb == qb))
                P.add("dve", lambda e, o4=o4: e.reciprocal(rcp[:, :].unsqueeze(2), o4[:, :, 64:65]),
                      reads=[o4[:, :, 64:65]], writes=[rcp[:, :]])
                P.tt("dve", ytok[:, 4 * qt:4 * qt + 4, (h % 2) * 64:(h % 2) * 64 + 64], o4[:, :, 0:64],
                     rcp[:, :].unsqueeze(2).to_broadcast([128, 4, 64]), ALU.mult)
            if h % 2 == 1:
                for half in range(2):
                    tp = bank(6 + half, 128, BF16)
                    for i in range(8):
                        qb = half * 8 + i
                        P.transpose(tp[:, i * 128:(i + 1) * 128], ytok[:, qb, :], ident_b)
                    P.copy("act", yT[:, h // 2, half * 1024:(half + 1) * 1024], tp)
        dump(f"y_fox{l}", yT[:, 0:4, :], [128, 4, S])
        if stop_after == "fox":
            return

        o = O_XT
        cv = V(o, [128, S + 2], F32); o += 8 * KB + 128
        acc = V(o, [128, S], F32); o += 8 * KB
        bsb = V(o, [128, S], F32); o += 8 * KB
        ctmp = [V(o + i * 2 * KB, [128, 512], F32) for i in range(2)]; o += 4 * KB
        P.memset("pool", cv[:, 0:2], 0.0)
        for j in range(4):
            wb_ = wload(l, f"scb{j}", 1024)
            proj_fm(wb_, 128, 0, lambda tt, ts_, pb: P.copy("act", bsb[:, ts_], pb))
            wc_ = wload(l, f"scc{j}", 1024)
            wv_ = wload(l, f"scv{j}", 1024)
            w3c = wc_.rearrange("p (k c) -> p k c", k=NCH)
            w3v = wv_.rearrange("p (k c) -> p k c", k=NCH)
            for tt in range(NTT):
                ts_ = slice(tt * 512, (tt + 1) * 512)
                pc = bank(psb(4, 0))
                for k in range(NCH):
                    P.mm(pc, w3c[:, k, :], hT[:, k, ts_], start=(k == 0), stop=(k == NCH - 1))
                P.copy("act", ctmp[tt % 2], pc)
                pv = bank(psb(4, 0))
                for k in range(NCH):
                    P.mm(pv, w3v[:, k, :], hT[:, k, ts_], start=(k == 0), stop=(k == NCH - 1))
                P.tt("dve", cv[:, 2 + tt * 512:2 + (tt + 1) * 512], pv, ctmp[tt % 2], ALU.mult)
            w0 = prm[:, P_SCW + j * 3:P_SCW + j * 3 + 1]
            w1 = prm[:, P_SCW + j * 3 + 1:P_SCW + j * 3 + 2]
            w2 = prm[:, P_SCW + j * 3 + 2:P_SCW + j * 3 + 3]
            P.ts("pool", acc, cv[:, 2:2 + S], w2, None, op0=ALU.mult)
            P.stt("dve", acc, cv[:, 1:1 + S], w1, acc, ALU.mult, ALU.add)
            P.stt("dve", acc, cv[:, 0:S], w0, acc, ALU.mult, ALU.add)
            P.tt("pool", yT[:, 4 + j, :], acc, bsb, ALU.mult)
        dump(f"y_sc{l}", yT[:, 4:8, :], [128, 4, S])
        if stop_after == "sc":
            return

        gtok = V(O_SCR, [64, 32, 4], F32)
        btok = V(O_SCR + 512, [64, 32, 4], F32)
        for src0, dst in ((32, gtok), (64, btok)):
            pb = bank(psb(2, 6), 64)
            p3 = pb[:, 0:128].rearrange("p (n h) -> p n h", h=4)
            for n in range(32):
                P.transpose(p3[:, n, :], smT[src0:src0 + 4, n * 64:(n + 1) * 64],
                            ident_f[src0:src0 + 4, src0:src0 + 4])
            P.copy("dve", dst, p3)
        o = O_XT
        raw = V(o, [128, S + 3], F32); o += 8 * KB + 128
        cacc = V(o, [128, S], F32); o += 8 * KB
        QT = V(o, [128, S], BF16); o += 4 * KB
        KT = V(o, [128, S], BF16); o += 4 * KB
        VT = V(o, [128, S], BF16); o += 4 * KB
        Kg = V(o, [64, 32, 128], BF16); o += 8 * KB
        Kd = V(o, [64, 32, 128], BF16); o += 8 * KB
        Vb = V(o, [64, 32, 128], BF16); o += 8 * KB
        qdT = V(o, [128, S], BF16); o += 4 * KB
        nkcT = V(o, [128, 32, 64], BF16); o += 4 * KB
        assert o <= 64 * KB, o
        o = O_SCR + 1 * KB
        cm = [V(o + i * 4 * KB, [64, 32, 64], BF16) for i in range(5)]; o += 20 * KB
        Mm, MTm, PTm, qkT, Mn = cm
        MTn = V(o, [64, 32, 64], BF16); o += 4 * KB
        PTn = V(o, [64, 32, 64], BF16); o += 4 * KB
        assert o <= SCR_END, o
        GB = V(O_XT, [128, S], F32)
        Xm = V(O_XT + 8 * KB + 128, [64, 32, 64], F32)
        sq2 = [V(O_SCR + 29 * KB + i * KB, [128, 512], BF16) for i in range(2)]
        eg = misc[0:64, 8:40]; bgc = misc[0:64, 40:72]; edc = misc[0:64, 72:104]
        gtot = misc[:, 104:136]; nbt = misc[0:64, 136:168]

        for h in range(4):
            for which, nm, dstT, scl in ((0, "dnq", QT, 128.0 ** -0.5), (1, "dnk", KT, 1.0), (2, "dnv", VT, None)):
                wt = wload(l, f"{nm}{h}", 1024)
                if which == 0:
                    P.memset("pool", raw[:, 0:3], 0.0)
                proj_fm(wt, 128, 0, lambda tt, ts_, pb: P.copy("act", raw[:, 3 + tt * 512:3 + (tt + 1) * 512], pb))
                cw = P_DNW + (which * 4 + h) * 4
                P.ts("pool", cacc, raw[:, 3:3 + S], prm[:, cw + 3:cw + 4], None, op0=ALU.mult)
                for j in range(3):
                    P.stt("dve", cacc, raw[:, j:j + S], prm[:, cw + j:cw + j + 1], cacc, ALU.mult, ALU.add)
                if scl is None:
                    P.act(dstT, cacc, AF.Silu)
                else:
                    P.act(cacc, cacc, AF.Silu)
                    for tt in range(NTT):
                        ts_ = slice(tt * 512, (tt + 1) * 512)
                        s_ = sq2[tt % 2]
                        P.act(s_, cacc[:, ts_], AF.Square)
                        p2 = bank(psb(2, 6))
                        P.mm(p2, ones_b, s_)
                        lnt = V(O_SCR + 25 * KB, [128, 512], F32)
                        rst = V(O_SCR + 27 * KB, [128, 512], F32)
                        P.act(lnt, p2, AF.Ln, bias=epsc, scale=1.0)
                        P.act(rst, lnt, AF.Exp, scale=-0.5)
                        P.stt("dve", dstT[:, ts_], cacc[:, ts_], scl, rst, ALU.mult, ALU.mult)
            if h == 0:
                dump(f"dn_q{l}", QT, [128, S]); dump(f"dn_k{l}", KT, [128, S]); dump(f"dn_v{l}", VT, [128, S])
            for tt in range(NTT):
                ts_ = slice(tt * 512, (tt + 1) * 512)
                pb = bank(psb(4, 0))
                P.mm(pb, sel_f[32:36, h * 128:(h + 1) * 128], smT[32:36, ts_])
                P.copy("act", GB[:, ts_], pb)
                egt = V(O_SCR + 25 * KB, [128, 512], F32)
                P.act(egt, pb, AF.Exp)
                P.tt("dve", qdT[:, ts_], QT[:, ts_], egt, ALU.mult)
            GB3 = GB.rearrange("p (n c) -> p n c", c=64)
            P.act(eg, gtok[:, :, h], AF.Exp)
            P.tt("dve", bgc, btok[:, :, h], eg, ALU.mult)
            P.tt("dve", edc, GB3[0:64, :, 63], gtok[:, :, h], ALU.subtract)
            P.act(edc, edc, AF.Exp)
            P.act(gtot, GB3[:, :, 63], AF.Exp)
            P.ts("dve", nbt, btok[:, :, h], -1.0, None, op0=ALU.mult)
            for g8 in range(4):
                for src, outs in ((KT, ((Kg, bgc), (Kd, edc))), (VT, ((Vb, btok[:, :, h]),))):
                    tp = bank(psb(2, 6), 64, BF16)
                    for i in range(8):
                        n = g8 * 8 + i
                        P.transpose(tp[:, i * 128:(i + 1) * 128], src[:, n * 64:(n + 1) * 64], ident_b)
                    t3 = tp.rearrange("p (n d) -> p n d", d=128)
                    for (dst, col) in outs:
                        P.tt("dve", dst[:, g8 * 8:(g8 + 1) * 8, :], t3,
                             col[:, g8 * 8:(g8 + 1) * 8].unsqueeze(2).to_broadcast([64, 8, 128]), ALU.mult)
            P.tt("dve", Xm, GB3[0:64, :, :], gtok[:, :, h].unsqueeze(2).to_broadcast([64, 32, 64]), ALU.subtract)
            DLn = V(O_SCR + 21 * KB, [64, 32, 64], F32)
            P.tt("dve", DLn, Xm, mposA.unsqueeze(1).to_broadcast([64, 32, 64]), ALU.add)
            P.act(DLn, DLn, AF.Exp, scale=-1.0)
            P.tt("dve", DLn, DLn, nbt[:, :].unsqueeze(2).to_broadcast([64, 32, 64]), ALU.mult)
            P.tt("dve", Xm, Xm, mposB.unsqueeze(1).to_broadcast([64, 32, 64]), ALU.subtract)
            P.act(Xm, Xm, AF.Exp)
            for g8 in range(4):
                pk = bank(psb(4, 0), 64)
                pq = bank(psb(4, 0), 64)
                for i in range(8):
                    n = g8 * 8 + i
                    cs = slice(n * 64, (n + 1) * 64)
                    P.mm(pk[:, i * 64:(i + 1) * 64], KT[:, cs], KT[:, cs])
                    P.mm(pq[:, i * 64:(i + 1) * 64], KT[:, cs], QT[:, cs])
                gs = slice(g8 * 8, (g8 + 1) * 8)
                P.tt("dve", Mm[:, gs, :], pk.rearrange("p (n c) -> p n c", c=64), DLn[:, gs, :], ALU.mult)
                P.tt("dve", qkT[:, gs, :], pq.rearrange("p (n c) -> p n c", c=64), Xm[:, gs, :], ALU.mult)
            for g8 in range(4):
                tp = bank(psb(2, 6), 64, BF16)
                for i in range(8):
                    n = g8 * 8 + i
                    P.transpose(tp[:, i * 64:(i + 1) * 64], Mm[:, n, :], ident_b[0:64, 0:64])
                P.copy("act", MTm[:, g8 * 8:(g8 + 1) * 8, :], tp[:, 0:512].rearrange("p (n c) -> p n c", c=64))
            P.tt("pool", PTm, MTm, ident_b[0:64, 0:64].unsqueeze(1).to_broadcast([64, 32, 64]), ALU.add)
            Wc, WTc, PTc = Mm, MTm, PTm
            Wn, WTn, PTx = Mn, MTn, PTn
            for it in range(5):
                for g8 in range(4):
                    pw = bank(psb(4, 0), 64)
                    pwt = bank(psb(4, 0), 64) if it < 4 else None
                    for i in range(8):
                        n = g8 * 8 + i
                        P.mm(pw[:, i * 64:(i + 1) * 64], WTc[:, n, :], Wc[:, n, :])
                        if pwt is not None:
                            P.mm(pwt[:, i * 64:(i + 1) * 64], Wc[:, n, :], WTc[:, n, :])
                    gs = slice(g8 * 8, (g8 + 1) * 8)
                    P.copy("act", Wn[:, gs, :], pw.rearrange("p (n c) -> p n c", c=64))
                    if pwt is not None:
                        P.copy("dve", WTn[:, gs, :], pwt.rearrange("p (n c) -> p n c", c=64))
                    pp = bank(psb(4, 0), 64)
                    for i in range(8):
                        n = g8 * 8 + i
                        P.mm(pp[:, i * 64:(i + 1) * 64], ident_b[0:64, 0:64], PTc[:, n, :], start=True, stop=False)
                        P.mm(pp[:, i * 64:(i + 1) * 64], Wn[:, n, :], PTc[:, n, :], start=False, stop=True)
                    P.copy("act", PTx[:, gs, :], pp.rearrange("p (n c) -> p n c", c=64))
                Wc, Wn = Wn, Wc
                WTc, WTn = WTn, WTc
                PTc, PTx = PTx, PTc
            TT = PTc
            for g8 in range(4):
                pb = bank(psb(4, 0))
                for i in range(8):
                    n = g8 * 8 + i
                    P.mm(pb[:, i * 64:(i + 1) * 64], Kg[:, n, :], TT[:, n, :])
                P.act(nkcT[:, g8 * 8:(g8 + 1) * 8, :], pb.rearrange("p (n c) -> p n c", c=64), AF.Copy, scale=-1.0)
            o3 = O_SCR + 29 * KB
            Sf = V(o3, [128, 128], F32)
            Sbb = [V(o3 + 512 + i * 256, [128, 128], BF16) for i in range(2)]
            vn = [V(o3 + 1024 + i * 256, [64, 128], BF16) for i in range(2)]
            oT = V(O_XT + 8 * KB + 128, [128, S], F32)
            P.memset("pool", Sf, 0.0)
            P.memset("pool", Sbb[0], 0.0)
            for n in range(32):
                sb_ = Sbb[n % 2]
                pv = bank(4, 64)[:, (n % 2) * 128:(n % 2) * 128 + 128]
                P.mm(pv, TT[:, n, :], Vb[:, n, :], start=True, stop=False)
                P.mm(pv, nkcT[:, n, :], sb_, start=False, stop=True)
                vn_ = vn[n % 2]
                P.copy("act", vn_, pv)
                if n % 8 == 0:
                    po = bank(psb(2, 2))
                pos = po[:, (n % 8) * 64:(n % 8 + 1) * 64]
                P.mm(pos, sb_, qdT[:, n * 64:(n + 1) * 64], start=True, stop=False)
                P.mm(pos, vn_, qkT[:, n, :], start=False, stop=True)
                pd = bank(5)[:, (n % 2) * 128:(n % 2) * 128 + 128]
                P.mm(pd, Kd[:, n, :], vn_)
                P.stt("dve", Sf, Sf, gtot[:, n:n + 1], pd, ALU.mult, ALU.add)
                P.copy("pool", Sbb[(n + 1) % 2], Sf)
                if n % 8 == 7:
                    tt = n // 8
                    P.copy("act", oT[:, tt * 512:(tt + 1) * 512], po)
            if h == 0:
                dump(f"o_dn{l}", oT, [128, S])
            wz = wload(l, f"dnz{h}", 1024)

            def z_cons(tt, ts_, pb, h=h):
                s_ = sq2[tt % 2]
                P.act(s_, oT[:, ts_], AF.Square)
                p2 = bank(psb(2, 6))
                P.mm(p2, ones_b, s_)
                lnt = V(O_SCR + 25 * KB, [128, 512], F32)
                rst = V(O_SCR + 27 * KB, [128, 512], F32)
                P.act(lnt, p2, AF.Ln, bias=epsc, scale=1.0 / 128)
                P.act(rst, lnt, AF.Exp, scale=-0.5)
                P.stt("dve", rst, oT[:, ts_], prm[:, P_DNG:P_DNG + 1], rst, ALU.mult, ALU.mult)
                P.act(lnt, pb, AF.Silu)
                P.tt("dve", yT[:, 8 + h, ts_], rst, lnt, ALU.mult)
            proj_fm(wz, 128, 0, z_cons)
        dump(f"y_dn{l}", yT[:, 8:12, :], [128, 4, S])
        if stop_after == "dn":
            return

        mT = V(O_SCR, [128, NCH, 1024], BF16)
        macc = V(O_SCR + 16 * KB, [128, 512], F32)
        mtmp = V(O_SCR + 18 * KB, [128, 512], F32)
        sgt = [[V(O_XT + b * 8 * KB + 4 * KB + t2 * 2 * KB, [128, 512], F32) for t2 in range(2)] for b in range(3)]
        for half in range(2):
            for dc in range(NCH):
                for b in range(3):
                    wgb = wload(l, f"g{b}_{dc}", 1024).rearrange("p (k c) -> p k c", k=NCH)
                    for t2 in range(2):
                        tt = half * 2 + t2
                        ts_ = slice(tt * 512, (tt + 1) * 512)
                        pg = bank(psb(4, 0))
                        for k in range(NCH):
                            P.mm(pg, wgb[:, k, :], hT[:, k, ts_], start=(k == 0), stop=(k == NCH - 1))
                        P.act(sgt[b][t2], pg, AF.Sigmoid)
                wA3 = wload(l, f"brA{dc}", 1024).rearrange("p (b c m) -> p b c m", b=2, c=4)
                wB3 = wload(l, f"brB{dc}", 512).rearrange("p (c m) -> p c m", c=4)
                for t2 in range(2):
                    tt = half * 2 + t2
                    ts_ = slice(tt * 512, (tt + 1) * 512)
                    for b in range(3):
                        pbr = bank(psb(4, 0))
                        for c in range(4):
                            lw = wA3[:, b, c, :] if b < 2 else wB3[:, c, :]
                            P.mm(pbr, lw, yT[:, 4 * b + c, ts_], start=(c == 0), stop=(c == 3))
                        if b == 0:
                            P.tt("dve", macc, pbr, sgt[b][t2], ALU.mult)
                        elif b == 1:
                            P.tt("dve", mtmp, pbr, sgt[b][t2], ALU.mult)
                            P.tt("pool", macc, macc, mtmp, ALU.add)
                        else:
                            P.tt("dve", mtmp, pbr, sgt[b][t2], ALU.mult)
                            P.tt("pool", mT[:, dc, t2 * 512:(t2 + 1) * 512], macc, mtmp, ALU.add)
            if half == 0:
                dump(f"merged{l}", mT[:, :, :], [128, NCH, 1024])
            for dc in range(NCH):
                wo = wload(l, f"wo{dc}", 1024)
                hs = slice(half * 1024, (half + 1) * 1024)
                P.dma("sp", xT[:, dc, hs], scrx[:, dc, hs], f"unsp{dc}")

                def wo_cons(tt, ts_, pb, dc=dc):
                    P.tt("dve", xT[:, dc, ts_], xT[:, dc, ts_], pb, ALU.add)
                proj_fm(wo, 128, 0, wo_cons, rhs_of=lambda k, ts_, half=half: mT[:, k, ts_.start - half * 1024: ts_.stop - half * 1024],
                        tts=range(half * 2, half * 2 + 2))
        dump(f"x_mix{l}", xT[:, :, :], [128, NCH, S])
        if stop_after == "mix":
            return

        rmsnorm_to_hT(P_GFFN, O_SCR)
        o = O_YT
        graw = V(o, [128, S + 2], F32); o += 8 * KB + 128
        vraw = V(o, [128, S + 2], F32); o += 8 * KB + 128
        gac = V(o, [128, S], F32); o += 8 * KB
        vac = V(o, [128, S], F32); o += 8 * KB
        assert o <= O_WS
        aT0 = V(O_SCR + 6 * KB, [128, GRP, S], BF16)
        aT1 = V(O_YT + 33 * KB, [128, 3, S], BF16)
        aT1b = V(O_SCR + 22 * KB, [128, 1, S], BF16)
        P.memset("pool", graw[:, 0:2], 0.0)
        P.memset("pool", vraw[:, 0:2], 0.0)

        def a_slot(gi, jj):
            if gi % 2 == 0:
                return aT0[:, jj, :]
            return aT1[:, jj, :] if jj < 3 else aT1b[:, 0, :]

        for gi, (j0, n) in enumerate(_ffn_groups()):
            for jj in range(n):
                j = j0 + jj
                for nm, rawb, accb, c0 in (("upg", graw, gac, j), ("upv", vraw, vac, NFF + j)):
                    wt = wload(l, f"{nm}{j}", 1024)
                    w2c = prm[:, P_FFW + c0 * 3 + 2:P_FFW + c0 * 3 + 3]

                    def up_cons(tt, ts_, pb, rawb=rawb, accb=accb, w2c=w2c):
                        P.copy("act", rawb[:, 2 + tt * 512:2 + (tt + 1) * 512], pb)
                        P.act(accb[:, ts_], pb, AF.Copy, scale=w2c)
                    proj_fm(wt, 128, 0, up_cons)
                    P.stt("dve", accb, rawb[:, 1:1 + S], prm[:, P_FFW + c0 * 3 + 1:P_FFW + c0 * 3 + 2], accb,
                          ALU.mult, ALU.add)
                    P.stt("dve", accb, rawb[:, 0:S], prm[:, P_FFW + c0 * 3:P_FFW + c0 * 3 + 1], accb,
                          ALU.mult, ALU.add)
                P.act(gac, gac, AF.Silu)
                P.tt("pool", a_slot(gi, jj), gac, vac, ALU.mult)
            for q in range(4):
                wd = wload(l, f"dn{gi}_{q}", n * 256)
                wd3 = wd.rearrange("p (j c) -> p j c", j=n)
                for dd in range(2):
                    dc = 2 * q + dd
                    for tt in range(NTT):
                        ts_ = slice(tt * 512, (tt + 1) * 512)
                        pb = bank(psb(4, 0))
                        for jj in range(n):
                            P.mm(pb, wd3[:, jj, dd * 128:(dd + 1) * 128], a_slot(gi, jj)[:, ts_],
                                 start=(jj == 0), stop=(jj == n - 1))
                        P.tt("dve", xT[:, dc, ts_], xT[:, dc, ts_], pb, ALU.add)
        dump(f"x_ffn{l}", xT[:, :, :], [128, NCH, S])
        if stop_after == "ffn":
            return

        rmsnorm_to_hT(P_GPLE, O_SCR)
        ptok = V(O_YT, [128, 16, 256], F32)
        pT = V(O_YT + 16 * KB, [128, 2, S], BF16)
        sgp = [V(O_YT + 24 * KB + i * 2 * KB, [128, 512], F32) for i in range(2)]
        ptmp = V(O_YT + 28 * KB, [128, 512], F32)
        for q in range(4):
            P.dma("sp", ptok[:, q * 4:(q + 1) * 4, :],
                  p_in[l, q * 512:(q + 1) * 512, :].rearrange("(a p) c -> p a c", p=128), f"pin{q}")
        for c in range(2):
            for g4 in range(4):
                tp = bank(psb(2, 6))
                for i in range(4):
                    tb = g4 * 4 + i
                    P.transpose(tp[:, i * 128:(i + 1) * 128], ptok[:, tb, c * 128:(c + 1) * 128], ident_f)
                P.copy("act", pT[:, c, g4 * 512:(g4 + 1) * 512], tp)
        for dc in range(NCH):
            wpg = wload(l, f"pg{dc}", 1024)
            wpl = wload(l, f"pl{dc}", 256)
            wpl3 = wpl.rearrange("p (c m) -> p c m", c=2)

            def pg_cons(tt, ts_, pb, dc=dc, wpl3=wpl3):
                s_ = sgp[tt % 2]
                P.act(s_, pb, AF.Sigmoid)
                pp = bank(psb(2, 4))
                for c in range(2):
                    P.mm(pp, wpl3[:, c, :], pT[:, c, ts_], start=(c == 0), stop=(c == 1))
                P.tt("dve", ptmp, pp, s_, ALU.mult)
                P.tt("pool", xT[:, dc, ts_], xT[:, dc, ts_], ptmp, ALU.add)
            proj_fm(wpg, 128, 0, pg_cons)
        dump(f"x_out{l}", xT[:, :, :], [128, NCH, S])

    xin = [V(O_YT + i * 4 * KB, [128, D], F32) for i in range(2)]
    for tb in range(16):
        xi = xin[tb % 2]
        P.dma("sp", xi, x_in[tb * 128:(tb + 1) * 128, :], f"xin{tb % 2}")
        for g2 in range(2):
            tp = bank(psb(2, 6))
            for i in range(4):
                c = g2 * 4 + i
                P.transpose(tp[:, i * 128:(i + 1) * 128], xi[:, c * 128:(c + 1) * 128], ident_f)
            eng = "act" if g2 == 0 else "dve"
            P.copy(eng, xT[:, g2 * 4:(g2 + 1) * 4, tb * 128:(tb + 1) * 128],
                   tp.rearrange("p (c t) -> p c t", c=4))

    for l in range(depth):
        layer(l)

    if stop_after is None:
        xo = [V(O_YT + i * 4 * KB, [128, D], F32) for i in range(2)]
        for tb in range(16):
            xo_ = xo[tb % 2]
            for g2 in range(2):
                tp = bank(psb(2, 6))
                for i in range(4):
                    c = g2 * 4 + i
                    P.transpose(tp[:, i * 128:(i + 1) * 128], xT[:, c, tb * 128:(tb + 1) * 128], ident_f)
                eng = "act" if g2 == 0 else "dve"
                P.copy(eng, xo_[:, g2 * 512:(g2 + 1) * 512], tp)
            P.dma("sp", out[tb * 128:(tb + 1) * 128, :], xo_, f"xout{tb % 2}", final=True)
    else:
        z = V(O_SCR, [128, 8], F32)
        P.memset("pool", z, 0.0)
        P.dma("sp", out[0:128, 0:8], z, "xout0", final=True)
    P.emit()
    return nc, dbg_outs, P


_CACHE = {}


def kernel(**inputs):
    inp = {k: np.asarray(v) for k, v in inputs.items()}
    if "nc" not in _CACHE:
        _CACHE["nc"] = build()[0]
    nc = _CACHE["nc"]
    wstream = np.stack([pack_weights(inp, l) for l in range(DEPTH)])
    prm = np.stack([pack_params(inp, l) for l in range(DEPTH)])
    cst = make_consts()
    in_maps = []
    for b in range(8):
        in_maps.append({"x": np.ascontiguousarray(inp["x"][b]),
                        "p": np.ascontiguousarray(inp["p"][:, b]),
                        "wst": wstream, "prm": prm, "cst": cst})
    res = run_bass_kernel_spmd(nc, in_maps, core_ids=list(range(8)))
    return np.stack([np.asarray(r["out"]) for r in res.results]).astype(np.float32)
```

```python
import numpy as np
import concourse.bass as bass
import concourse.mybir as mybir

F32 = mybir.dt.float32
BF16 = mybir.dt.bfloat16
ALU = mybir.AluOpType
AF = mybir.ActivationFunctionType

ENGS = ("pe", "act", "dve", "pool", "sp")


def _prod(xs):
    r = 1
    for v in xs:
        r *= int(v)
    return r


def region(ap):
    t = ap.tensor
    es = mybir.dt.size(ap.dtype)
    off = int(ap.offset)
    pat = ap.ap
    if str(ap.space) == "DRAM":
        ext = 0
        for st, cnt in pat:
            ext += (cnt - 1) * abs(st)
        return (t.name, 0, 1, off * es, (off + ext + 1) * es)
    shape = list(t.shape)
    F = _prod(shape[1:])
    p0 = off // F
    f0 = off % F
    pstep, pcnt = pat[0]
    npart = pcnt if pstep != 0 else 1
    ext = 0
    for st, cnt in pat[1:]:
        ext += (cnt - 1) * abs(st)
    return (t.name, p0, p0 + npart, f0 * es, (f0 + ext + 1) * es)


def _untracked(ap):
    return str(ap.space) == "DRAM" and not ap.tensor.name.startswith("scr_")


def _overlap(a, b):
    return a[1] < b[2] and b[1] < a[2] and a[3] < b[4] and b[3] < a[4]


def _covers(a, b):
    return a[1] <= b[1] and a[2] >= b[2] and a[3] <= b[3] and a[4] >= b[4]


class Instr:
    __slots__ = ("eng", "fn", "deps", "signal", "is_dma", "key", "val", "idx")

    def __init__(self, eng, fn, is_dma, key):
        self.eng = eng
        self.fn = fn
        self.deps = set()
        self.signal = False
        self.is_dma = is_dma
        self.key = key
        self.val = 0


class Prog:
    def __init__(self, nc):
        self.nc = nc
        self.instrs = []
        self.wr = {}
        self.rd = {}
        self.final_keys = set()
        self.last_dma = {}

    def add(self, eng, fn, reads=(), writes=(), dma_key=None):
        ins = Instr(eng, fn, dma_key is not None, dma_key if dma_key is not None else eng)
        idx = len(self.instrs)
        ins.idx = idx
        self.instrs.append(ins)
        deps = ins.deps
        for ap in reads:
            if ap is None or isinstance(ap, (int, float)):
                continue
            if _untracked(ap):
                continue
            r = region(ap)
            for (w, wi) in self.wr.get(r[0], ()):
                if _overlap(w, r):
                    deps.add(wi)
        for ap in writes:
            if _untracked(ap):
                continue
            r = region(ap)
            wl = self.wr.setdefault(r[0], [])
            rl = self.rd.setdefault(r[0], [])
            for (w, wi) in wl:
                if _overlap(w, r):
                    deps.add(wi)
            for (q, qi) in rl:
                if _overlap(q, r):
                    deps.add(qi)
            self.wr[r[0]] = [(w, wi) for (w, wi) in wl if not _covers(r, w)]
            self.rd[r[0]] = [(q, qi) for (q, qi) in rl if not _covers(r, q)]
            self.wr[r[0]].append((r, idx))
        for ap in reads:
            if ap is None or isinstance(ap, (int, float)):
                continue
            if _untracked(ap):
                continue
            r = region(ap)
            rl = self.rd.setdefault(r[0], [])
            if dma_key is None:
                rl[:] = [(q, qi) for (q, qi) in rl
                         if not (q == r and self.instrs[qi].eng == eng and not self.instrs[qi].is_dma)]
            rl.append((r, idx))
        if dma_key is not None:
            if dma_key in self.last_dma:
                deps.add(self.last_dma[dma_key])
            self.last_dma[dma_key] = idx
        deps.discard(idx)
        return ins

    def mm(self, out, lhsT, rhs, start=True, stop=True):
        rd = [lhsT, rhs]
        return self.add("pe", lambda e: e.matmul(out, lhsT, rhs, start=start, stop=stop),
                        reads=rd, writes=[out])

    def transpose(self, out, in_, ident):
        return self.add("pe", lambda e: e.transpose(out, in_, ident),
                        reads=[in_, ident], writes=[out])

    def act(self, out, in_, func, bias=None, scale=1.0, accum_out=None, eng="act"):
        kw = {}
        if bias is not None:
            kw["bias"] = bias
        if accum_out is not None:
            kw["accum_out"] = accum_out
        rd = [in_, bias if not isinstance(bias, (int, float)) else None,
              scale if not isinstance(scale, (int, float)) else None]
        wr = [out] + ([accum_out] if accum_out is not None else [])
        return self.add(eng, lambda e: e.activation(out, in_, func, scale=scale, **kw),
                        reads=rd, writes=wr)

    def tt(self, eng, out, in0, in1, op):
        return self.add(eng, lambda e: e.tensor_tensor(out, in0, in1, op),
                        reads=[in0, in1], writes=[out])

    def ts(self, eng, out, in0, s1, s2=None, op0=ALU.mult, op1=None, accum_out=None):
        kw = {}
        if op1 is not None:
            kw["op1"] = op1
        if accum_out is not None:
            kw["accum_out"] = accum_out
        rd = [in0, s1 if not isinstance(s1, (int, float)) else None,
              s2 if not isinstance(s2, (int, float)) else None]
        wr = [out] + ([accum_out] if accum_out is not None else [])
        return self.add(eng, lambda e: e.tensor_scalar(out, in0, s1, s2, op0, **kw),
                        reads=rd, writes=wr)

    def stt(self, eng, out, in0, scalar, in1, op0, op1):
        rd = [in0, in1, scalar if not isinstance(scalar, (int, float)) else None]
        return self.add(eng, lambda e: e.scalar_tensor_tensor(out, in0, scalar, in1, op0, op1),
                        reads=rd, writes=[out])

    def copy(self, eng, out, in_):
        if eng == "act":
            return self.add(eng, lambda e: e.copy(out, in_), reads=[in_], writes=[out])
        return self.add(eng, lambda e: e.tensor_copy(out, in_), reads=[in_], writes=[out])

    def memset(self, eng, out, val):
        return self.add(eng, lambda e: e.memset(out, val), reads=[], writes=[out])

    def scan(self, out, d0, d1, initial, op0, op1):
        rd = [d0, d1, initial if not isinstance(initial, (int, float)) else None]
        return self.add("dve", lambda e: e.tensor_tensor_scan(out, d0, d1, initial, op0, op1),
                        reads=rd, writes=[out])

    def dma(self, eng, out, in_, key, final=False):
        if final:
            self.final_keys.add(key)
        return self.add(eng, lambda e: e.dma_start(out, in_), reads=[in_], writes=[out], dma_key=key)

    def emit(self):
        nc = self.nc
        instrs = self.instrs
        for ins in instrs:
            if ins.eng == "pe" and not ins.is_dma:
                ins.deps = {d for d in ins.deps if not (instrs[d].eng == "pe" and not instrs[d].is_dma)}
            for d in ins.deps:
                instrs[d].signal = True
        keys = []
        for ins in instrs:
            if ins.key not in keys:
                keys.append(ins.key)
        for ins in instrs:
            if ins.is_dma:
                ins.signal = True
        cnt = {k: 0 for k in keys}
        for ins in instrs:
            if ins.signal:
                cnt[ins.key] += 16 if ins.is_dma else 1
            ins.val = cnt[ins.key]
        used = [k for k in keys if cnt[k] > 0]
        self.sem_totals = {k: cnt[k] for k in used}
        import contextlib
        with contextlib.ExitStack() as st:
            sems = {k: st.enter_context(nc.semaphore("s_" + str(k))) for k in used}
            block = st.enter_context(nc.Block())
            per_eng = {e: [i for i in instrs if i.eng == e] for e in ENGS}
            nwaits = [0]

            def run(engobj, lst, is_last_sp=False):
                clock = {}
                for ins in lst:
                    need = {}
                    for d in ins.deps:
                        di = instrs[d]
                        if di.val > need.get(di.key, 0):
                            need[di.key] = di.val
                    for k, v in need.items():
                        if clock.get(k, 0) < v:
                            engobj.wait_ge(sems[k], v)
                            clock[k] = v
                            nwaits[0] += 1
                    bi = ins.fn(engobj)
                    if ins.signal:
                        bi.then_inc(sems[ins.key], 16 if ins.is_dma else 1)
                if is_last_sp:
                    for k in sorted(self.final_keys, key=str):
                        if k in sems:
                            engobj.wait_ge(sems[k], cnt[k])

            @block.tensor
            def _(e):
                run(e, per_eng["pe"])

            @block.scalar
            def _(e):
                run(e, per_eng["act"])

            @block.vector
            def _(e):
                run(e, per_eng["dve"])

            @block.gpsimd
            def _(e):
                run(e, per_eng["pool"])

            @block.sync
            def _(e):
                run(e, per_eng["sp"], True)
            self.nwaits = nwaits[0]

from concourse.bass_utils import run_bass_kernel_spmd
import ml_dtypes

S = 2048
D = 1024
DEPTH = 2
NCH = D // 128
NTT = S // 512
DFF = 2816
NFF = DFF // 128
EPS = 1e-6
GRP = 4
NPRM = 224
NCST = 1536

C_FQ, C_FK, C_FV, C_FF = 0, 512, 1024, 1536
C_SB, C_SC, C_SV = 1544, 2056, 2568
C_DQ, C_DK, C_DV = 3080, 3592, 4104
C_DB, C_DA, C_DZ, C_G = 4616, 4620, 4624, 5136

P_GMIX, P_GFFN, P_GPLE = 0, 8, 16
P_FQG, P_FKG, P_BF, P_ALOG, P_DTB, P_DNG = 24, 25, 26, 27, 28, 29
P_SCW, P_DNW, P_FFW = 30, 42, 90

K_ID, K_M01, K_ONE, K_SEL, K_MPA, K_MPB = 0, 128, 384, 512, 1024, 1088


def _ffn_groups():
    gs = []
    j = 0
    while j < NFF:
        n = min(GRP, NFF - j)
        gs.append((j, n))
        j += n
    return gs


def wtile_index():
    idx = {}
    names = ["small"]
    names += [f"foxv{c}" for c in range(4)]
    names += [f"foxqk{h}" for h in range(8)]
    for j in range(4):
        names += [f"scb{j}", f"scc{j}", f"scv{j}"]
    for h in range(4):
        names += [f"dnq{h}", f"dnk{h}", f"dnv{h}", f"dnz{h}"]
    for dc in range(8):
        names += [f"g0_{dc}", f"g1_{dc}", f"g2_{dc}", f"brA{dc}", f"brB{dc}"]
    names += [f"wo{dc}" for dc in range(8)]
    for j in range(NFF):
        names += [f"upg{j}", f"upv{j}"]
    for gi, (j0, n) in enumerate(_ffn_groups()):
        names += [f"dn{gi}_{q}" for q in range(4)]
    for dc in range(8):
        names += [f"pg{dc}", f"pl{dc}"]
    for i, n in enumerate(names):
        idx[n] = i
    return idx


WIDX = wtile_index()
NT = len(WIDX)


def pack_weights(inp, l):
    W = np.zeros((NT, 128, 1024), np.float32)
    w_in = inp["w_in"][l]

    def kc(cols):
        n = cols.shape[1]
        return cols.reshape(8, 128, n).transpose(1, 0, 2).reshape(128, 8 * n)

    def put(name, arr):
        W[WIDX[name], :, :arr.shape[1]] = arr

    sm = np.zeros((1024, 96), np.float32)
    sm[:, 0:8] = w_in[:, C_FF:C_FF + 8]
    sm[:, 32:36] = w_in[:, C_DA:C_DA + 4]
    sm[:, 64:68] = w_in[:, C_DB:C_DB + 4]
    put("small", kc(sm))
    for c in range(4):
        put(f"foxv{c}", kc(w_in[:, C_FV + c * 128:C_FV + (c + 1) * 128]))
    for h in range(8):
        qk = np.concatenate([w_in[:, C_FQ + h * 64:C_FQ + (h + 1) * 64],
                             w_in[:, C_FK + h * 64:C_FK + (h + 1) * 64]], axis=1)
        put(f"foxqk{h}", kc(qk))
    for j in range(4):
        put(f"scb{j}", kc(w_in[:, C_SB + j * 128:C_SB + (j + 1) * 128]))
        put(f"scc{j}", kc(w_in[:, C_SC + j * 128:C_SC + (j + 1) * 128]))
        put(f"scv{j}", kc(w_in[:, C_SV + j * 128:C_SV + (j + 1) * 128]))
    for h in range(4):
        put(f"dnq{h}", kc(w_in[:, C_DQ + h * 128:C_DQ + (h + 1) * 128]))
        put(f"dnk{h}", kc(w_in[:, C_DK + h * 128:C_DK + (h + 1) * 128]))
        put(f"dnv{h}", kc(w_in[:, C_DV + h * 128:C_DV + (h + 1) * 128]))
        put(f"dnz{h}", kc(w_in[:, C_DZ + h * 128:C_DZ + (h + 1) * 128]))
    wb = inp["w_branch"][l]
    for dc in range(8):
        for b in range(3):
            put(f"g{b}_{dc}", kc(w_in[:, C_G + b * 1024 + dc * 128:C_G + b * 1024 + (dc + 1) * 128]))

        def br(b):
            a = wb[b][:, dc * 128:(dc + 1) * 128]
            return a.reshape(4, 128, 128).transpose(1, 0, 2).reshape(128, 512)
        put(f"brA{dc}", np.concatenate([br(0), br(1)], axis=1))
        put(f"brB{dc}", br(2))
        put(f"wo{dc}", kc(inp["w_o"][l][:, dc * 128:(dc + 1) * 128]))
        put(f"pg{dc}", kc(inp["w_ple_gate"][l][:, dc * 128:(dc + 1) * 128]))
        a = inp["w_ple"][l][:, dc * 128:(dc + 1) * 128]
        put(f"pl{dc}", a.reshape(2, 128, 128).transpose(1, 0, 2).reshape(128, 256))
    w_up = inp["w_up"][l]
    for j in range(NFF):
        put(f"upg{j}", kc(w_up[:, j * 128:(j + 1) * 128]))
        put(f"upv{j}", kc(w_up[:, DFF + j * 128:DFF + (j + 1) * 128]))
    w_dn = inp["w_down"][l]
    for gi, (j0, n) in enumerate(_ffn_groups()):
        for q in range(4):
            a = w_dn[j0 * 128:(j0 + n) * 128, q * 256:(q + 1) * 256]
            put(f"dn{gi}_{q}", a.reshape(n, 128, 256).transpose(1, 0, 2).reshape(128, n * 256))
    return W


def pack_params(inp, l):
    Pm = np.zeros((128, NPRM), np.float32)
    Pm[:, P_GMIX:P_GMIX + 8] = inp["g_mix"][l].reshape(8, 128).T
    Pm[:, P_GFFN:P_GFFN + 8] = inp["g_ffn"][l].reshape(8, 128).T
    Pm[:, P_GPLE:P_GPLE + 8] = inp["g_ple"][l].reshape(8, 128).T
    Pm[0:64, P_FQG] = inp["fox_q_gain"][l]
    Pm[0:64, P_FKG] = inp["fox_k_gain"][l]
    Pm[0:8, P_BF] = inp["b_fox_f"][l]
    Pm[32:36, P_ALOG] = inp["dn_a_log"][l]
    Pm[32:36, P_DTB] = inp["dn_dt_bias"][l]
    Pm[:, P_DNG] = inp["dn_norm_gain"][l]
    Pm[:, P_SCW:P_SCW + 12] = inp["sc_conv_w"][l].reshape(3, 4, 128).transpose(2, 1, 0).reshape(128, 12)
    Pm[:, P_DNW:P_DNW + 48] = inp["dn_conv_w"][l].reshape(4, 12, 128).transpose(2, 1, 0).reshape(128, 48)
    Pm[:, P_FFW:P_FFW + 132] = inp["ffn_conv_w"][l].reshape(3, 44, 128).transpose(2, 1, 0).reshape(128, 132)
    return Pm


def make_consts():
    C = np.zeros((128, NCST), np.float32)
    r = np.arange(128)[:, None]
    c = np.arange(128)[None, :]
    C[:, K_ID:K_ID + 128] = (r == c)
    C[:, K_M01:K_M01 + 128] = (c >= r)
    C[:, K_M01 + 128:K_M01 + 256] = (c > r)
    C[:, K_ONE:K_ONE + 128] = 1.0
    for h in range(4):
        C[32 + h, K_SEL + h * 128:K_SEL + (h + 1) * 128] = 1.0
    r6 = np.arange(64)[:, None]
    c6 = np.arange(64)[None, :]
    C[0:64, K_MPA:K_MPA + 64] = np.where(r6 > c6, 0.0, 1.0e4)
    C[0:64, K_MPB:K_MPB + 64] = np.where(c6 >= r6, 0.0, 1.0e4)
    return C


def build(depth=DEPTH, dbg=None, stop_after=None):
    nc = bass.Bass("TRN2", target_bir_lowering=False)
    x_in = nc.dram_tensor("x", [S, D], F32, kind="ExternalInput").ap()
    p_in = nc.dram_tensor("p", [DEPTH, S, 256], F32, kind="ExternalInput").ap()
    wst = nc.dram_tensor("wst", [DEPTH, NT, 128, 1024], F32, kind="ExternalInput").ap()
    prm_in = nc.dram_tensor("prm", [DEPTH, 128, NPRM], F32, kind="ExternalInput").ap()
    cst_in = nc.dram_tensor("cst", [128, NCST], F32, kind="ExternalInput").ap()
    out = nc.dram_tensor("out", [S, D], F32, kind="ExternalOutput").ap()
    scrx = nc.dram_tensor("scr_x", [128, NCH, S], F32).ap()
    dbg_outs = {}

    TOT = 211456
    SB = nc.alloc_sbuf_tensor("SB", [128, TOT // 4], F32)
    PS = nc.alloc_psum_tensor("PS", [128, 8 * 512], F32)
    P = Prog(nc)

    def V(off, shape, dt=F32, p0=0):
        es = mybir.dt.size(dt)
        n = _prod(shape[1:])
        assert off % 4 == 0 and off + n * es <= TOT, (off, shape)
        nw = (n * es + 3) // 4
        v = SB[p0:p0 + shape[0], off // 4: off // 4 + nw]
        if dt != F32:
            v = v.bitcast(dt)
        v = v[:, 0:n]
        if len(shape) == 3:
            v = v.rearrange("p (a b) -> p a b", a=shape[1])
        elif len(shape) == 4:
            v = v.rearrange("p (a b c) -> p a b c", a=shape[1], b=shape[2])
        return v

    def bank(b, parts=128, dt=F32, p0=0):
        v = PS[p0:p0 + parts, b * 512:(b + 1) * 512]
        if dt != F32:
            v = v.bitcast(dt)
        return v

    KB = 1024
    O_XT = 0
    O_HT = 64 * KB
    O_YT = 96 * KB
    O_WS = 144 * KB
    O_WB = 152 * KB
    O_CST = 158 * KB
    O_CBF = 164 * KB
    O_PRM = 165 * KB
    O_MISC = 166 * KB
    O_SMT = 167 * KB
    O_SCR = 175 * KB
    SCR_END = TOT

    xT = V(O_XT, [128, NCH, S], F32)
    hT = V(O_HT, [128, NCH, S], BF16)
    yT = V(O_YT, [128, 12, S], BF16)
    wstage = [V(O_WS + i * 4 * KB, [128, 1024], F32) for i in range(2)]
    wbf = [V(O_WB + i * 2 * KB, [128, 1024], BF16) for i in range(3)]
    cst = V(O_CST, [128, NCST], F32)
    cbf = V(O_CBF, [128, 512], BF16)
    prm = V(O_PRM, [128, NPRM], F32)
    misc = V(O_MISC, [128, 256], F32)
    smT = V(O_SMT, [128, S], F32)

    ident_f = cst[:, K_ID:K_ID + 128]
    ident_b = cbf[:, 0:128]
    mask01_b = cbf[:, 128:256]
    ones_b = cbf[:, 384:512]
    sel_f = cst[:, K_SEL:K_SEL + 512]
    mposA = cst[0:64, K_MPA:K_MPA + 64]
    mposB = cst[0:64, K_MPB:K_MPB + 64]
    epsc = misc[:, 0:1]
    onec = misc[:, 1:2]
    qgs = misc[:, 2:3]
    negb = misc[:, 3:4]
    negA = misc[:, 4:5]

    state = {"ws": 0, "wb": 0, "ps": 0}

    def wload(l, name, X):
        si = state["ws"] % 2
        bi = state["wb"] % 3
        state["ws"] += 1
        state["wb"] += 1
        P.dma("sp", wstage[si][:, 0:X], wst[l, WIDX[name], :, 0:X], f"ws{si}")
        P.copy("pool", wbf[bi][:, 0:X], wstage[si][:, 0:X])
        return wbf[bi][:, 0:X]

    def psb(n=4, base=0):
        k = ("ps", base, n)
        state[k] = state.get(k, 0) + 1
        return base + (state[k] - 1) % n

    def dump(name, ap, shape):
        if dbg is None or name not in dbg:
            return
        t = nc.dram_tensor("dbg_" + name, list(shape), ap.dtype, kind="ExternalOutput").ap()
        dbg_outs[name] = t
        P.dma("sp", t, ap, "dbgout", final=True)

    P.dma("sp", cst[:], cst_in, "cstin")
    P.copy("dve", cbf[:], cst[:, 0:512])
    P.memset("dve", epsc, EPS)
    P.memset("dve", onec, 1.0)

    def rmsnorm_to_hT(gcol0, o_scr):
        sq = [V(o_scr + i * KB, [128, 512], BF16) for i in range(2)]
        lnt = V(o_scr + 2 * KB, [128, 512], F32)
        rstd = V(o_scr + 4 * KB, [128, 512], F32)
        for tt in range(NTT):
            ts_ = slice(tt * 512, (tt + 1) * 512)
            pb = bank(psb(2, 6))
            for c in range(NCH):
                s_ = sq[c % 2]
                P.act(s_, xT[:, c, ts_], AF.Square)
                P.mm(pb, ones_b, s_, start=(c == 0), stop=(c == NCH - 1))
            P.act(lnt, pb, AF.Ln, bias=epsc, scale=1.0 / D)
            P.act(rstd, lnt, AF.Exp, scale=-0.5)
            for c in range(NCH):
                P.stt("dve", hT[:, c, ts_], xT[:, c, ts_], prm[:, gcol0 + c:gcol0 + c + 1], rstd,
                      ALU.mult, ALU.mult)

    def proj_fm(wt, M, m0, consumer, kchunks=NCH, rhs_of=None, tts=range(NTT), defer=False):
        w3 = wt.rearrange("p (k c) -> p k c", k=kchunks)
        pend = None
        for tt in tts:
            ts_ = slice(tt * 512, (tt + 1) * 512)
            pb = bank(psb(4, 0), M)
            for k in range(kchunks):
                rhs = hT[:, k, ts_] if rhs_of is None else rhs_of(k, ts_)
                P.mm(pb, w3[:, k, m0:m0 + M], rhs, start=(k == 0), stop=(k == kchunks - 1))
            if defer:
                if pend is not None:
                    consumer(*pend)
                pend = (tt, ts_, pb)
            else:
                consumer(tt, ts_, pb)
        if pend is not None:
            consumer(*pend)

    def layer(l):
        P.dma("sp", prm[:], prm_in[l], "prmin")
        P.ts("dve", qgs[0:64], prm[0:64, P_FQG:P_FQG + 1], 0.125, None, op0=ALU.mult)
        P.ts("dve", negb[0:8], prm[0:8, P_BF:P_BF + 1], -1.0, None, op0=ALU.mult)
        P.act(negA[32:36], prm[32:36, P_ALOG:P_ALOG + 1], AF.Exp)
        P.ts("dve", negA[32:36], negA[32:36], -1.0, None, op0=ALU.mult)

        rmsnorm_to_hT(P_GMIX, O_SCR)
        dump(f"h{l}", hT[:, :, :], [128, NCH, S])
        for c in range(NCH):
            P.dma("sp", scrx[:, c, :], xT[:, c, :], f"spill{c}")
        if stop_after == "norm1":
            return

        wt = wload(l, "small", 8 * 96)

        def small_cons(tt, ts_, pb):
            P.act(smT[0:8, ts_], pb[0:8, :], AF.Exp, bias=negb[0:8], scale=-1.0)
            P.act(smT[32:36, ts_], pb[32:36, :], AF.Exp, bias=prm[32:36, P_DTB:P_DTB + 1], scale=1.0)
            P.act(smT[64:68, ts_], pb[64:68, :], AF.Exp, scale=-1.0)
        proj_fm(wt, 96, 0, small_cons)
        P.act(smT[0:8, :], smT[0:8, :], AF.Ln, bias=onec[0:8], scale=1.0)
        P.act(smT[32:36, :], smT[32:36, :], AF.Ln, bias=onec[32:36], scale=1.0)
        P.ts("dve", smT[64:68, :], smT[64:68, :], 1.0, None, op0=ALU.add)
        P.add("dve", lambda e: e.reciprocal(smT[64:68, :], smT[64:68, :]), reads=[smT[64:68, :]],
              writes=[smT[64:68, :]])
        aux = V(O_SCR + 16 * KB, [128, S], F32)
        P.scan(aux[0:8, :], onec[0:8].to_broadcast([8, S]), smT[0:8, :], 0.0, ALU.mult, ALU.subtract)
        cqf = aux[0:8, :]
        P.ts("dve", smT[32:36, :], smT[32:36, :], negA[32:36], None, op0=ALU.mult)
        dump(f"dn_g{l}", smT[32:36, :], [4, S])
        dump(f"dn_beta{l}", smT[64:68, :], [4, S])
        P.scan(aux[32:36, :], onec[32:36].to_broadcast([4, S]), smT[32:36, :], 0.0, ALU.mult, ALU.add)
        a3 = aux[32:36, :].rearrange("p (n c) -> p n c", c=64)
        g3 = smT[32:36, :].rearrange("p (n c) -> p n c", c=64)
        gl = V(O_SCR, [128, 32], F32)
        P.copy("dve", gl[32:36, :], a3[:, :, 63])
        P.copy("dve", g3[:, 0, :], a3[:, 0, :])
        P.tt("dve", g3[:, 1:32, :], a3[:, 1:32, :], gl[32:36, 0:31].unsqueeze(2).to_broadcast([4, 31, 64]),
             ALU.subtract)
        dump(f"cq{l}", cqf, [8, S])
        dump(f"gcum{l}", smT[32:36, :], [4, S])

        o = O_XT
        cqs = V(o, [8, 3, S], BF16); o += 12 * KB
        vext = V(o, [128, 16, 8, 65], BF16); o += 17 * KB
        qa = [V(o + i * 8 * KB, [128, S], BF16) for i in range(2)]
        ka = [V(o + i * 8 * KB + 4 * KB, [128, S], BF16) for i in range(2)]
        o += 16 * KB
        ytok = V(o, [128, 16, 128], BF16); o += 4 * KB
        ptile = [V(o + i * KB, [128, 512], BF16) for i in range(3)]; o += 3 * KB
        sqh = [V(o + i * KB, [64, 512], BF16) for i in range(2)]; o += 2 * KB
        lnh = V(o, [64, 512], F32); o += 2 * KB
        rsh = V(o, [64, 512], F32); o += 2 * KB
        rcp = V(o, [128, 4], F32); o += 128
        assert o <= 64 * KB
        cr = V(O_SCR + 8 * KB, [8, S], F32)
        P.copy("dve", cqs[:, 0, :], cqf)
        P.tt("dve", cr, cqf, cqs[:, 0, :], ALU.subtract)
        P.copy("dve", cqs[:, 1, :], cr)
        P.tt("dve", cr, cr, cqs[:, 1, :], ALU.subtract)
        P.copy("dve", cqs[:, 2, :], cr)
        P.memset("dve", vext[:, :, :, 64:65], 1.0)
        for ct in range(4):
            wt = wload(l, f"foxv{ct}", 1024)
            w3 = wt.rearrange("p (k c) -> p k c", k=NCH)
            for g4 in range(4):
                pb = bank(psb(4, 0))
                for t4 in range(4):
                    tb = g4 * 4 + t4
                    for k in range(NCH):
                        P.mm(pb[:, t4 * 128:(t4 + 1) * 128], hT[:, k, tb * 128:(tb + 1) * 128], w3[:, k, :],
                             start=(k == 0), stop=(k == NCH - 1))
                P.copy("act", vext[:, g4 * 4:(g4 + 1) * 4, 2 * ct:2 * ct + 2, 0:64],
                       pb.rearrange("p (a b c) -> p a b c", a=4, b=2))
        for h in range(8):
            wt = wload(l, f"foxqk{h}", 1024)
            qa_h, ka_h = qa[h % 2], ka[h % 2]
            P.memset("dve", qa_h[64:70, :], -1.0)
            P.memset("dve", ka_h[64:70, :], 1.0)
            P.dma("sp", qa_h[64:67, :], cqs[h:h + 1, :, :], f"augq{h % 2}")
            P.dma("sp", ka_h[67:70, :], cqs[h:h + 1, :, :], f"augk{h % 2}")
            for which, dst, m0, gcol in ((0, qa_h, 0, qgs), (1, ka_h, 64, prm[:, P_FKG:P_FKG + 1])):
                def qk_cons(tt, ts_, pb, dst=dst, gcol=gcol):
                    s_ = sqh[tt % 2]
                    P.act(s_, pb, AF.Square)
                    p2 = bank(psb(2, 6), 64)
                    P.mm(p2, ones_b[0:64, 0:64], s_)
                    P.act(lnh, p2, AF.Ln, bias=epsc[0:64], scale=1.0 / 64)
                    P.act(rsh, lnh, AF.Exp, scale=-0.5)
                    P.stt("dve", dst[0:64, ts_], pb, gcol[0:64], rsh, ALU.mult, ALU.mult)
                proj_fm(wt, 64, m0, qk_cons, defer=True)
            tasks = [(qt, kb) for qt in range(4) for kb in range(4 * qt + 4)]
            obanks = {}

            def emit_S(qt, kb):
                n0 = max(kb * 128, qt * 512)
                ncols = (qt + 1) * 512 - n0
                pb = bank(psb(4, 0))
                P.mm(pb[:, 0:ncols], ka_h[0:70, kb * 128:(kb + 1) * 128], qa_h[0:70, n0:n0 + ncols])
                pt = ptile[state["ps"] % 3]
                state["ps"] += 1
                P.act(pt[:, 0:ncols], pb[:, 0:ncols], AF.Exp)
                if kb * 128 >= qt * 512:
                    P.tt("dve", pt[:, 0:128], pt[:, 0:128], mask01_b, ALU.mult)
                return pt, n0

            def emit_PV(qt, kb, pt, n0):
                ob = bank(4 + qt % 2)
                o4 = ob.rearrange("p (a b) -> p a b", a=4)
                for qb in range(n0 // 128, 4 * qt + 4):
                    c0 = qb * 128 - n0
                    P.mm(o4[:, qb - 4 * qt, 0:65], pt[:, c0:c0 + 128], vext[:, kb, h, :],
                         start=(kb == 0 and qb == 4 * qt), stop=(kb == qb))
                if kb == 4 * qt + 3:
                    P.add("dve", lambda e, o4=o4: e.reciprocal(rcp[:, :].unsqueeze(2), o4[:, :, 64:65]),
                          reads=[o4[:, :, 64:65]], writes=[rcp[:, :]])
                    P.tt("dve", ytok[:, 4 * qt:4 * qt + 4, (h % 2) * 64:(h % 2) * 64 + 64], o4[:, :, 0:64],
                         rcp[:, :].unsqueeze(2).to_broadcast([128, 4, 64]), ALU.mult)

            LA = 2
            pend = [emit_S(*tasks[i]) for i in range(LA)]
            for i, (qt, kb) in enumerate(tasks):
                if i + LA < len(tasks):
                    pend.append(emit_S(*tasks[i + LA]))
                emit_PV(qt, kb, *pend.pop(0))
            if h % 2 == 1:
                for half in range(2):
                    tp = bank(6 + half, 128, BF16)
                    for i in range(8):
                        qb = half * 8 + i
                        P.transpose(tp[:, i * 128:(i + 1) * 128], ytok[:, qb, :], ident_b)
                    P.copy("act", yT[:, h // 2, half * 1024:(half + 1) * 1024], tp)
        dump(f"y_fox{l}", yT[:, 0:4, :], [128, 4, S])
        if stop_after == "fox":
            return

        o = O_XT
        cv = V(o, [128, S + 2], F32); o += 8 * KB + 128
        acc = V(o, [128, S], F32); o += 8 * KB
        bsb = V(o, [128, S], F32); o += 8 * KB
        ctmp = [V(o + i * 2 * KB, [128, 512], F32) for i in range(2)]; o += 4 * KB
        P.memset("dve", cv[:, 0:2], 0.0)
        for j in range(4):
            wb_ = wload(l, f"scb{j}", 1024)
            proj_fm(wb_, 128, 0, lambda tt, ts_, pb: P.copy("act", bsb[:, ts_], pb))
            wc_ = wload(l, f"scc{j}", 1024)
            wv_ = wload(l, f"scv{j}", 1024)
            w3c = wc_.rearrange("p (k c) -> p k c", k=NCH)
            w3v = wv_.rearrange("p (k c) -> p k c", k=NCH)
            for tt in range(NTT):
                ts_ = slice(tt * 512, (tt + 1) * 512)
                pc = bank(psb(4, 0))
                for k in range(NCH):
                    P.mm(pc, w3c[:, k, :], hT[:, k, ts_], start=(k == 0), stop=(k == NCH - 1))
                P.copy("act", ctmp[tt % 2], pc)
                pv = bank(psb(4, 0))
                for k in range(NCH):
                    P.mm(pv, w3v[:, k, :], hT[:, k, ts_], start=(k == 0), stop=(k == NCH - 1))
                P.tt("dve", cv[:, 2 + tt * 512:2 + (tt + 1) * 512], pv, ctmp[tt % 2], ALU.mult)
            w0 = prm[:, P_SCW + j * 3:P_SCW + j * 3 + 1]
            w1 = prm[:, P_SCW + j * 3 + 1:P_SCW + j * 3 + 2]
            w2 = prm[:, P_SCW + j * 3 + 2:P_SCW + j * 3 + 3]
            P.act(acc, cv[:, 2:2 + S], AF.Copy, scale=w2)
            P.stt("dve", acc, cv[:, 1:1 + S], w1, acc, ALU.mult, ALU.add)
            P.stt("dve", acc, cv[:, 0:S], w0, acc, ALU.mult, ALU.add)
            P.tt("dve", yT[:, 4 + j, :], acc, bsb, ALU.mult)
        dump(f"y_sc{l}", yT[:, 4:8, :], [128, 4, S])
        if stop_after == "sc":
            return

        gtok = V(O_SCR, [64, 32, 4], F32)
        btok = V(O_SCR + 512, [64, 32, 4], F32)
        for src0, dst in ((32, gtok), (64, btok)):
            pb = bank(psb(2, 6), 64)
            p3 = pb[:, 0:128].rearrange("p (n h) -> p n h", h=4)
            for n in range(32):
                P.transpose(p3[:, n, :], smT[src0:src0 + 4, n * 64:(n + 1) * 64],
                            ident_f[src0:src0 + 4, src0:src0 + 4])
            P.copy("dve", dst, p3)
        o = O_XT
        raw = V(o, [128, S + 3], F32); o += 8 * KB + 128
        cacc = V(o, [128, S], F32); o += 8 * KB
        QT = V(o, [128, S], BF16); o += 4 * KB
        KT = V(o, [128, S], BF16); o += 4 * KB
        VT = V(o, [128, S], BF16); o += 4 * KB
        Kg = V(o, [64, 32, 128], BF16); o += 8 * KB
        Kd = V(o, [64, 32, 128], BF16); o += 8 * KB
        Vb = V(o, [64, 32, 128], BF16); o += 8 * KB
        qdT = V(o, [128, S], BF16); o += 4 * KB
        nkcT = V(o, [128, 32, 64], BF16); o += 4 * KB
        assert o <= 64 * KB, o
        o = O_SCR + 1 * KB
        cm = [V(o + i * 4 * KB, [64, 32, 64], BF16) for i in range(5)]; o += 20 * KB
        Mm, MTm, PTm, qkT, Mn = cm
        MTn = V(o, [64, 32, 64], BF16); o += 4 * KB
        PTn = V(o, [64, 32, 64], BF16); o += 4 * KB
        assert o <= SCR_END, o
        GB = V(O_XT, [128, S], F32)
        Xm = V(O_XT + 8 * KB + 128, [64, 32, 64], F32)
        sq2 = [V(O_SCR + 29 * KB + i * KB, [128, 512], BF16) for i in range(2)]
        eg = misc[0:64, 8:40]; bgc = misc[0:64, 40:72]; edc = misc[0:64, 72:104]
        gtot = misc[:, 104:136]; nbt = misc[0:64, 136:168]

        for h in range(4):
            for which, nm, dstT, scl in ((0, "dnq", QT, 128.0 ** -0.5), (1, "dnk", KT, 1.0), (2, "dnv", VT, None)):
                wt = wload(l, f"{nm}{h}", 1024)
                if which == 0:
                    P.memset("dve", raw[:, 0:3], 0.0)
                proj_fm(wt, 128, 0, lambda tt, ts_, pb: P.copy("act", raw[:, 3 + tt * 512:3 + (tt + 1) * 512], pb))
                cw = P_DNW + (which * 4 + h) * 4
                P.act(cacc, raw[:, 3:3 + S], AF.Copy, scale=prm[:, cw + 3:cw + 4])
                for j in range(3):
                    P.stt("dve", cacc, raw[:, j:j + S], prm[:, cw + j:cw + j + 1], cacc, ALU.mult, ALU.add)
                if scl is None:
                    P.act(dstT, cacc, AF.Silu)
                else:
                    P.act(cacc, cacc, AF.Silu)
                    for tt in range(NTT):
                        ts_ = slice(tt * 512, (tt + 1) * 512)
                        s_ = sq2[tt % 2]
                        P.act(s_, cacc[:, ts_], AF.Square)
                        p2 = bank(psb(2, 6))
                        P.mm(p2, ones_b, s_)
                        lnt = V(O_SCR + 25 * KB, [128, 512], F32)
                        rst = V(O_SCR + 27 * KB, [128, 512], F32)
                        P.act(lnt, p2, AF.Ln, bias=epsc, scale=1.0)
                        P.act(rst, lnt, AF.Exp, scale=-0.5)
                        P.stt("dve", dstT[:, ts_], cacc[:, ts_], scl, rst, ALU.mult, ALU.mult)
            if h == 0:
                dump(f"dn_q{l}", QT, [128, S]); dump(f"dn_k{l}", KT, [128, S]); dump(f"dn_v{l}", VT, [128, S])
            for tt in range(NTT):
                ts_ = slice(tt * 512, (tt + 1) * 512)
                pb = bank(psb(4, 0))
                P.mm(pb, sel_f[32:36, h * 128:(h + 1) * 128], smT[32:36, ts_])
                P.copy("act", GB[:, ts_], pb)
                egt = V(O_SCR + 25 * KB, [128, 512], F32)
                P.act(egt, pb, AF.Exp)
                P.tt("dve", qdT[:, ts_], QT[:, ts_], egt, ALU.mult)
            GB3 = GB.rearrange("p (n c) -> p n c", c=64)
            P.act(eg, gtok[:, :, h], AF.Exp)
            P.tt("dve", bgc, btok[:, :, h], eg, ALU.mult)
            P.tt("dve", edc, GB3[0:64, :, 63], gtok[:, :, h], ALU.subtract)
            P.act(edc, edc, AF.Exp)
            P.act(gtot, GB3[:, :, 63], AF.Exp)
            P.ts("dve", nbt, btok[:, :, h], -1.0, None, op0=ALU.mult)
            for g8 in range(4):
                for src, outs in ((KT, ((Kg, bgc), (Kd, edc))), (VT, ((Vb, btok[:, :, h]),))):
                    tp = bank(psb(2, 6), 64, BF16)
                    for i in range(8):
                        n = g8 * 8 + i
                        P.transpose(tp[:, i * 128:(i + 1) * 128], src[:, n * 64:(n + 1) * 64], ident_b)
                    t3 = tp.rearrange("p (n d) -> p n d", d=128)
                    for (dst, col) in outs:
                        P.tt("dve", dst[:, g8 * 8:(g8 + 1) * 8, :], t3,
                             col[:, g8 * 8:(g8 + 1) * 8].unsqueeze(2).to_broadcast([64, 8, 128]), ALU.mult)
            P.tt("dve", Xm, GB3[0:64, :, :], gtok[:, :, h].unsqueeze(2).to_broadcast([64, 32, 64]), ALU.subtract)
            DLn = V(O_SCR + 21 * KB, [64, 32, 64], F32)
            P.tt("dve", DLn, Xm, mposA.unsqueeze(1).to_broadcast([64, 32, 64]), ALU.add)
            P.act(DLn, DLn, AF.Exp, scale=-1.0)
            P.tt("dve", DLn, DLn, nbt[:, :].unsqueeze(2).to_broadcast([64, 32, 64]), ALU.mult)
            P.tt("dve", Xm, Xm, mposB.unsqueeze(1).to_broadcast([64, 32, 64]), ALU.subtract)
            P.act(Xm, Xm, AF.Exp)
            for g8 in range(4):
                pk = bank(psb(4, 0), 64)
                pq = bank(psb(4, 0), 64)
                for i in range(8):
                    n = g8 * 8 + i
                    cs = slice(n * 64, (n + 1) * 64)
                    P.mm(pk[:, i * 64:(i + 1) * 64], KT[:, cs], KT[:, cs])
                    P.mm(pq[:, i * 64:(i + 1) * 64], KT[:, cs], QT[:, cs])
                gs = slice(g8 * 8, (g8 + 1) * 8)
                P.tt("dve", Mm[:, gs, :], pk.rearrange("p (n c) -> p n c", c=64), DLn[:, gs, :], ALU.mult)
                P.tt("dve", qkT[:, gs, :], pq.rearrange("p (n c) -> p n c", c=64), Xm[:, gs, :], ALU.mult)
            for g8 in range(4):
                tp = bank(psb(2, 6), 64, BF16)
                for i in range(8):
                    n = g8 * 8 + i
                    P.transpose(tp[:, i * 64:(i + 1) * 64], Mm[:, n, :], ident_b[0:64, 0:64])
                P.copy("act", MTm[:, g8 * 8:(g8 + 1) * 8, :], tp[:, 0:512].rearrange("p (n c) -> p n c", c=64))
            P.tt("dve", PTm, MTm, ident_b[0:64, 0:64].unsqueeze(1).to_broadcast([64, 32, 64]), ALU.add)
            Wc, WTc, PTc = Mm, MTm, PTm
            Wn, WTn, PTx = Mn, MTn, PTn
            for it in range(5):
                def stA(g8, it=it, Wc=Wc, WTc=WTc, Wn=Wn, WTn=WTn):
                    pw = bank(psb(6, 0), 64)
                    pwt = bank(psb(6, 0), 64) if it < 4 else None
                    for i in range(8):
                        n = g8 * 8 + i
                        P.mm(pw[:, i * 64:(i + 1) * 64], WTc[:, n, :], Wc[:, n, :])
                        if pwt is not None:
                            P.mm(pwt[:, i * 64:(i + 1) * 64], Wc[:, n, :], WTc[:, n, :])
                    gs = slice(g8 * 8, (g8 + 1) * 8)
                    P.copy("act", Wn[:, gs, :], pw.rearrange("p (n c) -> p n c", c=64))
                    if pwt is not None:
                        P.copy("dve", WTn[:, gs, :], pwt.rearrange("p (n c) -> p n c", c=64))

                def stB(g8, Wn=Wn, PTc=PTc, PTx=PTx):
                    pp = bank(psb(6, 0), 64)
                    for i in range(8):
                        n = g8 * 8 + i
                        P.mm(pp[:, i * 64:(i + 1) * 64], ident_b[0:64, 0:64], PTc[:, n, :], start=True, stop=False)
                        P.mm(pp[:, i * 64:(i + 1) * 64], Wn[:, n, :], PTc[:, n, :], start=False, stop=True)
                    gs = slice(g8 * 8, (g8 + 1) * 8)
                    P.copy("act", PTx[:, gs, :], pp.rearrange("p (n c) -> p n c", c=64))
                stA(0); stA(1); stB(0); stA(2); stB(1); stA(3); stB(2); stB(3)
                Wc, Wn = Wn, Wc
                WTc, WTn = WTn, WTc
                PTc, PTx = PTx, PTc
            TT = PTc
            for g8 in range(4):
                pb = bank(psb(4, 0))
                for i in range(8):
                    n = g8 * 8 + i
                    P.mm(pb[:, i * 64:(i + 1) * 64], Kg[:, n, :], TT[:, n, :])
                P.act(nkcT[:, g8 * 8:(g8 + 1) * 8, :], pb.rearrange("p (n c) -> p n c", c=64), AF.Copy, scale=-1.0)
            o3 = O_SCR + 29 * KB
            Sf = V(o3, [128, 128], F32)
            Sbb = [V(o3 + 512 + i * 256, [128, 128], BF16) for i in range(2)]
            vn = [V(o3 + 1024 + i * 256, [64, 128], BF16) for i in range(2)]
            oT = V(O_XT + 8 * KB + 128, [128, S], F32)
            P.memset("dve", Sf, 0.0)
            P.memset("dve", Sbb[0], 0.0)
            for n in range(32):
                sb_ = Sbb[n % 2]
                pv = bank(4, 64)[:, (n % 2) * 128:(n % 2) * 128 + 128]
                P.mm(pv, TT[:, n, :], Vb[:, n, :], start=True, stop=False)
                P.mm(pv, nkcT[:, n, :], sb_, start=False, stop=True)
                vn_ = vn[n % 2]
                P.copy("act", vn_, pv)
                if n % 8 == 0:
                    po = bank(psb(2, 2))
                pos = po[:, (n % 8) * 64:(n % 8 + 1) * 64]
                P.mm(pos, sb_, qdT[:, n * 64:(n + 1) * 64], start=True, stop=False)
                P.mm(pos, vn_, qkT[:, n, :], start=False, stop=True)
                pd = bank(5)[:, (n % 2) * 128:(n % 2) * 128 + 128]
                P.mm(pd, Kd[:, n, :], vn_)
                P.stt("dve", Sbb[(n + 1) % 2], Sf, gtot[:, n:n + 1], pd, ALU.mult, ALU.add)
                P.stt("dve", Sf, Sf, gtot[:, n:n + 1], pd, ALU.mult, ALU.add)
                if n % 8 == 7:
                    tt = n // 8
                    P.copy("act", oT[:, tt * 512:(tt + 1) * 512], po)
            if h == 0:
                dump(f"o_dn{l}", oT, [128, S])
            wz = wload(l, f"dnz{h}", 1024)

            def z_cons(tt, ts_, pb, h=h):
                s_ = sq2[tt % 2]
                P.act(s_, oT[:, ts_], AF.Square)
                p2 = bank(psb(2, 6))
                P.mm(p2, ones_b, s_)
                lnt = V(O_SCR + 25 * KB, [128, 512], F32)
                rst = V(O_SCR + 27 * KB, [128, 512], F32)
                P.act(lnt, p2, AF.Ln, bias=epsc, scale=1.0 / 128)
                P.act(rst, lnt, AF.Exp, scale=-0.5)
                P.stt("dve", rst, oT[:, ts_], prm[:, P_DNG:P_DNG + 1], rst, ALU.mult, ALU.mult)
                P.act(lnt, pb, AF.Silu)
                P.tt("dve", yT[:, 8 + h, ts_], rst, lnt, ALU.mult)
            proj_fm(wz, 128, 0, z_cons, defer=True)
        dump(f"y_dn{l}", yT[:, 8:12, :], [128, 4, S])
        if stop_after == "dn":
            return

        mT = V(O_SCR, [128, NCH, 1024], BF16)
        macc = V(O_SCR + 16 * KB, [128, 512], F32)
        mtmp = V(O_SCR + 18 * KB, [128, 512], F32)
        sgt = [[V(O_XT + b * 8 * KB + 4 * KB + t2 * 2 * KB, [128, 512], F32) for t2 in range(2)] for b in range(3)]
        for half in range(2):
            for dc in range(NCH):
                for b in range(3):
                    wgb = wload(l, f"g{b}_{dc}", 1024).rearrange("p (k c) -> p k c", k=NCH)
                    for t2 in range(2):
                        tt = half * 2 + t2
                        ts_ = slice(tt * 512, (tt + 1) * 512)
                        pg = bank(psb(4, 0))
                        for k in range(NCH):
                            P.mm(pg, wgb[:, k, :], hT[:, k, ts_], start=(k == 0), stop=(k == NCH - 1))
                        P.act(sgt[b][t2], pg, AF.Sigmoid)
                wA3 = wload(l, f"brA{dc}", 1024).rearrange("p (b c m) -> p b c m", b=2, c=4)
                wB3 = wload(l, f"brB{dc}", 512).rearrange("p (c m) -> p c m", c=4)
                for t2 in range(2):
                    tt = half * 2 + t2
                    ts_ = slice(tt * 512, (tt + 1) * 512)
                    for b in range(3):
                        pbr = bank(psb(4, 0))
                        for c in range(4):
                            lw = wA3[:, b, c, :] if b < 2 else wB3[:, c, :]
                            P.mm(pbr, lw, yT[:, 4 * b + c, ts_], start=(c == 0), stop=(c == 3))
                        if b == 0:
                            P.tt("dve", macc, pbr, sgt[b][t2], ALU.mult)
                        elif b == 1:
                            P.tt("dve", mtmp, pbr, sgt[b][t2], ALU.mult)
                            P.tt("dve", macc, macc, mtmp, ALU.add)
                        else:
                            P.tt("dve", mtmp, pbr, sgt[b][t2], ALU.mult)
                            P.tt("dve", mT[:, dc, t2 * 512:(t2 + 1) * 512], macc, mtmp, ALU.add)
            if half == 0:
                dump(f"merged{l}", mT[:, :, :], [128, NCH, 1024])
            for dc in range(NCH):
                wo = wload(l, f"wo{dc}", 1024)
                hs = slice(half * 1024, (half + 1) * 1024)
                P.dma("sp", xT[:, dc, hs], scrx[:, dc, hs], f"unsp{dc}")

                def wo_cons(tt, ts_, pb, dc=dc):
                    P.tt("dve", xT[:, dc, ts_], xT[:, dc, ts_], pb, ALU.add)
                proj_fm(wo, 128, 0, wo_cons, rhs_of=lambda k, ts_, half=half: mT[:, k, ts_.start - half * 1024: ts_.stop - half * 1024],
                        tts=range(half * 2, half * 2 + 2))
        dump(f"x_mix{l}", xT[:, :, :], [128, NCH, S])
        if stop_after == "mix":
            return

        rmsnorm_to_hT(P_GFFN, O_SCR)
        o = O_YT
        graw = V(o, [128, S + 2], F32); o += 8 * KB + 128
        vraw = V(o, [128, S + 2], F32); o += 8 * KB + 128
        gac = V(o, [128, S], F32); o += 8 * KB
        vac = V(o, [128, S], F32); o += 8 * KB
        assert o <= O_WS
        aT0 = V(O_SCR + 6 * KB, [128, GRP, S], BF16)
        aT1 = V(O_YT + 33 * KB, [128, 3, S], BF16)
        aT1b = V(O_SCR + 22 * KB, [128, 1, S], BF16)
        P.memset("dve", graw[:, 0:2], 0.0)
        P.memset("dve", vraw[:, 0:2], 0.0)

        def a_slot(gi, jj):
            if gi % 2 == 0:
                return aT0[:, jj, :]
            return aT1[:, jj, :] if jj < 3 else aT1b[:, 0, :]

        def ffn_up(gi, j0, n):
                for jj in range(n):
                    j = j0 + jj
                    for nm, rawb, accb, c0 in (("upg", graw, gac, j), ("upv", vraw, vac, NFF + j)):
                        wt = wload(l, f"{nm}{j}", 1024)
                        w2c = prm[:, P_FFW + c0 * 3 + 2:P_FFW + c0 * 3 + 3]

                        def up_cons(tt, ts_, pb, rawb=rawb, accb=accb, w2c=w2c):
                            P.copy("act", rawb[:, 2 + tt * 512:2 + (tt + 1) * 512], pb)
                            P.act(accb[:, ts_], pb, AF.Copy, scale=w2c)
                        proj_fm(wt, 128, 0, up_cons)
                        P.stt("dve", accb, rawb[:, 1:1 + S], prm[:, P_FFW + c0 * 3 + 1:P_FFW + c0 * 3 + 2], accb,
                              ALU.mult, ALU.add)
                        P.stt("dve", accb, rawb[:, 0:S], prm[:, P_FFW + c0 * 3:P_FFW + c0 * 3 + 1], accb,
                              ALU.mult, ALU.add)
                    P.act(gac, gac, AF.Silu)
                    P.tt("dve", a_slot(gi, jj), gac, vac, ALU.mult)

        def ffn_down(gi, j0, n):
                for q in range(4):
                    wd = wload(l, f"dn{gi}_{q}", n * 256)
                    wd3 = wd.rearrange("p (j c) -> p j c", j=n)
                    for dd in range(2):
                        dc = 2 * q + dd
                        for tt in range(NTT):
                            ts_ = slice(tt * 512, (tt + 1) * 512)
                            pb = bank(psb(4, 0))
                            for jj in range(n):
                                P.mm(pb, wd3[:, jj, dd * 128:(dd + 1) * 128], a_slot(gi, jj)[:, ts_],
                                     start=(jj == 0), stop=(jj == n - 1))
                            P.tt("dve", xT[:, dc, ts_], xT[:, dc, ts_], pb, ALU.add)

        groups = _ffn_groups()
        for gi, (j0, n) in enumerate(groups):
            ffn_up(gi, j0, n)
            if gi > 0:
                ffn_down(gi - 1, *groups[gi - 1])
        ffn_down(len(groups) - 1, *groups[-1])
        dump(f"x_ffn{l}", xT[:, :, :], [128, NCH, S])
        if stop_after == "ffn":
            return

        rmsnorm_to_hT(P_GPLE, O_SCR)
        ptok = V(O_YT, [128, 16, 256], F32)
        pT = V(O_YT + 16 * KB, [128, 2, S], BF16)
        sgp = [V(O_YT + 24 * KB + i * 2 * KB, [128, 512], F32) for i in range(2)]
        ptmp = V(O_YT + 28 * KB, [128, 512], F32)
        for q in range(4):
            P.dma("sp", ptok[:, q * 4:(q + 1) * 4, :],
                  p_in[l, q * 512:(q + 1) * 512, :].rearrange("(a p) c -> p a c", p=128), f"pin{q}")
        for c in range(2):
            for g4 in range(4):
                tp = bank(psb(2, 6))
                for i in range(4):
                    tb = g4 * 4 + i
                    P.transpose(tp[:, i * 128:(i + 1) * 128], ptok[:, tb, c * 128:(c + 1) * 128], ident_f)
                P.copy("act", pT[:, c, g4 * 512:(g4 + 1) * 512], tp)
        for dc in range(NCH):
            wpg = wload(l, f"pg{dc}", 1024)
            wpl = wload(l, f"pl{dc}", 256)
            wpl3 = wpl.rearrange("p (c m) -> p c m", c=2)

            def pg_cons(tt, ts_, pb, dc=dc, wpl3=wpl3):
                s_ = sgp[tt % 2]
                P.act(s_, pb, AF.Sigmoid)
                pp = bank(psb(2, 4))
                for c in range(2):
                    P.mm(pp, wpl3[:, c, :], pT[:, c, ts_], start=(c == 0), stop=(c == 1))
                P.tt("dve", ptmp, pp, s_, ALU.mult)
                P.tt("dve", xT[:, dc, ts_], xT[:, dc, ts_], ptmp, ALU.add)
            proj_fm(wpg, 128, 0, pg_cons, defer=True)
        dump(f"x_out{l}", xT[:, :, :], [128, NCH, S])

    xin = [V(O_YT + i * 4 * KB, [128, D], F32) for i in range(2)]
    for tb in range(16):
        xi = xin[tb % 2]
        P.dma("sp", xi, x_in[tb * 128:(tb + 1) * 128, :], f"xin{tb % 2}")
        for g2 in range(2):
            tp = bank(psb(2, 6))
            for i in range(4):
                c = g2 * 4 + i
                P.transpose(tp[:, i * 128:(i + 1) * 128], xi[:, c * 128:(c + 1) * 128], ident_f)
            eng = "act" if g2 == 0 else "dve"
            P.copy(eng, xT[:, g2 * 4:(g2 + 1) * 4, tb * 128:(tb + 1) * 128],
                   tp.rearrange("p (c t) -> p c t", c=4))

    for l in range(depth):
        layer(l)

    if stop_after is None:
        xo = [V(O_YT + i * 4 * KB, [128, D], F32) for i in range(2)]
        for tb in range(16):
            xo_ = xo[tb % 2]
            for g2 in range(2):
                tp = bank(psb(2, 6))
                for i in range(4):
                    c = g2 * 4 + i
                    P.transpose(tp[:, i * 128:(i + 1) * 128], xT[:, c, tb * 128:(tb + 1) * 128], ident_f)
                eng = "act" if g2 == 0 else "dve"
                P.copy(eng, xo_[:, g2 * 512:(g2 + 1) * 512], tp)
            P.dma("sp", out[tb * 128:(tb + 1) * 128, :], xo_, f"xout{tb % 2}", final=True)
    else:
        z = V(O_SCR, [128, 8], F32)
        P.memset("pool", z, 0.0)
        P.dma("sp", out[0:128, 0:8], z, "xout0", final=True)
    P.emit()
    return nc, dbg_outs, P


_CACHE = {}


def kernel(**inputs):
    inp = {k: np.asarray(v) for k, v in inputs.items()}
    if "nc" not in _CACHE:
        _CACHE["nc"] = build()[0]
    nc = _CACHE["nc"]
    wstream = np.stack([pack_weights(inp, l) for l in range(DEPTH)])
    prm = np.stack([pack_params(inp, l) for l in range(DEPTH)])
    cst = make_consts()
    in_maps = []
    for b in range(8):
        in_maps.append({"x": np.ascontiguousarray(inp["x"][b]),
                        "p": np.ascontiguousarray(inp["p"][:, b]),
                        "wst": wstream, "prm": prm, "cst": cst})
    res = run_bass_kernel_spmd(nc, in_maps, core_ids=list(range(8)))
    return np.stack([np.asarray(r["out"]) for r in res.results]).astype(np.float32)
```

```python
import numpy as np
import concourse.bass as bass
import concourse.mybir as mybir

F32 = mybir.dt.float32
BF16 = mybir.dt.bfloat16
ALU = mybir.AluOpType
AF = mybir.ActivationFunctionType

ENGS = ("pe", "act", "dve", "pool", "sp")


def _prod(xs):
    r = 1
    for v in xs:
        r *= int(v)
    return r


def region(ap):
    t = ap.tensor
    es = mybir.dt.size(ap.dtype)
    off = int(ap.offset)
    pat = ap.ap
    if str(ap.space) == "DRAM":
        ext = 0
        for st, cnt in pat:
            ext += (cnt - 1) * abs(st)
        return (t.name, 0, 1, off * es, (off + ext + 1) * es)
    shape = list(t.shape)
    F = _prod(shape[1:])
    p0 = off // F
    f0 = off % F
    pstep, pcnt = pat[0]
    npart = pcnt if pstep != 0 else 1
    ext = 0
    for st, cnt in pat[1:]:
        ext += (cnt - 1) * abs(st)
    return (t.name, p0, p0 + npart, f0 * es, (f0 + ext + 1) * es)


def _untracked(ap):
    return str(ap.space) == "DRAM" and not ap.tensor.name.startswith("scr_")


def _overlap(a, b):
    return a[1] < b[2] and b[1] < a[2] and a[3] < b[4] and b[3] < a[4]


def _covers(a, b):
    return a[1] <= b[1] and a[2] >= b[2] and a[3] <= b[3] and a[4] >= b[4]


class Instr:
    __slots__ = ("eng", "fn", "deps", "signal", "is_dma", "key", "val", "idx")

    def __init__(self, eng, fn, is_dma, key):
        self.eng = eng
        self.fn = fn
        self.deps = set()
        self.signal = False
        self.is_dma = is_dma
        self.key = key
        self.val = 0


class Prog:
    def __init__(self, nc):
        self.nc = nc
        self.instrs = []
        self.wr = {}
        self.rd = {}
        self.final_keys = set()
        self.last_dma = {}

    def add(self, eng, fn, reads=(), writes=(), dma_key=None):
        ins = Instr(eng, fn, dma_key is not None, dma_key if dma_key is not None else eng)
        idx = len(self.instrs)
        ins.idx = idx
        self.instrs.append(ins)
        deps = ins.deps
        for ap in reads:
            if ap is None or isinstance(ap, (int, float)):
                continue
            if _untracked(ap):
                continue
            r = region(ap)
            for (w, wi) in self.wr.get(r[0], ()):
                if _overlap(w, r):
                    deps.add(wi)
        for ap in writes:
            if _untracked(ap):
                continue
            r = region(ap)
            wl = self.wr.setdefault(r[0], [])
            rl = self.rd.setdefault(r[0], [])
            for (w, wi) in wl:
                if _overlap(w, r):
                    deps.add(wi)
            for (q, qi) in rl:
                if _overlap(q, r):
                    deps.add(qi)
            self.wr[r[0]] = [(w, wi) for (w, wi) in wl if not _covers(r, w)]
            self.rd[r[0]] = [(q, qi) for (q, qi) in rl if not _covers(r, q)]
            self.wr[r[0]].append((r, idx))
        for ap in reads:
            if ap is None or isinstance(ap, (int, float)):
                continue
            if _untracked(ap):
                continue
            r = region(ap)
            rl = self.rd.setdefault(r[0], [])
            if dma_key is None:
                rl[:] = [(q, qi) for (q, qi) in rl
                         if not (q == r and self.instrs[qi].eng == eng and not self.instrs[qi].is_dma)]
            rl.append((r, idx))
        if dma_key is not None:
            if dma_key in self.last_dma:
                deps.add(self.last_dma[dma_key])
            self.last_dma[dma_key] = idx
        deps.discard(idx)
        return ins

    def mm(self, out, lhsT, rhs, start=True, stop=True):
        rd = [lhsT, rhs]
        return self.add("pe", lambda e: e.matmul(out, lhsT, rhs, start=start, stop=stop),
                        reads=rd, writes=[out])

    def transpose(self, out, in_, ident):
        return self.add("pe", lambda e: e.transpose(out, in_, ident),
                        reads=[in_, ident], writes=[out])

    def act(self, out, in_, func, bias=None, scale=1.0, accum_out=None, eng="act"):
        kw = {}
        if bias is not None:
            kw["bias"] = bias
        if accum_out is not None:
            kw["accum_out"] = accum_out
        rd = [in_, bias if not isinstance(bias, (int, float)) else None,
              scale if not isinstance(scale, (int, float)) else None]
        wr = [out] + ([accum_out] if accum_out is not None else [])
        return self.add(eng, lambda e: e.activation(out, in_, func, scale=scale, **kw),
                        reads=rd, writes=wr)

    def tt(self, eng, out, in0, in1, op):
        return self.add(eng, lambda e: e.tensor_tensor(out, in0, in1, op),
                        reads=[in0, in1], writes=[out])

    def ts(self, eng, out, in0, s1, s2=None, op0=ALU.mult, op1=None, accum_out=None):
        kw = {}
        if op1 is not None:
            kw["op1"] = op1
        if accum_out is not None:
            kw["accum_out"] = accum_out
        rd = [in0, s1 if not isinstance(s1, (int, float)) else None,
              s2 if not isinstance(s2, (int, float)) else None]
        wr = [out] + ([accum_out] if accum_out is not None else [])
        return self.add(eng, lambda e: e.tensor_scalar(out, in0, s1, s2, op0, **kw),
                        reads=rd, writes=wr)

    def stt(self, eng, out, in0, scalar, in1, op0, op1):
        rd = [in0, in1, scalar if not isinstance(scalar, (int, float)) else None]
        return self.add(eng, lambda e: e.scalar_tensor_tensor(out, in0, scalar, in1, op0, op1),
                        reads=rd, writes=[out])

    def copy(self, eng, out, in_):
        if eng == "act":
            return self.add(eng, lambda e: e.copy(out, in_), reads=[in_], writes=[out])
        return self.add(eng, lambda e: e.tensor_copy(out, in_), reads=[in_], writes=[out])

    def memset(self, eng, out, val):
        return self.add(eng, lambda e: e.memset(out, val), reads=[], writes=[out])

    def scan(self, out, d0, d1, initial, op0, op1):
        rd = [d0, d1, initial if not isinstance(initial, (int, float)) else None]
        return self.add("dve", lambda e: e.tensor_tensor_scan(out, d0, d1, initial, op0, op1),
                        reads=rd, writes=[out])

    def dma(self, eng, out, in_, key, final=False):
        if final:
            self.final_keys.add(key)
        return self.add(eng, lambda e: e.dma_start(out, in_), reads=[in_], writes=[out], dma_key=key)

    def emit(self):
        nc = self.nc
        instrs = self.instrs
        for ins in instrs:
            if ins.eng == "pe" and not ins.is_dma:
                ins.deps = {d for d in ins.deps if not (instrs[d].eng == "pe" and not instrs[d].is_dma)}
            for d in ins.deps:
                instrs[d].signal = True
        keys = []
        for ins in instrs:
            if ins.key not in keys:
                keys.append(ins.key)
        for ins in instrs:
            if ins.is_dma:
                ins.signal = True
        cnt = {k: 0 for k in keys}
        for ins in instrs:
            if ins.signal:
                cnt[ins.key] += 16 if ins.is_dma else 1
            ins.val = cnt[ins.key]
        used = [k for k in keys if cnt[k] > 0]
        self.sem_totals = {k: cnt[k] for k in used}
        import contextlib
        with contextlib.ExitStack() as st:
            sems = {k: st.enter_context(nc.semaphore("s_" + str(k))) for k in used}
            block = st.enter_context(nc.Block())
            per_eng = {e: [i for i in instrs if i.eng == e] for e in ENGS}
            nwaits = [0]

            def run(engobj, lst, is_last_sp=False):
                clock = {}
                for ins in lst:
                    need = {}
                    for d in ins.deps:
                        di = instrs[d]
                        if di.val > need.get(di.key, 0):
                            need[di.key] = di.val
                    for k, v in need.items():
                        if clock.get(k, 0) < v:
                            engobj.wait_ge(sems[k], v)
                            clock[k] = v
                            nwaits[0] += 1
                    bi = ins.fn(engobj)
                    if ins.signal:
                        bi.then_inc(sems[ins.key], 16 if ins.is_dma else 1)
                if is_last_sp:
                    for k in sorted(self.final_keys, key=str):
                        if k in sems:
                            engobj.wait_ge(sems[k], cnt[k])

            @block.tensor
            def _(e):
                run(e, per_eng["pe"])

            @block.scalar
            def _(e):
                run(e, per_eng["act"])

            @block.vector
            def _(e):
                run(e, per_eng["dve"])

            @block.gpsimd
            def _(e):
                run(e, per_eng["pool"])

            @block.sync
            def _(e):
                run(e, per_eng["sp"], True)
            self.nwaits = nwaits[0]

from concourse.bass_utils import run_bass_kernel_spmd
import ml_dtypes

S = 2048
D = 1024
DEPTH = 2
NCH = D // 128
NTT = S // 512
DFF = 2816
NFF = DFF // 128
EPS = 1e-6
GRP = 4
NPRM = 224
NCST = 1536

C_FQ, C_FK, C_FV, C_FF = 0, 512, 1024, 1536
C_SB, C_SC, C_SV = 1544, 2056, 2568
C_DQ, C_DK, C_DV = 3080, 3592, 4104
C_DB, C_DA, C_DZ, C_G = 4616, 4620, 4624, 5136

P_GMIX, P_GFFN, P_GPLE = 0, 8, 16
P_FQG, P_FKG, P_BF, P_ALOG, P_DTB, P_DNG = 24, 25, 26, 27, 28, 29
P_SCW, P_DNW, P_FFW = 30, 42, 90

K_ID, K_M01, K_ONE, K_SEL, K_MPA, K_MPB = 0, 128, 384, 512, 1024, 1088


def _ffn_groups():
    gs = []
    j = 0
    while j < NFF:
        n = min(GRP, NFF - j)
        gs.append((j, n))
        j += n
    return gs


def wtile_index():
    idx = {}
    names = ["small"]
    names += [f"foxv{c}" for c in range(4)]
    names += [f"foxqk{h}" for h in range(8)]
    for j in range(4):
        names += [f"scb{j}", f"scc{j}", f"scv{j}"]
    for h in range(4):
        names += [f"dnq{h}", f"dnk{h}", f"dnv{h}", f"dnz{h}"]
    for dc in range(8):
        names += [f"g0_{dc}", f"g1_{dc}", f"g2_{dc}", f"brA{dc}", f"brB{dc}"]
    names += [f"wo{dc}" for dc in range(8)]
    for j in range(NFF):
        names += [f"upg{j}", f"upv{j}"]
    for gi, (j0, n) in enumerate(_ffn_groups()):
        names += [f"dn{gi}_{q}" for q in range(4)]
    for dc in range(8):
        names += [f"pg{dc}", f"pl{dc}"]
    for i, n in enumerate(names):
        idx[n] = i
    return idx


WIDX = wtile_index()
NT = len(WIDX)


def pack_weights(inp, l):
    W = np.zeros((NT, 128, 1024), np.float32)
    w_in = inp["w_in"][l]

    def kc(cols):
        n = cols.shape[1]
        return cols.reshape(8, 128, n).transpose(1, 0, 2).reshape(128, 8 * n)

    def put(name, arr):
        W[WIDX[name], :, :arr.shape[1]] = arr

    sm = np.zeros((1024, 96), np.float32)
    sm[:, 0:8] = w_in[:, C_FF:C_FF + 8]
    sm[:, 32:36] = w_in[:, C_DA:C_DA + 4]
    sm[:, 64:68] = w_in[:, C_DB:C_DB + 4]
    put("small", kc(sm))
    for c in range(4):
        put(f"foxv{c}", kc(w_in[:, C_FV + c * 128:C_FV + (c + 1) * 128]))
    for h in range(8):
        qk = np.concatenate([w_in[:, C_FQ + h * 64:C_FQ + (h + 1) * 64],
                             w_in[:, C_FK + h * 64:C_FK + (h + 1) * 64]], axis=1)
        put(f"foxqk{h}", kc(qk))
    for j in range(4):
        put(f"scb{j}", kc(w_in[:, C_SB + j * 128:C_SB + (j + 1) * 128]))
        put(f"scc{j}", kc(w_in[:, C_SC + j * 128:C_SC + (j + 1) * 128]))
        put(f"scv{j}", kc(w_in[:, C_SV + j * 128:C_SV + (j + 1) * 128]))
    for h in range(4):
        put(f"dnq{h}", kc(w_in[:, C_DQ + h * 128:C_DQ + (h + 1) * 128]))
        put(f"dnk{h}", kc(w_in[:, C_DK + h * 128:C_DK + (h + 1) * 128]))
        put(f"dnv{h}", kc(w_in[:, C_DV + h * 128:C_DV + (h + 1) * 128]))
        put(f"dnz{h}", kc(w_in[:, C_DZ + h * 128:C_DZ + (h + 1) * 128]))
    wb = inp["w_branch"][l]
    for dc in range(8):
        for b in range(3):
            put(f"g{b}_{dc}", kc(w_in[:, C_G + b * 1024 + dc * 128:C_G + b * 1024 + (dc + 1) * 128]))

        def br(b):
            a = wb[b][:, dc * 128:(dc + 1) * 128]
            return a.reshape(4, 128, 128).transpose(1, 0, 2).reshape(128, 512)
        put(f"brA{dc}", np.concatenate([br(0), br(1)], axis=1))
        put(f"brB{dc}", br(2))
        put(f"wo{dc}", kc(inp["w_o"][l][:, dc * 128:(dc + 1) * 128]))
        put(f"pg{dc}", kc(inp["w_ple_gate"][l][:, dc * 128:(dc + 1) * 128]))
        a = inp["w_ple"][l][:, dc * 128:(dc + 1) * 128]
        put(f"pl{dc}", a.reshape(2, 128, 128).transpose(1, 0, 2).reshape(128, 256))
    w_up = inp["w_up"][l]
    for j in range(NFF):
        put(f"upg{j}", kc(w_up[:, j * 128:(j + 1) * 128]))
        put(f"upv{j}", kc(w_up[:, DFF + j * 128:DFF + (j + 1) * 128]))
    w_dn = inp["w_down"][l]
    for gi, (j0, n) in enumerate(_ffn_groups()):
        for q in range(4):
            a = w_dn[j0 * 128:(j0 + n) * 128, q * 256:(q + 1) * 256]
            put(f"dn{gi}_{q}", a.reshape(n, 128, 256).transpose(1, 0, 2).reshape(128, n * 256))
    return W


def pack_params(inp, l):
    Pm = np.zeros((128, NPRM), np.float32)
    Pm[:, P_GMIX:P_GMIX + 8] = inp["g_mix"][l].reshape(8, 128).T
    Pm[:, P_GFFN:P_GFFN + 8] = inp["g_ffn"][l].reshape(8, 128).T
    Pm[:, P_GPLE:P_GPLE + 8] = inp["g_ple"][l].reshape(8, 128).T
    Pm[0:64, P_FQG] = inp["fox_q_gain"][l]
    Pm[0:64, P_FKG] = inp["fox_k_gain"][l]
    Pm[0:8, P_BF] = inp["b_fox_f"][l]
    Pm[32:36, P_ALOG] = inp["dn_a_log"][l]
    Pm[32:36, P_DTB] = inp["dn_dt_bias"][l]
    Pm[:, P_DNG] = inp["dn_norm_gain"][l]
    Pm[:, P_SCW:P_SCW + 12] = inp["sc_conv_w"][l].reshape(3, 4, 128).transpose(2, 1, 0).reshape(128, 12)
    Pm[:, P_DNW:P_DNW + 48] = inp["dn_conv_w"][l].reshape(4, 12, 128).transpose(2, 1, 0).reshape(128, 48)
    Pm[:, P_FFW:P_FFW + 132] = inp["ffn_conv_w"][l].reshape(3, 44, 128).transpose(2, 1, 0).reshape(128, 132)
    return Pm


def make_consts():
    C = np.zeros((128, NCST), np.float32)
    r = np.arange(128)[:, None]
    c = np.arange(128)[None, :]
    C[:, K_ID:K_ID + 128] = (r == c)
    C[:, K_M01:K_M01 + 128] = (c >= r)
    C[:, K_M01 + 128:K_M01 + 256] = (c > r)
    C[:, K_ONE:K_ONE + 128] = 1.0
    for h in range(4):
        C[32 + h, K_SEL + h * 128:K_SEL + (h + 1) * 128] = 1.0
    r6 = np.arange(64)[:, None]
    c6 = np.arange(64)[None, :]
    C[0:64, K_MPA:K_MPA + 64] = np.where(r6 > c6, 0.0, 1.0e4)
    C[0:64, K_MPB:K_MPB + 64] = np.where(c6 >= r6, 0.0, 1.0e4)
    return C


def build(depth=DEPTH, dbg=None, stop_after=None):
    nc = bass.Bass("TRN2", target_bir_lowering=False)
    x_in = nc.dram_tensor("x", [S, D], F32, kind="ExternalInput").ap()
    p_in = nc.dram_tensor("p", [DEPTH, S, 256], F32, kind="ExternalInput").ap()
    wst = nc.dram_tensor("wst", [DEPTH, NT, 128, 1024], F32, kind="ExternalInput").ap()
    prm_in = nc.dram_tensor("prm", [DEPTH, 128, NPRM], F32, kind="ExternalInput").ap()
    cst_in = nc.dram_tensor("cst", [128, NCST], F32, kind="ExternalInput").ap()
    out = nc.dram_tensor("out", [S, D], F32, kind="ExternalOutput").ap()
    scrx = nc.dram_tensor("scr_x", [128, NCH, S], F32).ap()
    dbg_outs = {}

    TOT = 211456
    SB = nc.alloc_sbuf_tensor("SB", [128, TOT // 4], F32)
    PS = nc.alloc_psum_tensor("PS", [128, 8 * 512], F32)
    P = Prog(nc)

    def V(off, shape, dt=F32, p0=0):
        es = mybir.dt.size(dt)
        n = _prod(shape[1:])
        assert off % 4 == 0 and off + n * es <= TOT, (off, shape)
        nw = (n * es + 3) // 4
        v = SB[p0:p0 + shape[0], off // 4: off // 4 + nw]
        if dt != F32:
            v = v.bitcast(dt)
        v = v[:, 0:n]
        if len(shape) == 3:
            v = v.rearrange("p (a b) -> p a b", a=shape[1])
        elif len(shape) == 4:
            v = v.rearrange("p (a b c) -> p a b c", a=shape[1], b=shape[2])
        return v

    def bank(b, parts=128, dt=F32, p0=0):
        v = PS[p0:p0 + parts, b * 512:(b + 1) * 512]
        if dt != F32:
            v = v.bitcast(dt)
        return v

    KB = 1024
    O_XT = 0
    O_HT = 64 * KB
    O_YT = 96 * KB
    O_WS = 144 * KB
    O_WB = 152 * KB
    O_CST = 158 * KB
    O_CBF = 164 * KB
    O_PRM = 165 * KB
    O_MISC = 166 * KB
    O_SMT = 167 * KB
    O_SCR = 175 * KB
    SCR_END = TOT

    xT = V(O_XT, [128, NCH, S], F32)
    hT = V(O_HT, [128, NCH, S], BF16)
    yT = V(O_YT, [128, 12, S], BF16)
    wstage = [V(O_WS + i * 4 * KB, [128, 1024], F32) for i in range(2)]
    wbf = [V(O_WB + i * 2 * KB, [128, 1024], BF16) for i in range(3)]
    cst = V(O_CST, [128, NCST], F32)
    cbf = V(O_CBF, [128, 512], BF16)
    prm = V(O_PRM, [128, NPRM], F32)
    misc = V(O_MISC, [128, 256], F32)
    smT = V(O_SMT, [128, S], F32)

    ident_f = cst[:, K_ID:K_ID + 128]
    ident_b = cbf[:, 0:128]
    mask01_b = cbf[:, 128:256]
    ones_b = cbf[:, 384:512]
    sel_f = cst[:, K_SEL:K_SEL + 512]
    mposA = cst[0:64, K_MPA:K_MPA + 64]
    mposB = cst[0:64, K_MPB:K_MPB + 64]
    epsc = misc[:, 0:1]
    onec = misc[:, 1:2]
    qgs = misc[:, 2:3]
    negb = misc[:, 3:4]
    negA = misc[:, 4:5]

    state = {"ws": 0, "wb": 0, "ps": 0}

    def wload(l, name, X):
        si = state["ws"] % 2
        bi = state["wb"] % 3
        state["ws"] += 1
        state["wb"] += 1
        P.dma("sp", wstage[si][:, 0:X], wst[l, WIDX[name], :, 0:X], f"ws{si}")
        P.copy("pool", wbf[bi][:, 0:X], wstage[si][:, 0:X])
        return wbf[bi][:, 0:X]

    def psb(n=4, base=0):
        k = ("ps", base, n)
        state[k] = state.get(k, 0) + 1
        return base + (state[k] - 1) % n

    def dump(name, ap, shape):
        if dbg is None or name not in dbg:
            return
        t = nc.dram_tensor("dbg_" + name, list(shape), ap.dtype, kind="ExternalOutput").ap()
        dbg_outs[name] = t
        P.dma("sp", t, ap, "dbgout", final=True)

    P.dma("sp", cst[:], cst_in, "cstin")
    P.copy("dve", cbf[:], cst[:, 0:512])
    P.memset("dve", epsc, EPS)
    P.memset("dve", onec, 1.0)

    def rmsnorm_to_hT(gcol0, o_scr):
        sq = [V(o_scr + i * KB, [128, 512], BF16) for i in range(2)]
        lnt = V(o_scr + 2 * KB, [128, 512], F32)
        rstd = V(o_scr + 4 * KB, [128, 512], F32)
        for tt in range(NTT):
            ts_ = slice(tt * 512, (tt + 1) * 512)
            pb = bank(psb(2, 6))
            for c in range(NCH):
                s_ = sq[c % 2]
                P.act(s_, xT[:, c, ts_], AF.Square)
                P.mm(pb, ones_b, s_, start=(c == 0), stop=(c == NCH - 1))
            P.act(lnt, pb, AF.Ln, bias=epsc, scale=1.0 / D)
            P.act(rstd, lnt, AF.Exp, scale=-0.5)
            for c in range(NCH):
                P.stt("dve", hT[:, c, ts_], xT[:, c, ts_], prm[:, gcol0 + c:gcol0 + c + 1], rstd,
                      ALU.mult, ALU.mult)

    def proj_fm(wt, M, m0, consumer, kchunks=NCH, rhs_of=None, tts=range(NTT), defer=False):
        w3 = wt.rearrange("p (k c) -> p k c", k=kchunks)
        pend = None
        for tt in tts:
            ts_ = slice(tt * 512, (tt + 1) * 512)
            pb = bank(psb(4, 0), M)
            for k in range(kchunks):
                rhs = hT[:, k, ts_] if rhs_of is None else rhs_of(k, ts_)
                P.mm(pb, w3[:, k, m0:m0 + M], rhs, start=(k == 0), stop=(k == kchunks - 1))
            if defer:
                if pend is not None:
                    consumer(*pend)
                pend = (tt, ts_, pb)
            else:
                consumer(tt, ts_, pb)
        if pend is not None:
            consumer(*pend)

    def layer(l):
        P.dma("sp", prm[:], prm_in[l], "prmin")
        P.ts("dve", qgs[0:64], prm[0:64, P_FQG:P_FQG + 1], 0.125, None, op0=ALU.mult)
        P.ts("dve", negb[0:8], prm[0:8, P_BF:P_BF + 1], -1.0, None, op0=ALU.mult)
        P.act(negA[32:36], prm[32:36, P_ALOG:P_ALOG + 1], AF.Exp)
        P.ts("dve", negA[32:36], negA[32:36], -1.0, None, op0=ALU.mult)

        rmsnorm_to_hT(P_GMIX, O_SCR)
        dump(f"h{l}", hT[:, :, :], [128, NCH, S])
        for c in range(NCH):
            P.dma("sp", scrx[:, c, :], xT[:, c, :], f"spill{c}")
        if stop_after == "norm1":
            return

        wt = wload(l, "small", 8 * 96)

        def small_cons(tt, ts_, pb):
            P.act(smT[0:8, ts_], pb[0:8, :], AF.Exp, bias=negb[0:8], scale=-1.0)
            P.act(smT[32:36, ts_], pb[32:36, :], AF.Exp, bias=prm[32:36, P_DTB:P_DTB + 1], scale=1.0)
            P.act(smT[64:68, ts_], pb[64:68, :], AF.Exp, scale=-1.0)
        proj_fm(wt, 96, 0, small_cons)
        P.act(smT[0:8, :], smT[0:8, :], AF.Ln, bias=onec[0:8], scale=1.0)
        P.act(smT[32:36, :], smT[32:36, :], AF.Ln, bias=onec[32:36], scale=1.0)
        P.ts("dve", smT[64:68, :], smT[64:68, :], 1.0, None, op0=ALU.add)
        P.add("dve", lambda e: e.reciprocal(smT[64:68, :], smT[64:68, :]), reads=[smT[64:68, :]],
              writes=[smT[64:68, :]])
        aux = V(O_SCR + 16 * KB, [128, S], F32)
        P.scan(aux[0:8, :], onec[0:8].to_broadcast([8, S]), smT[0:8, :], 0.0, ALU.mult, ALU.subtract)
        cqf = aux[0:8, :]
        P.ts("dve", smT[32:36, :], smT[32:36, :], negA[32:36], None, op0=ALU.mult)
        dump(f"dn_g{l}", smT[32:36, :], [4, S])
        dump(f"dn_beta{l}", smT[64:68, :], [4, S])
        P.scan(aux[32:36, :], onec[32:36].to_broadcast([4, S]), smT[32:36, :], 0.0, ALU.mult, ALU.add)
        a3 = aux[32:36, :].rearrange("p (n c) -> p n c", c=64)
        g3 = smT[32:36, :].rearrange("p (n c) -> p n c", c=64)
        gl = V(O_SCR, [128, 32], F32)
        P.copy("dve", gl[32:36, :], a3[:, :, 63])
        P.copy("dve", g3[:, 0, :], a3[:, 0, :])
        P.tt("dve", g3[:, 1:32, :], a3[:, 1:32, :], gl[32:36, 0:31].unsqueeze(2).to_broadcast([4, 31, 64]),
             ALU.subtract)
        dump(f"cq{l}", cqf, [8, S])
        dump(f"gcum{l}", smT[32:36, :], [4, S])

        o = O_XT
        cqs = V(o, [8, 3, S], BF16); o += 12 * KB
        vext = V(o, [128, 16, 8, 65], BF16); o += 17 * KB
        qa = [V(o + i * 8 * KB, [128, S], BF16) for i in range(2)]
        ka = [V(o + i * 8 * KB + 4 * KB, [128, S], BF16) for i in range(2)]
        o += 16 * KB
        ytok = V(o, [128, 16, 128], BF16); o += 4 * KB
        ptile = [V(o + i * KB, [128, 512], BF16) for i in range(3)]; o += 3 * KB
        sqh = [V(o + i * KB, [64, 512], BF16) for i in range(2)]; o += 2 * KB
        lnh = V(o, [64, 512], F32); o += 2 * KB
        rsh = V(o, [64, 512], F32); o += 2 * KB
        rcp = V(o, [128, 4], F32); o += 128
        assert o <= 64 * KB
        cr = V(O_SCR + 8 * KB, [8, S], F32)
        P.copy("dve", cqs[:, 0, :], cqf)
        P.tt("dve", cr, cqf, cqs[:, 0, :], ALU.subtract)
        P.copy("dve", cqs[:, 1, :], cr)
        P.tt("dve", cr, cr, cqs[:, 1, :], ALU.subtract)
        P.copy("dve", cqs[:, 2, :], cr)
        P.memset("dve", vext[:, :, :, 64:65], 1.0)
        for ct in range(4):
            wt = wload(l, f"foxv{ct}", 1024)
            w3 = wt.rearrange("p (k c) -> p k c", k=NCH)
            for g4 in range(4):
                pb = bank(psb(4, 0))
                for t4 in range(4):
                    tb = g4 * 4 + t4
                    for k in range(NCH):
                        P.mm(pb[:, t4 * 128:(t4 + 1) * 128], hT[:, k, tb * 128:(tb + 1) * 128], w3[:, k, :],
                             start=(k == 0), stop=(k == NCH - 1))
                P.copy("act", vext[:, g4 * 4:(g4 + 1) * 4, 2 * ct:2 * ct + 2, 0:64],
                       pb.rearrange("p (a b c) -> p a b c", a=4, b=2))
        for h in range(8):
            wt = wload(l, f"foxqk{h}", 1024)
            qa_h, ka_h = qa[h % 2], ka[h % 2]
            P.memset("dve", qa_h[64:70, :], -1.0)
            P.memset("dve", ka_h[64:70, :], 1.0)
            P.dma("sp", qa_h[64:67, :], cqs[h:h + 1, :, :], f"augq{h % 2}")
            P.dma("sp", ka_h[67:70, :], cqs[h:h + 1, :, :], f"augk{h % 2}")
            for which, dst, m0, gcol in ((0, qa_h, 0, qgs), (1, ka_h, 64, prm[:, P_FKG:P_FKG + 1])):
                def qk_cons(tt, ts_, pb, dst=dst, gcol=gcol):
                    s_ = sqh[tt % 2]
                    P.act(s_, pb, AF.Square)
                    p2 = bank(psb(2, 6), 64)
                    P.mm(p2, ones_b[0:64, 0:64], s_)
                    P.act(lnh, p2, AF.Ln, bias=epsc[0:64], scale=1.0 / 64)
                    P.act(rsh, lnh, AF.Exp, scale=-0.5)
                    P.stt("dve", dst[0:64, ts_], pb, gcol[0:64], rsh, ALU.mult, ALU.mult)
                proj_fm(wt, 64, m0, qk_cons, defer=True)
            tasks = [(qt, kb) for qt in range(4) for kb in range(4 * qt + 4)]
            obanks = {}

            def emit_S(qt, kb):
                n0 = max(kb * 128, qt * 512)
                ncols = (qt + 1) * 512 - n0
                pb = bank(psb(4, 0))
                P.mm(pb[:, 0:ncols], ka_h[0:70, kb * 128:(kb + 1) * 128], qa_h[0:70, n0:n0 + ncols])
                pt = ptile[state["ps"] % 3]
                state["ps"] += 1
                P.act(pt[:, 0:ncols], pb[:, 0:ncols], AF.Exp)
                if kb * 128 >= qt * 512:
                    P.tt("dve", pt[:, 0:128], pt[:, 0:128], mask01_b, ALU.mult)
                return pt, n0

            def emit_PV(qt, kb, pt, n0):
                ob = bank(4 + qt % 2)
                o4 = ob.rearrange("p (a b) -> p a b", a=4)
                for qb in range(n0 // 128, 4 * qt + 4):
                    c0 = qb * 128 - n0
                    P.mm(o4[:, qb - 4 * qt, 0:65], pt[:, c0:c0 + 128], vext[:, kb, h, :],
                         start=(kb == 0 and qb == 4 * qt), stop=(kb == qb))
                if kb == 4 * qt + 3:
                    P.add("dve", lambda e, o4=o4: e.reciprocal(rcp[:, :].unsqueeze(2), o4[:, :, 64:65]),
                          reads=[o4[:, :, 64:65]], writes=[rcp[:, :]])
                    P.tt("dve", ytok[:, 4 * qt:4 * qt + 4, (h % 2) * 64:(h % 2) * 64 + 64], o4[:, :, 0:64],
                         rcp[:, :].unsqueeze(2).to_broadcast([128, 4, 64]), ALU.mult)

            LA = 2
            pend = [emit_S(*tasks[i]) for i in range(LA)]
            for i, (qt, kb) in enumerate(tasks):
                if i + LA < len(tasks):
                    pend.append(emit_S(*tasks[i + LA]))
                emit_PV(qt, kb, *pend.pop(0))
            if h % 2 == 1:
                for half in range(2):
                    tp = bank(6 + half, 128, BF16)
                    for i in range(8):
                        qb = half * 8 + i
                        P.transpose(tp[:, i * 128:(i + 1) * 128], ytok[:, qb, :], ident_b)
                    P.copy("act", yT[:, h // 2, half * 1024:(half + 1) * 1024], tp)
        dump(f"y_fox{l}", yT[:, 0:4, :], [128, 4, S])
        if stop_after == "fox":
            return

        o = O_XT
        cv = V(o, [128, S + 2], F32); o += 8 * KB + 128
        acc = V(o, [128, S], F32); o += 8 * KB
        bsb = V(o, [128, S], F32); o += 8 * KB
        ctmp = [V(o + i * 2 * KB, [128, 512], F32) for i in range(2)]; o += 4 * KB
        P.memset("dve", cv[:, 0:2], 0.0)
        for j in range(4):
            wb_ = wload(l, f"scb{j}", 1024)
            proj_fm(wb_, 128, 0, lambda tt, ts_, pb: P.copy("act", bsb[:, ts_], pb))
            wc_ = wload(l, f"scc{j}", 1024)
            wv_ = wload(l, f"scv{j}", 1024)
            w3c = wc_.rearrange("p (k c) -> p k c", k=NCH)
            w3v = wv_.rearrange("p (k c) -> p k c", k=NCH)
            for tt in range(NTT):
                ts_ = slice(tt * 512, (tt + 1) * 512)
                pc = bank(psb(4, 0))
                for k in range(NCH):
                    P.mm(pc, w3c[:, k, :], hT[:, k, ts_], start=(k == 0), stop=(k == NCH - 1))
                P.copy("act", ctmp[tt % 2], pc)
                pv = bank(psb(4, 0))
                for k in range(NCH):
                    P.mm(pv, w3v[:, k, :], hT[:, k, ts_], start=(k == 0), stop=(k == NCH - 1))
                P.tt("dve", cv[:, 2 + tt * 512:2 + (tt + 1) * 512], pv, ctmp[tt % 2], ALU.mult)
            w0 = prm[:, P_SCW + j * 3:P_SCW + j * 3 + 1]
            w1 = prm[:, P_SCW + j * 3 + 1:P_SCW + j * 3 + 2]
            w2 = prm[:, P_SCW + j * 3 + 2:P_SCW + j * 3 + 3]
            P.act(acc, cv[:, 2:2 + S], AF.Copy, scale=w2)
            P.stt("dve", acc, cv[:, 1:1 + S], w1, acc, ALU.mult, ALU.add)
            P.stt("dve", acc, cv[:, 0:S], w0, acc, ALU.mult, ALU.add)
            P.tt("dve", yT[:, 4 + j, :], acc, bsb, ALU.mult)
        dump(f"y_sc{l}", yT[:, 4:8, :], [128, 4, S])
        if stop_after == "sc":
            return

        gtok = V(O_SCR, [64, 32, 4], F32)
        btok = V(O_SCR + 512, [64, 32, 4], F32)
        for src0, dst in ((32, gtok), (64, btok)):
            pb = bank(psb(2, 6), 64)
            p3 = pb[:, 0:128].rearrange("p (n h) -> p n h", h=4)
            for n in range(32):
                P.transpose(p3[:, n, :], smT[src0:src0 + 4, n * 64:(n + 1) * 64],
                            ident_f[src0:src0 + 4, src0:src0 + 4])
            P.copy("dve", dst, p3)
        o = O_XT
        raw = V(o, [128, S + 3], F32); o += 8 * KB + 128
        cacc = V(o, [128, S], F32); o += 8 * KB
        QT = V(o, [128, S], BF16); o += 4 * KB
        KT = V(o, [128, S], BF16); o += 4 * KB
        VT = V(o, [128, S], BF16); o += 4 * KB
        cacc2 = V(o, [128, S], F32)
        qdT = V(o, [128, S], BF16); o += 4 * KB
        nkcT = V(o, [128, 32, 64], BF16); o += 4 * KB
        raw_k = V(o, [128, S + 3], F32)
        raw_v = V(o + 8 * KB + 128, [128, S + 3], F32)
        Kg = V(o, [64, 32, 128], BF16); o += 8 * KB
        Kd = V(o, [64, 32, 128], BF16); o += 8 * KB
        Vb = V(o, [64, 32, 128], BF16); o += 8 * KB
        assert o <= 64 * KB, o
        o = O_SCR + 1 * KB
        cm = [V(o + i * 4 * KB, [64, 32, 64], BF16) for i in range(5)]; o += 20 * KB
        Mm, MTm, PTm, qkT, Mn = cm
        MTn = V(o, [64, 32, 64], BF16); o += 4 * KB
        PTn = V(o, [64, 32, 64], BF16); o += 4 * KB
        assert o <= SCR_END, o
        GB = V(O_XT, [128, S], F32)
        Xm = V(O_XT + 8 * KB + 128, [64, 32, 64], F32)
        sq2 = [V(O_SCR + 29 * KB + i * KB, [128, 512], BF16) for i in range(2)]
        eg = misc[0:64, 8:40]; bgc = misc[0:64, 40:72]; edc = misc[0:64, 72:104]
        gtot = misc[:, 104:136]; nbt = misc[0:64, 136:168]

        for h in range(4):
            raws = (raw, raw_k, raw_v)
            for which, nm in ((0, "dnq"), (1, "dnk"), (2, "dnv")):
                wt = wload(l, f"{nm}{h}", 1024)
                rw = raws[which]
                P.memset("dve", rw[:, 0:3], 0.0)
                proj_fm(wt, 128, 0, lambda tt, ts_, pb, rw=rw: P.copy("act", rw[:, 3 + tt * 512:3 + (tt + 1) * 512], pb))
            for which, dstT, scl, ca in ((0, QT, 128.0 ** -0.5, cacc), (1, KT, 1.0, cacc2), (2, VT, None, cacc)):
                rw = raws[which]
                cw = P_DNW + (which * 4 + h) * 4
                P.act(ca, rw[:, 3:3 + S], AF.Copy, scale=prm[:, cw + 3:cw + 4])
                for j in range(3):
                    P.stt("dve", ca, rw[:, j:j + S], prm[:, cw + j:cw + j + 1], ca, ALU.mult, ALU.add)
                if scl is None:
                    P.act(dstT, ca, AF.Silu)
                else:
                    P.act(ca, ca, AF.Silu)
                    for tt in range(NTT):
                        ts_ = slice(tt * 512, (tt + 1) * 512)
                        s_ = sq2[tt % 2]
                        P.act(s_, ca[:, ts_], AF.Square)
                        p2 = bank(psb(2, 6))
                        P.mm(p2, ones_b, s_)
                        lnt = V(O_SCR + 25 * KB, [128, 512], F32)
                        rst = V(O_SCR + 27 * KB, [128, 512], F32)
                        P.act(lnt, p2, AF.Ln, bias=epsc, scale=1.0)
                        P.act(rst, lnt, AF.Exp, scale=-0.5)
                        P.stt("dve", dstT[:, ts_], ca[:, ts_], scl, rst, ALU.mult, ALU.mult)
            if h == 0:
                dump(f"dn_q{l}", QT, [128, S]); dump(f"dn_k{l}", KT, [128, S]); dump(f"dn_v{l}", VT, [128, S])
            for tt in range(NTT):
                ts_ = slice(tt * 512, (tt + 1) * 512)
                pb = bank(psb(4, 0))
                P.mm(pb, sel_f[32:36, h * 128:(h + 1) * 128], smT[32:36, ts_])
                P.copy("act", GB[:, ts_], pb)
                egt = V(O_SCR + 25 * KB, [128, 512], F32)
                P.act(egt, pb, AF.Exp)
                P.tt("dve", qdT[:, ts_], QT[:, ts_], egt, ALU.mult)
            GB3 = GB.rearrange("p (n c) -> p n c", c=64)
            P.act(eg, gtok[:, :, h], AF.Exp)
            P.tt("dve", bgc, btok[:, :, h], eg, ALU.mult)
            P.tt("dve", edc, GB3[0:64, :, 63], gtok[:, :, h], ALU.subtract)
            P.act(edc, edc, AF.Exp)
            P.act(gtot, GB3[:, :, 63], AF.Exp)
            P.ts("dve", nbt, btok[:, :, h], -1.0, None, op0=ALU.mult)
            for g8 in range(4):
                for src, outs in ((KT, ((Kg, bgc), (Kd, edc))), (VT, ((Vb, btok[:, :, h]),))):
                    tp = bank(psb(2, 6), 64, BF16)
                    for i in range(8):
                        n = g8 * 8 + i
                        P.transpose(tp[:, i * 128:(i + 1) * 128], src[:, n * 64:(n + 1) * 64], ident_b)
                    t3 = tp.rearrange("p (n d) -> p n d", d=128)
                    for (dst, col) in outs:
                        P.tt("dve", dst[:, g8 * 8:(g8 + 1) * 8, :], t3,
                             col[:, g8 * 8:(g8 + 1) * 8].unsqueeze(2).to_broadcast([64, 8, 128]), ALU.mult)
            P.tt("dve", Xm, GB3[0:64, :, :], gtok[:, :, h].unsqueeze(2).to_broadcast([64, 32, 64]), ALU.subtract)
            DLn = V(O_SCR + 21 * KB, [64, 32, 64], F32)
            P.tt("dve", DLn, Xm, mposA.unsqueeze(1).to_broadcast([64, 32, 64]), ALU.add)
            P.act(DLn, DLn, AF.Exp, scale=-1.0)
            P.tt("dve", DLn, DLn, nbt[:, :].unsqueeze(2).to_broadcast([64, 32, 64]), ALU.mult)
            P.tt("dve", Xm, Xm, mposB.unsqueeze(1).to_broadcast([64, 32, 64]), ALU.subtract)
            P.act(Xm, Xm, AF.Exp)
            for g8 in range(4):
                pk = bank(psb(4, 0), 64)
                pq = bank(psb(4, 0), 64)
                for i in range(8):
                    n = g8 * 8 + i
                    cs = slice(n * 64, (n + 1) * 64)
                    P.mm(pk[:, i * 64:(i + 1) * 64], KT[:, cs], KT[:, cs])
                    P.mm(pq[:, i * 64:(i + 1) * 64], KT[:, cs], QT[:, cs])
                gs = slice(g8 * 8, (g8 + 1) * 8)
                P.tt("dve", Mm[:, gs, :], pk.rearrange("p (n c) -> p n c", c=64), DLn[:, gs, :], ALU.mult)
                P.tt("dve", qkT[:, gs, :], pq.rearrange("p (n c) -> p n c", c=64), Xm[:, gs, :], ALU.mult)
            for g8 in range(4):
                tp = bank(psb(2, 6), 64, BF16)
                for i in range(8):
                    n = g8 * 8 + i
                    P.transpose(tp[:, i * 64:(i + 1) * 64], Mm[:, n, :], ident_b[0:64, 0:64])
                P.copy("act", MTm[:, g8 * 8:(g8 + 1) * 8, :], tp[:, 0:512].rearrange("p (n c) -> p n c", c=64))
            P.tt("dve", PTm, MTm, ident_b[0:64, 0:64].unsqueeze(1).to_broadcast([64, 32, 64]), ALU.add)
            Wc, WTc, PTc = Mm, MTm, PTm
            Wn, WTn, PTx = Mn, MTn, PTn
            for it in range(5):
                def stA(g8, it=it, Wc=Wc, WTc=WTc, Wn=Wn, WTn=WTn):
                    pw = bank(psb(6, 0), 64)
                    pwt = bank(psb(6, 0), 64) if it < 4 else None
                    for i in range(8):
                        n = g8 * 8 + i
                        P.mm(pw[:, i * 64:(i + 1) * 64], WTc[:, n, :], Wc[:, n, :])
                        if pwt is not None:
                            P.mm(pwt[:, i * 64:(i + 1) * 64], Wc[:, n, :], WTc[:, n, :])
                    gs = slice(g8 * 8, (g8 + 1) * 8)
                    P.copy("act", Wn[:, gs, :], pw.rearrange("p (n c) -> p n c", c=64))
                    if pwt is not None:
                        P.copy("dve", WTn[:, gs, :], pwt.rearrange("p (n c) -> p n c", c=64))

                def stB(g8, Wn=Wn, PTc=PTc, PTx=PTx):
                    pp = bank(psb(6, 0), 64)
                    for i in range(8):
                        n = g8 * 8 + i
                        P.mm(pp[:, i * 64:(i + 1) * 64], ident_b[0:64, 0:64], PTc[:, n, :], start=True, stop=False)
                        P.mm(pp[:, i * 64:(i + 1) * 64], Wn[:, n, :], PTc[:, n, :], start=False, stop=True)
                    gs = slice(g8 * 8, (g8 + 1) * 8)
                    P.copy("act", PTx[:, gs, :], pp.rearrange("p (n c) -> p n c", c=64))
                stA(0); stA(1); stB(0); stA(2); stB(1); stA(3); stB(2); stB(3)
                Wc, Wn = Wn, Wc
                WTc, WTn = WTn, WTc
                PTc, PTx = PTx, PTc
            TT = PTc
            for g8 in range(4):
                pb = bank(psb(4, 0))
                for i in range(8):
                    n = g8 * 8 + i
                    P.mm(pb[:, i * 64:(i + 1) * 64], Kg[:, n, :], TT[:, n, :])
                P.act(nkcT[:, g8 * 8:(g8 + 1) * 8, :], pb.rearrange("p (n c) -> p n c", c=64), AF.Copy, scale=-1.0)
            o3 = O_SCR + 29 * KB
            Sf = V(o3, [128, 128], F32)
            Sbb = [V(o3 + 512 + i * 256, [128, 128], BF16) for i in range(2)]
            vn = [V(o3 + 1024 + i * 256, [64, 128], BF16) for i in range(2)]
            oT = V(O_XT + 8 * KB + 128, [128, S], F32)
            P.memset("dve", Sf, 0.0)
            P.memset("dve", Sbb[0], 0.0)
            for n in range(32):
                sb_ = Sbb[n % 2]
                pv = bank(4, 64)[:, (n % 2) * 128:(n % 2) * 128 + 128]
                P.mm(pv, TT[:, n, :], Vb[:, n, :], start=True, stop=False)
                P.mm(pv, nkcT[:, n, :], sb_, start=False, stop=True)
                vn_ = vn[n % 2]
                P.copy("act", vn_, pv)
                if n % 8 == 0:
                    po = bank(psb(2, 2))
                pos = po[:, (n % 8) * 64:(n % 8 + 1) * 64]
                P.mm(pos, sb_, qdT[:, n * 64:(n + 1) * 64], start=True, stop=False)
                P.mm(pos, vn_, qkT[:, n, :], start=False, stop=True)
                pd = bank(5)[:, (n % 2) * 128:(n % 2) * 128 + 128]
                P.mm(pd, Kd[:, n, :], vn_)
                P.stt("dve", Sbb[(n + 1) % 2], Sf, gtot[:, n:n + 1], pd, ALU.mult, ALU.add)
                P.stt("dve", Sf, Sf, gtot[:, n:n + 1], pd, ALU.mult, ALU.add)
                if n % 8 == 7:
                    tt = n // 8
                    P.copy("act", oT[:, tt * 512:(tt + 1) * 512], po)
            if h == 0:
                dump(f"o_dn{l}", oT, [128, S])
            wz = wload(l, f"dnz{h}", 1024)

            for tt in range(NTT):
                ts_ = slice(tt * 512, (tt + 1) * 512)
                s_ = sq2[tt % 2]
                P.act(s_, oT[:, ts_], AF.Square)
                p2 = bank(psb(2, 6))
                P.mm(p2, ones_b, s_)
                lnt = V(O_SCR + 25 * KB, [128, 512], F32)
                rst = V(O_SCR + 27 * KB, [128, 512], F32)
                P.act(lnt, p2, AF.Ln, bias=epsc, scale=1.0 / 128)
                P.act(rst, lnt, AF.Exp, scale=-0.5)
                P.stt("dve", oT[:, ts_], oT[:, ts_], prm[:, P_DNG:P_DNG + 1], rst, ALU.mult, ALU.mult)

            def z_cons(tt, ts_, pb, h=h):
                zt = V(O_SCR + 25 * KB + (tt % 2) * 2 * KB, [128, 512], F32)
                P.act(zt, pb, AF.Silu)
                P.tt("dve", yT[:, 8 + h, ts_], oT[:, ts_], zt, ALU.mult)
            proj_fm(wz, 128, 0, z_cons)
        dump(f"y_dn{l}", yT[:, 8:12, :], [128, 4, S])
        if stop_after == "dn":
            return

        mT = V(O_SCR, [128, NCH, 1024], BF16)
        macc = V(O_SCR + 16 * KB, [128, 512], F32)
        mtmp = V(O_SCR + 18 * KB, [128, 512], F32)
        sgt = [[V(O_XT + b * 8 * KB + 4 * KB + t2 * 2 * KB, [128, 512], F32) for t2 in range(2)] for b in range(3)]
        for half in range(2):
            for dc in range(NCH):
                for b in range(3):
                    wgb = wload(l, f"g{b}_{dc}", 1024).rearrange("p (k c) -> p k c", k=NCH)
                    for t2 in range(2):
                        tt = half * 2 + t2
                        ts_ = slice(tt * 512, (tt + 1) * 512)
                        pg = bank(psb(4, 0))
                        for k in range(NCH):
                            P.mm(pg, wgb[:, k, :], hT[:, k, ts_], start=(k == 0), stop=(k == NCH - 1))
                        P.act(sgt[b][t2], pg, AF.Sigmoid)
                wA3 = wload(l, f"brA{dc}", 1024).rearrange("p (b c m) -> p b c m", b=2, c=4)
                wB3 = wload(l, f"brB{dc}", 512).rearrange("p (c m) -> p c m", c=4)
                for t2 in range(2):
                    tt = half * 2 + t2
                    ts_ = slice(tt * 512, (tt + 1) * 512)
                    for b in range(3):
                        pbr = bank(psb(4, 0))
                        for c in range(4):
                            lw = wA3[:, b, c, :] if b < 2 else wB3[:, c, :]
                            P.mm(pbr, lw, yT[:, 4 * b + c, ts_], start=(c == 0), stop=(c == 3))
                        if b == 0:
                            P.tt("dve", macc, pbr, sgt[b][t2], ALU.mult)
                        elif b == 1:
                            P.tt("dve", mtmp, pbr, sgt[b][t2], ALU.mult)
                            P.tt("dve", macc, macc, mtmp, ALU.add)
                        else:
                            P.tt("dve", mtmp, pbr, sgt[b][t2], ALU.mult)
                            P.tt("dve", mT[:, dc, t2 * 512:(t2 + 1) * 512], macc, mtmp, ALU.add)
            if half == 0:
                dump(f"merged{l}", mT[:, :, :], [128, NCH, 1024])
            for dc in range(NCH):
                wo = wload(l, f"wo{dc}", 1024)
                hs = slice(half * 1024, (half + 1) * 1024)
                P.dma("sp", xT[:, dc, hs], scrx[:, dc, hs], f"unsp{dc}")

                def wo_cons(tt, ts_, pb, dc=dc):
                    P.tt("dve", xT[:, dc, ts_], xT[:, dc, ts_], pb, ALU.add)
                proj_fm(wo, 128, 0, wo_cons, rhs_of=lambda k, ts_, half=half: mT[:, k, ts_.start - half * 1024: ts_.stop - half * 1024],
                        tts=range(half * 2, half * 2 + 2))
        dump(f"x_mix{l}", xT[:, :, :], [128, NCH, S])
        if stop_after == "mix":
            return

        rmsnorm_to_hT(P_GFFN, O_SCR)
        o = O_YT
        graw = V(o, [128, S + 2], F32); o += 8 * KB + 128
        vraw = V(o, [128, S + 2], F32); o += 8 * KB + 128
        gacs = [V(o, [128, S], F32), V(O_SMT, [128, S], F32)]; o += 8 * KB
        vacs = [V(o, [128, S], F32), V(O_SCR + 20 * KB, [128, S], F32)]; o += 8 * KB
        assert o <= O_WS
        aT0 = V(O_SCR + 0 * KB, [128, GRP, S], BF16)
        aT1 = V(O_YT + 33 * KB, [128, 3, S], BF16)
        aT1b = V(O_SCR + 16 * KB, [128, 1, S], BF16)
        P.memset("dve", graw[:, 0:2], 0.0)
        P.memset("dve", vraw[:, 0:2], 0.0)

        def a_slot(gi, jj):
            if gi % 2 == 0:
                return aT0[:, jj, :]
            return aT1[:, jj, :] if jj < 3 else aT1b[:, 0, :]

        def ffn_up(gi, j0, n):
                for jj in range(n):
                    j = j0 + jj
                    gac = gacs[j % 2]
                    vac = vacs[j % 2]
                    for nm, rawb, accb, c0 in (("upg", graw, gac, j), ("upv", vraw, vac, NFF + j)):
                        wt = wload(l, f"{nm}{j}", 1024)
                        w2c = prm[:, P_FFW + c0 * 3 + 2:P_FFW + c0 * 3 + 3]

                        def up_cons(tt, ts_, pb, rawb=rawb, accb=accb, w2c=w2c):
                            P.copy("act", rawb[:, 2 + tt * 512:2 + (tt + 1) * 512], pb)
                            P.act(accb[:, ts_], pb, AF.Copy, scale=w2c)
                        proj_fm(wt, 128, 0, up_cons)
                        P.stt("dve", accb, rawb[:, 1:1 + S], prm[:, P_FFW + c0 * 3 + 1:P_FFW + c0 * 3 + 2], accb,
                              ALU.mult, ALU.add)
                        P.stt("dve", accb, rawb[:, 0:S], prm[:, P_FFW + c0 * 3:P_FFW + c0 * 3 + 1], accb,
                              ALU.mult, ALU.add)
                    P.act(gac, gac, AF.Silu)
                    P.tt("dve", a_slot(gi, jj), gac, vac, ALU.mult)

        def ffn_down(gi, j0, n):
                for q in range(4):
                    wd = wload(l, f"dn{gi}_{q}", n * 256)
                    wd3 = wd.rearrange("p (j c) -> p j c", j=n)
                    for dd in range(2):
                        dc = 2 * q + dd
                        for tt in range(NTT):
                            ts_ = slice(tt * 512, (tt + 1) * 512)
                            pb = bank(psb(4, 0))
                            for jj in range(n):
                                P.mm(pb, wd3[:, jj, dd * 128:(dd + 1) * 128], a_slot(gi, jj)[:, ts_],
                                     start=(jj == 0), stop=(jj == n - 1))
                            P.tt("dve", xT[:, dc, ts_], xT[:, dc, ts_], pb, ALU.add)

        groups = _ffn_groups()
        for gi, (j0, n) in enumerate(groups):
            ffn_up(gi, j0, n)
            if gi > 0:
                ffn_down(gi - 1, *groups[gi - 1])
        ffn_down(len(groups) - 1, *groups[-1])
        dump(f"x_ffn{l}", xT[:, :, :], [128, NCH, S])
        if stop_after == "ffn":
            return

        rmsnorm_to_hT(P_GPLE, O_SCR)
        ptok = V(O_YT, [128, 16, 256], F32)
        pT = V(O_YT + 16 * KB, [128, 2, S], BF16)
        sgp = [V(O_YT + 24 * KB + i * 2 * KB, [128, 512], F32) for i in range(2)]
        ptmp = V(O_YT + 28 * KB, [128, 512], F32)
        for q in range(4):
            P.dma("sp", ptok[:, q * 4:(q + 1) * 4, :],
                  p_in[l, q * 512:(q + 1) * 512, :].rearrange("(a p) c -> p a c", p=128), f"pin{q}")
        for c in range(2):
            for g4 in range(4):
                tp = bank(psb(2, 6))
                for i in range(4):
                    tb = g4 * 4 + i
                    P.transpose(tp[:, i * 128:(i + 1) * 128], ptok[:, tb, c * 128:(c + 1) * 128], ident_f)
                P.copy("act", pT[:, c, g4 * 512:(g4 + 1) * 512], tp)
        for dc in range(NCH):
            wpg = wload(l, f"pg{dc}", 1024)
            wpl = wload(l, f"pl{dc}", 256)
            wpl3 = wpl.rearrange("p (c m) -> p c m", c=2)

            def pg_cons(tt, ts_, pb, dc=dc, wpl3=wpl3):
                s_ = sgp[tt % 2]
                P.act(s_, pb, AF.Sigmoid)
                pp = bank(psb(2, 4))
                for c in range(2):
                    P.mm(pp, wpl3[:, c, :], pT[:, c, ts_], start=(c == 0), stop=(c == 1))
                P.tt("dve", ptmp, pp, s_, ALU.mult)
                P.tt("dve", xT[:, dc, ts_], xT[:, dc, ts_], ptmp, ALU.add)
            proj_fm(wpg, 128, 0, pg_cons, defer=True)
        dump(f"x_out{l}", xT[:, :, :], [128, NCH, S])

    xin = [V(O_YT + i * 4 * KB, [128, D], F32) for i in range(2)]
    for tb in range(16):
        xi = xin[tb % 2]
        P.dma("sp", xi, x_in[tb * 128:(tb + 1) * 128, :], f"xin{tb % 2}")
        for g2 in range(2):
            tp = bank(psb(2, 6))
            for i in range(4):
                c = g2 * 4 + i
                P.transpose(tp[:, i * 128:(i + 1) * 128], xi[:, c * 128:(c + 1) * 128], ident_f)
            eng = "act" if g2 == 0 else "dve"
            P.copy(eng, xT[:, g2 * 4:(g2 + 1) * 4, tb * 128:(tb + 1) * 128],
                   tp.rearrange("p (c t) -> p c t", c=4))

    for l in range(depth):
        layer(l)

    if stop_after is None:
        xo = [V(O_YT + i * 4 * KB, [128, D], F32) for i in range(2)]
        for tb in range(16):
            xo_ = xo[tb % 2]
            for g2 in range(2):
                tp = bank(psb(2, 6))
                for i in range(4):
                    c = g2 * 4 + i
                    P.transpose(tp[:, i * 128:(i + 1) * 128], xT[:, c, tb * 128:(tb + 1) * 128], ident_f)
                eng = "act" if g2 == 0 else "dve"
                P.copy(eng, xo_[:, g2 * 512:(g2 + 1) * 512], tp)
            P.dma("sp", out[tb * 128:(tb + 1) * 128, :], xo_, f"xout{tb % 2}", final=True)
    else:
        z = V(O_SCR, [128, 8], F32)
        P.memset("pool", z, 0.0)
        P.dma("sp", out[0:128, 0:8], z, "xout0", final=True)
    P.emit()
    return nc, dbg_outs, P


_CACHE = {}


def kernel(**inputs):
    inp = {k: np.asarray(v) for k, v in inputs.items()}
    if "nc" not in _CACHE:
        _CACHE["nc"] = build()[0]
    nc = _CACHE["nc"]
    wstream = np.stack([pack_weights(inp, l) for l in range(DEPTH)])
    prm = np.stack([pack_params(inp, l) for l in range(DEPTH)])
    cst = make_consts()
    in_maps = []
    for b in range(8):
        in_maps.append({"x": np.ascontiguousarray(inp["x"][b]),
                        "p": np.ascontiguousarray(inp["p"][:, b]),
                        "wst": wstream, "prm": prm, "cst": cst})
    res = run_bass_kernel_spmd(nc, in_maps, core_ids=list(range(8)))
    return np.stack([np.asarray(r["out"]) for r in res.results]).astype(np.float32)
```

```python
import numpy as np
import concourse.bass as bass
import concourse.mybir as mybir

F32 = mybir.dt.float32
BF16 = mybir.dt.bfloat16
ALU = mybir.AluOpType
AF = mybir.ActivationFunctionType

ENGS = ("pe", "act", "dve", "pool", "sp")


def _prod(xs):
    r = 1
    for v in xs:
        r *= int(v)
    return r


def region(ap):
    t = ap.tensor
    es = mybir.dt.size(ap.dtype)
    off = int(ap.offset)
    pat = ap.ap
    if str(ap.space) == "DRAM":
        ext = 0
        for st, cnt in pat:
            ext += (cnt - 1) * abs(st)
        return (t.name, 0, 1, off * es, (off + ext + 1) * es)
    shape = list(t.shape)
    F = _prod(shape[1:])
    p0 = off // F
    f0 = off % F
    pstep, pcnt = pat[0]
    npart = pcnt if pstep != 0 else 1
    ext = 0
    for st, cnt in pat[1:]:
        ext += (cnt - 1) * abs(st)
    return (t.name, p0, p0 + npart, f0 * es, (f0 + ext + 1) * es)


def _untracked(ap):
    return str(ap.space) == "DRAM" and not ap.tensor.name.startswith("scr_")


def _overlap(a, b):
    return a[1] < b[2] and b[1] < a[2] and a[3] < b[4] and b[3] < a[4]


def _covers(a, b):
    return a[1] <= b[1] and a[2] >= b[2] and a[3] <= b[3] and a[4] >= b[4]


class Instr:
    __slots__ = ("eng", "fn", "deps", "signal", "is_dma", "key", "val", "idx")

    def __init__(self, eng, fn, is_dma, key):
        self.eng = eng
        self.fn = fn
        self.deps = set()
        self.signal = False
        self.is_dma = is_dma
        self.key = key
        self.val = 0


class Prog:
    def __init__(self, nc):
        self.nc = nc
        self.instrs = []
        self.wr = {}
        self.rd = {}
        self.final_keys = set()
        self.last_dma = {}

    def add(self, eng, fn, reads=(), writes=(), dma_key=None):
        ins = Instr(eng, fn, dma_key is not None, dma_key if dma_key is not None else eng)
        idx = len(self.instrs)
        ins.idx = idx
        self.instrs.append(ins)
        deps = ins.deps
        for ap in reads:
            if ap is None or isinstance(ap, (int, float)):
                continue
            if _untracked(ap):
                continue
            r = region(ap)
            for (w, wi) in self.wr.get(r[0], ()):
                if _overlap(w, r):
                    deps.add(wi)
        for ap in writes:
            if _untracked(ap):
                continue
            r = region(ap)
            wl = self.wr.setdefault(r[0], [])
            rl = self.rd.setdefault(r[0], [])
            for (w, wi) in wl:
                if _overlap(w, r):
                    deps.add(wi)
            for (q, qi) in rl:
                if _overlap(q, r):
                    deps.add(qi)
            self.wr[r[0]] = [(w, wi) for (w, wi) in wl if not _covers(r, w)]
            self.rd[r[0]] = [(q, qi) for (q, qi) in rl if not _covers(r, q)]
            self.wr[r[0]].append((r, idx))
        for ap in reads:
            if ap is None or isinstance(ap, (int, float)):
                continue
            if _untracked(ap):
                continue
            r = region(ap)
            rl = self.rd.setdefault(r[0], [])
            if dma_key is None:
                rl[:] = [(q, qi) for (q, qi) in rl
                         if not (q == r and self.instrs[qi].eng == eng and not self.instrs[qi].is_dma)]
            rl.append((r, idx))
        if dma_key is not None:
            if dma_key in self.last_dma:
                deps.add(self.last_dma[dma_key])
            self.last_dma[dma_key] = idx
        deps.discard(idx)
        return ins

    def mm(self, out, lhsT, rhs, start=True, stop=True):
        rd = [lhsT, rhs]
        return self.add("pe", lambda e: e.matmul(out, lhsT, rhs, start=start, stop=stop),
                        reads=rd, writes=[out])

    def transpose(self, out, in_, ident):
        return self.add("pe", lambda e: e.transpose(out, in_, ident),
                        reads=[in_, ident], writes=[out])

    def act(self, out, in_, func, bias=None, scale=1.0, accum_out=None, eng="act"):
        kw = {}
        if bias is not None:
            kw["bias"] = bias
        if accum_out is not None:
            kw["accum_out"] = accum_out
        rd = [in_, bias if not isinstance(bias, (int, float)) else None,
              scale if not isinstance(scale, (int, float)) else None]
        wr = [out] + ([accum_out] if accum_out is not None else [])
        return self.add(eng, lambda e: e.activation(out, in_, func, scale=scale, **kw),
                        reads=rd, writes=wr)

    def tt(self, eng, out, in0, in1, op):
        return self.add(eng, lambda e: e.tensor_tensor(out, in0, in1, op),
                        reads=[in0, in1], writes=[out])

    def ts(self, eng, out, in0, s1, s2=None, op0=ALU.mult, op1=None, accum_out=None):
        kw = {}
        if op1 is not None:
            kw["op1"] = op1
        if accum_out is not None:
            kw["accum_out"] = accum_out
        rd = [in0, s1 if not isinstance(s1, (int, float)) else None,
              s2 if not isinstance(s2, (int, float)) else None]
        wr = [out] + ([accum_out] if accum_out is not None else [])
        return self.add(eng, lambda e: e.tensor_scalar(out, in0, s1, s2, op0, **kw),
                        reads=rd, writes=wr)

    def stt(self, eng, out, in0, scalar, in1, op0, op1):
        rd = [in0, in1, scalar if not isinstance(scalar, (int, float)) else None]
        return self.add(eng, lambda e: e.scalar_tensor_tensor(out, in0, scalar, in1, op0, op1),
                        reads=rd, writes=[out])

    def copy(self, eng, out, in_):
        if eng == "act":
            return self.add(eng, lambda e: e.copy(out, in_), reads=[in_], writes=[out])
        return self.add(eng, lambda e: e.tensor_copy(out, in_), reads=[in_], writes=[out])

    def memset(self, eng, out, val):
        return self.add(eng, lambda e: e.memset(out, val), reads=[], writes=[out])

    def scan(self, out, d0, d1, initial, op0, op1):
        rd = [d0, d1, initial if not isinstance(initial, (int, float)) else None]
        return self.add("dve", lambda e: e.tensor_tensor_scan(out, d0, d1, initial, op0, op1),
                        reads=rd, writes=[out])

    def dma(self, eng, out, in_, key, final=False):
        if final:
            self.final_keys.add(key)
        return self.add(eng, lambda e: e.dma_start(out, in_), reads=[in_], writes=[out], dma_key=key)

    def emit(self):
        nc = self.nc
        instrs = self.instrs
        for ins in instrs:
            if ins.eng == "pe" and not ins.is_dma:
                ins.deps = {d for d in ins.deps if not (instrs[d].eng == "pe" and not instrs[d].is_dma)}
            for d in ins.deps:
                instrs[d].signal = True
        keys = []
        for ins in instrs:
            if ins.key not in keys:
                keys.append(ins.key)
        for ins in instrs:
            if ins.is_dma:
                ins.signal = True
        cnt = {k: 0 for k in keys}
        for ins in instrs:
            if ins.signal:
                cnt[ins.key] += 16 if ins.is_dma else 1
            ins.val = cnt[ins.key]
        used = [k for k in keys if cnt[k] > 0]
        self.sem_totals = {k: cnt[k] for k in used}
        import contextlib
        with contextlib.ExitStack() as st:
            sems = {k: st.enter_context(nc.semaphore("s_" + str(k))) for k in used}
            block = st.enter_context(nc.Block())
            per_eng = {e: [i for i in instrs if i.eng == e] for e in ENGS}
            nwaits = [0]

            def run(engobj, lst, is_last_sp=False):
                clock = {}
                for ins in lst:
                    need = {}
                    for d in ins.deps:
                        di = instrs[d]
                        if di.val > need.get(di.key, 0):
                            need[di.key] = di.val
                    for k, v in need.items():
                        if clock.get(k, 0) < v:
                            engobj.wait_ge(sems[k], v)
                            clock[k] = v
                            nwaits[0] += 1
                    bi = ins.fn(engobj)
                    if ins.signal:
                        bi.then_inc(sems[ins.key], 16 if ins.is_dma else 1)
                if is_last_sp:
                    for k in sorted(self.final_keys, key=str):
                        if k in sems:
                            engobj.wait_ge(sems[k], cnt[k])

            @block.tensor
            def _(e):
                run(e, per_eng["pe"])

            @block.scalar
            def _(e):
                run(e, per_eng["act"])

            @block.vector
            def _(e):
                run(e, per_eng["dve"])

            @block.gpsimd
            def _(e):
                run(e, per_eng["pool"])

            @block.sync
            def _(e):
                run(e, per_eng["sp"], True)
            self.nwaits = nwaits[0]

from concourse.bass_utils import run_bass_kernel_spmd
import ml_dtypes

S = 2048
D = 1024
DEPTH = 2
NCH = D // 128
NTT = S // 512
DFF = 2816
NFF = DFF // 128
EPS = 1e-6
GRP = 4
import os as _os
PULL_N = int(_os.environ.get('PULL_N', '0'))
PULL_S = int(_os.environ.get('PULL_S', '1'))
NPRM = 224
NCST = 1536

C_FQ, C_FK, C_FV, C_FF = 0, 512, 1024, 1536
C_SB, C_SC, C_SV = 1544, 2056, 2568
C_DQ, C_DK, C_DV = 3080, 3592, 4104
C_DB, C_DA, C_DZ, C_G = 4616, 4620, 4624, 5136

P_GMIX, P_GFFN, P_GPLE = 0, 8, 16
P_FQG, P_FKG, P_BF, P_ALOG, P_DTB, P_DNG = 24, 25, 26, 27, 28, 29
P_SCW, P_DNW, P_FFW = 30, 42, 90

K_ID, K_M01, K_ONE, K_SEL, K_MPA, K_MPB = 0, 128, 384, 512, 1024, 1088


def _ffn_groups():
    gs = []
    j = 0
    while j < NFF:
        n = min(GRP, NFF - j)
        gs.append((j, n))
        j += n
    return gs


def wtile_index():
    idx = {}
    names = ["small"]
    names += [f"foxv{c}" for c in range(4)]
    names += [f"foxqk{h}" for h in range(8)]
    for j in range(4):
        names += [f"scb{j}", f"scc{j}", f"scv{j}"]
    for h in range(4):
        names += [f"dnq{h}", f"dnk{h}", f"dnv{h}", f"dnz{h}"]
    for dc in range(8):
        names += [f"g0_{dc}", f"g1_{dc}", f"g2_{dc}", f"brA{dc}", f"brB{dc}"]
    names += [f"wo{dc}" for dc in range(8)]
    for j in range(NFF):
        names += [f"upg{j}", f"upv{j}"]
    for gi, (j0, n) in enumerate(_ffn_groups()):
        names += [f"dn{gi}_{q}" for q in range(4)]
    for dc in range(8):
        names += [f"pg{dc}", f"pl{dc}"]
    for i, n in enumerate(names):
        idx[n] = i
    return idx


WIDX = wtile_index()
NT = len(WIDX)


def pack_weights(inp, l):
    W = np.zeros((NT, 128, 1024), np.float32)
    w_in = inp["w_in"][l]

    def kc(cols):
        n = cols.shape[1]
        return cols.reshape(8, 128, n).transpose(1, 0, 2).reshape(128, 8 * n)

    def put(name, arr):
        W[WIDX[name], :, :arr.shape[1]] = arr

    sm = np.zeros((1024, 96), np.float32)
    sm[:, 0:8] = w_in[:, C_FF:C_FF + 8]
    sm[:, 32:36] = w_in[:, C_DA:C_DA + 4]
    sm[:, 64:68] = w_in[:, C_DB:C_DB + 4]
    put("small", kc(sm))
    for c in range(4):
        put(f"foxv{c}", kc(w_in[:, C_FV + c * 128:C_FV + (c + 1) * 128]))
    for h in range(8):
        qk = np.concatenate([w_in[:, C_FQ + h * 64:C_FQ + (h + 1) * 64],
                             w_in[:, C_FK + h * 64:C_FK + (h + 1) * 64]], axis=1)
        put(f"foxqk{h}", kc(qk))
    for j in range(4):
        put(f"scb{j}", kc(w_in[:, C_SB + j * 128:C_SB + (j + 1) * 128]))
        put(f"scc{j}", kc(w_in[:, C_SC + j * 128:C_SC + (j + 1) * 128]))
        put(f"scv{j}", kc(w_in[:, C_SV + j * 128:C_SV + (j + 1) * 128]))
    for h in range(4):
        put(f"dnq{h}", kc(w_in[:, C_DQ + h * 128:C_DQ + (h + 1) * 128]))
        put(f"dnk{h}", kc(w_in[:, C_DK + h * 128:C_DK + (h + 1) * 128]))
        put(f"dnv{h}", kc(w_in[:, C_DV + h * 128:C_DV + (h + 1) * 128]))
        put(f"dnz{h}", kc(w_in[:, C_DZ + h * 128:C_DZ + (h + 1) * 128]))
    wb = inp["w_branch"][l]
    for dc in range(8):
        for b in range(3):
            put(f"g{b}_{dc}", kc(w_in[:, C_G + b * 1024 + dc * 128:C_G + b * 1024 + (dc + 1) * 128]))

        def br(b):
            a = wb[b][:, dc * 128:(dc + 1) * 128]
            return a.reshape(4, 128, 128).transpose(1, 0, 2).reshape(128, 512)
        put(f"brA{dc}", np.concatenate([br(0), br(1)], axis=1))
        put(f"brB{dc}", br(2))
        put(f"wo{dc}", kc(inp["w_o"][l][:, dc * 128:(dc + 1) * 128]))
        put(f"pg{dc}", kc(inp["w_ple_gate"][l][:, dc * 128:(dc + 1) * 128]))
        a = inp["w_ple"][l][:, dc * 128:(dc + 1) * 128]
        put(f"pl{dc}", a.reshape(2, 128, 128).transpose(1, 0, 2).reshape(128, 256))
    w_up = inp["w_up"][l]
    for j in range(NFF):
        put(f"upg{j}", kc(w_up[:, j * 128:(j + 1) * 128]))
        put(f"upv{j}", kc(w_up[:, DFF + j * 128:DFF + (j + 1) * 128]))
    w_dn = inp["w_down"][l]
    for gi, (j0, n) in enumerate(_ffn_groups()):
        for q in range(4):
            a = w_dn[j0 * 128:(j0 + n) * 128, q * 256:(q + 1) * 256]
            put(f"dn{gi}_{q}", a.reshape(n, 128, 256).transpose(1, 0, 2).reshape(128, n * 256))
    return W


def pack_params(inp, l):
    Pm = np.zeros((128, NPRM), np.float32)
    Pm[:, P_GMIX:P_GMIX + 8] = inp["g_mix"][l].reshape(8, 128).T
    Pm[:, P_GFFN:P_GFFN + 8] = inp["g_ffn"][l].reshape(8, 128).T
    Pm[:, P_GPLE:P_GPLE + 8] = inp["g_ple"][l].reshape(8, 128).T
    Pm[0:64, P_FQG] = inp["fox_q_gain"][l]
    Pm[64:128, P_FQG] = inp["fox_k_gain"][l]
    Pm[0:64, P_FKG] = inp["fox_k_gain"][l]
    Pm[0:8, P_BF] = inp["b_fox_f"][l]
    Pm[32:36, P_ALOG] = inp["dn_a_log"][l]
    Pm[32:36, P_DTB] = inp["dn_dt_bias"][l]
    Pm[:, P_DNG] = inp["dn_norm_gain"][l]
    Pm[:, P_SCW:P_SCW + 12] = inp["sc_conv_w"][l].reshape(3, 4, 128).transpose(2, 1, 0).reshape(128, 12)
    Pm[:, P_DNW:P_DNW + 48] = inp["dn_conv_w"][l].reshape(4, 12, 128).transpose(2, 1, 0).reshape(128, 48)
    Pm[:, P_FFW:P_FFW + 132] = inp["ffn_conv_w"][l].reshape(3, 44, 128).transpose(2, 1, 0).reshape(128, 132)
    return Pm


def make_consts():
    C = np.zeros((128, NCST), np.float32)
    r = np.arange(128)[:, None]
    c = np.arange(128)[None, :]
    C[:, K_ID:K_ID + 128] = (r == c)
    C[:, K_M01:K_M01 + 128] = (c >= r)
    C[:, K_M01 + 128:K_M01 + 256] = ((r // 64) == (c // 64))
    C[:, K_ONE:K_ONE + 128] = 1.0
    for h in range(4):
        C[32 + h, K_SEL + h * 128:K_SEL + (h + 1) * 128] = 1.0
    r6 = np.arange(64)[:, None]
    c6 = np.arange(64)[None, :]
    C[0:64, K_MPA:K_MPA + 64] = np.where(r6 > c6, 0.0, 1.0e4)
    C[0:64, K_MPB:K_MPB + 64] = np.where(c6 >= r6, 0.0, 1.0e4)
    return C


def build(depth=DEPTH, dbg=None, stop_after=None):
    nc = bass.Bass("TRN2", target_bir_lowering=False)
    x_in = nc.dram_tensor("x", [S, D], F32, kind="ExternalInput").ap()
    p_in = nc.dram_tensor("p", [DEPTH, S, 256], F32, kind="ExternalInput").ap()
    wst = nc.dram_tensor("wst", [DEPTH, NT, 128, 1024], F32, kind="ExternalInput").ap()
    prm_in = nc.dram_tensor("prm", [DEPTH, 128, NPRM], F32, kind="ExternalInput").ap()
    cst_in = nc.dram_tensor("cst", [128, NCST], F32, kind="ExternalInput").ap()
    out = nc.dram_tensor("out", [S, D], F32, kind="ExternalOutput").ap()
    scrx = nc.dram_tensor("scr_x", [128, NCH, S], F32).ap()
    dbg_outs = {}

    TOT = 211456
    SB = nc.alloc_sbuf_tensor("SB", [128, TOT // 4], F32)
    PS = nc.alloc_psum_tensor("PS", [128, 8 * 512], F32)
    P = Prog(nc)

    def V(off, shape, dt=F32, p0=0):
        es = mybir.dt.size(dt)
        n = _prod(shape[1:])
        assert off % 4 == 0 and off + n * es <= TOT, (off, shape)
        nw = (n * es + 3) // 4
        v = SB[p0:p0 + shape[0], off // 4: off // 4 + nw]
        if dt != F32:
            v = v.bitcast(dt)
        v = v[:, 0:n]
        if len(shape) == 3:
            v = v.rearrange("p (a b) -> p a b", a=shape[1])
        elif len(shape) == 4:
            v = v.rearrange("p (a b c) -> p a b c", a=shape[1], b=shape[2])
        return v

    def bank(b, parts=128, dt=F32, p0=0):
        v = PS[p0:p0 + parts, b * 512:(b + 1) * 512]
        if dt != F32:
            v = v.bitcast(dt)
        return v

    KB = 1024
    O_XT = 0
    O_HT = 64 * KB
    O_YT = 96 * KB
    O_WS = 144 * KB
    O_WB = 152 * KB
    O_CST = 158 * KB
    O_CBF = 164 * KB
    O_PRM = 165 * KB
    O_MISC = 166 * KB
    O_SMT = 167 * KB
    O_SCR = 175 * KB
    SCR_END = TOT

    xT = V(O_XT, [128, NCH, S], F32)
    hT = V(O_HT, [128, NCH, S], BF16)
    yT = V(O_YT, [128, 12, S], BF16)
    wstage = [V(O_WS + i * 4 * KB, [128, 1024], F32) for i in range(2)]
    wbf = [V(O_WB + i * 2 * KB, [128, 1024], BF16) for i in range(3)]
    cst = V(O_CST, [128, NCST], F32)
    cbf = V(O_CBF, [128, 512], BF16)
    prm = V(O_PRM, [128, NPRM], F32)
    misc = V(O_MISC, [128, 256], F32)
    smT = V(O_SMT, [128, S], F32)

    ident_f = cst[:, K_ID:K_ID + 128]
    ident_b = cbf[:, 0:128]
    mask01_b = cbf[:, 128:256]
    ones_b = cbf[:, 384:512]
    bd_b = cbf[:, 256:384]
    sel_f = cst[:, K_SEL:K_SEL + 512]
    mposA = cst[0:64, K_MPA:K_MPA + 64]
    mposB = cst[0:64, K_MPB:K_MPB + 64]
    epsc = misc[:, 0:1]
    onec = misc[:, 1:2]
    qgs = misc[:, 2:3]
    negb = misc[:, 3:4]
    negA = misc[:, 4:5]

    state = {"ws": 0, "wb": 0, "ps": 0}

    def wload(l, name, X):
        si = state["ws"] % 2
        bi = state["wb"] % 3
        state["ws"] += 1
        state["wb"] += 1
        P.dma("sp", wstage[si][:, 0:X], wst[l, WIDX[name], :, 0:X], f"ws{si}")
        P.copy("pool", wbf[bi][:, 0:X], wstage[si][:, 0:X])
        return wbf[bi][:, 0:X]

    def psb(n=4, base=0):
        k = ("ps", base, n)
        state[k] = state.get(k, 0) + 1
        return base + (state[k] - 1) % n

    def dump(name, ap, shape):
        if dbg is None or name not in dbg:
            return
        t = nc.dram_tensor("dbg_" + name, list(shape), ap.dtype, kind="ExternalOutput").ap()
        dbg_outs[name] = t
        P.dma("sp", t, ap, "dbgout", final=True)

    P.dma("sp", cst[:], cst_in, "cstin")
    P.copy("dve", cbf[:], cst[:, 0:512])
    P.memset("dve", epsc, EPS)
    P.memset("dve", onec, 1.0)

    def rmsnorm_to_hT(gcol0, o_scr):
        sq = [V(o_scr + i * KB, [128, 512], BF16) for i in range(2)]
        lnt = V(o_scr + 2 * KB, [128, 512], F32)
        rstd = V(o_scr + 4 * KB, [128, 512], F32)
        for tt in range(NTT):
            ts_ = slice(tt * 512, (tt + 1) * 512)
            pb = bank(psb(2, 6))
            for c in range(NCH):
                s_ = sq[c % 2]
                P.act(s_, xT[:, c, ts_], AF.Square)
                P.mm(pb, ones_b, s_, start=(c == 0), stop=(c == NCH - 1))
            P.act(lnt, pb, AF.Ln, bias=epsc, scale=1.0 / D)
            P.act(rstd, lnt, AF.Exp, scale=-0.5)
            for c in range(NCH):
                P.stt("dve", hT[:, c, ts_], xT[:, c, ts_], prm[:, gcol0 + c:gcol0 + c + 1], rstd,
                      ALU.mult, ALU.mult)

    def proj_fm(wt, M, m0, consumer, kchunks=NCH, rhs_of=None, tts=range(NTT), defer=False, nb=4):
        w3 = wt.rearrange("p (k c) -> p k c", k=kchunks)
        pend = None
        for tt in tts:
            ts_ = slice(tt * 512, (tt + 1) * 512)
            pb = bank(psb(nb, 0), M)
            for k in range(kchunks):
                rhs = hT[:, k, ts_] if rhs_of is None else rhs_of(k, ts_)
                P.mm(pb, w3[:, k, m0:m0 + M], rhs, start=(k == 0), stop=(k == kchunks - 1))
            if defer:
                if pend is not None:
                    consumer(*pend)
                pend = (tt, ts_, pb)
            else:
                consumer(tt, ts_, pb)
        if pend is not None:
            consumer(*pend)

    def layer(l):
        P.dma("sp", prm[:], prm_in[l], "prmin")
        P.ts("dve", qgs[0:64], prm[0:64, P_FQG:P_FQG + 1], 0.125, None, op0=ALU.mult)
        P.copy("dve", qgs[64:128], prm[64:128, P_FQG:P_FQG + 1])
        P.ts("dve", negb[0:8], prm[0:8, P_BF:P_BF + 1], -1.0, None, op0=ALU.mult)
        P.act(negA[32:36], prm[32:36, P_ALOG:P_ALOG + 1], AF.Exp)
        P.ts("dve", negA[32:36], negA[32:36], -1.0, None, op0=ALU.mult)

        rmsnorm_to_hT(P_GMIX, O_SCR)
        dump(f"h{l}", hT[:, :, :], [128, NCH, S])
        for c in range(NCH):
            P.dma("sp", scrx[:, c, :], xT[:, c, :], f"spill{c}")
        if stop_after == "norm1":
            return

        wt = wload(l, "small", 8 * 96)

        def small_cons(tt, ts_, pb):
            P.act(smT[0:8, ts_], pb[0:8, :], AF.Exp, bias=negb[0:8], scale=-1.0)
            P.act(smT[32:36, ts_], pb[32:36, :], AF.Exp, bias=prm[32:36, P_DTB:P_DTB + 1], scale=1.0)
            P.act(smT[64:68, ts_], pb[64:68, :], AF.Exp, scale=-1.0)
        proj_fm(wt, 96, 0, small_cons)
        o = O_XT
        cqs = V(o, [8, 3, S], BF16); o += 12 * KB
        vext = V(o, [128, 16, 8, 65], BF16); o += 17 * KB
        qa = [V(o + i * 8 * KB, [128, S], BF16) for i in range(2)]
        ka = [V(o + i * 8 * KB + 4 * KB, [128, S], BF16) for i in range(2)]
        o += 16 * KB
        ytok = V(o, [128, 16, 128], BF16); o += 4 * KB
        ptile = [V(o + i * KB, [128, 512], BF16) for i in range(3)]; o += 3 * KB
        sqh = [V(o + i * KB, [128, 512], BF16) for i in range(2)]; o += 2 * KB
        lnh = V(o, [128, 512], F32); o += 2 * KB
        rsh = V(o, [128, 512], F32); o += 2 * KB
        rcp = V(o, [128, 4], F32); o += 128
        assert o <= 64 * KB
        P.memset("dve", vext[:, :, :, 64:65], 1.0)
        for ct in range(4):
            wt = wload(l, f"foxv{ct}", 1024)
            w3 = wt.rearrange("p (k c) -> p k c", k=NCH)
            for g4 in range(4):
                pb = bank(psb(4, 0))
                for t4 in range(4):
                    tb = g4 * 4 + t4
                    for k in range(NCH):
                        P.mm(pb[:, t4 * 128:(t4 + 1) * 128], hT[:, k, tb * 128:(tb + 1) * 128], w3[:, k, :],
                             start=(k == 0), stop=(k == NCH - 1))
                P.copy("act", vext[:, g4 * 4:(g4 + 1) * 4, 2 * ct:2 * ct + 2, 0:64],
                       pb.rearrange("p (a b c) -> p a b c", a=4, b=2))
        P.act(smT[0:8, :], smT[0:8, :], AF.Ln, bias=onec[0:8], scale=1.0)
        P.act(smT[32:36, :], smT[32:36, :], AF.Ln, bias=onec[32:36], scale=1.0)
        P.act(smT[64:68, :], smT[64:68, :], AF.Ln, bias=onec[64:68], scale=1.0)
        P.act(smT[64:68, :], smT[64:68, :], AF.Exp, scale=-1.0)
        aux = V(O_SCR + 16 * KB, [128, S], F32)
        P.scan(aux[0:8, :], onec[0:8].to_broadcast([8, S]), smT[0:8, :], 0.0, ALU.mult, ALU.subtract)
        cqf = aux[0:8, :]
        P.ts("dve", smT[32:36, :], smT[32:36, :], negA[32:36], None, op0=ALU.mult)
        dump(f"dn_g{l}", smT[32:36, :], [4, S])
        dump(f"dn_beta{l}", smT[64:68, :], [4, S])
        P.scan(aux[32:36, :], onec[32:36].to_broadcast([4, S]), smT[32:36, :], 0.0, ALU.mult, ALU.add)
        a3 = aux[32:36, :].rearrange("p (n c) -> p n c", c=64)
        g3 = smT[32:36, :].rearrange("p (n c) -> p n c", c=64)
        gl = V(O_SCR, [128, 32], F32)
        P.copy("dve", gl[32:36, :], a3[:, :, 63])
        P.copy("dve", g3[:, 0, :], a3[:, 0, :])
        P.tt("dve", g3[:, 1:32, :], a3[:, 1:32, :], gl[32:36, 0:31].unsqueeze(2).to_broadcast([4, 31, 64]),
             ALU.subtract)
        dump(f"cq{l}", cqf, [8, S])
        dump(f"gcum{l}", smT[32:36, :], [4, S])

        cr = V(O_SCR + 8 * KB, [8, S], F32)
        P.copy("dve", cqs[:, 0, :], cqf)
        P.tt("dve", cr, cqf, cqs[:, 0, :], ALU.subtract)
        P.copy("dve", cqs[:, 1, :], cr)
        P.tt("dve", cr, cr, cqs[:, 1, :], ALU.subtract)
        P.copy("dve", cqs[:, 2, :], cr)
        def fox_proj(h):
            wt = wload(l, f"foxqk{h}", 1024)
            qa_h, ka_h = qa[h % 2], ka[h % 2]

            def qk_cons(tt, ts_, pb):
                s_ = sqh[tt % 2]
                P.act(s_, pb, AF.Square)
                p2 = bank(psb(2, 6))
                P.mm(p2, bd_b, s_)
                P.act(lnh, p2, AF.Ln, bias=epsc, scale=1.0 / 64)
                P.act(rsh, lnh, AF.Exp, scale=-0.5)
                P.stt("dve", qa_h[:, ts_], pb, qgs, rsh, ALU.mult, ALU.mult)
            proj_fm(wt, 128, 0, qk_cons, defer=True)
            P.dma("sp", ka_h[0:64, :], qa_h[64:128, :], f"kmov{h % 2}")
            P.memset("dve", ka_h[64:70, :], 1.0)
            P.dma("sp", ka_h[67:70, :], cqs[h:h + 1, :, :], f"augk{h % 2}")
            P.memset("dve", qa_h[64:70, :], -1.0)
            P.dma("sp", qa_h[64:67, :], cqs[h:h + 1, :, :], f"augq{h % 2}")

        fox_proj(0)
        for h in range(8):
            qa_h, ka_h = qa[h % 2], ka[h % 2]
            if h + 1 < 8:
                fox_proj(h + 1)
            tasks = [(qt, kb) for qt in range(4) for kb in range(4 * qt + 4)]
            obanks = {}

            def emit_S(qt, kb):
                n0 = max(kb * 128, qt * 512)
                ncols = (qt + 1) * 512 - n0
                pb = bank(psb(4, 0))
                P.mm(pb[:, 0:ncols], ka_h[0:70, kb * 128:(kb + 1) * 128], qa_h[0:70, n0:n0 + ncols])
                pt = ptile[state["ps"] % 3]
                state["ps"] += 1
                P.act(pt[:, 0:ncols], pb[:, 0:ncols], AF.Exp)
                if kb * 128 >= qt * 512:
                    P.tt("dve", pt[:, 0:128], pt[:, 0:128], mask01_b, ALU.mult)
                return pt, n0

            def emit_PV(qt, kb, pt, n0):
                ob = bank(4 + qt % 2)
                o4 = ob.rearrange("p (a b) -> p a b", a=4)
                for qb in range(n0 // 128, 4 * qt + 4):
                    c0 = qb * 128 - n0
                    P.mm(o4[:, qb - 4 * qt, 0:65], pt[:, c0:c0 + 128], vext[:, kb, h, :],
                         start=(kb == 0 and qb == 4 * qt), stop=(kb == qb))
                if kb == 4 * qt + 3:
                    P.add("dve", lambda e, o4=o4: e.reciprocal(rcp[:, :].unsqueeze(2), o4[:, :, 64:65]),
                          reads=[o4[:, :, 64:65]], writes=[rcp[:, :]])
                    P.tt("dve", ytok[:, 4 * qt:4 * qt + 4, (h % 2) * 64:(h % 2) * 64 + 64], o4[:, :, 0:64],
                         rcp[:, :].unsqueeze(2).to_broadcast([128, 4, 64]), ALU.mult)

            LA = 2
            pend = [emit_S(*tasks[i]) for i in range(LA)]
            for i, (qt, kb) in enumerate(tasks):
                if i + LA < len(tasks):
                    pend.append(emit_S(*tasks[i + LA]))
                emit_PV(qt, kb, *pend.pop(0))
            if h % 2 == 1:
                for half in range(2):
                    tp = bank(6 + half, 128, BF16)
                    for i in range(8):
                        qb = half * 8 + i
                        P.transpose(tp[:, i * 128:(i + 1) * 128], ytok[:, qb, :], ident_b)
                    P.copy("act", yT[:, h // 2, half * 1024:(half + 1) * 1024], tp)
        dump(f"y_fox{l}", yT[:, 0:4, :], [128, 4, S])
        if stop_after == "fox":
            return

        o = O_XT
        cv = V(o, [128, S + 2], F32); o += 8 * KB + 128
        acc = V(o, [128, S], F32); o += 8 * KB
        bsb = V(o, [128, S], F32); o += 8 * KB
        ctmp = [V(o + i * 2 * KB, [128, 512], F32) for i in range(2)]; o += 4 * KB
        P.memset("dve", cv[:, 0:2], 0.0)
        for j in range(4):
            wb_ = wload(l, f"scb{j}", 1024)
            proj_fm(wb_, 128, 0, lambda tt, ts_, pb: P.copy("act", bsb[:, ts_], pb))
            wc_ = wload(l, f"scc{j}", 1024)
            wv_ = wload(l, f"scv{j}", 1024)
            w3c = wc_.rearrange("p (k c) -> p k c", k=NCH)
            w3v = wv_.rearrange("p (k c) -> p k c", k=NCH)
            for tt in range(NTT):
                ts_ = slice(tt * 512, (tt + 1) * 512)
                pc = bank(psb(4, 0))
                for k in range(NCH):
                    P.mm(pc, w3c[:, k, :], hT[:, k, ts_], start=(k == 0), stop=(k == NCH - 1))
                P.copy("act", ctmp[tt % 2], pc)
                pv = bank(psb(4, 0))
                for k in range(NCH):
                    P.mm(pv, w3v[:, k, :], hT[:, k, ts_], start=(k == 0), stop=(k == NCH - 1))
                P.tt("dve", cv[:, 2 + tt * 512:2 + (tt + 1) * 512], pv, ctmp[tt % 2], ALU.mult)
            w0 = prm[:, P_SCW + j * 3:P_SCW + j * 3 + 1]
            w1 = prm[:, P_SCW + j * 3 + 1:P_SCW + j * 3 + 2]
            w2 = prm[:, P_SCW + j * 3 + 2:P_SCW + j * 3 + 3]
            P.act(acc, cv[:, 2:2 + S], AF.Copy, scale=w2)
            P.stt("dve", acc, cv[:, 1:1 + S], w1, acc, ALU.mult, ALU.add)
            P.stt("dve", acc, cv[:, 0:S], w0, acc, ALU.mult, ALU.add)
            P.tt("dve", yT[:, 4 + j, :], acc, bsb, ALU.mult)
        dump(f"y_sc{l}", yT[:, 4:8, :], [128, 4, S])
        if stop_after == "sc":
            return

        gtok = V(O_SCR, [64, 32, 4], F32)
        btok = V(O_SCR + 512, [64, 32, 4], F32)
        for src0, dst in ((32, gtok), (64, btok)):
            pb = bank(psb(2, 6), 64)
            p3 = pb[:, 0:128].rearrange("p (n h) -> p n h", h=4)
            for n in range(32):
                P.transpose(p3[:, n, :], smT[src0:src0 + 4, n * 64:(n + 1) * 64],
                            ident_f[src0:src0 + 4, src0:src0 + 4])
            P.copy("dve", dst, p3)
        o = O_XT
        raw = V(o, [128, S + 3], F32); o += 8 * KB + 128
        cacc = V(o, [128, S], F32); o += 8 * KB
        QT = V(o, [128, S], BF16); o += 4 * KB
        KT = V(o, [128, S], BF16); o += 4 * KB
        VT = V(o, [128, S], BF16); o += 4 * KB
        qdT = V(o, [128, S], BF16); o += 4 * KB
        nkcT = V(o, [128, 32, 64], BF16); o += 4 * KB
        Kg = V(o, [64, 32, 128], BF16); o += 8 * KB
        Kd = V(o, [64, 32, 128], BF16); o += 8 * KB
        Vb = V(o, [64, 32, 128], BF16); o += 8 * KB
        sqp = [V(o + i * KB, [128, 512], BF16) for i in range(2)]; o += 2 * KB
        assert o <= 64 * KB, o
        o = O_SCR + 1 * KB
        cm = [V(o + i * 4 * KB, [64, 32, 64], BF16) for i in range(5)]; o += 20 * KB
        Mm, MTm, PTm, qkT, Mn = cm
        MTn = V(o, [64, 32, 64], BF16); o += 4 * KB
        PTn = V(o, [64, 32, 64], BF16); o += 4 * KB
        assert o <= SCR_END, o
        GB = V(O_XT, [128, S], F32)
        Xm = V(O_XT + 8 * KB + 128, [64, 32, 64], F32)
        sq2 = [V(O_SCR + 29 * KB + i * KB, [128, 512], BF16) for i in range(2)]
        eg = misc[0:64, 8:40]; bgc = misc[0:64, 40:72]; edc = misc[0:64, 72:104]
        gtot = misc[:, 104:136]; nbt = misc[0:64, 136:168]

        def prep_gen(h):
            lnp = V(O_YT + (8 + h) * 4 * KB, [128, 512], F32)
            rsp = V(O_YT + (8 + h) * 4 * KB + 2 * KB, [128, 512], F32)
            for which, nm, dstT, scl in ((0, "dnq", QT, 128.0 ** -0.5), (1, "dnk", KT, 1.0), (2, "dnv", VT, None)):
                wt = wload(l, f"{nm}{h}", 1024)
                P.memset("dve", raw[:, 0:3], 0.0)
                w3 = wt.rearrange("p (k c) -> p k c", k=NCH)
                for tt in range(NTT):
                    ts_ = slice(tt * 512, (tt + 1) * 512)
                    pb = bank(psb(2, 6))
                    for k in range(NCH):
                        P.mm(pb, w3[:, k, :], hT[:, k, ts_], start=(k == 0), stop=(k == NCH - 1))
                    P.copy("act", raw[:, 3 + tt * 512:3 + (tt + 1) * 512], pb)
                    yield
                cw = P_DNW + (which * 4 + h) * 4
                P.act(cacc, raw[:, 3:3 + S], AF.Copy, scale=prm[:, cw + 3:cw + 4])
                yield
                for j in range(3):
                    P.stt("dve", cacc, raw[:, j:j + S], prm[:, cw + j:cw + j + 1], cacc, ALU.mult, ALU.add)
                    yield
                if scl is None:
                    P.act(dstT, cacc, AF.Silu)
                    yield
                else:
                    P.act(cacc, cacc, AF.Silu)
                    yield
                    for tt in range(NTT):
                        ts_ = slice(tt * 512, (tt + 1) * 512)
                        s_ = sqp[tt % 2]
                        P.act(s_, cacc[:, ts_], AF.Square)
                        p2 = bank(psb(2, 6))
                        P.mm(p2, ones_b, s_)
                        yield
                        P.act(lnp, p2, AF.Ln, bias=epsc, scale=1.0)
                        P.act(rsp, lnp, AF.Exp, scale=-0.5)
                        yield
                        P.stt("dve", dstT[:, ts_], cacc[:, ts_], scl, rsp, ALU.mult, ALU.mult)
                        yield

        gen_box = [None]

        def pull(k):
            g = gen_box[0]
            if g is None:
                return
            for _ in range(k):
                try:
                    next(g)
                except StopIteration:
                    gen_box[0] = None
                    return

        def drain():
            while gen_box[0] is not None:
                pull(8)

        gen_box[0] = prep_gen(0)
        drain()
        for h in range(4):
            if h == 0:
                dump(f"dn_q{l}", QT, [128, S]); dump(f"dn_k{l}", KT, [128, S]); dump(f"dn_v{l}", VT, [128, S])
            for tt in range(NTT):
                ts_ = slice(tt * 512, (tt + 1) * 512)
                pb = bank(psb(4, 0))
                P.mm(pb, sel_f[32:36, h * 128:(h + 1) * 128], smT[32:36, ts_])
                P.copy("act", GB[:, ts_], pb)
                egt = V(O_SCR + 25 * KB, [128, 512], F32)
                P.act(egt, pb, AF.Exp)
                P.tt("dve", qdT[:, ts_], QT[:, ts_], egt, ALU.mult)
            GB3 = GB.rearrange("p (n c) -> p n c", c=64)
            P.act(eg, gtok[:, :, h], AF.Exp)
            P.tt("dve", bgc, btok[:, :, h], eg, ALU.mult)
            P.tt("dve", edc, GB3[0:64, :, 63], gtok[:, :, h], ALU.subtract)
            P.act(edc, edc, AF.Exp)
            P.act(gtot, GB3[:, :, 63], AF.Exp)
            P.ts("dve", nbt, btok[:, :, h], -1.0, None, op0=ALU.mult)
            for g8 in range(4):
                for src, outs in ((KT, ((Kg, bgc), (Kd, edc))), (VT, ((Vb, btok[:, :, h]),))):
                    tp = bank(psb(2, 6), 64, BF16)
                    for i in range(8):
                        n = g8 * 8 + i
                        P.transpose(tp[:, i * 128:(i + 1) * 128], src[:, n * 64:(n + 1) * 64], ident_b)
                    t3 = tp.rearrange("p (n d) -> p n d", d=128)
                    for (dst, col) in outs:
                        P.tt("dve", dst[:, g8 * 8:(g8 + 1) * 8, :], t3,
                             col[:, g8 * 8:(g8 + 1) * 8].unsqueeze(2).to_broadcast([64, 8, 128]), ALU.mult)
            P.tt("dve", Xm, GB3[0:64, :, :], gtok[:, :, h].unsqueeze(2).to_broadcast([64, 32, 64]), ALU.subtract)
            DLn = V(O_SCR + 21 * KB, [64, 32, 64], F32)
            P.tt("dve", DLn, Xm, mposA.unsqueeze(1).to_broadcast([64, 32, 64]), ALU.add)
            P.act(DLn, DLn, AF.Exp, scale=-1.0)
            P.tt("dve", DLn, DLn, nbt[:, :].unsqueeze(2).to_broadcast([64, 32, 64]), ALU.mult)
            P.tt("dve", Xm, Xm, mposB.unsqueeze(1).to_broadcast([64, 32, 64]), ALU.subtract)
            P.act(Xm, Xm, AF.Exp)
            for g8 in range(4):
                pk = bank(psb(4, 0), 64)
                pq = bank(psb(4, 0), 64)
                for i in range(8):
                    n = g8 * 8 + i
                    cs = slice(n * 64, (n + 1) * 64)
                    P.mm(pk[:, i * 64:(i + 1) * 64], KT[:, cs], KT[:, cs])
                    P.mm(pq[:, i * 64:(i + 1) * 64], KT[:, cs], QT[:, cs])
                gs = slice(g8 * 8, (g8 + 1) * 8)
                P.tt("dve", Mm[:, gs, :], pk.rearrange("p (n c) -> p n c", c=64), DLn[:, gs, :], ALU.mult)
                P.tt("dve", qkT[:, gs, :], pq.rearrange("p (n c) -> p n c", c=64), Xm[:, gs, :], ALU.mult)
            for g8 in range(4):
                tp = bank(psb(2, 6), 64, BF16)
                for i in range(8):
                    n = g8 * 8 + i
                    P.transpose(tp[:, i * 64:(i + 1) * 64], Mm[:, n, :], ident_b[0:64, 0:64])
                P.copy("act", MTm[:, g8 * 8:(g8 + 1) * 8, :], tp[:, 0:512].rearrange("p (n c) -> p n c", c=64))
            P.tt("dve", PTm, MTm, ident_b[0:64, 0:64].unsqueeze(1).to_broadcast([64, 32, 64]), ALU.add)
            if h < 3:
                gen_box[0] = prep_gen(h + 1)
            Wc, WTc, PTc = Mm, MTm, PTm
            Wn, WTn, PTx = Mn, MTn, PTn
            for it in range(5):
                def stA(g8, it=it, Wc=Wc, WTc=WTc, Wn=Wn, WTn=WTn):
                    pw = bank(psb(6, 0), 64)
                    pwt = bank(psb(6, 0), 64) if it < 4 else None
                    for i in range(8):
                        n = g8 * 8 + i
                        P.mm(pw[:, i * 64:(i + 1) * 64], WTc[:, n, :], Wc[:, n, :])
                        if pwt is not None:
                            P.mm(pwt[:, i * 64:(i + 1) * 64], Wc[:, n, :], WTc[:, n, :])
                    gs = slice(g8 * 8, (g8 + 1) * 8)
                    P.copy("act", Wn[:, gs, :], pw.rearrange("p (n c) -> p n c", c=64))
                    if pwt is not None:
                        P.copy("dve", WTn[:, gs, :], pwt.rearrange("p (n c) -> p n c", c=64))

                def stB(g8, Wn=Wn, PTc=PTc, PTx=PTx):
                    pp = bank(psb(6, 0), 64)
                    for i in range(8):
                        n = g8 * 8 + i
                        P.mm(pp[:, i * 64:(i + 1) * 64], ident_b[0:64, 0:64], PTc[:, n, :], start=True, stop=False)
                        P.mm(pp[:, i * 64:(i + 1) * 64], Wn[:, n, :], PTc[:, n, :], start=False, stop=True)
                    gs = slice(g8 * 8, (g8 + 1) * 8)
                    P.copy("dve", PTx[:, gs, :], pp.rearrange("p (n c) -> p n c", c=64))
                for st_, g_ in ((stA, 0), (stA, 1), (stB, 0), (stA, 2), (stB, 1), (stA, 3), (stB, 2), (stB, 3)):
                    st_(g_)
                    pull(PULL_N)
                Wc, Wn = Wn, Wc
                WTc, WTn = WTn, WTc
                PTc, PTx = PTx, PTc
            TT = PTc
            for g8 in range(4):
                pb = bank(psb(4, 0))
                for i in range(8):
                    n = g8 * 8 + i
                    P.mm(pb[:, i * 64:(i + 1) * 64], Kg[:, n, :], TT[:, n, :])
                P.act(nkcT[:, g8 * 8:(g8 + 1) * 8, :], pb.rearrange("p (n c) -> p n c", c=64), AF.Copy, scale=-1.0)
            o3 = O_SCR + 29 * KB
            Sf = V(o3, [128, 128], F32)
            Sbb = [V(o3 + 512 + i * 256, [128, 128], BF16) for i in range(2)]
            vn = [V(o3 + 1024 + i * 256, [64, 128], BF16) for i in range(2)]
            oT = V(O_SCR + 1 * KB, [128, S], F32)
            P.memset("dve", Sf, 0.0)
            P.memset("dve", Sbb[0], 0.0)
            for n in range(32):
                sb_ = Sbb[n % 2]
                pv = bank(4, 64)[:, (n % 2) * 128:(n % 2) * 128 + 128]
                P.mm(pv, TT[:, n, :], Vb[:, n, :], start=True, stop=False)
                P.mm(pv, nkcT[:, n, :], sb_, start=False, stop=True)
                vn_ = vn[n % 2]
                P.copy("act", vn_, pv)
                if n % 8 == 0:
                    po = bank(psb(2, 2))
                pos = po[:, (n % 8) * 64:(n % 8 + 1) * 64]
                P.mm(pos, sb_, qdT[:, n * 64:(n + 1) * 64], start=True, stop=False)
                P.mm(pos, vn_, qkT[:, n, :], start=False, stop=True)
                pd = bank(5)[:, (n % 2) * 128:(n % 2) * 128 + 128]
                P.mm(pd, Kd[:, n, :], vn_)
                P.stt("dve", Sbb[(n + 1) % 2], Sf, gtot[:, n:n + 1], pd, ALU.mult, ALU.add)
                P.stt("dve", Sf, Sf, gtot[:, n:n + 1], pd, ALU.mult, ALU.add)
                if n % 8 == 7:
                    tt = n // 8
                    P.copy("act", oT[:, tt * 512:(tt + 1) * 512], po)
                pull(PULL_S)
            if h == 0:
                dump(f"o_dn{l}", oT, [128, S])
            drain()
            wz = wload(l, f"dnz{h}", 1024)

            for tt in range(NTT):
                ts_ = slice(tt * 512, (tt + 1) * 512)
                s_ = sq2[tt % 2]
                P.act(s_, oT[:, ts_], AF.Square)
                p2 = bank(psb(2, 6))
                P.mm(p2, ones_b, s_)
                lnt = V(O_SCR + 25 * KB, [128, 512], F32)
                rst = V(O_SCR + 27 * KB, [128, 512], F32)
                P.act(lnt, p2, AF.Ln, bias=epsc, scale=1.0 / 128)
                P.act(rst, lnt, AF.Exp, scale=-0.5)
                P.stt("dve", oT[:, ts_], oT[:, ts_], prm[:, P_DNG:P_DNG + 1], rst, ALU.mult, ALU.mult)

            def z_cons(tt, ts_, pb, h=h):
                zt = V(O_SCR + 25 * KB + (tt % 2) * 2 * KB, [128, 512], F32)
                P.act(zt, pb, AF.Silu)
                P.tt("dve", yT[:, 8 + h, ts_], oT[:, ts_], zt, ALU.mult)
            proj_fm(wz, 128, 0, z_cons)
        dump(f"y_dn{l}", yT[:, 8:12, :], [128, 4, S])
        if stop_after == "dn":
            return

        mT = V(O_SCR, [128, NCH, 1024], BF16)
        macc = V(O_SCR + 16 * KB, [128, 512], F32)
        mtmp = V(O_SCR + 18 * KB, [128, 512], F32)
        sgt = [[V(O_XT + b * 8 * KB + 4 * KB + t2 * 2 * KB, [128, 512], F32) for t2 in range(2)] for b in range(3)]
        for half in range(2):
            for dc in range(NCH):
                for b in range(3):
                    wgb = wload(l, f"g{b}_{dc}", 1024).rearrange("p (k c) -> p k c", k=NCH)
                    for t2 in range(2):
                        tt = half * 2 + t2
                        ts_ = slice(tt * 512, (tt + 1) * 512)
                        pg = bank(psb(6, 0))
                        for k in range(NCH):
                            P.mm(pg, wgb[:, k, :], hT[:, k, ts_], start=(k == 0), stop=(k == NCH - 1))
                        P.act(sgt[b][t2], pg, AF.Sigmoid)
                wA3 = wload(l, f"brA{dc}", 1024).rearrange("p (b c m) -> p b c m", b=2, c=4)
                wB3 = wload(l, f"brB{dc}", 512).rearrange("p (c m) -> p c m", c=4)
                for t2 in range(2):
                    tt = half * 2 + t2
                    ts_ = slice(tt * 512, (tt + 1) * 512)
                    for b in range(3):
                        pbr = bank(psb(6, 0))
                        for c in range(4):
                            lw = wA3[:, b, c, :] if b < 2 else wB3[:, c, :]
                            P.mm(pbr, lw, yT[:, 4 * b + c, ts_], start=(c == 0), stop=(c == 3))
                        if b == 0:
                            P.tt("dve", macc, pbr, sgt[b][t2], ALU.mult)
                        elif b == 1:
                            P.tt("dve", mtmp, pbr, sgt[b][t2], ALU.mult)
                            P.tt("dve", macc, macc, mtmp, ALU.add)
                        else:
                            P.tt("dve", mtmp, pbr, sgt[b][t2], ALU.mult)
                            P.tt("dve", mT[:, dc, t2 * 512:(t2 + 1) * 512], macc, mtmp, ALU.add)
            if half == 0:
                dump(f"merged{l}", mT[:, :, :], [128, NCH, 1024])
            for dc in range(NCH):
                wo = wload(l, f"wo{dc}", 1024)
                hs = slice(half * 1024, (half + 1) * 1024)
                P.dma("sp", xT[:, dc, hs], scrx[:, dc, hs], f"unsp{dc}")

                def wo_cons(tt, ts_, pb, dc=dc):
                    P.tt("dve", xT[:, dc, ts_], xT[:, dc, ts_], pb, ALU.add)
                proj_fm(wo, 128, 0, wo_cons, rhs_of=lambda k, ts_, half=half: mT[:, k, ts_.start - half * 1024: ts_.stop - half * 1024],
                        tts=range(half * 2, half * 2 + 2), nb=6)
        dump(f"x_mix{l}", xT[:, :, :], [128, NCH, S])
        if stop_after == "mix":
            return

        rmsnorm_to_hT(P_GFFN, O_SCR)
        o = O_YT
        graw = V(o, [128, S + 2], F32); o += 8 * KB + 128
        vraw = V(o, [128, S + 2], F32); o += 8 * KB + 128
        gacs = [V(o, [128, S], F32), V(O_SMT, [128, S], F32)]; o += 8 * KB
        vacs = [V(o, [128, S], F32), V(O_SCR + 20 * KB, [128, S], F32)]; o += 8 * KB
        assert o <= O_WS
        aT0 = V(O_SCR + 0 * KB, [128, GRP, S], BF16)
        aT1 = V(O_YT + 33 * KB, [128, 3, S], BF16)
        aT1b = V(O_SCR + 16 * KB, [128, 1, S], BF16)
        P.memset("dve", graw[:, 0:2], 0.0)
        P.memset("dve", vraw[:, 0:2], 0.0)

        def a_slot(gi, jj):
            if gi % 2 == 0:
                return aT0[:, jj, :]
            return aT1[:, jj, :] if jj < 3 else aT1b[:, 0, :]

        def ffn_up(gi, j0, n):
                for jj in range(n):
                    j = j0 + jj
                    gac = gacs[j % 2]
                    vac = vacs[j % 2]
                    for nm, rawb, accb, c0 in (("upg", graw, gac, j), ("upv", vraw, vac, NFF + j)):
                        wt = wload(l, f"{nm}{j}", 1024)
                        w2c = prm[:, P_FFW + c0 * 3 + 2:P_FFW + c0 * 3 + 3]

                        def up_cons(tt, ts_, pb, rawb=rawb, accb=accb, w2c=w2c):
                            P.copy("act", rawb[:, 2 + tt * 512:2 + (tt + 1) * 512], pb)
                            P.act(accb[:, ts_], pb, AF.Copy, scale=w2c)
                        proj_fm(wt, 128, 0, up_cons, nb=6)
                        P.stt("dve", accb, rawb[:, 1:1 + S], prm[:, P_FFW + c0 * 3 + 1:P_FFW + c0 * 3 + 2], accb,
                              ALU.mult, ALU.add)
                        P.stt("dve", accb, rawb[:, 0:S], prm[:, P_FFW + c0 * 3:P_FFW + c0 * 3 + 1], accb,
                              ALU.mult, ALU.add)
                    P.act(gac, gac, AF.Silu)
                    P.tt("dve", a_slot(gi, jj), gac, vac, ALU.mult)

        def ffn_down(gi, j0, n):
                for q in range(4):
                    wd = wload(l, f"dn{gi}_{q}", n * 256)
                    wd3 = wd.rearrange("p (j c) -> p j c", j=n)
                    for dd in range(2):
                        dc = 2 * q + dd
                        for tt in range(NTT):
                            ts_ = slice(tt * 512, (tt + 1) * 512)
                            pb = bank(psb(6, 0))
                            for jj in range(n):
                                P.mm(pb, wd3[:, jj, dd * 128:(dd + 1) * 128], a_slot(gi, jj)[:, ts_],
                                     start=(jj == 0), stop=(jj == n - 1))
                            P.tt("dve", xT[:, dc, ts_], xT[:, dc, ts_], pb, ALU.add)

        groups = _ffn_groups()
        for gi, (j0, n) in enumerate(groups):
            ffn_up(gi, j0, n)
            if gi > 0:
                ffn_down(gi - 1, *groups[gi - 1])
        ffn_down(len(groups) - 1, *groups[-1])
        dump(f"x_ffn{l}", xT[:, :, :], [128, NCH, S])
        if stop_after == "ffn":
            return

        rmsnorm_to_hT(P_GPLE, O_SCR)
        ptok = V(O_YT, [128, 16, 256], F32)
        pT = V(O_YT + 16 * KB, [128, 2, S], BF16)
        sgp = [V(O_YT + 24 * KB + i * 2 * KB, [128, 512], F32) for i in range(2)]
        ptmp = V(O_YT + 28 * KB, [128, 512], F32)
        for q in range(4):
            P.dma("sp", ptok[:, q * 4:(q + 1) * 4, :],
                  p_in[l, q * 512:(q + 1) * 512, :].rearrange("(a p) c -> p a c", p=128), f"pin{q}")
        for c in range(2):
            for g4 in range(4):
                tp = bank(psb(2, 6))
                for i in range(4):
                    tb = g4 * 4 + i
                    P.transpose(tp[:, i * 128:(i + 1) * 128], ptok[:, tb, c * 128:(c + 1) * 128], ident_f)
                P.copy("act", pT[:, c, g4 * 512:(g4 + 1) * 512], tp)
        for dc in range(NCH):
            wpg = wload(l, f"pg{dc}", 1024)
            wpl = wload(l, f"pl{dc}", 256)
            wpl3 = wpl.rearrange("p (c m) -> p c m", c=2)

            def pg_cons(tt, ts_, pb, dc=dc, wpl3=wpl3):
                s_ = sgp[tt % 2]
                P.act(s_, pb, AF.Sigmoid)
                pp = bank(psb(2, 4))
                for c in range(2):
                    P.mm(pp, wpl3[:, c, :], pT[:, c, ts_], start=(c == 0), stop=(c == 1))
                P.tt("dve", ptmp, pp, s_, ALU.mult)
                P.tt("dve", xT[:, dc, ts_], xT[:, dc, ts_], ptmp, ALU.add)
            proj_fm(wpg, 128, 0, pg_cons, defer=True)
        dump(f"x_out{l}", xT[:, :, :], [128, NCH, S])

    xin = [V(O_YT + i * 4 * KB, [128, D], F32) for i in range(2)]
    for tb in range(16):
        xi = xin[tb % 2]
        P.dma("sp", xi, x_in[tb * 128:(tb + 1) * 128, :], f"xin{tb % 2}")
        for g2 in range(2):
            tp = bank(psb(2, 6))
            for i in range(4):
                c = g2 * 4 + i
                P.transpose(tp[:, i * 128:(i + 1) * 128], xi[:, c * 128:(c + 1) * 128], ident_f)
            eng = "act" if g2 == 0 else "dve"
            P.copy(eng, xT[:, g2 * 4:(g2 + 1) * 4, tb * 128:(tb + 1) * 128],
                   tp.rearrange("p (c t) -> p c t", c=4))

    for l in range(depth):
        layer(l)

    if stop_after is None:
        xo = [V(O_YT + i * 4 * KB, [128, D], F32) for i in range(2)]
        for tb in range(16):
            xo_ = xo[tb % 2]
            for g2 in range(2):
                tp = bank(psb(2, 6))
                for i in range(4):
                    c = g2 * 4 + i
                    P.transpose(tp[:, i * 128:(i + 1) * 128], xT[:, c, tb * 128:(tb + 1) * 128], ident_f)
                eng = "act" if g2 == 0 else "dve"
                P.copy(eng, xo_[:, g2 * 512:(g2 + 1) * 512], tp)
            P.dma("sp", out[tb * 128:(tb + 1) * 128, :], xo_, f"xout{tb % 2}", final=True)
    else:
        z = V(O_SCR, [128, 8], F32)
        P.memset("pool", z, 0.0)
        P.dma("sp", out[0:128, 0:8], z, "xout0", final=True)
    P.emit()
    return nc, dbg_outs, P


_CACHE = {}


def kernel(**inputs):
    inp = {k: np.asarray(v) for k, v in inputs.items()}
    if "nc" not in _CACHE:
        _CACHE["nc"] = build()[0]
    nc = _CACHE["nc"]
    wstream = np.stack([pack_weights(inp, l) for l in range(DEPTH)])
    prm = np.stack([pack_params(inp, l) for l in range(DEPTH)])
    cst = make_consts()
    in_maps = []
    for b in range(8):
        in_maps.append({"x": np.ascontiguousarray(inp["x"][b]),
                        "p": np.ascontiguousarray(inp["p"][:, b]),
                        "wst": wstream, "prm": prm, "cst": cst})
    res = run_bass_kernel_spmd(nc, in_maps, core_ids=list(range(8)))
    return np.stack([np.asarray(r["out"]) for r in res.results]).astype(np.float32)
```

```python
import numpy as np
import concourse.bass as bass
import concourse.mybir as mybir

F32 = mybir.dt.float32
BF16 = mybir.dt.bfloat16
ALU = mybir.AluOpType
AF = mybir.ActivationFunctionType

ENGS = ("pe", "act", "dve", "pool", "sp")


def _prod(xs):
    r = 1
    for v in xs:
        r *= int(v)
    return r


def region(ap):
    t = ap.tensor
    es = mybir.dt.size(ap.dtype)
    off = int(ap.offset)
    pat = ap.ap
    if str(ap.space) == "DRAM":
        ext = 0
        for st, cnt in pat:
            ext += (cnt - 1) * abs(st)
        return (t.name, 0, 1, off * es, (off + ext + 1) * es)
    shape = list(t.shape)
    F = _prod(shape[1:])
    p0 = off // F
    f0 = off % F
    pstep, pcnt = pat[0]
    npart = pcnt if pstep != 0 else 1
    ext = 0
    for st, cnt in pat[1:]:
        ext += (cnt - 1) * abs(st)
    return (t.name, p0, p0 + npart, f0 * es, (f0 + ext + 1) * es)


def _untracked(ap):
    return str(ap.space) == "DRAM" and not ap.tensor.name.startswith("scr_")


def _overlap(a, b):
    return a[1] < b[2] and b[1] < a[2] and a[3] < b[4] and b[3] < a[4]


def _covers(a, b):
    return a[1] <= b[1] and a[2] >= b[2] and a[3] <= b[3] and a[4] >= b[4]


class Instr:
    __slots__ = ("eng", "fn", "deps", "signal", "is_dma", "key", "val", "idx")

    def __init__(self, eng, fn, is_dma, key):
        self.eng = eng
        self.fn = fn
        self.deps = set()
        self.signal = False
        self.is_dma = is_dma
        self.key = key
        self.val = 0


class Prog:
    def __init__(self, nc):
        self.nc = nc
        self.instrs = []
        self.wr = {}
        self.rd = {}
        self.final_keys = set()
        self.last_dma = {}

    def add(self, eng, fn, reads=(), writes=(), dma_key=None):
        ins = Instr(eng, fn, dma_key is not None, dma_key if dma_key is not None else eng)
        idx = len(self.instrs)
        ins.idx = idx
        self.instrs.append(ins)
        deps = ins.deps
        for ap in reads:
            if ap is None or isinstance(ap, (int, float)):
                continue
            if _untracked(ap):
                continue
            r = region(ap)
            for (w, wi) in self.wr.get(r[0], ()):
                if _overlap(w, r):
                    deps.add(wi)
        for ap in writes:
            if _untracked(ap):
                continue
            r = region(ap)
            wl = self.wr.setdefault(r[0], [])
            rl = self.rd.setdefault(r[0], [])
            for (w, wi) in wl:
                if _overlap(w, r):
                    deps.add(wi)
            for (q, qi) in rl:
                if _overlap(q, r):
                    deps.add(qi)
            self.wr[r[0]] = [(w, wi) for (w, wi) in wl if not _covers(r, w)]
            self.rd[r[0]] = [(q, qi) for (q, qi) in rl if not _covers(r, q)]
            self.wr[r[0]].append((r, idx))
        for ap in reads:
            if ap is None or isinstance(ap, (int, float)):
                continue
            if _untracked(ap):
                continue
            r = region(ap)
            rl = self.rd.setdefault(r[0], [])
            if dma_key is None:
                rl[:] = [(q, qi) for (q, qi) in rl
                         if not (q == r and self.instrs[qi].eng == eng and not self.instrs[qi].is_dma)]
            rl.append((r, idx))
        if dma_key is not None:
            if dma_key in self.last_dma:
                deps.add(self.last_dma[dma_key])
            self.last_dma[dma_key] = idx
        deps.discard(idx)
        return ins

    def mm(self, out, lhsT, rhs, start=True, stop=True):
        rd = [lhsT, rhs]
        return self.add("pe", lambda e: e.matmul(out, lhsT, rhs, start=start, stop=stop),
                        reads=rd, writes=[out])

    def transpose(self, out, in_, ident):
        return self.add("pe", lambda e: e.transpose(out, in_, ident),
                        reads=[in_, ident], writes=[out])

    def act(self, out, in_, func, bias=None, scale=1.0, accum_out=None, eng="act"):
        kw = {}
        if bias is not None:
            kw["bias"] = bias
        if accum_out is not None:
            kw["accum_out"] = accum_out
        rd = [in_, bias if not isinstance(bias, (int, float)) else None,
              scale if not isinstance(scale, (int, float)) else None]
        wr = [out] + ([accum_out] if accum_out is not None else [])
        return self.add(eng, lambda e: e.activation(out, in_, func, scale=scale, **kw),
                        reads=rd, writes=wr)

    def tt(self, eng, out, in0, in1, op):
        return self.add(eng, lambda e: e.tensor_tensor(out, in0, in1, op),
                        reads=[in0, in1], writes=[out])

    def ts(self, eng, out, in0, s1, s2=None, op0=ALU.mult, op1=None, accum_out=None):
        kw = {}
        if op1 is not None:
            kw["op1"] = op1
        if accum_out is not None:
            kw["accum_out"] = accum_out
        rd = [in0, s1 if not isinstance(s1, (int, float)) else None,
              s2 if not isinstance(s2, (int, float)) else None]
        wr = [out] + ([accum_out] if accum_out is not None else [])
        return self.add(eng, lambda e: e.tensor_scalar(out, in0, s1, s2, op0, **kw),
                        reads=rd, writes=wr)

    def stt(self, eng, out, in0, scalar, in1, op0, op1):
        rd = [in0, in1, scalar if not isinstance(scalar, (int, float)) else None]
        return self.add(eng, lambda e: e.scalar_tensor_tensor(out, in0, scalar, in1, op0, op1),
                        reads=rd, writes=[out])

    def copy(self, eng, out, in_):
        if eng == "act":
            return self.add(eng, lambda e: e.copy(out, in_), reads=[in_], writes=[out])
        return self.add(eng, lambda e: e.tensor_copy(out, in_), reads=[in_], writes=[out])

    def memset(self, eng, out, val):
        return self.add(eng, lambda e: e.memset(out, val), reads=[], writes=[out])

    def scan(self, out, d0, d1, initial, op0, op1):
        rd = [d0, d1, initial if not isinstance(initial, (int, float)) else None]
        return self.add("dve", lambda e: e.tensor_tensor_scan(out, d0, d1, initial, op0, op1),
                        reads=rd, writes=[out])

    def dma(self, eng, out, in_, key, final=False):
        if final:
            self.final_keys.add(key)
        return self.add(eng, lambda e: e.dma_start(out, in_), reads=[in_], writes=[out], dma_key=key)

    def emit(self):
        nc = self.nc
        instrs = self.instrs
        for ins in instrs:
            if ins.eng == "pe" and not ins.is_dma:
                ins.deps = {d for d in ins.deps if not (instrs[d].eng == "pe" and not instrs[d].is_dma)}
            for d in ins.deps:
                instrs[d].signal = True
        keys = []
        for ins in instrs:
            if ins.key not in keys:
                keys.append(ins.key)
        for ins in instrs:
            if ins.is_dma:
                ins.signal = True
        cnt = {k: 0 for k in keys}
        for ins in instrs:
            if ins.signal:
                cnt[ins.key] += 16 if ins.is_dma else 1
            ins.val = cnt[ins.key]
        used = [k for k in keys if cnt[k] > 0]
        self.sem_totals = {k: cnt[k] for k in used}
        import contextlib
        with contextlib.ExitStack() as st:
            sems = {k: st.enter_context(nc.semaphore("s_" + str(k))) for k in used}
            block = st.enter_context(nc.Block())
            per_eng = {e: [i for i in instrs if i.eng == e] for e in ENGS}
            nwaits = [0]

            def run(engobj, lst, is_last_sp=False):
                clock = {}
                for ins in lst:
                    need = {}
                    for d in ins.deps:
                        di = instrs[d]
                        if di.val > need.get(di.key, 0):
                            need[di.key] = di.val
                    for k, v in need.items():
                        if clock.get(k, 0) < v:
                            engobj.wait_ge(sems[k], v)
                            clock[k] = v
                            nwaits[0] += 1
                    bi = ins.fn(engobj)
                    if ins.signal:
                        bi.then_inc(sems[ins.key], 16 if ins.is_dma else 1)
                if is_last_sp:
                    for k in sorted(self.final_keys, key=str):
                        if k in sems:
                            engobj.wait_ge(sems[k], cnt[k])

            @block.tensor
            def _(e):
                run(e, per_eng["pe"])

            @block.scalar
            def _(e):
                run(e, per_eng["act"])

            @block.vector
            def _(e):
                run(e, per_eng["dve"])

            @block.gpsimd
            def _(e):
                run(e, per_eng["pool"])

            @block.sync
            def _(e):
                run(e, per_eng["sp"], True)
            self.nwaits = nwaits[0]

from concourse.bass_utils import run_bass_kernel_spmd
import ml_dtypes

S = 2048
D = 1024
DEPTH = 2
NCH = D // 128
NTT = S // 512
DFF = 2816
NFF = DFF // 128
EPS = 1e-6
GRP = 4
import os as _os
PULL_N = int(_os.environ.get('PULL_N', '0'))
PULL_S = int(_os.environ.get('PULL_S', '1'))
NPRM = 224
NCST = 1536

C_FQ, C_FK, C_FV, C_FF = 0, 512, 1024, 1536
C_SB, C_SC, C_SV = 1544, 2056, 2568
C_DQ, C_DK, C_DV = 3080, 3592, 4104
C_DB, C_DA, C_DZ, C_G = 4616, 4620, 4624, 5136

P_GMIX, P_GFFN, P_GPLE = 0, 8, 16
P_FQG, P_FKG, P_BF, P_ALOG, P_DTB, P_DNG = 24, 25, 26, 27, 28, 29
P_SCW, P_DNW, P_FFW = 30, 42, 90

K_ID, K_M01, K_ONE, K_SEL, K_MPA, K_MPB = 0, 128, 384, 512, 1024, 1088


def _ffn_groups():
    gs = []
    j = 0
    while j < NFF:
        n = min(GRP, NFF - j)
        gs.append((j, n))
        j += n
    return gs


def wtile_index():
    idx = {}
    names = ["small"]
    names += [f"foxv{c}" for c in range(4)]
    names += [f"foxqk{h}" for h in range(8)]
    for j in range(4):
        names += [f"scb{j}", f"scc{j}", f"scv{j}"]
    for h in range(4):
        names += [f"dnq{h}", f"dnk{h}", f"dnv{h}", f"dnz{h}"]
    for dc in range(8):
        names += [f"g0_{dc}", f"g1_{dc}", f"g2_{dc}", f"brA{dc}", f"brB{dc}"]
    names += [f"wo{dc}" for dc in range(8)]
    for j in range(NFF):
        names += [f"upg{j}", f"upv{j}"]
    for gi, (j0, n) in enumerate(_ffn_groups()):
        names += [f"dn{gi}_{q}" for q in range(4)]
    for dc in range(8):
        names += [f"pg{dc}", f"pl{dc}"]
    for i, n in enumerate(names):
        idx[n] = i
    return idx


WIDX = wtile_index()
NT = len(WIDX)


def pack_weights(inp, l):
    W = np.zeros((NT, 128, 1024), np.float32)
    w_in = inp["w_in"][l]

    def kc(cols):
        n = cols.shape[1]
        return cols.reshape(8, 128, n).transpose(1, 0, 2).reshape(128, 8 * n)

    def put(name, arr):
        W[WIDX[name], :, :arr.shape[1]] = arr

    sm = np.zeros((1024, 96), np.float32)
    sm[:, 0:8] = w_in[:, C_FF:C_FF + 8]
    sm[:, 32:36] = w_in[:, C_DA:C_DA + 4]
    sm[:, 64:68] = w_in[:, C_DB:C_DB + 4]
    put("small", kc(sm))
    for c in range(4):
        put(f"foxv{c}", kc(w_in[:, C_FV + c * 128:C_FV + (c + 1) * 128]))
    for h in range(8):
        qk = np.concatenate([w_in[:, C_FQ + h * 64:C_FQ + (h + 1) * 64],
                             w_in[:, C_FK + h * 64:C_FK + (h + 1) * 64]], axis=1)
        put(f"foxqk{h}", kc(qk))
    for j in range(4):
        put(f"scb{j}", kc(w_in[:, C_SB + j * 128:C_SB + (j + 1) * 128]))
        put(f"scc{j}", kc(w_in[:, C_SC + j * 128:C_SC + (j + 1) * 128]))
        put(f"scv{j}", kc(w_in[:, C_SV + j * 128:C_SV + (j + 1) * 128]))
    for h in range(4):
        put(f"dnq{h}", kc(w_in[:, C_DQ + h * 128:C_DQ + (h + 1) * 128]))
        put(f"dnk{h}", kc(w_in[:, C_DK + h * 128:C_DK + (h + 1) * 128]))
        put(f"dnv{h}", kc(w_in[:, C_DV + h * 128:C_DV + (h + 1) * 128]))
        put(f"dnz{h}", kc(w_in[:, C_DZ + h * 128:C_DZ + (h + 1) * 128]))
    wb = inp["w_branch"][l]
    for dc in range(8):
        for b in range(3):
            put(f"g{b}_{dc}", kc(w_in[:, C_G + b * 1024 + dc * 128:C_G + b * 1024 + (dc + 1) * 128]))

        def br(b):
            a = wb[b][:, dc * 128:(dc + 1) * 128]
            return a.reshape(4, 128, 128).transpose(1, 0, 2).reshape(128, 512)
        put(f"brA{dc}", np.concatenate([br(0), br(1)], axis=1))
        put(f"brB{dc}", br(2))
        put(f"wo{dc}", kc(inp["w_o"][l][:, dc * 128:(dc + 1) * 128]))
        put(f"pg{dc}", kc(inp["w_ple_gate"][l][:, dc * 128:(dc + 1) * 128]))
        a = inp["w_ple"][l][:, dc * 128:(dc + 1) * 128]
        put(f"pl{dc}", a.reshape(2, 128, 128).transpose(1, 0, 2).reshape(128, 256))
    w_up = inp["w_up"][l]
    for j in range(NFF):
        put(f"upg{j}", kc(w_up[:, j * 128:(j + 1) * 128]))
        put(f"upv{j}", kc(w_up[:, DFF + j * 128:DFF + (j + 1) * 128]))
    w_dn = inp["w_down"][l]
    for gi, (j0, n) in enumerate(_ffn_groups()):
        for q in range(4):
            a = w_dn[j0 * 128:(j0 + n) * 128, q * 256:(q + 1) * 256]
            put(f"dn{gi}_{q}", a.reshape(n, 128, 256).transpose(1, 0, 2).reshape(128, n * 256))
    return W


def pack_params(inp, l):
    Pm = np.zeros((128, NPRM), np.float32)
    Pm[:, P_GMIX:P_GMIX + 8] = inp["g_mix"][l].reshape(8, 128).T
    Pm[:, P_GFFN:P_GFFN + 8] = inp["g_ffn"][l].reshape(8, 128).T
    Pm[:, P_GPLE:P_GPLE + 8] = inp["g_ple"][l].reshape(8, 128).T
    Pm[0:64, P_FQG] = inp["fox_q_gain"][l]
    Pm[64:128, P_FQG] = inp["fox_k_gain"][l]
    Pm[0:64, P_FKG] = inp["fox_k_gain"][l]
    Pm[0:8, P_BF] = inp["b_fox_f"][l]
    Pm[32:36, P_ALOG] = inp["dn_a_log"][l]
    Pm[32:36, P_DTB] = inp["dn_dt_bias"][l]
    Pm[:, P_DNG] = inp["dn_norm_gain"][l]
    Pm[:, P_SCW:P_SCW + 12] = inp["sc_conv_w"][l].reshape(3, 4, 128).transpose(2, 1, 0).reshape(128, 12)
    Pm[:, P_DNW:P_DNW + 48] = inp["dn_conv_w"][l].reshape(4, 12, 128).transpose(2, 1, 0).reshape(128, 48)
    Pm[:, P_FFW:P_FFW + 132] = inp["ffn_conv_w"][l].reshape(3, 44, 128).transpose(2, 1, 0).reshape(128, 132)
    return Pm


def make_consts():
    C = np.zeros((128, NCST), np.float32)
    r = np.arange(128)[:, None]
    c = np.arange(128)[None, :]
    C[:, K_ID:K_ID + 128] = (r == c)
    C[:, K_M01:K_M01 + 128] = (c >= r)
    C[:, K_M01 + 128:K_M01 + 256] = ((r // 64) == (c // 64))
    C[:, K_ONE:K_ONE + 128] = 1.0
    for h in range(4):
        C[32 + h, K_SEL + h * 128:K_SEL + (h + 1) * 128] = 1.0
    r6 = np.arange(64)[:, None]
    c6 = np.arange(64)[None, :]
    C[0:64, K_MPA:K_MPA + 64] = np.where(r6 > c6, 0.0, 1.0e4)
    C[0:64, K_MPB:K_MPB + 64] = np.where(c6 >= r6, 0.0, 1.0e4)
    return C


def build(depth=DEPTH, dbg=None, stop_after=None):
    nc = bass.Bass("TRN2", target_bir_lowering=False)
    x_in = nc.dram_tensor("x", [S, D], F32, kind="ExternalInput").ap()
    p_in = nc.dram_tensor("p", [DEPTH, S, 256], F32, kind="ExternalInput").ap()
    wst = nc.dram_tensor("wst", [DEPTH, NT, 128, 1024], F32, kind="ExternalInput").ap()
    prm_in = nc.dram_tensor("prm", [DEPTH, 128, NPRM], F32, kind="ExternalInput").ap()
    cst_in = nc.dram_tensor("cst", [128, NCST], F32, kind="ExternalInput").ap()
    out = nc.dram_tensor("out", [S, D], F32, kind="ExternalOutput").ap()
    scrx = nc.dram_tensor("scr_x", [128, NCH, S], F32).ap()
    dbg_outs = {}

    TOT = 211456
    SB = nc.alloc_sbuf_tensor("SB", [128, TOT // 4], F32)
    PS = nc.alloc_psum_tensor("PS", [128, 8 * 512], F32)
    P = Prog(nc)

    def V(off, shape, dt=F32, p0=0):
        es = mybir.dt.size(dt)
        n = _prod(shape[1:])
        assert off % 4 == 0 and off + n * es <= TOT, (off, shape)
        nw = (n * es + 3) // 4
        v = SB[p0:p0 + shape[0], off // 4: off // 4 + nw]
        if dt != F32:
            v = v.bitcast(dt)
        v = v[:, 0:n]
        if len(shape) == 3:
            v = v.rearrange("p (a b) -> p a b", a=shape[1])
        elif len(shape) == 4:
            v = v.rearrange("p (a b c) -> p a b c", a=shape[1], b=shape[2])
        return v

    def bank(b, parts=128, dt=F32, p0=0):
        v = PS[p0:p0 + parts, b * 512:(b + 1) * 512]
        if dt != F32:
            v = v.bitcast(dt)
        return v

    KB = 1024
    O_XT = 0
    O_HT = 64 * KB
    O_YT = 96 * KB
    O_WS = 144 * KB
    O_WB = 152 * KB
    O_CST = 158 * KB
    O_CBF = 164 * KB
    O_PRM = 165 * KB
    O_MISC = 166 * KB
    O_SMT = 167 * KB
    O_SCR = 175 * KB
    SCR_END = TOT

    xT = V(O_XT, [128, NCH, S], F32)
    hT = V(O_HT, [128, NCH, S], BF16)
    yT = V(O_YT, [128, 12, S], BF16)
    wstage = [V(O_WS + i * 4 * KB, [128, 1024], F32) for i in range(2)]
    wbf = [V(O_WB + i * 2 * KB, [128, 1024], BF16) for i in range(3)]
    cst = V(O_CST, [128, NCST], F32)
    cbf = V(O_CBF, [128, 512], BF16)
    prm = V(O_PRM, [128, NPRM], F32)
    misc = V(O_MISC, [128, 256], F32)
    smT = V(O_SMT, [128, S], F32)

    ident_f = cst[:, K_ID:K_ID + 128]
    ident_b = cbf[:, 0:128]
    mask01_b = cbf[:, 128:256]
    ones_b = cbf[:, 384:512]
    bd_b = cbf[:, 256:384]
    sel_f = cst[:, K_SEL:K_SEL + 512]
    mposA = cst[0:64, K_MPA:K_MPA + 64]
    mposB = cst[0:64, K_MPB:K_MPB + 64]
    epsc = misc[:, 0:1]
    onec = misc[:, 1:2]
    qgs = misc[:, 2:3]
    negb = misc[:, 3:4]
    negA = misc[:, 4:5]

    state = {"ws": 0, "wb": 0, "ps": 0}

    def wload(l, name, X):
        si = state["ws"] % 2
        bi = state["wb"] % 3
        state["ws"] += 1
        state["wb"] += 1
        P.dma("sp", wstage[si][:, 0:X], wst[l, WIDX[name], :, 0:X], f"ws{si}")
        P.copy("pool", wbf[bi][:, 0:X], wstage[si][:, 0:X])
        return wbf[bi][:, 0:X]

    def psb(n=4, base=0):
        k = ("ps", base, n)
        state[k] = state.get(k, 0) + 1
        return base + (state[k] - 1) % n

    def dump(name, ap, shape):
        if dbg is None or name not in dbg:
            return
        t = nc.dram_tensor("dbg_" + name, list(shape), ap.dtype, kind="ExternalOutput").ap()
        dbg_outs[name] = t
        P.dma("sp", t, ap, "dbgout", final=True)

    P.dma("sp", cst[:], cst_in, "cstin")
    P.copy("dve", cbf[:], cst[:, 0:512])
    P.memset("dve", epsc, EPS)
    P.memset("dve", onec, 1.0)

    def rmsnorm_to_hT(gcol0, o_scr):
        sq = [V(o_scr + i * KB, [128, 512], BF16) for i in range(2)]
        lnt = V(o_scr + 2 * KB, [128, 512], F32)
        rstd = V(o_scr + 4 * KB, [128, 512], F32)
        for tt in range(NTT):
            ts_ = slice(tt * 512, (tt + 1) * 512)
            pb = bank(psb(2, 6))
            for c in range(NCH):
                s_ = sq[c % 2]
                P.act(s_, xT[:, c, ts_], AF.Square)
                P.mm(pb, ones_b, s_, start=(c == 0), stop=(c == NCH - 1))
            P.act(lnt, pb, AF.Ln, bias=epsc, scale=1.0 / D)
            P.act(rstd, lnt, AF.Exp, scale=-0.5)
            for c in range(NCH):
                P.stt("dve", hT[:, c, ts_], xT[:, c, ts_], prm[:, gcol0 + c:gcol0 + c + 1], rstd,
                      ALU.mult, ALU.mult)

    def proj_fm(wt, M, m0, consumer, kchunks=NCH, rhs_of=None, tts=range(NTT), defer=False, nb=4):
        w3 = wt.rearrange("p (k c) -> p k c", k=kchunks)
        pend = None
        for tt in tts:
            ts_ = slice(tt * 512, (tt + 1) * 512)
            pb = bank(psb(nb, 0), M)
            for k in range(kchunks):
                rhs = hT[:, k, ts_] if rhs_of is None else rhs_of(k, ts_)
                P.mm(pb, w3[:, k, m0:m0 + M], rhs, start=(k == 0), stop=(k == kchunks - 1))
            if defer:
                if pend is not None:
                    consumer(*pend)
                pend = (tt, ts_, pb)
            else:
                consumer(tt, ts_, pb)
        if pend is not None:
            consumer(*pend)

    def layer(l):
        P.dma("sp", prm[:], prm_in[l], "prmin")
        P.ts("dve", qgs[0:64], prm[0:64, P_FQG:P_FQG + 1], 0.125, None, op0=ALU.mult)
        P.copy("dve", qgs[64:128], prm[64:128, P_FQG:P_FQG + 1])
        P.ts("dve", negb[0:8], prm[0:8, P_BF:P_BF + 1], -1.0, None, op0=ALU.mult)
        P.act(negA[32:36], prm[32:36, P_ALOG:P_ALOG + 1], AF.Exp)
        P.ts("dve", negA[32:36], negA[32:36], -1.0, None, op0=ALU.mult)

        rmsnorm_to_hT(P_GMIX, O_SCR)
        dump(f"h{l}", hT[:, :, :], [128, NCH, S])
        for c in range(NCH):
            P.dma("sp", scrx[:, c, :], xT[:, c, :], f"spill{c}")
        if stop_after == "norm1":
            return

        wt = wload(l, "small", 8 * 96)

        def small_cons(tt, ts_, pb):
            P.act(smT[0:8, ts_], pb[0:8, :], AF.Exp, bias=negb[0:8], scale=-1.0)
            P.act(smT[32:36, ts_], pb[32:36, :], AF.Exp, bias=prm[32:36, P_DTB:P_DTB + 1], scale=1.0)
            P.act(smT[64:68, ts_], pb[64:68, :], AF.Exp, scale=-1.0)
        proj_fm(wt, 96, 0, small_cons)
        o = O_XT
        cqs = V(o, [8, 3, S], BF16); o += 12 * KB
        vext = V(o, [128, 16, 8, 65], BF16); o += 17 * KB
        qa = [V(o + i * 8 * KB, [128, S], BF16) for i in range(2)]
        ka = [V(o + i * 8 * KB + 4 * KB, [128, S], BF16) for i in range(2)]
        o += 16 * KB
        ytok = V(o, [128, 16, 128], BF16); o += 4 * KB
        ptile = [V(o + i * KB, [128, 512], BF16) for i in range(3)]; o += 3 * KB
        sqh = [V(o + i * KB, [128, 512], BF16) for i in range(2)]; o += 2 * KB
        lnh = V(o, [128, 512], F32); o += 2 * KB
        rsh = V(o, [128, 512], F32); o += 2 * KB
        rcp = V(o, [128, 4], F32); o += 128
        assert o <= 64 * KB
        P.memset("dve", vext[:, :, :, 64:65], 1.0)
        for ct in range(4):
            wt = wload(l, f"foxv{ct}", 1024)
            w3 = wt.rearrange("p (k c) -> p k c", k=NCH)
            for g4 in range(4):
                pb = bank(psb(4, 0))
                for t4 in range(4):
                    tb = g4 * 4 + t4
                    for k in range(NCH):
                        P.mm(pb[:, t4 * 128:(t4 + 1) * 128], hT[:, k, tb * 128:(tb + 1) * 128], w3[:, k, :],
                             start=(k == 0), stop=(k == NCH - 1))
                P.copy("act", vext[:, g4 * 4:(g4 + 1) * 4, 2 * ct:2 * ct + 2, 0:64],
                       pb.rearrange("p (a b c) -> p a b c", a=4, b=2))
        P.act(smT[0:8, :], smT[0:8, :], AF.Ln, bias=onec[0:8], scale=1.0)
        P.act(smT[32:36, :], smT[32:36, :], AF.Ln, bias=onec[32:36], scale=1.0)
        P.act(smT[64:68, :], smT[64:68, :], AF.Ln, bias=onec[64:68], scale=1.0)
        P.act(smT[64:68, :], smT[64:68, :], AF.Exp, scale=-1.0)
        aux = V(O_SCR + 16 * KB, [128, S], F32)
        P.scan(aux[0:8, :], onec[0:8].to_broadcast([8, S]), smT[0:8, :], 0.0, ALU.mult, ALU.subtract)
        cqf = aux[0:8, :]
        P.ts("dve", smT[32:36, :], smT[32:36, :], negA[32:36], None, op0=ALU.mult)
        dump(f"dn_g{l}", smT[32:36, :], [4, S])
        dump(f"dn_beta{l}", smT[64:68, :], [4, S])
        P.scan(aux[32:36, :], onec[32:36].to_broadcast([4, S]), smT[32:36, :], 0.0, ALU.mult, ALU.add)
        a3 = aux[32:36, :].rearrange("p (n c) -> p n c", c=64)
        g3 = smT[32:36, :].rearrange("p (n c) -> p n c", c=64)
        gl = V(O_SCR, [128, 32], F32)
        P.copy("dve", gl[32:36, :], a3[:, :, 63])
        P.copy("dve", g3[:, 0, :], a3[:, 0, :])
        P.tt("dve", g3[:, 1:32, :], a3[:, 1:32, :], gl[32:36, 0:31].unsqueeze(2).to_broadcast([4, 31, 64]),
             ALU.subtract)
        dump(f"cq{l}", cqf, [8, S])
        dump(f"gcum{l}", smT[32:36, :], [4, S])

        cr = V(O_SCR + 8 * KB, [8, S], F32)
        P.copy("dve", cqs[:, 0, :], cqf)
        P.tt("dve", cr, cqf, cqs[:, 0, :], ALU.subtract)
        P.copy("dve", cqs[:, 1, :], cr)
        P.tt("dve", cr, cr, cqs[:, 1, :], ALU.subtract)
        P.copy("dve", cqs[:, 2, :], cr)
        def fox_proj(h):
            wt = wload(l, f"foxqk{h}", 1024)
            qa_h, ka_h = qa[h % 2], ka[h % 2]

            def qk_cons(tt, ts_, pb):
                s_ = sqh[tt % 2]
                P.act(s_, pb, AF.Square)
                p2 = bank(psb(2, 6))
                P.mm(p2, bd_b, s_)
                P.act(lnh, p2, AF.Ln, bias=epsc, scale=1.0 / 64)
                P.act(rsh, lnh, AF.Exp, scale=-0.5)
                P.stt("dve", qa_h[:, ts_], pb, qgs, rsh, ALU.mult, ALU.mult)
            proj_fm(wt, 128, 0, qk_cons, defer=True)
            P.dma("sp", ka_h[0:64, :], qa_h[64:128, :], f"kmov{h % 2}")
            P.memset("pool", ka_h[64:70, :], 1.0)
            P.dma("sp", ka_h[67:70, :], cqs[h:h + 1, :, :], f"augk{h % 2}")
            P.memset("pool", qa_h[64:70, :], -1.0)
            P.dma("sp", qa_h[64:67, :], cqs[h:h + 1, :, :], f"augq{h % 2}")

        fox_proj(0)
        for h in range(8):
            qa_h, ka_h = qa[h % 2], ka[h % 2]
            if h + 1 < 8:
                fox_proj(h + 1)
            tasks = [(qt, kb) for qt in range(4) for kb in range(4 * qt + 4)]
            obanks = {}

            def emit_S(qt, kb):
                n0 = max(kb * 128, qt * 512)
                ncols = (qt + 1) * 512 - n0
                pb = bank(psb(4, 0))
                P.mm(pb[:, 0:ncols], ka_h[0:70, kb * 128:(kb + 1) * 128], qa_h[0:70, n0:n0 + ncols])
                pt = ptile[state["ps"] % 3]
                state["ps"] += 1
                P.act(pt[:, 0:ncols], pb[:, 0:ncols], AF.Exp)
                if kb * 128 >= qt * 512:
                    P.tt("dve", pt[:, 0:128], pt[:, 0:128], mask01_b, ALU.mult)
                return pt, n0

            def emit_PV(qt, kb, pt, n0):
                ob = bank(4 + qt % 2)
                o4 = ob.rearrange("p (a b) -> p a b", a=4)
                for qb in range(n0 // 128, 4 * qt + 4):
                    c0 = qb * 128 - n0
                    P.mm(o4[:, qb - 4 * qt, 0:65], pt[:, c0:c0 + 128], vext[:, kb, h, :],
                         start=(kb == 0 and qb == 4 * qt), stop=(kb == qb))
                if kb == 4 * qt + 3:
                    P.add("dve", lambda e, o4=o4: e.reciprocal(rcp[:, :].unsqueeze(2), o4[:, :, 64:65]),
                          reads=[o4[:, :, 64:65]], writes=[rcp[:, :]])
                    P.tt("dve", ytok[:, 4 * qt:4 * qt + 4, (h % 2) * 64:(h % 2) * 64 + 64], o4[:, :, 0:64],
                         rcp[:, :].unsqueeze(2).to_broadcast([128, 4, 64]), ALU.mult)

            LA = 2
            pend = [emit_S(*tasks[i]) for i in range(LA)]
            for i, (qt, kb) in enumerate(tasks):
                if i + LA < len(tasks):
                    pend.append(emit_S(*tasks[i + LA]))
                emit_PV(qt, kb, *pend.pop(0))
            if h % 2 == 1:
                for half in range(2):
                    tp = bank(6 + half, 128, BF16)
                    for i in range(8):
                        qb = half * 8 + i
                        P.transpose(tp[:, i * 128:(i + 1) * 128], ytok[:, qb, :], ident_b)
                    P.copy("act", yT[:, h // 2, half * 1024:(half + 1) * 1024], tp)
        dump(f"y_fox{l}", yT[:, 0:4, :], [128, 4, S])
        if stop_after == "fox":
            return

        o = O_XT
        cv = V(o, [128, S + 2], F32); o += 8 * KB + 128
        acc = V(o, [128, S], F32); o += 8 * KB
        bsb = V(o, [128, S], F32); o += 8 * KB
        ctmp = [V(o + i * 2 * KB, [128, 512], F32) for i in range(2)]; o += 4 * KB
        P.memset("dve", cv[:, 0:2], 0.0)
        for j in range(4):
            wb_ = wload(l, f"scb{j}", 1024)
            proj_fm(wb_, 128, 0, lambda tt, ts_, pb: P.copy("act", bsb[:, ts_], pb))
            wc_ = wload(l, f"scc{j}", 1024)
            wv_ = wload(l, f"scv{j}", 1024)
            w3c = wc_.rearrange("p (k c) -> p k c", k=NCH)
            w3v = wv_.rearrange("p (k c) -> p k c", k=NCH)
            for tt in range(NTT):
                ts_ = slice(tt * 512, (tt + 1) * 512)
                pc = bank(psb(4, 0))
                for k in range(NCH):
                    P.mm(pc, w3c[:, k, :], hT[:, k, ts_], start=(k == 0), stop=(k == NCH - 1))
                P.copy("act", ctmp[tt % 2], pc)
                pv = bank(psb(4, 0))
                for k in range(NCH):
                    P.mm(pv, w3v[:, k, :], hT[:, k, ts_], start=(k == 0), stop=(k == NCH - 1))
                P.tt("dve", cv[:, 2 + tt * 512:2 + (tt + 1) * 512], pv, ctmp[tt % 2], ALU.mult)
            w0 = prm[:, P_SCW + j * 3:P_SCW + j * 3 + 1]
            w1 = prm[:, P_SCW + j * 3 + 1:P_SCW + j * 3 + 2]
            w2 = prm[:, P_SCW + j * 3 + 2:P_SCW + j * 3 + 3]
            P.act(acc, cv[:, 2:2 + S], AF.Copy, scale=w2)
            P.stt("dve", acc, cv[:, 1:1 + S], w1, acc, ALU.mult, ALU.add)
            P.stt("dve", acc, cv[:, 0:S], w0, acc, ALU.mult, ALU.add)
            P.tt("dve", yT[:, 4 + j, :], acc, bsb, ALU.mult)
        dump(f"y_sc{l}", yT[:, 4:8, :], [128, 4, S])
        if stop_after == "sc":
            return

        gtok = V(O_SCR, [64, 32, 4], F32)
        btok = V(O_SCR + 512, [64, 32, 4], F32)
        for src0, dst in ((32, gtok), (64, btok)):
            pb = bank(psb(2, 6), 64)
            p3 = pb[:, 0:128].rearrange("p (n h) -> p n h", h=4)
            for n in range(32):
                P.transpose(p3[:, n, :], smT[src0:src0 + 4, n * 64:(n + 1) * 64],
                            ident_f[src0:src0 + 4, src0:src0 + 4])
            P.copy("dve", dst, p3)
        o = O_XT
        raw = V(o, [128, S + 3], F32); o += 8 * KB + 128
        cacc = V(o, [128, S], F32); o += 8 * KB
        QT = V(o, [128, S], BF16); o += 4 * KB
        KT = V(o, [128, S], BF16); o += 4 * KB
        VT = V(o, [128, S], BF16); o += 4 * KB
        qdT = V(o, [128, S], BF16); o += 4 * KB
        nkcT = V(o, [128, 32, 64], BF16); o += 4 * KB
        Kg = V(o, [64, 32, 128], BF16); o += 8 * KB
        Kd = V(o, [64, 32, 128], BF16); o += 8 * KB
        Vb = V(o, [64, 32, 128], BF16); o += 8 * KB
        sqp = [V(o + i * KB, [128, 512], BF16) for i in range(2)]; o += 2 * KB
        assert o <= 64 * KB, o
        o = O_SCR + 1 * KB
        cm = [V(o + i * 4 * KB, [64, 32, 64], BF16) for i in range(5)]; o += 20 * KB
        Mm, MTm, PTm, qkT, Mn = cm
        MTn = V(o, [64, 32, 64], BF16); o += 4 * KB
        PTn = V(o, [64, 32, 64], BF16); o += 4 * KB
        assert o <= SCR_END, o
        GB = V(O_XT, [128, S], F32)
        Xm = V(O_XT + 8 * KB + 128, [64, 32, 64], F32)
        sq2 = [V(O_SCR + 29 * KB + i * KB, [128, 512], BF16) for i in range(2)]
        eg = misc[0:64, 8:40]; bgc = misc[0:64, 40:72]; edc = misc[0:64, 72:104]
        gtot = misc[:, 104:136]; nbt = misc[0:64, 136:168]

        def prep_gen(h):
            lnp = V(O_YT + (8 + h) * 4 * KB, [128, 512], F32)
            rsp = V(O_YT + (8 + h) * 4 * KB + 2 * KB, [128, 512], F32)
            for which, nm, dstT, scl in ((0, "dnq", QT, 128.0 ** -0.5), (1, "dnk", KT, 1.0), (2, "dnv", VT, None)):
                wt = wload(l, f"{nm}{h}", 1024)
                P.memset("dve", raw[:, 0:3], 0.0)
                w3 = wt.rearrange("p (k c) -> p k c", k=NCH)
                for tt in range(NTT):
                    ts_ = slice(tt * 512, (tt + 1) * 512)
                    pb = bank(psb(2, 6))
                    for k in range(NCH):
                        P.mm(pb, w3[:, k, :], hT[:, k, ts_], start=(k == 0), stop=(k == NCH - 1))
                    P.copy("act", raw[:, 3 + tt * 512:3 + (tt + 1) * 512], pb)
                    yield
                cw = P_DNW + (which * 4 + h) * 4
                P.act(cacc, raw[:, 3:3 + S], AF.Copy, scale=prm[:, cw + 3:cw + 4])
                yield
                for j in range(3):
                    P.stt("dve", cacc, raw[:, j:j + S], prm[:, cw + j:cw + j + 1], cacc, ALU.mult, ALU.add)
                    yield
                if scl is None:
                    P.act(dstT, cacc, AF.Silu)
                    yield
                else:
                    P.act(cacc, cacc, AF.Silu)
                    yield
                    for tt in range(NTT):
                        ts_ = slice(tt * 512, (tt + 1) * 512)
                        s_ = sqp[tt % 2]
                        P.act(s_, cacc[:, ts_], AF.Square)
                        p2 = bank(psb(2, 6))
                        P.mm(p2, ones_b, s_)
                        yield
                        P.act(lnp, p2, AF.Ln, bias=epsc, scale=1.0)
                        P.act(rsp, lnp, AF.Exp, scale=-0.5)
                        yield
                        P.stt("dve", dstT[:, ts_], cacc[:, ts_], scl, rsp, ALU.mult, ALU.mult)
                        yield

        gen_box = [None]

        def pull(k):
            g = gen_box[0]
            if g is None:
                return
            for _ in range(k):
                try:
                    next(g)
                except StopIteration:
                    gen_box[0] = None
                    return

        def drain():
            while gen_box[0] is not None:
                pull(8)

        gen_box[0] = prep_gen(0)
        drain()
        for h in range(4):
            if h == 0:
                dump(f"dn_q{l}", QT, [128, S]); dump(f"dn_k{l}", KT, [128, S]); dump(f"dn_v{l}", VT, [128, S])
            for tt in range(NTT):
                ts_ = slice(tt * 512, (tt + 1) * 512)
                pb = bank(psb(4, 0))
                P.mm(pb, sel_f[32:36, h * 128:(h + 1) * 128], smT[32:36, ts_])
                P.copy("act", GB[:, ts_], pb)
                egt = V(O_SCR + 25 * KB, [128, 512], F32)
                P.act(egt, pb, AF.Exp)
                P.tt("dve", qdT[:, ts_], QT[:, ts_], egt, ALU.mult)
            GB3 = GB.rearrange("p (n c) -> p n c", c=64)
            P.act(eg, gtok[:, :, h], AF.Exp)
            P.tt("dve", bgc, btok[:, :, h], eg, ALU.mult)
            P.tt("dve", edc, GB3[0:64, :, 63], gtok[:, :, h], ALU.subtract)
            P.act(edc, edc, AF.Exp)
            P.act(gtot, GB3[:, :, 63], AF.Exp)
            P.ts("dve", nbt, btok[:, :, h], -1.0, None, op0=ALU.mult)
            for g8 in range(4):
                for src, outs in ((KT, ((Kg, bgc), (Kd, edc))), (VT, ((Vb, btok[:, :, h]),))):
                    tp = bank(psb(2, 6), 64, BF16)
                    for i in range(8):
                        n = g8 * 8 + i
                        P.transpose(tp[:, i * 128:(i + 1) * 128], src[:, n * 64:(n + 1) * 64], ident_b)
                    t3 = tp.rearrange("p (n d) -> p n d", d=128)
                    for (dst, col) in outs:
                        P.tt("dve", dst[:, g8 * 8:(g8 + 1) * 8, :], t3,
                             col[:, g8 * 8:(g8 + 1) * 8].unsqueeze(2).to_broadcast([64, 8, 128]), ALU.mult)
            P.tt("dve", Xm, GB3[0:64, :, :], gtok[:, :, h].unsqueeze(2).to_broadcast([64, 32, 64]), ALU.subtract)
            DLn = V(O_SCR + 21 * KB, [64, 32, 64], F32)
            P.tt("dve", DLn, Xm, mposA.unsqueeze(1).to_broadcast([64, 32, 64]), ALU.add)
            P.act(DLn, DLn, AF.Exp, scale=-1.0)
            P.tt("dve", DLn, DLn, nbt[:, :].unsqueeze(2).to_broadcast([64, 32, 64]), ALU.mult)
            P.tt("dve", Xm, Xm, mposB.unsqueeze(1).to_broadcast([64, 32, 64]), ALU.subtract)
            P.act(Xm, Xm, AF.Exp)
            for g8 in range(4):
                pk = bank(psb(4, 0), 64)
                pq = bank(psb(4, 0), 64)
                for i in range(8):
                    n = g8 * 8 + i
                    cs = slice(n * 64, (n + 1) * 64)
                    P.mm(pk[:, i * 64:(i + 1) * 64], KT[:, cs], KT[:, cs])
                    P.mm(pq[:, i * 64:(i + 1) * 64], KT[:, cs], QT[:, cs])
                gs = slice(g8 * 8, (g8 + 1) * 8)
                P.tt("dve", Mm[:, gs, :], pk.rearrange("p (n c) -> p n c", c=64), DLn[:, gs, :], ALU.mult)
                P.tt("dve", qkT[:, gs, :], pq.rearrange("p (n c) -> p n c", c=64), Xm[:, gs, :], ALU.mult)
            for g8 in range(4):
                tp = bank(psb(2, 6), 64, BF16)
                for i in range(8):
                    n = g8 * 8 + i
                    P.transpose(tp[:, i * 64:(i + 1) * 64], Mm[:, n, :], ident_b[0:64, 0:64])
                P.copy("act", MTm[:, g8 * 8:(g8 + 1) * 8, :], tp[:, 0:512].rearrange("p (n c) -> p n c", c=64))
            P.tt("dve", PTm, MTm, ident_b[0:64, 0:64].unsqueeze(1).to_broadcast([64, 32, 64]), ALU.add)
            if h < 3:
                gen_box[0] = prep_gen(h + 1)
            Wc, WTc, PTc = Mm, MTm, PTm
            Wn, WTn, PTx = Mn, MTn, PTn
            for it in range(5):
                def stA(g8, it=it, Wc=Wc, WTc=WTc, Wn=Wn, WTn=WTn):
                    pw = bank(psb(6, 0), 64)
                    pwt = bank(psb(6, 0), 64) if it < 4 else None
                    for i in range(8):
                        n = g8 * 8 + i
                        P.mm(pw[:, i * 64:(i + 1) * 64], WTc[:, n, :], Wc[:, n, :])
                        if pwt is not None:
                            P.mm(pwt[:, i * 64:(i + 1) * 64], Wc[:, n, :], WTc[:, n, :])
                    gs = slice(g8 * 8, (g8 + 1) * 8)
                    P.copy("act", Wn[:, gs, :], pw.rearrange("p (n c) -> p n c", c=64))
                    if pwt is not None:
                        P.copy("dve", WTn[:, gs, :], pwt.rearrange("p (n c) -> p n c", c=64))

                def stB(g8, Wn=Wn, PTc=PTc, PTx=PTx):
                    pp = bank(psb(6, 0), 64)
                    for i in range(8):
                        n = g8 * 8 + i
                        P.mm(pp[:, i * 64:(i + 1) * 64], ident_b[0:64, 0:64], PTc[:, n, :], start=True, stop=False)
                        P.mm(pp[:, i * 64:(i + 1) * 64], Wn[:, n, :], PTc[:, n, :], start=False, stop=True)
                    gs = slice(g8 * 8, (g8 + 1) * 8)
                    P.copy("dve", PTx[:, gs, :], pp.rearrange("p (n c) -> p n c", c=64))
                for st_, g_ in ((stA, 0), (stA, 1), (stB, 0), (stA, 2), (stB, 1), (stA, 3), (stB, 2), (stB, 3)):
                    st_(g_)
                    pull(PULL_N)
                Wc, Wn = Wn, Wc
                WTc, WTn = WTn, WTc
                PTc, PTx = PTx, PTc
            TT = PTc
            for g8 in range(4):
                pb = bank(psb(4, 0))
                for i in range(8):
                    n = g8 * 8 + i
                    P.mm(pb[:, i * 64:(i + 1) * 64], Kg[:, n, :], TT[:, n, :])
                P.act(nkcT[:, g8 * 8:(g8 + 1) * 8, :], pb.rearrange("p (n c) -> p n c", c=64), AF.Copy, scale=-1.0)
            o3 = O_SCR + 29 * KB
            Sf = V(o3, [128, 128], F32)
            Sbb = [V(o3 + 512 + i * 256, [128, 128], BF16) for i in range(2)]
            vn = [V(o3 + 1024 + i * 256, [64, 128], BF16) for i in range(2)]
            oT = V(O_SCR + 1 * KB, [128, S], F32)
            P.memset("dve", Sf, 0.0)
            P.memset("dve", Sbb[0], 0.0)
            for n in range(32):
                sb_ = Sbb[n % 2]
                pv = bank(4, 64)[:, (n % 2) * 128:(n % 2) * 128 + 128]
                P.mm(pv, TT[:, n, :], Vb[:, n, :], start=True, stop=False)
                P.mm(pv, nkcT[:, n, :], sb_, start=False, stop=True)
                vn_ = vn[n % 2]
                P.copy("act", vn_, pv)
                if n % 8 == 0:
                    po = bank(psb(2, 2))
                pos = po[:, (n % 8) * 64:(n % 8 + 1) * 64]
                P.mm(pos, sb_, qdT[:, n * 64:(n + 1) * 64], start=True, stop=False)
                P.mm(pos, vn_, qkT[:, n, :], start=False, stop=True)
                pd = bank(5)[:, (n % 2) * 128:(n % 2) * 128 + 128]
                P.mm(pd, Kd[:, n, :], vn_)
                P.stt("dve", Sbb[(n + 1) % 2], Sf, gtot[:, n:n + 1], pd, ALU.mult, ALU.add)
                P.stt("dve", Sf, Sf, gtot[:, n:n + 1], pd, ALU.mult, ALU.add)
                if n % 8 == 7:
                    tt = n // 8
                    P.copy("act", oT[:, tt * 512:(tt + 1) * 512], po)
                pull(PULL_S)
            if h == 0:
                dump(f"o_dn{l}", oT, [128, S])
            drain()
            wz = wload(l, f"dnz{h}", 1024)

            for tt in range(NTT):
                ts_ = slice(tt * 512, (tt + 1) * 512)
                s_ = sq2[tt % 2]
                P.act(s_, oT[:, ts_], AF.Square)
                p2 = bank(psb(2, 6))
                P.mm(p2, ones_b, s_)
                lnt = V(O_SCR + 25 * KB, [128, 512], F32)
                rst = V(O_SCR + 27 * KB, [128, 512], F32)
                P.act(lnt, p2, AF.Ln, bias=epsc, scale=1.0 / 128)
                P.act(rst, lnt, AF.Exp, scale=-0.5)
                P.stt("dve", oT[:, ts_], oT[:, ts_], prm[:, P_DNG:P_DNG + 1], rst, ALU.mult, ALU.mult)

            def z_cons(tt, ts_, pb, h=h):
                zt = V(O_SCR + 25 * KB + (tt % 2) * 2 * KB, [128, 512], F32)
                P.act(zt, pb, AF.Silu)
                P.tt("dve", yT[:, 8 + h, ts_], oT[:, ts_], zt, ALU.mult)
            proj_fm(wz, 128, 0, z_cons)
        dump(f"y_dn{l}", yT[:, 8:12, :], [128, 4, S])
        if stop_after == "dn":
            return

        mT = V(O_SMT, [128, NCH, S], BF16)
        macc = V(O_SMT + 33 * KB, [128, 512], F32)
        mtmp = V(O_SMT + 35 * KB, [128, 512], F32)
        sgt = [[V(O_XT + (b * NTT + tt) * 2 * KB, [128, 512], F32) for tt in range(NTT)] for b in range(3)]
        for dc in range(NCH):
            for b in range(3):
                wgb = wload(l, f"g{b}_{dc}", 1024).rearrange("p (k c) -> p k c", k=NCH)
                for tt in range(NTT):
                    ts_ = slice(tt * 512, (tt + 1) * 512)
                    pg = bank(psb(6, 0))
                    for k in range(NCH):
                        P.mm(pg, wgb[:, k, :], hT[:, k, ts_], start=(k == 0), stop=(k == NCH - 1))
                    P.act(sgt[b][tt], pg, AF.Sigmoid)
            wA3 = wload(l, f"brA{dc}", 1024).rearrange("p (b c m) -> p b c m", b=2, c=4)
            wB3 = wload(l, f"brB{dc}", 512).rearrange("p (c m) -> p c m", c=4)
            for tt in range(NTT):
                ts_ = slice(tt * 512, (tt + 1) * 512)
                for b in range(3):
                    pbr = bank(psb(6, 0))
                    for c in range(4):
                        lw = wA3[:, b, c, :] if b < 2 else wB3[:, c, :]
                        P.mm(pbr, lw, yT[:, 4 * b + c, ts_], start=(c == 0), stop=(c == 3))
                    if b == 0:
                        P.tt("dve", macc, pbr, sgt[b][tt], ALU.mult)
                    elif b == 1:
                        P.tt("dve", mtmp, pbr, sgt[b][tt], ALU.mult)
                        P.tt("dve", macc, macc, mtmp, ALU.add)
                    else:
                        P.tt("dve", mtmp, pbr, sgt[b][tt], ALU.mult)
                        P.tt("dve", mT[:, dc, ts_], macc, mtmp, ALU.add)
        dump(f"merged{l}", mT[:, :, 0:1024], [128, NCH, 1024])
        for dc in range(NCH):
            wo = wload(l, f"wo{dc}", 1024)
            P.dma("sp", xT[:, dc, :], scrx[:, dc, :], f"unsp{dc}")

            def wo_cons(tt, ts_, pb, dc=dc):
                P.tt("dve", xT[:, dc, ts_], xT[:, dc, ts_], pb, ALU.add)
            proj_fm(wo, 128, 0, wo_cons, rhs_of=lambda k, ts_: mT[:, k, ts_], nb=6)
        dump(f"x_mix{l}", xT[:, :, :], [128, NCH, S])
        if stop_after == "mix":
            return

        rmsnorm_to_hT(P_GFFN, O_SCR)
        o = O_YT
        graw = V(o, [128, S + 2], F32); o += 8 * KB + 128
        vraw = V(o, [128, S + 2], F32); o += 8 * KB + 128
        gacs = [V(o, [128, S], F32), V(O_SMT, [128, S], F32)]; o += 8 * KB
        vacs = [V(o, [128, S], F32), V(O_SCR + 20 * KB, [128, S], F32)]; o += 8 * KB
        assert o <= O_WS
        aT0 = V(O_SCR + 0 * KB, [128, GRP, S], BF16)
        aT1 = V(O_YT + 33 * KB, [128, 3, S], BF16)
        aT1b = V(O_SCR + 16 * KB, [128, 1, S], BF16)
        P.memset("dve", graw[:, 0:2], 0.0)
        P.memset("dve", vraw[:, 0:2], 0.0)

        def a_slot(gi, jj):
            if gi % 2 == 0:
                return aT0[:, jj, :]
            return aT1[:, jj, :] if jj < 3 else aT1b[:, 0, :]

        def ffn_up(gi, j0, n):
                for jj in range(n):
                    j = j0 + jj
                    gac = gacs[j % 2]
                    vac = vacs[j % 2]
                    for nm, rawb, accb, c0 in (("upg", graw, gac, j), ("upv", vraw, vac, NFF + j)):
                        wt = wload(l, f"{nm}{j}", 1024)
                        w2c = prm[:, P_FFW + c0 * 3 + 2:P_FFW + c0 * 3 + 3]

                        def up_cons(tt, ts_, pb, rawb=rawb, accb=accb, w2c=w2c):
                            P.copy("act", rawb[:, 2 + tt * 512:2 + (tt + 1) * 512], pb)
                            P.act(accb[:, ts_], pb, AF.Copy, scale=w2c)
                        proj_fm(wt, 128, 0, up_cons, nb=6)
                        P.stt("dve", accb, rawb[:, 1:1 + S], prm[:, P_FFW + c0 * 3 + 1:P_FFW + c0 * 3 + 2], accb,
                              ALU.mult, ALU.add)
                        P.stt("dve", accb, rawb[:, 0:S], prm[:, P_FFW + c0 * 3:P_FFW + c0 * 3 + 1], accb,
                              ALU.mult, ALU.add)
                    P.act(gac, gac, AF.Silu)
                    P.tt("dve", a_slot(gi, jj), gac, vac, ALU.mult)

        def ffn_down(gi, j0, n):
                for q in range(4):
                    wd = wload(l, f"dn{gi}_{q}", n * 256)
                    wd3 = wd.rearrange("p (j c) -> p j c", j=n)
                    for dd in range(2):
                        dc = 2 * q + dd
                        for tt in range(NTT):
                            ts_ = slice(tt * 512, (tt + 1) * 512)
                            pb = bank(psb(6, 0))
                            for jj in range(n):
                                P.mm(pb, wd3[:, jj, dd * 128:(dd + 1) * 128], a_slot(gi, jj)[:, ts_],
                                     start=(jj == 0), stop=(jj == n - 1))
                            P.tt("dve", xT[:, dc, ts_], xT[:, dc, ts_], pb, ALU.add)

        groups = _ffn_groups()
        for gi, (j0, n) in enumerate(groups):
            ffn_up(gi, j0, n)
            if gi > 0:
                ffn_down(gi - 1, *groups[gi - 1])
        ffn_down(len(groups) - 1, *groups[-1])
        dump(f"x_ffn{l}", xT[:, :, :], [128, NCH, S])
        if stop_after == "ffn":
            return

        rmsnorm_to_hT(P_GPLE, O_SCR)
        ptok = V(O_YT, [128, 16, 256], F32)
        pT = V(O_YT + 16 * KB, [128, 2, S], BF16)
        sgp = [V(O_YT + 24 * KB + i * 2 * KB, [128, 512], F32) for i in range(2)]
        ptmp = V(O_YT + 28 * KB, [128, 512], F32)
        for q in range(4):
            P.dma("sp", ptok[:, q * 4:(q + 1) * 4, :],
                  p_in[l, q * 512:(q + 1) * 512, :].rearrange("(a p) c -> p a c", p=128), f"pin{q}")
        for c in range(2):
            for g4 in range(4):
                tp = bank(psb(2, 6))
                for i in range(4):
                    tb = g4 * 4 + i
                    P.transpose(tp[:, i * 128:(i + 1) * 128], ptok[:, tb, c * 128:(c + 1) * 128], ident_f)
                P.copy("act", pT[:, c, g4 * 512:(g4 + 1) * 512], tp)
        for dc in range(NCH):
            wpg = wload(l, f"pg{dc}", 1024)
            wpl = wload(l, f"pl{dc}", 256)
            wpl3 = wpl.rearrange("p (c m) -> p c m", c=2)

            def pg_cons(tt, ts_, pb, dc=dc, wpl3=wpl3):
                s_ = sgp[tt % 2]
                P.act(s_, pb, AF.Sigmoid)
                pp = bank(psb(2, 4))
                for c in range(2):
                    P.mm(pp, wpl3[:, c, :], pT[:, c, ts_], start=(c == 0), stop=(c == 1))
                P.tt("dve", ptmp, pp, s_, ALU.mult)
                P.tt("dve", xT[:, dc, ts_], xT[:, dc, ts_], ptmp, ALU.add)
            proj_fm(wpg, 128, 0, pg_cons, defer=True)
        dump(f"x_out{l}", xT[:, :, :], [128, NCH, S])

    xin = [V(O_YT + i * 4 * KB, [128, D], F32) for i in range(2)]
    for tb in range(16):
        xi = xin[tb % 2]
        P.dma("sp", xi, x_in[tb * 128:(tb + 1) * 128, :], f"xin{tb % 2}")
        for g2 in range(2):
            tp = bank(psb(2, 6))
            for i in range(4):
                c = g2 * 4 + i
                P.transpose(tp[:, i * 128:(i + 1) * 128], xi[:, c * 128:(c + 1) * 128], ident_f)
            eng = "act" if g2 == 0 else "dve"
            P.copy(eng, xT[:, g2 * 4:(g2 + 1) * 4, tb * 128:(tb + 1) * 128],
                   tp.rearrange("p (c t) -> p c t", c=4))

    for l in range(depth):
        layer(l)

    if stop_after is None:
        xo = [V(O_YT + i * 4 * KB, [128, D], F32) for i in range(2)]
        for tb in range(16):
            xo_ = xo[tb % 2]
            for g2 in range(2):
                tp = bank(psb(2, 6))
                for i in range(4):
                    c = g2 * 4 + i
                    P.transpose(tp[:, i * 128:(i + 1) * 128], xT[:, c, tb * 128:(tb + 1) * 128], ident_f)
                eng = "act" if g2 == 0 else "dve"
                P.copy(eng, xo_[:, g2 * 512:(g2 + 1) * 512], tp)
            P.dma("sp", out[tb * 128:(tb + 1) * 128, :], xo_, f"xout{tb % 2}", final=True)
    else:
        z = V(O_SCR, [128, 8], F32)
        P.memset("pool", z, 0.0)
        P.dma("sp", out[0:128, 0:8], z, "xout0", final=True)
    P.emit()
    return nc, dbg_outs, P


_CACHE = {}


def kernel(**inputs):
    inp = {k: np.asarray(v) for k, v in inputs.items()}
    if "nc" not in _CACHE:
        _CACHE["nc"] = build()[0]
    nc = _CACHE["nc"]
    wstream = np.stack([pack_weights(inp, l) for l in range(DEPTH)])
    prm = np.stack([pack_params(inp, l) for l in range(DEPTH)])
    cst = make_consts()
    in_maps = []
    for b in range(8):
        in_maps.append({"x": np.ascontiguousarray(inp["x"][b]),
                        "p": np.ascontiguousarray(inp["p"][:, b]),
                        "wst": wstream, "prm": prm, "cst": cst})
    res = run_bass_kernel_spmd(nc, in_maps, core_ids=list(range(8)))
    return np.stack([np.asarray(r["out"]) for r in res.results]).astype(np.float32)
```

```python
import numpy as np
import concourse.bass as bass
import concourse.mybir as mybir

F32 = mybir.dt.float32
BF16 = mybir.dt.bfloat16
ALU = mybir.AluOpType
AF = mybir.ActivationFunctionType

ENGS = ("pe", "act", "dve", "pool", "sp")


def _prod(xs):
    r = 1
    for v in xs:
        r *= int(v)
    return r


def region(ap):
    t = ap.tensor
    es = mybir.dt.size(ap.dtype)
    off = int(ap.offset)
    pat = ap.ap
    if str(ap.space) == "DRAM":
        ext = 0
        for st, cnt in pat:
            ext += (cnt - 1) * abs(st)
        return (t.name, 0, 1, off * es, (off + ext + 1) * es)
    shape = list(t.shape)
    F = _prod(shape[1:])
    p0 = off // F
    f0 = off % F
    pstep, pcnt = pat[0]
    npart = pcnt if pstep != 0 else 1
    ext = 0
    for st, cnt in pat[1:]:
        ext += (cnt - 1) * abs(st)
    return (t.name, p0, p0 + npart, f0 * es, (f0 + ext + 1) * es)


def _untracked(ap):
    return str(ap.space) == "DRAM" and not ap.tensor.name.startswith("scr_")


def _overlap(a, b):
    return a[1] < b[2] and b[1] < a[2] and a[3] < b[4] and b[3] < a[4]


def _covers(a, b):
    return a[1] <= b[1] and a[2] >= b[2] and a[3] <= b[3] and a[4] >= b[4]


class Instr:
    __slots__ = ("eng", "fn", "deps", "signal", "is_dma", "key", "val", "idx")

    def __init__(self, eng, fn, is_dma, key):
        self.eng = eng
        self.fn = fn
        self.deps = set()
        self.signal = False
        self.is_dma = is_dma
        self.key = key
        self.val = 0


class Prog:
    def __init__(self, nc):
        self.nc = nc
        self.instrs = []
        self.wr = {}
        self.rd = {}
        self.final_keys = set()
        self.last_dma = {}

    def add(self, eng, fn, reads=(), writes=(), dma_key=None):
        ins = Instr(eng, fn, dma_key is not None, dma_key if dma_key is not None else eng)
        idx = len(self.instrs)
        ins.idx = idx
        self.instrs.append(ins)
        deps = ins.deps
        for ap in reads:
            if ap is None or isinstance(ap, (int, float)):
                continue
            if _untracked(ap):
                continue
            r = region(ap)
            for (w, wi) in self.wr.get(r[0], ()):
                if _overlap(w, r):
                    deps.add(wi)
        for ap in writes:
            if _untracked(ap):
                continue
            r = region(ap)
            wl = self.wr.setdefault(r[0], [])
            rl = self.rd.setdefault(r[0], [])
            for (w, wi) in wl:
                if _overlap(w, r):
                    deps.add(wi)
            for (q, qi) in rl:
                if _overlap(q, r):
                    deps.add(qi)
            self.wr[r[0]] = [(w, wi) for (w, wi) in wl if not _covers(r, w)]
            self.rd[r[0]] = [(q, qi) for (q, qi) in rl if not _covers(r, q)]
            self.wr[r[0]].append((r, idx))
        for ap in reads:
            if ap is None or isinstance(ap, (int, float)):
                continue
            if _untracked(ap):
                continue
            r = region(ap)
            rl = self.rd.setdefault(r[0], [])
            if dma_key is None:
                rl[:] = [(q, qi) for (q, qi) in rl
                         if not (q == r and self.instrs[qi].eng == eng and not self.instrs[qi].is_dma)]
            rl.append((r, idx))
        if dma_key is not None:
            if dma_key in self.last_dma:
                deps.add(self.last_dma[dma_key])
            self.last_dma[dma_key] = idx
        deps.discard(idx)
        return ins

    def mm(self, out, lhsT, rhs, start=True, stop=True):
        rd = [lhsT, rhs]
        return self.add("pe", lambda e: e.matmul(out, lhsT, rhs, start=start, stop=stop),
                        reads=rd, writes=[out])

    def transpose(self, out, in_, ident):
        return self.add("pe", lambda e: e.transpose(out, in_, ident),
                        reads=[in_, ident], writes=[out])

    def act(self, out, in_, func, bias=None, scale=1.0, accum_out=None, eng="act"):
        kw = {}
        if bias is not None:
            kw["bias"] = bias
        if accum_out is not None:
            kw["accum_out"] = accum_out
        rd = [in_, bias if not isinstance(bias, (int, float)) else None,
              scale if not isinstance(scale, (int, float)) else None]
        wr = [out] + ([accum_out] if accum_out is not None else [])
        return self.add(eng, lambda e: e.activation(out, in_, func, scale=scale, **kw),
                        reads=rd, writes=wr)

    def tt(self, eng, out, in0, in1, op):
        return self.add(eng, lambda e: e.tensor_tensor(out, in0, in1, op),
                        reads=[in0, in1], writes=[out])

    def ts(self, eng, out, in0, s1, s2=None, op0=ALU.mult, op1=None, accum_out=None):
        kw = {}
        if op1 is not None:
            kw["op1"] = op1
        if accum_out is not None:
            kw["accum_out"] = accum_out
        rd = [in0, s1 if not isinstance(s1, (int, float)) else None,
              s2 if not isinstance(s2, (int, float)) else None]
        wr = [out] + ([accum_out] if accum_out is not None else [])
        return self.add(eng, lambda e: e.tensor_scalar(out, in0, s1, s2, op0, **kw),
                        reads=rd, writes=wr)

    def stt(self, eng, out, in0, scalar, in1, op0, op1):
        rd = [in0, in1, scalar if not isinstance(scalar, (int, float)) else None]
        return self.add(eng, lambda e: e.scalar_tensor_tensor(out, in0, scalar, in1, op0, op1),
                        reads=rd, writes=[out])

    def copy(self, eng, out, in_):
        if eng == "act":
            return self.add(eng, lambda e: e.copy(out, in_), reads=[in_], writes=[out])
        return self.add(eng, lambda e: e.tensor_copy(out, in_), reads=[in_], writes=[out])

    def memset(self, eng, out, val):
        return self.add(eng, lambda e: e.memset(out, val), reads=[], writes=[out])

    def scan(self, out, d0, d1, initial, op0, op1):
        rd = [d0, d1, initial if not isinstance(initial, (int, float)) else None]
        return self.add("dve", lambda e: e.tensor_tensor_scan(out, d0, d1, initial, op0, op1),
                        reads=rd, writes=[out])

    def dma(self, eng, out, in_, key, final=False):
        if final:
            self.final_keys.add(key)
        return self.add(eng, lambda e: e.dma_start(out, in_), reads=[in_], writes=[out], dma_key=key)

    def emit(self):
        nc = self.nc
        instrs = self.instrs
        for ins in instrs:
            if ins.eng == "pe" and not ins.is_dma:
                ins.deps = {d for d in ins.deps if not (instrs[d].eng == "pe" and not instrs[d].is_dma)}
            for d in ins.deps:
                instrs[d].signal = True
        keys = []
        for ins in instrs:
            if ins.key not in keys:
                keys.append(ins.key)
        for ins in instrs:
            if ins.is_dma:
                ins.signal = True
        cnt = {k: 0 for k in keys}
        for ins in instrs:
            if ins.signal:
                cnt[ins.key] += 16 if ins.is_dma else 1
            ins.val = cnt[ins.key]
        used = [k for k in keys if cnt[k] > 0]
        self.sem_totals = {k: cnt[k] for k in used}
        import contextlib
        with contextlib.ExitStack() as st:
            sems = {k: st.enter_context(nc.semaphore("s_" + str(k))) for k in used}
            block = st.enter_context(nc.Block())
            per_eng = {e: [i for i in instrs if i.eng == e] for e in ENGS}
            nwaits = [0]

            def run(engobj, lst, is_last_sp=False):
                clock = {}
                for ins in lst:
                    need = {}
                    for d in ins.deps:
                        di = instrs[d]
                        if di.val > need.get(di.key, 0):
                            need[di.key] = di.val
                    for k, v in need.items():
                        if clock.get(k, 0) < v:
                            engobj.wait_ge(sems[k], v)
                            clock[k] = v
                            nwaits[0] += 1
                    bi = ins.fn(engobj)
                    if ins.signal:
                        bi.then_inc(sems[ins.key], 16 if ins.is_dma else 1)
                if is_last_sp:
                    for k in sorted(self.final_keys, key=str):
                        if k in sems:
                            engobj.wait_ge(sems[k], cnt[k])

            @block.tensor
            def _(e):
                run(e, per_eng["pe"])

            @block.scalar
            def _(e):
                run(e, per_eng["act"])

            @block.vector
            def _(e):
                run(e, per_eng["dve"])

            @block.gpsimd
            def _(e):
                run(e, per_eng["pool"])

            @block.sync
            def _(e):
                run(e, per_eng["sp"], True)
            self.nwaits = nwaits[0]

from concourse.bass_utils import run_bass_kernel_spmd
import ml_dtypes

S = 2048
D = 1024
DEPTH = 2
NCH = D // 128
NTT = S // 512
DFF = 2816
NFF = DFF // 128
EPS = 1e-6
GRP = 4
import os as _os
PULL_N = int(_os.environ.get('PULL_N', '0'))
PULL_S = int(_os.environ.get('PULL_S', '1'))
FFN_CAST = _os.environ.get('FFN_CAST', 'act')
DENSE_CAST = _os.environ.get('DENSE_CAST', 'act')
NPRM = 224
NCST = 1536

C_FQ, C_FK, C_FV, C_FF = 0, 512, 1024, 1536
C_SB, C_SC, C_SV = 1544, 2056, 2568
C_DQ, C_DK, C_DV = 3080, 3592, 4104
C_DB, C_DA, C_DZ, C_G = 4616, 4620, 4624, 5136

P_GMIX, P_GFFN, P_GPLE = 0, 8, 16
P_FQG, P_FKG, P_BF, P_ALOG, P_DTB, P_DNG = 24, 25, 26, 27, 28, 29
P_SCW, P_DNW, P_FFW = 30, 42, 90

K_ID, K_M01, K_ONE, K_SEL, K_MPA, K_MPB = 0, 128, 384, 512, 1024, 1088


def _ffn_groups():
    gs = []
    j = 0
    while j < NFF:
        n = min(GRP, NFF - j)
        gs.append((j, n))
        j += n
    return gs


def wtile_index():
    idx = {}
    names = ["small"]
    names += [f"foxv{c}" for c in range(4)]
    names += [f"foxqk{h}" for h in range(8)]
    for j in range(4):
        names += [f"scb{j}", f"scc{j}", f"scv{j}"]
    for h in range(4):
        names += [f"dnq{h}", f"dnk{h}", f"dnv{h}", f"dnz{h}"]
    for dc in range(8):
        names += [f"g0_{dc}", f"g1_{dc}", f"g2_{dc}", f"brA{dc}", f"brB{dc}"]
    names += [f"wo{dc}" for dc in range(8)]
    for j in range(NFF):
        names += [f"upg{j}", f"upv{j}"]
    for gi, (j0, n) in enumerate(_ffn_groups()):
        names += [f"dn{gi}_{q}" for q in range(4)]
    for dc in range(8):
        names += [f"pg{dc}", f"pl{dc}"]
    for i, n in enumerate(names):
        idx[n] = i
    return idx


WIDX = wtile_index()
NT = len(WIDX)


def pack_weights(inp, l):
    W = np.zeros((NT, 128, 1024), np.float32)
    w_in = inp["w_in"][l]

    def kc(cols):
        n = cols.shape[1]
        return cols.reshape(8, 128, n).transpose(1, 0, 2).reshape(128, 8 * n)

    def put(name, arr):
        W[WIDX[name], :, :arr.shape[1]] = arr

    sm = np.zeros((1024, 96), np.float32)
    sm[:, 0:8] = w_in[:, C_FF:C_FF + 8]
    sm[:, 32:36] = w_in[:, C_DA:C_DA + 4]
    sm[:, 64:68] = w_in[:, C_DB:C_DB + 4]
    put("small", kc(sm))
    for c in range(4):
        put(f"foxv{c}", kc(w_in[:, C_FV + c * 128:C_FV + (c + 1) * 128]))
    for h in range(8):
        qk = np.concatenate([w_in[:, C_FQ + h * 64:C_FQ + (h + 1) * 64],
                             w_in[:, C_FK + h * 64:C_FK + (h + 1) * 64]], axis=1)
        put(f"foxqk{h}", kc(qk))
    for j in range(4):
        put(f"scb{j}", kc(w_in[:, C_SB + j * 128:C_SB + (j + 1) * 128]))
        put(f"scc{j}", kc(w_in[:, C_SC + j * 128:C_SC + (j + 1) * 128]))
        put(f"scv{j}", kc(w_in[:, C_SV + j * 128:C_SV + (j + 1) * 128]))
    for h in range(4):
        put(f"dnq{h}", kc(w_in[:, C_DQ + h * 128:C_DQ + (h + 1) * 128]))
        put(f"dnk{h}", kc(w_in[:, C_DK + h * 128:C_DK + (h + 1) * 128]))
        put(f"dnv{h}", kc(w_in[:, C_DV + h * 128:C_DV + (h + 1) * 128]))
        put(f"dnz{h}", kc(w_in[:, C_DZ + h * 128:C_DZ + (h + 1) * 128]))
    wb = inp["w_branch"][l]
    for dc in range(8):
        for b in range(3):
            put(f"g{b}_{dc}", kc(w_in[:, C_G + b * 1024 + dc * 128:C_G + b * 1024 + (dc + 1) * 128]))

        def br(b):
            a = wb[b][:, dc * 128:(dc + 1) * 128]
            return a.reshape(4, 128, 128).transpose(1, 0, 2).reshape(128, 512)
        put(f"brA{dc}", np.concatenate([br(0), br(1)], axis=1))
        put(f"brB{dc}", br(2))
        put(f"wo{dc}", kc(inp["w_o"][l][:, dc * 128:(dc + 1) * 128]))
        put(f"pg{dc}", kc(inp["w_ple_gate"][l][:, dc * 128:(dc + 1) * 128]))
        a = inp["w_ple"][l][:, dc * 128:(dc + 1) * 128]
        put(f"pl{dc}", a.reshape(2, 128, 128).transpose(1, 0, 2).reshape(128, 256))
    w_up = inp["w_up"][l]
    for j in range(NFF):
        put(f"upg{j}", kc(w_up[:, j * 128:(j + 1) * 128]))
        put(f"upv{j}", kc(w_up[:, DFF + j * 128:DFF + (j + 1) * 128]))
    w_dn = inp["w_down"][l]
    for gi, (j0, n) in enumerate(_ffn_groups()):
        for q in range(4):
            a = w_dn[j0 * 128:(j0 + n) * 128, q * 256:(q + 1) * 256]
            put(f"dn{gi}_{q}", a.reshape(n, 128, 256).transpose(1, 0, 2).reshape(128, n * 256))
    return W


def pack_params(inp, l):
    Pm = np.zeros((128, NPRM), np.float32)
    Pm[:, P_GMIX:P_GMIX + 8] = inp["g_mix"][l].reshape(8, 128).T
    Pm[:, P_GFFN:P_GFFN + 8] = inp["g_ffn"][l].reshape(8, 128).T
    Pm[:, P_GPLE:P_GPLE + 8] = inp["g_ple"][l].reshape(8, 128).T
    Pm[0:64, P_FQG] = inp["fox_q_gain"][l]
    Pm[64:128, P_FQG] = inp["fox_k_gain"][l]
    Pm[0:64, P_FKG] = inp["fox_k_gain"][l]
    Pm[0:8, P_BF] = inp["b_fox_f"][l]
    Pm[32:36, P_ALOG] = inp["dn_a_log"][l]
    Pm[32:36, P_DTB] = inp["dn_dt_bias"][l]
    Pm[:, P_DNG] = inp["dn_norm_gain"][l]
    Pm[:, P_SCW:P_SCW + 12] = inp["sc_conv_w"][l].reshape(3, 4, 128).transpose(2, 1, 0).reshape(128, 12)
    Pm[:, P_DNW:P_DNW + 48] = inp["dn_conv_w"][l].reshape(4, 12, 128).transpose(2, 1, 0).reshape(128, 48)
    Pm[:, P_FFW:P_FFW + 132] = inp["ffn_conv_w"][l].reshape(3, 44, 128).transpose(2, 1, 0).reshape(128, 132)
    return Pm


def make_consts():
    C = np.zeros((128, NCST), np.float32)
    r = np.arange(128)[:, None]
    c = np.arange(128)[None, :]
    C[:, K_ID:K_ID + 128] = (r == c)
    C[:, K_M01:K_M01 + 128] = (c >= r)
    C[:, K_M01 + 128:K_M01 + 256] = ((r // 64) == (c // 64))
    C[:, K_ONE:K_ONE + 128] = 1.0
    for h in range(4):
        C[32 + h, K_SEL + h * 128:K_SEL + (h + 1) * 128] = 1.0
    r6 = np.arange(64)[:, None]
    c6 = np.arange(64)[None, :]
    C[0:64, K_MPA:K_MPA + 64] = np.where(r6 > c6, 0.0, 1.0e4)
    C[0:64, K_MPB:K_MPB + 64] = np.where(c6 >= r6, 0.0, 1.0e4)
    return C


def build(depth=DEPTH, dbg=None, stop_after=None):
    nc = bass.Bass("TRN2", target_bir_lowering=False)
    x_in = nc.dram_tensor("x", [S, D], F32, kind="ExternalInput").ap()
    p_in = nc.dram_tensor("p", [DEPTH, S, 256], F32, kind="ExternalInput").ap()
    wst = nc.dram_tensor("wst", [DEPTH, NT, 128, 1024], F32, kind="ExternalInput").ap()
    prm_in = nc.dram_tensor("prm", [DEPTH, 128, NPRM], F32, kind="ExternalInput").ap()
    cst_in = nc.dram_tensor("cst", [128, NCST], F32, kind="ExternalInput").ap()
    out = nc.dram_tensor("out", [S, D], F32, kind="ExternalOutput").ap()
    scrx = nc.dram_tensor("scr_x", [128, NCH, S], F32).ap()
    dbg_outs = {}

    TOT = 211456
    SB = nc.alloc_sbuf_tensor("SB", [128, TOT // 4], F32)
    PS = nc.alloc_psum_tensor("PS", [128, 8 * 512], F32)
    P = Prog(nc)

    def V(off, shape, dt=F32, p0=0):
        es = mybir.dt.size(dt)
        n = _prod(shape[1:])
        assert off % 4 == 0 and off + n * es <= TOT, (off, shape)
        nw = (n * es + 3) // 4
        v = SB[p0:p0 + shape[0], off // 4: off // 4 + nw]
        if dt != F32:
            v = v.bitcast(dt)
        v = v[:, 0:n]
        if len(shape) == 3:
            v = v.rearrange("p (a b) -> p a b", a=shape[1])
        elif len(shape) == 4:
            v = v.rearrange("p (a b c) -> p a b c", a=shape[1], b=shape[2])
        return v

    def bank(b, parts=128, dt=F32, p0=0):
        v = PS[p0:p0 + parts, b * 512:(b + 1) * 512]
        if dt != F32:
            v = v.bitcast(dt)
        return v

    KB = 1024
    O_XT = 0
    O_HT = 64 * KB
    O_YT = 96 * KB
    O_WS = 144 * KB
    O_WB = 152 * KB
    O_CST = 158 * KB
    O_CBF = 164 * KB
    O_PRM = 165 * KB
    O_MISC = 166 * KB
    O_SMT = 167 * KB
    O_SCR = 175 * KB
    SCR_END = TOT

    xT = V(O_XT, [128, NCH, S], F32)
    hT = V(O_HT, [128, NCH, S], BF16)
    yT = V(O_YT, [128, 12, S], BF16)
    wstage = [V(O_WS + i * 4 * KB, [128, 1024], F32) for i in range(2)]
    wbf = [V(O_WB + i * 2 * KB, [128, 1024], BF16) for i in range(3)]
    cst = V(O_CST, [128, NCST], F32)
    cbf = V(O_CBF, [128, 512], BF16)
    prm = V(O_PRM, [128, NPRM], F32)
    misc = V(O_MISC, [128, 256], F32)
    smT = V(O_SMT, [128, S], F32)

    ident_f = cst[:, K_ID:K_ID + 128]
    ident_b = cbf[:, 0:128]
    mask01_b = cbf[:, 128:256]
    ones_b = cbf[:, 384:512]
    bd_b = cbf[:, 256:384]
    sel_f = cst[:, K_SEL:K_SEL + 512]
    mposA = cst[0:64, K_MPA:K_MPA + 64]
    mposB = cst[0:64, K_MPB:K_MPB + 64]
    epsc = misc[:, 0:1]
    onec = misc[:, 1:2]
    qgs = misc[:, 2:3]
    negb = misc[:, 3:4]
    negA = misc[:, 4:5]

    state = {"ws": 0, "wb": 0, "ps": 0}

    def wload(l, name, X, cast_eng="pool"):
        si = state["ws"] % 2
        bi = state["wb"] % 3
        state["ws"] += 1
        state["wb"] += 1
        P.dma("sp", wstage[si][:, 0:X], wst[l, WIDX[name], :, 0:X], f"ws{si}")
        P.copy(cast_eng, wbf[bi][:, 0:X], wstage[si][:, 0:X])
        return wbf[bi][:, 0:X]

    class WStream:
        def __init__(self, l, items, eng):
            self.l, self.items, self.eng, self.i, self.nxt = l, items, eng, 0, None

        def get(self, name):
            nm, X = self.items[self.i]
            assert nm == name, (nm, name)
            cur = self.nxt if self.nxt is not None else wload(self.l, nm, X, self.eng)
            self.i += 1
            self.nxt = None
            if self.i < len(self.items):
                n2, X2 = self.items[self.i]
                self.nxt = wload(self.l, n2, X2, self.eng)
            return cur

    def psb(n=4, base=0):
        k = ("ps", base, n)
        state[k] = state.get(k, 0) + 1
        return base + (state[k] - 1) % n

    def dump(name, ap, shape):
        if dbg is None or name not in dbg:
            return
        t = nc.dram_tensor("dbg_" + name, list(shape), ap.dtype, kind="ExternalOutput").ap()
        dbg_outs[name] = t
        P.dma("sp", t, ap, "dbgout", final=True)

    P.dma("sp", cst[:], cst_in, "cstin")
    P.copy("dve", cbf[:], cst[:, 0:512])
    P.memset("dve", epsc, EPS)
    P.memset("dve", onec, 1.0)

    def rmsnorm_to_hT(gcol0, o_scr):
        sq = [V(o_scr + i * KB, [128, 512], BF16) for i in range(2)]
        lnt = V(o_scr + 2 * KB, [128, 512], F32)
        rstd = V(o_scr + 4 * KB, [128, 512], F32)
        for tt in range(NTT):
            ts_ = slice(tt * 512, (tt + 1) * 512)
            pb = bank(psb(2, 6))
            for c in range(NCH):
                s_ = sq[c % 2]
                P.act(s_, xT[:, c, ts_], AF.Square)
                P.mm(pb, ones_b, s_, start=(c == 0), stop=(c == NCH - 1))
            P.act(lnt, pb, AF.Ln, bias=epsc, scale=1.0 / D)
            P.act(rstd, lnt, AF.Exp, scale=-0.5)
            for c in range(NCH):
                P.stt("dve", hT[:, c, ts_], xT[:, c, ts_], prm[:, gcol0 + c:gcol0 + c + 1], rstd,
                      ALU.mult, ALU.mult)

    def proj_fm(wt, M, m0, consumer, kchunks=NCH, rhs_of=None, tts=range(NTT), defer=False, nb=4):
        w3 = wt.rearrange("p (k c) -> p k c", k=kchunks)
        pend = None
        for tt in tts:
            ts_ = slice(tt * 512, (tt + 1) * 512)
            pb = bank(psb(nb, 0), M)
            for k in range(kchunks):
                rhs = hT[:, k, ts_] if rhs_of is None else rhs_of(k, ts_)
                P.mm(pb, w3[:, k, m0:m0 + M], rhs, start=(k == 0), stop=(k == kchunks - 1))
            if defer:
                if pend is not None:
                    consumer(*pend)
                pend = (tt, ts_, pb)
            else:
                consumer(tt, ts_, pb)
        if pend is not None:
            consumer(*pend)

    def layer(l):
        P.dma("sp", prm[:], prm_in[l], "prmin")
        P.ts("dve", qgs[0:64], prm[0:64, P_FQG:P_FQG + 1], 0.125, None, op0=ALU.mult)
        P.copy("dve", qgs[64:128], prm[64:128, P_FQG:P_FQG + 1])
        P.ts("dve", negb[0:8], prm[0:8, P_BF:P_BF + 1], -1.0, None, op0=ALU.mult)
        P.act(negA[32:36], prm[32:36, P_ALOG:P_ALOG + 1], AF.Exp)
        P.ts("dve", negA[32:36], negA[32:36], -1.0, None, op0=ALU.mult)

        rmsnorm_to_hT(P_GMIX, O_SCR)
        dump(f"h{l}", hT[:, :, :], [128, NCH, S])
        for c in range(NCH):
            P.dma("sp", scrx[:, c, :], xT[:, c, :], f"spill{c}")
        if stop_after == "norm1":
            return

        wt = wload(l, "small", 8 * 96)

        def small_cons(tt, ts_, pb):
            P.act(smT[0:8, ts_], pb[0:8, :], AF.Exp, bias=negb[0:8], scale=-1.0)
            P.act(smT[32:36, ts_], pb[32:36, :], AF.Exp, bias=prm[32:36, P_DTB:P_DTB + 1], scale=1.0)
            P.act(smT[64:68, ts_], pb[64:68, :], AF.Exp, scale=-1.0)
        proj_fm(wt, 96, 0, small_cons)
        o = O_XT
        cqs = V(o, [8, 3, S], BF16); o += 12 * KB
        vext = V(o, [128, 16, 8, 65], BF16); o += 17 * KB
        qa = [V(o + i * 8 * KB, [128, S], BF16) for i in range(2)]
        ka = [V(o + i * 8 * KB + 4 * KB, [128, S], BF16) for i in range(2)]
        o += 16 * KB
        ytok = V(o, [128, 16, 128], BF16); o += 4 * KB
        ptile = [V(o + i * KB, [128, 512], BF16) for i in range(3)]; o += 3 * KB
        sqh = [V(o + i * KB, [128, 512], BF16) for i in range(2)]; o += 2 * KB
        lnh = V(o, [128, 512], F32); o += 2 * KB
        rsh = V(o, [128, 512], F32); o += 2 * KB
        rcp = V(o, [128, 4], F32); o += 128
        assert o <= 64 * KB
        P.memset("dve", vext[:, :, :, 64:65], 1.0)
        for ct in range(4):
            wt = wload(l, f"foxv{ct}", 1024)
            w3 = wt.rearrange("p (k c) -> p k c", k=NCH)
            for g4 in range(4):
                pb = bank(psb(4, 0))
                for t4 in range(4):
                    tb = g4 * 4 + t4
                    for k in range(NCH):
                        P.mm(pb[:, t4 * 128:(t4 + 1) * 128], hT[:, k, tb * 128:(tb + 1) * 128], w3[:, k, :],
                             start=(k == 0), stop=(k == NCH - 1))
                P.copy("act", vext[:, g4 * 4:(g4 + 1) * 4, 2 * ct:2 * ct + 2, 0:64],
                       pb.rearrange("p (a b c) -> p a b c", a=4, b=2))
        P.act(smT[0:8, :], smT[0:8, :], AF.Ln, bias=onec[0:8], scale=1.0)
        P.act(smT[32:36, :], smT[32:36, :], AF.Ln, bias=onec[32:36], scale=1.0)
        P.act(smT[64:68, :], smT[64:68, :], AF.Ln, bias=onec[64:68], scale=1.0)
        P.act(smT[64:68, :], smT[64:68, :], AF.Exp, scale=-1.0)
        aux = V(O_SCR + 16 * KB, [128, S], F32)
        P.scan(aux[0:8, :], onec[0:8].to_broadcast([8, S]), smT[0:8, :], 0.0, ALU.mult, ALU.subtract)
        cqf = aux[0:8, :]
        P.ts("dve", smT[32:36, :], smT[32:36, :], negA[32:36], None, op0=ALU.mult)
        dump(f"dn_g{l}", smT[32:36, :], [4, S])
        dump(f"dn_beta{l}", smT[64:68, :], [4, S])
        P.scan(aux[32:36, :], onec[32:36].to_broadcast([4, S]), smT[32:36, :], 0.0, ALU.mult, ALU.add)
        a3 = aux[32:36, :].rearrange("p (n c) -> p n c", c=64)
        g3 = smT[32:36, :].rearrange("p (n c) -> p n c", c=64)
        gl = V(O_SCR, [128, 32], F32)
        P.copy("dve", gl[32:36, :], a3[:, :, 63])
        P.copy("dve", g3[:, 0, :], a3[:, 0, :])
        P.tt("dve", g3[:, 1:32, :], a3[:, 1:32, :], gl[32:36, 0:31].unsqueeze(2).to_broadcast([4, 31, 64]),
             ALU.subtract)
        dump(f"cq{l}", cqf, [8, S])
        dump(f"gcum{l}", smT[32:36, :], [4, S])

        cr = V(O_SCR + 8 * KB, [8, S], F32)
        P.copy("dve", cqs[:, 0, :], cqf)
        P.tt("dve", cr, cqf, cqs[:, 0, :], ALU.subtract)
        P.copy("dve", cqs[:, 1, :], cr)
        P.tt("dve", cr, cr, cqs[:, 1, :], ALU.subtract)
        P.copy("dve", cqs[:, 2, :], cr)
        def fox_proj(h):
            wt = wload(l, f"foxqk{h}", 1024)
            qa_h, ka_h = qa[h % 2], ka[h % 2]

            def qk_cons(tt, ts_, pb):
                s_ = sqh[tt % 2]
                P.act(s_, pb, AF.Square)
                p2 = bank(psb(2, 6))
                P.mm(p2, bd_b, s_)
                P.act(lnh, p2, AF.Ln, bias=epsc, scale=1.0 / 64)
                P.act(rsh, lnh, AF.Exp, scale=-0.5)
                P.stt("dve", qa_h[:, ts_], pb, qgs, rsh, ALU.mult, ALU.mult)
            proj_fm(wt, 128, 0, qk_cons, defer=True)
            P.dma("sp", ka_h[0:64, :], qa_h[64:128, :], f"kmov{h % 2}")
            P.memset("pool", ka_h[64:70, :], 1.0)
            P.dma("sp", ka_h[67:70, :], cqs[h:h + 1, :, :], f"augk{h % 2}")
            P.memset("pool", qa_h[64:70, :], -1.0)
            P.dma("sp", qa_h[64:67, :], cqs[h:h + 1, :, :], f"augq{h % 2}")

        fox_proj(0)
        for h in range(8):
            qa_h, ka_h = qa[h % 2], ka[h % 2]
            if h + 1 < 8:
                fox_proj(h + 1)
            tasks = [(qt, kb) for qt in range(4) for kb in range(4 * qt + 4)]
            obanks = {}

            def emit_S(qt, kb):
                n0 = max(kb * 128, qt * 512)
                ncols = (qt + 1) * 512 - n0
                pb = bank(psb(4, 0))
                P.mm(pb[:, 0:ncols], ka_h[0:70, kb * 128:(kb + 1) * 128], qa_h[0:70, n0:n0 + ncols])
                pt = ptile[state["ps"] % 3]
                state["ps"] += 1
                P.act(pt[:, 0:ncols], pb[:, 0:ncols], AF.Exp)
                if kb * 128 >= qt * 512:
                    P.tt("dve", pt[:, 0:128], pt[:, 0:128], mask01_b, ALU.mult)
                return pt, n0

            def emit_PV(qt, kb, pt, n0):
                ob = bank(4 + qt % 2)
                o4 = ob.rearrange("p (a b) -> p a b", a=4)
                for qb in range(n0 // 128, 4 * qt + 4):
                    c0 = qb * 128 - n0
                    P.mm(o4[:, qb - 4 * qt, 0:65], pt[:, c0:c0 + 128], vext[:, kb, h, :],
                         start=(kb == 0 and qb == 4 * qt), stop=(kb == qb))
                if kb == 4 * qt + 3:
                    P.add("dve", lambda e, o4=o4: e.reciprocal(rcp[:, :].unsqueeze(2), o4[:, :, 64:65]),
                          reads=[o4[:, :, 64:65]], writes=[rcp[:, :]])
                    P.tt("dve", ytok[:, 4 * qt:4 * qt + 4, (h % 2) * 64:(h % 2) * 64 + 64], o4[:, :, 0:64],
                         rcp[:, :].unsqueeze(2).to_broadcast([128, 4, 64]), ALU.mult)

            LA = 2
            pend = [emit_S(*tasks[i]) for i in range(LA)]
            for i, (qt, kb) in enumerate(tasks):
                if i + LA < len(tasks):
                    pend.append(emit_S(*tasks[i + LA]))
                emit_PV(qt, kb, *pend.pop(0))
            if h % 2 == 1:
                for half in range(2):
                    tp = bank(6 + half, 128, BF16)
                    for i in range(8):
                        qb = half * 8 + i
                        P.transpose(tp[:, i * 128:(i + 1) * 128], ytok[:, qb, :], ident_b)
                    P.copy("act", yT[:, h // 2, half * 1024:(half + 1) * 1024], tp)
        dump(f"y_fox{l}", yT[:, 0:4, :], [128, 4, S])
        if stop_after == "fox":
            return

        o = O_XT
        cv = V(o, [128, S + 2], F32); o += 8 * KB + 128
        acc = V(o, [128, S], F32); o += 8 * KB
        bsb = V(o, [128, S], F32); o += 8 * KB
        ctmp = [V(o + i * 2 * KB, [128, 512], F32) for i in range(2)]; o += 4 * KB
        P.memset("dve", cv[:, 0:2], 0.0)
        order = []
        for j in range(4):
            order += [(f"scb{j}", 1024), (f"scc{j}", 1024), (f"scv{j}", 1024)]
        scs = WStream(l, order, DENSE_CAST)
        for j in range(4):
            wb_ = scs.get(f"scb{j}")
            proj_fm(wb_, 128, 0, lambda tt, ts_, pb: P.copy("act", bsb[:, ts_], pb))
            wc_ = scs.get(f"scc{j}")
            wv_ = scs.get(f"scv{j}")
            w3c = wc_.rearrange("p (k c) -> p k c", k=NCH)
            w3v = wv_.rearrange("p (k c) -> p k c", k=NCH)
            for tt in range(NTT):
                ts_ = slice(tt * 512, (tt + 1) * 512)
                pc = bank(psb(4, 0))
                for k in range(NCH):
                    P.mm(pc, w3c[:, k, :], hT[:, k, ts_], start=(k == 0), stop=(k == NCH - 1))
                P.copy("act", ctmp[tt % 2], pc)
                pv = bank(psb(4, 0))
                for k in range(NCH):
                    P.mm(pv, w3v[:, k, :], hT[:, k, ts_], start=(k == 0), stop=(k == NCH - 1))
                P.tt("dve", cv[:, 2 + tt * 512:2 + (tt + 1) * 512], pv, ctmp[tt % 2], ALU.mult)
            w0 = prm[:, P_SCW + j * 3:P_SCW + j * 3 + 1]
            w1 = prm[:, P_SCW + j * 3 + 1:P_SCW + j * 3 + 2]
            w2 = prm[:, P_SCW + j * 3 + 2:P_SCW + j * 3 + 3]
            P.act(acc, cv[:, 2:2 + S], AF.Copy, scale=w2)
            P.stt("dve", acc, cv[:, 1:1 + S], w1, acc, ALU.mult, ALU.add)
            P.stt("dve", acc, cv[:, 0:S], w0, acc, ALU.mult, ALU.add)
            P.tt("dve", yT[:, 4 + j, :], acc, bsb, ALU.mult)
        dump(f"y_sc{l}", yT[:, 4:8, :], [128, 4, S])
        if stop_after == "sc":
            return

        gtok = V(O_SCR, [64, 32, 4], F32)
        btok = V(O_SCR + 512, [64, 32, 4], F32)
        for src0, dst in ((32, gtok), (64, btok)):
            pb = bank(psb(2, 6), 64)
            p3 = pb[:, 0:128].rearrange("p (n h) -> p n h", h=4)
            for n in range(32):
                P.transpose(p3[:, n, :], smT[src0:src0 + 4, n * 64:(n + 1) * 64],
                            ident_f[src0:src0 + 4, src0:src0 + 4])
            P.copy("dve", dst, p3)
        o = O_XT
        raw = V(o, [128, S + 3], F32); o += 8 * KB + 128
        cacc = V(o, [128, S], F32); o += 8 * KB
        QT = V(o, [128, S], BF16); o += 4 * KB
        KT = V(o, [128, S], BF16); o += 4 * KB
        VT = V(o, [128, S], BF16); o += 4 * KB
        qdT = V(o, [128, S], BF16); o += 4 * KB
        nkcT = V(o, [128, 32, 64], BF16); o += 4 * KB
        Kg = V(o, [64, 32, 128], BF16); o += 8 * KB
        Kd = V(o, [64, 32, 128], BF16); o += 8 * KB
        Vb = V(o, [64, 32, 128], BF16); o += 8 * KB
        sqp = [V(o + i * KB, [128, 512], BF16) for i in range(2)]; o += 2 * KB
        assert o <= 64 * KB, o
        o = O_SCR + 1 * KB
        cm = [V(o + i * 4 * KB, [64, 32, 64], BF16) for i in range(5)]; o += 20 * KB
        Mm, MTm, PTm, qkT, Mn = cm
        MTn = V(o, [64, 32, 64], BF16); o += 4 * KB
        PTn = V(o, [64, 32, 64], BF16); o += 4 * KB
        assert o <= SCR_END, o
        GB = V(O_XT, [128, S], F32)
        Xm = V(O_XT + 8 * KB + 128, [64, 32, 64], F32)
        sq2 = [V(O_SCR + 29 * KB + i * KB, [128, 512], BF16) for i in range(2)]
        eg = misc[0:64, 8:40]; bgc = misc[0:64, 40:72]; edc = misc[0:64, 72:104]
        gtot = misc[:, 104:136]; nbt = misc[0:64, 136:168]

        def prep_gen(h):
            lnp = V(O_YT + (8 + h) * 4 * KB, [128, 512], F32)
            rsp = V(O_YT + (8 + h) * 4 * KB + 2 * KB, [128, 512], F32)
            for which, nm, dstT, scl in ((0, "dnq", QT, 128.0 ** -0.5), (1, "dnk", KT, 1.0), (2, "dnv", VT, None)):
                wt = wload(l, f"{nm}{h}", 1024)
                P.memset("dve", raw[:, 0:3], 0.0)
                w3 = wt.rearrange("p (k c) -> p k c", k=NCH)
                for tt in range(NTT):
                    ts_ = slice(tt * 512, (tt + 1) * 512)
                    pb = bank(psb(2, 6))
                    for k in range(NCH):
                        P.mm(pb, w3[:, k, :], hT[:, k, ts_], start=(k == 0), stop=(k == NCH - 1))
                    P.copy("act", raw[:, 3 + tt * 512:3 + (tt + 1) * 512], pb)
                    yield
                cw = P_DNW + (which * 4 + h) * 4
                P.act(cacc, raw[:, 3:3 + S], AF.Copy, scale=prm[:, cw + 3:cw + 4])
                yield
                for j in range(3):
                    P.stt("dve", cacc, raw[:, j:j + S], prm[:, cw + j:cw + j + 1], cacc, ALU.mult, ALU.add)
                    yield
                if scl is None:
                    P.act(dstT, cacc, AF.Silu)
                    yield
                else:
                    P.act(cacc, cacc, AF.Silu)
                    yield
                    for tt in range(NTT):
                        ts_ = slice(tt * 512, (tt + 1) * 512)
                        s_ = sqp[tt % 2]
                        P.act(s_, cacc[:, ts_], AF.Square)
                        p2 = bank(psb(2, 6))
                        P.mm(p2, ones_b, s_)
                        yield
                        P.act(lnp, p2, AF.Ln, bias=epsc, scale=1.0)
                        P.act(rsp, lnp, AF.Exp, scale=-0.5)
                        yield
                        P.stt("dve", dstT[:, ts_], cacc[:, ts_], scl, rsp, ALU.mult, ALU.mult)
                        yield

        gen_box = [None]

        def pull(k):
            g = gen_box[0]
            if g is None:
                return
            for _ in range(k):
                try:
                    next(g)
                except StopIteration:
                    gen_box[0] = None
                    return

        def drain():
            while gen_box[0] is not None:
                pull(8)

        gen_box[0] = prep_gen(0)
        drain()
        for h in range(4):
            if h == 0:
                dump(f"dn_q{l}", QT, [128, S]); dump(f"dn_k{l}", KT, [128, S]); dump(f"dn_v{l}", VT, [128, S])
            for tt in range(NTT):
                ts_ = slice(tt * 512, (tt + 1) * 512)
                pb = bank(psb(4, 0))
                P.mm(pb, sel_f[32:36, h * 128:(h + 1) * 128], smT[32:36, ts_])
                P.copy("act", GB[:, ts_], pb)
                egt = V(O_SCR + 25 * KB, [128, 512], F32)
                P.act(egt, pb, AF.Exp)
                P.tt("dve", qdT[:, ts_], QT[:, ts_], egt, ALU.mult)
            GB3 = GB.rearrange("p (n c) -> p n c", c=64)
            P.act(eg, gtok[:, :, h], AF.Exp)
            P.tt("dve", bgc, btok[:, :, h], eg, ALU.mult)
            P.tt("dve", edc, GB3[0:64, :, 63], gtok[:, :, h], ALU.subtract)
            P.act(edc, edc, AF.Exp)
            P.act(gtot, GB3[:, :, 63], AF.Exp)
            P.ts("dve", nbt, btok[:, :, h], -1.0, None, op0=ALU.mult)
            for g8 in range(4):
                for src, outs in ((KT, ((Kg, bgc), (Kd, edc))), (VT, ((Vb, btok[:, :, h]),))):
                    tp = bank(psb(2, 6), 64, BF16)
                    for i in range(8):
                        n = g8 * 8 + i
                        P.transpose(tp[:, i * 128:(i + 1) * 128], src[:, n * 64:(n + 1) * 64], ident_b)
                    t3 = tp.rearrange("p (n d) -> p n d", d=128)
                    for (dst, col) in outs:
                        P.tt("dve", dst[:, g8 * 8:(g8 + 1) * 8, :], t3,
                             col[:, g8 * 8:(g8 + 1) * 8].unsqueeze(2).to_broadcast([64, 8, 128]), ALU.mult)
            P.tt("dve", Xm, GB3[0:64, :, :], gtok[:, :, h].unsqueeze(2).to_broadcast([64, 32, 64]), ALU.subtract)
            DLn = V(O_SCR + 21 * KB, [64, 32, 64], F32)
            P.tt("dve", DLn, Xm, mposA.unsqueeze(1).to_broadcast([64, 32, 64]), ALU.add)
            P.act(DLn, DLn, AF.Exp, scale=-1.0)
            P.tt("dve", DLn, DLn, nbt[:, :].unsqueeze(2).to_broadcast([64, 32, 64]), ALU.mult)
            P.tt("dve", Xm, Xm, mposB.unsqueeze(1).to_broadcast([64, 32, 64]), ALU.subtract)
            P.act(Xm, Xm, AF.Exp)
            for g8 in range(4):
                pk = bank(psb(4, 0), 64)
                pq = bank(psb(4, 0), 64)
                for i in range(8):
                    n = g8 * 8 + i
                    cs = slice(n * 64, (n + 1) * 64)
                    P.mm(pk[:, i * 64:(i + 1) * 64], KT[:, cs], KT[:, cs])
                    P.mm(pq[:, i * 64:(i + 1) * 64], KT[:, cs], QT[:, cs])
                gs = slice(g8 * 8, (g8 + 1) * 8)
                P.tt("dve", Mm[:, gs, :], pk.rearrange("p (n c) -> p n c", c=64), DLn[:, gs, :], ALU.mult)
                P.tt("dve", qkT[:, gs, :], pq.rearrange("p (n c) -> p n c", c=64), Xm[:, gs, :], ALU.mult)
            for g8 in range(4):
                tp = bank(psb(2, 6), 64, BF16)
                for i in range(8):
                    n = g8 * 8 + i
                    P.transpose(tp[:, i * 64:(i + 1) * 64], Mm[:, n, :], ident_b[0:64, 0:64])
                P.copy("act", MTm[:, g8 * 8:(g8 + 1) * 8, :], tp[:, 0:512].rearrange("p (n c) -> p n c", c=64))
            P.tt("dve", PTm, MTm, ident_b[0:64, 0:64].unsqueeze(1).to_broadcast([64, 32, 64]), ALU.add)
            if h < 3:
                gen_box[0] = prep_gen(h + 1)
            Wc, WTc, PTc = Mm, MTm, PTm
            Wn, WTn, PTx = Mn, MTn, PTn
            for it in range(5):
                def stA(g8, it=it, Wc=Wc, WTc=WTc, Wn=Wn, WTn=WTn):
                    pw = bank(psb(6, 0), 64)
                    pwt = bank(psb(6, 0), 64) if it < 4 else None
                    for i in range(8):
                        n = g8 * 8 + i
                        P.mm(pw[:, i * 64:(i + 1) * 64], WTc[:, n, :], Wc[:, n, :])
                        if pwt is not None:
                            P.mm(pwt[:, i * 64:(i + 1) * 64], Wc[:, n, :], WTc[:, n, :])
                    gs = slice(g8 * 8, (g8 + 1) * 8)
                    P.copy("act", Wn[:, gs, :], pw.rearrange("p (n c) -> p n c", c=64))
                    if pwt is not None:
                        P.copy("dve", WTn[:, gs, :], pwt.rearrange("p (n c) -> p n c", c=64))

                def stB(g8, Wn=Wn, PTc=PTc, PTx=PTx):
                    pp = bank(psb(6, 0), 64)
                    for i in range(8):
                        n = g8 * 8 + i
                        P.mm(pp[:, i * 64:(i + 1) * 64], ident_b[0:64, 0:64], PTc[:, n, :], start=True, stop=False)
                        P.mm(pp[:, i * 64:(i + 1) * 64], Wn[:, n, :], PTc[:, n, :], start=False, stop=True)
                    gs = slice(g8 * 8, (g8 + 1) * 8)
                    P.copy("dve", PTx[:, gs, :], pp.rearrange("p (n c) -> p n c", c=64))
                for st_, g_ in ((stA, 0), (stA, 1), (stB, 0), (stA, 2), (stB, 1), (stA, 3), (stB, 2), (stB, 3)):
                    st_(g_)
                    pull(PULL_N)
                Wc, Wn = Wn, Wc
                WTc, WTn = WTn, WTc
                PTc, PTx = PTx, PTc
            TT = PTc
            for g8 in range(4):
                pb = bank(psb(4, 0))
                for i in range(8):
                    n = g8 * 8 + i
                    P.mm(pb[:, i * 64:(i + 1) * 64], Kg[:, n, :], TT[:, n, :])
                P.act(nkcT[:, g8 * 8:(g8 + 1) * 8, :], pb.rearrange("p (n c) -> p n c", c=64), AF.Copy, scale=-1.0)
            o3 = O_SCR + 29 * KB
            Sf = V(o3, [128, 128], F32)
            Sbb = [V(o3 + 512 + i * 256, [128, 128], BF16) for i in range(2)]
            vn = [V(o3 + 1024 + i * 256, [64, 128], BF16) for i in range(2)]
            oT = V(O_SCR + 1 * KB, [128, S], F32)
            P.memset("dve", Sf, 0.0)
            P.memset("dve", Sbb[0], 0.0)
            for n in range(32):
                sb_ = Sbb[n % 2]
                pv = bank(4, 64)[:, (n % 2) * 128:(n % 2) * 128 + 128]
                P.mm(pv, TT[:, n, :], Vb[:, n, :], start=True, stop=False)
                P.mm(pv, nkcT[:, n, :], sb_, start=False, stop=True)
                vn_ = vn[n % 2]
                P.copy("act", vn_, pv)
                if n % 8 == 0:
                    po = bank(psb(2, 2))
                pos = po[:, (n % 8) * 64:(n % 8 + 1) * 64]
                P.mm(pos, sb_, qdT[:, n * 64:(n + 1) * 64], start=True, stop=False)
                P.mm(pos, vn_, qkT[:, n, :], start=False, stop=True)
                pd = bank(5)[:, (n % 2) * 128:(n % 2) * 128 + 128]
                P.mm(pd, Kd[:, n, :], vn_)
                P.stt("dve", Sbb[(n + 1) % 2], Sf, gtot[:, n:n + 1], pd, ALU.mult, ALU.add)
                P.stt("dve", Sf, Sf, gtot[:, n:n + 1], pd, ALU.mult, ALU.add)
                if n % 8 == 7:
                    tt = n // 8
                    P.copy("act", oT[:, tt * 512:(tt + 1) * 512], po)
                pull(PULL_S)
            if h == 0:
                dump(f"o_dn{l}", oT, [128, S])
            drain()
            wz = wload(l, f"dnz{h}", 1024)

            for tt in range(NTT):
                ts_ = slice(tt * 512, (tt + 1) * 512)
                s_ = sq2[tt % 2]
                P.act(s_, oT[:, ts_], AF.Square)
                p2 = bank(psb(2, 6))
                P.mm(p2, ones_b, s_)
                lnt = V(O_SCR + 25 * KB, [128, 512], F32)
                rst = V(O_SCR + 27 * KB, [128, 512], F32)
                P.act(lnt, p2, AF.Ln, bias=epsc, scale=1.0 / 128)
                P.act(rst, lnt, AF.Exp, scale=-0.5)
                P.stt("dve", oT[:, ts_], oT[:, ts_], prm[:, P_DNG:P_DNG + 1], rst, ALU.mult, ALU.mult)

            def z_cons(tt, ts_, pb, h=h):
                zt = V(O_SCR + 25 * KB + (tt % 2) * 2 * KB, [128, 512], F32)
                P.act(zt, pb, AF.Silu)
                P.tt("dve", yT[:, 8 + h, ts_], oT[:, ts_], zt, ALU.mult)
            proj_fm(wz, 128, 0, z_cons)
        dump(f"y_dn{l}", yT[:, 8:12, :], [128, 4, S])
        if stop_after == "dn":
            return

        mT = V(O_SMT, [128, NCH, S], BF16)
        macc = V(O_SMT + 33 * KB, [128, 512], F32)
        mtmp = V(O_SMT + 35 * KB, [128, 512], F32)
        sgt = [[V(O_XT + (b * NTT + tt) * 2 * KB, [128, 512], F32) for tt in range(NTT)] for b in range(3)]
        order = []
        for dc in range(NCH):
            order += [(f"g0_{dc}", 1024), (f"g1_{dc}", 1024), (f"g2_{dc}", 1024), (f"brA{dc}", 1024), (f"brB{dc}", 512)]
        order += [(f"wo{dc}", 1024) for dc in range(NCH)]
        mgs = WStream(l, order, DENSE_CAST)
        for dc in range(NCH):
            for b in range(3):
                wgb = mgs.get(f"g{b}_{dc}").rearrange("p (k c) -> p k c", k=NCH)
                for tt in range(NTT):
                    ts_ = slice(tt * 512, (tt + 1) * 512)
                    pg = bank(psb(6, 0))
                    for k in range(NCH):
                        P.mm(pg, wgb[:, k, :], hT[:, k, ts_], start=(k == 0), stop=(k == NCH - 1))
                    P.act(sgt[b][tt], pg, AF.Sigmoid)
            wA3 = mgs.get(f"brA{dc}").rearrange("p (b c m) -> p b c m", b=2, c=4)
            wB3 = mgs.get(f"brB{dc}").rearrange("p (c m) -> p c m", c=4)
            for tt in range(NTT):
                ts_ = slice(tt * 512, (tt + 1) * 512)
                for b in range(3):
                    pbr = bank(psb(6, 0))
                    for c in range(4):
                        lw = wA3[:, b, c, :] if b < 2 else wB3[:, c, :]
                        P.mm(pbr, lw, yT[:, 4 * b + c, ts_], start=(c == 0), stop=(c == 3))
                    if b == 0:
                        P.tt("dve", macc, pbr, sgt[b][tt], ALU.mult)
                    elif b == 1:
                        P.tt("dve", mtmp, pbr, sgt[b][tt], ALU.mult)
                        P.tt("dve", macc, macc, mtmp, ALU.add)
                    else:
                        P.tt("dve", mtmp, pbr, sgt[b][tt], ALU.mult)
                        P.tt("dve", mT[:, dc, ts_], macc, mtmp, ALU.add)
        dump(f"merged{l}", mT[:, :, 0:1024], [128, NCH, 1024])
        for dc in range(NCH):
            wo = mgs.get(f"wo{dc}")
            P.dma("sp", xT[:, dc, :], scrx[:, dc, :], f"unsp{dc}")

            def wo_cons(tt, ts_, pb, dc=dc):
                P.tt("dve", xT[:, dc, ts_], xT[:, dc, ts_], pb, ALU.add)
            proj_fm(wo, 128, 0, wo_cons, rhs_of=lambda k, ts_: mT[:, k, ts_], nb=6)
        dump(f"x_mix{l}", xT[:, :, :], [128, NCH, S])
        if stop_after == "mix":
            return

        rmsnorm_to_hT(P_GFFN, O_SCR)
        o = O_YT
        graw = V(o, [128, S + 2], F32); o += 8 * KB + 128
        vraw = V(o, [128, S + 2], F32); o += 8 * KB + 128
        gacs = [V(o, [128, S], F32), V(O_SMT, [128, S], F32)]; o += 8 * KB
        vacs = [V(o, [128, S], F32), V(O_SCR + 20 * KB, [128, S], F32)]; o += 8 * KB
        assert o <= O_WS
        aT0 = V(O_SCR + 0 * KB, [128, GRP, S], BF16)
        aT1 = V(O_YT + 33 * KB, [128, 3, S], BF16)
        aT1b = V(O_SCR + 16 * KB, [128, 1, S], BF16)
        P.memset("dve", graw[:, 0:2], 0.0)
        P.memset("dve", vraw[:, 0:2], 0.0)

        def a_slot(gi, jj):
            if gi % 2 == 0:
                return aT0[:, jj, :]
            return aT1[:, jj, :] if jj < 3 else aT1b[:, 0, :]

        def ffn_up(gi, j0, n):
                for jj in range(n):
                    j = j0 + jj
                    gac = gacs[j % 2]
                    vac = vacs[j % 2]
                    for nm, rawb, accb, c0 in (("upg", graw, gac, j), ("upv", vraw, vac, NFF + j)):
                        wt = ffs.get(f"{nm}{j}")
                        w2c = prm[:, P_FFW + c0 * 3 + 2:P_FFW + c0 * 3 + 3]

                        def up_cons(tt, ts_, pb, rawb=rawb, accb=accb, w2c=w2c):
                            P.copy("act", rawb[:, 2 + tt * 512:2 + (tt + 1) * 512], pb)
                            P.act(accb[:, ts_], pb, AF.Copy, scale=w2c)
                        proj_fm(wt, 128, 0, up_cons, nb=6)
                        P.stt("dve", accb, rawb[:, 1:1 + S], prm[:, P_FFW + c0 * 3 + 1:P_FFW + c0 * 3 + 2], accb,
                              ALU.mult, ALU.add)
                        P.stt("dve", accb, rawb[:, 0:S], prm[:, P_FFW + c0 * 3:P_FFW + c0 * 3 + 1], accb,
                              ALU.mult, ALU.add)
                    P.act(gac, gac, AF.Silu)
                    P.tt("dve", a_slot(gi, jj), gac, vac, ALU.mult)

        def ffn_down(gi, j0, n):
                for q in range(4):
                    wd = ffs.get(f"dn{gi}_{q}")
                    wd3 = wd.rearrange("p (j c) -> p j c", j=n)
                    for dd in range(2):
                        dc = 2 * q + dd
                        for tt in range(NTT):
                            ts_ = slice(tt * 512, (tt + 1) * 512)
                            pb = bank(psb(6, 0))
                            for jj in range(n):
                                P.mm(pb, wd3[:, jj, dd * 128:(dd + 1) * 128], a_slot(gi, jj)[:, ts_],
                                     start=(jj == 0), stop=(jj == n - 1))
                            P.tt("dve", xT[:, dc, ts_], xT[:, dc, ts_], pb, ALU.add)

        groups = _ffn_groups()
        order = []
        for gi, (j0, n) in enumerate(groups):
            for jj in range(n):
                order += [(f"upg{j0 + jj}", 1024), (f"upv{j0 + jj}", 1024)]
            if gi > 0:
                order += [(f"dn{gi - 1}_{q}", groups[gi - 1][1] * 256) for q in range(4)]
        order += [(f"dn{len(groups) - 1}_{q}", groups[-1][1] * 256) for q in range(4)]
        ffs = WStream(l, order, FFN_CAST)
        for gi, (j0, n) in enumerate(groups):
            ffn_up(gi, j0, n)
            if gi > 0:
                ffn_down(gi - 1, *groups[gi - 1])
        ffn_down(len(groups) - 1, *groups[-1])
        dump(f"x_ffn{l}", xT[:, :, :], [128, NCH, S])
        if stop_after == "ffn":
            return

        rmsnorm_to_hT(P_GPLE, O_SCR)
        ptok = V(O_YT, [128, 16, 256], F32)
        pT = V(O_YT + 16 * KB, [128, 2, S], BF16)
        sgp = [V(O_YT + 24 * KB + i * 2 * KB, [128, 512], F32) for i in range(2)]
        ptmp = V(O_YT + 28 * KB, [128, 512], F32)
        for q in range(4):
            P.dma("sp", ptok[:, q * 4:(q + 1) * 4, :],
                  p_in[l, q * 512:(q + 1) * 512, :].rearrange("(a p) c -> p a c", p=128), f"pin{q}")
        for c in range(2):
            for g4 in range(4):
                tp = bank(psb(2, 6))
                for i in range(4):
                    tb = g4 * 4 + i
                    P.transpose(tp[:, i * 128:(i + 1) * 128], ptok[:, tb, c * 128:(c + 1) * 128], ident_f)
                P.copy("act", pT[:, c, g4 * 512:(g4 + 1) * 512], tp)
        order = []
        for dc in range(NCH):
            order += [(f"pg{dc}", 1024), (f"pl{dc}", 256)]
        pls = WStream(l, order, DENSE_CAST)
        for dc in range(NCH):
            wpg = pls.get(f"pg{dc}")
            wpl = pls.get(f"pl{dc}")
            wpl3 = wpl.rearrange("p (c m) -> p c m", c=2)

            def pg_cons(tt, ts_, pb, dc=dc, wpl3=wpl3):
                s_ = sgp[tt % 2]
                P.act(s_, pb, AF.Sigmoid)
                pp = bank(psb(2, 4))
                for c in range(2):
                    P.mm(pp, wpl3[:, c, :], pT[:, c, ts_], start=(c == 0), stop=(c == 1))
                P.tt("dve", ptmp, pp, s_, ALU.mult)
                P.tt("dve", xT[:, dc, ts_], xT[:, dc, ts_], ptmp, ALU.add)
            proj_fm(wpg, 128, 0, pg_cons, defer=True)
        dump(f"x_out{l}", xT[:, :, :], [128, NCH, S])

    xin = [V(O_YT + i * 4 * KB, [128, D], F32) for i in range(2)]
    for tb in range(16):
        xi = xin[tb % 2]
        P.dma("sp", xi, x_in[tb * 128:(tb + 1) * 128, :], f"xin{tb % 2}")
        for g2 in range(2):
            tp = bank(psb(2, 6))
            for i in range(4):
                c = g2 * 4 + i
                P.transpose(tp[:, i * 128:(i + 1) * 128], xi[:, c * 128:(c + 1) * 128], ident_f)
            eng = "act" if g2 == 0 else "dve"
            P.copy(eng, xT[:, g2 * 4:(g2 + 1) * 4, tb * 128:(tb + 1) * 128],
                   tp.rearrange("p (c t) -> p c t", c=4))

    for l in range(depth):
        layer(l)

    if stop_after is None:
        xo = [V(O_YT + i * 4 * KB, [128, D], F32) for i in range(2)]
        for tb in range(16):
            xo_ = xo[tb % 2]
            for g2 in range(2):
                tp = bank(psb(2, 6))
                for i in range(4):
                    c = g2 * 4 + i
                    P.transpose(tp[:, i * 128:(i + 1) * 128], xT[:, c, tb * 128:(tb + 1) * 128], ident_f)
                eng = "act" if g2 == 0 else "dve"
                P.copy(eng, xo_[:, g2 * 512:(g2 + 1) * 512], tp)
            P.dma("sp", out[tb * 128:(tb + 1) * 128, :], xo_, f"xout{tb % 2}", final=True)
    else:
        z = V(O_SCR, [128, 8], F32)
        P.memset("pool", z, 0.0)
        P.dma("sp", out[0:128, 0:8], z, "xout0", final=True)
    P.emit()
    return nc, dbg_outs, P


_CACHE = {}


def kernel(**inputs):
    inp = {k: np.asarray(v) for k, v in inputs.items()}
    if "nc" not in _CACHE:
        _CACHE["nc"] = build()[0]
    nc = _CACHE["nc"]
    wstream = np.stack([pack_weights(inp, l) for l in range(DEPTH)])
    prm = np.stack([pack_params(inp, l) for l in range(DEPTH)])
    cst = make_consts()
    in_maps = []
    for b in range(8):
        in_maps.append({"x": np.ascontiguousarray(inp["x"][b]),
                        "p": np.ascontiguousarray(inp["p"][:, b]),
                        "wst": wstream, "prm": prm, "cst": cst})
    res = run_bass_kernel_spmd(nc, in_maps, core_ids=list(range(8)))
    return np.stack([np.asarray(r["out"]) for r in res.results]).astype(np.float32)
```

```python
import numpy as np
import concourse.bass as bass
import concourse.mybir as mybir

F32 = mybir.dt.float32
BF16 = mybir.dt.bfloat16
ALU = mybir.AluOpType
AF = mybir.ActivationFunctionType

ENGS = ("pe", "act", "dve", "pool", "sp")


def _prod(xs):
    r = 1
    for v in xs:
        r *= int(v)
    return r


def region(ap):
    t = ap.tensor
    es = mybir.dt.size(ap.dtype)
    off = int(ap.offset)
    pat = ap.ap
    if str(ap.space) == "DRAM":
        ext = 0
        for st, cnt in pat:
            ext += (cnt - 1) * abs(st)
        return (t.name, 0, 1, off * es, (off + ext + 1) * es)
    shape = list(t.shape)
    F = _prod(shape[1:])
    p0 = off // F
    f0 = off % F
    pstep, pcnt = pat[0]
    npart = pcnt if pstep != 0 else 1
    ext = 0
    for st, cnt in pat[1:]:
        ext += (cnt - 1) * abs(st)
    return (t.name, p0, p0 + npart, f0 * es, (f0 + ext + 1) * es)


def _untracked(ap):
    return str(ap.space) == "DRAM" and not ap.tensor.name.startswith("scr_")


def _overlap(a, b):
    return a[1] < b[2] and b[1] < a[2] and a[3] < b[4] and b[3] < a[4]


def _covers(a, b):
    return a[1] <= b[1] and a[2] >= b[2] and a[3] <= b[3] and a[4] >= b[4]


class Instr:
    __slots__ = ("eng", "fn", "deps", "signal", "is_dma", "key", "val", "idx")

    def __init__(self, eng, fn, is_dma, key):
        self.eng = eng
        self.fn = fn
        self.deps = set()
        self.signal = False
        self.is_dma = is_dma
        self.key = key
        self.val = 0


class Prog:
    def __init__(self, nc):
        self.nc = nc
        self.instrs = []
        self.wr = {}
        self.rd = {}
        self.final_keys = set()
        self.last_dma = {}

    def add(self, eng, fn, reads=(), writes=(), dma_key=None):
        ins = Instr(eng, fn, dma_key is not None, dma_key if dma_key is not None else eng)
        idx = len(self.instrs)
        ins.idx = idx
        self.instrs.append(ins)
        deps = ins.deps
        for ap in reads:
            if ap is None or isinstance(ap, (int, float)):
                continue
            if _untracked(ap):
                continue
            r = region(ap)
            for (w, wi) in self.wr.get(r[0], ()):
                if _overlap(w, r):
                    deps.add(wi)
        for ap in writes:
            if _untracked(ap):
                continue
            r = region(ap)
            wl = self.wr.setdefault(r[0], [])
            rl = self.rd.setdefault(r[0], [])
            for (w, wi) in wl:
                if _overlap(w, r):
                    deps.add(wi)
            for (q, qi) in rl:
                if _overlap(q, r):
                    deps.add(qi)
            self.wr[r[0]] = [(w, wi) for (w, wi) in wl if not _covers(r, w)]
            self.rd[r[0]] = [(q, qi) for (q, qi) in rl if not _covers(r, q)]
            self.wr[r[0]].append((r, idx))
        for ap in reads:
            if ap is None or isinstance(ap, (int, float)):
                continue
            if _untracked(ap):
                continue
            r = region(ap)
            rl = self.rd.setdefault(r[0], [])
            if dma_key is None:
                rl[:] = [(q, qi) for (q, qi) in rl
                         if not (q == r and self.instrs[qi].eng == eng and not self.instrs[qi].is_dma)]
            rl.append((r, idx))
        if dma_key is not None:
            if dma_key in self.last_dma:
                deps.add(self.last_dma[dma_key])
            self.last_dma[dma_key] = idx
        deps.discard(idx)
        return ins

    def mm(self, out, lhsT, rhs, start=True, stop=True):
        rd = [lhsT, rhs]
        return self.add("pe", lambda e: e.matmul(out, lhsT, rhs, start=start, stop=stop),
                        reads=rd, writes=[out])

    def transpose(self, out, in_, ident):
        return self.add("pe", lambda e: e.transpose(out, in_, ident),
                        reads=[in_, ident], writes=[out])

    def act(self, out, in_, func, bias=None, scale=1.0, accum_out=None, eng="act"):
        kw = {}
        if bias is not None:
            kw["bias"] = bias
        if accum_out is not None:
            kw["accum_out"] = accum_out
        rd = [in_, bias if not isinstance(bias, (int, float)) else None,
              scale if not isinstance(scale, (int, float)) else None]
        wr = [out] + ([accum_out] if accum_out is not None else [])
        return self.add(eng, lambda e: e.activation(out, in_, func, scale=scale, **kw),
                        reads=rd, writes=wr)

    def tt(self, eng, out, in0, in1, op):
        return self.add(eng, lambda e: e.tensor_tensor(out, in0, in1, op),
                        reads=[in0, in1], writes=[out])

    def ts(self, eng, out, in0, s1, s2=None, op0=ALU.mult, op1=None, accum_out=None):
        kw = {}
        if op1 is not None:
            kw["op1"] = op1
        if accum_out is not None:
            kw["accum_out"] = accum_out
        rd = [in0, s1 if not isinstance(s1, (int, float)) else None,
              s2 if not isinstance(s2, (int, float)) else None]
        wr = [out] + ([accum_out] if accum_out is not None else [])
        return self.add(eng, lambda e: e.tensor_scalar(out, in0, s1, s2, op0, **kw),
                        reads=rd, writes=wr)

    def stt(self, eng, out, in0, scalar, in1, op0, op1):
        rd = [in0, in1, scalar if not isinstance(scalar, (int, float)) else None]
        return self.add(eng, lambda e: e.scalar_tensor_tensor(out, in0, scalar, in1, op0, op1),
                        reads=rd, writes=[out])

    def copy(self, eng, out, in_):
        if eng == "act":
            return self.add(eng, lambda e: e.copy(out, in_), reads=[in_], writes=[out])
        return self.add(eng, lambda e: e.tensor_copy(out, in_), reads=[in_], writes=[out])

    def memset(self, eng, out, val):
        return self.add(eng, lambda e: e.memset(out, val), reads=[], writes=[out])

    def scan(self, out, d0, d1, initial, op0, op1):
        rd = [d0, d1, initial if not isinstance(initial, (int, float)) else None]
        return self.add("dve", lambda e: e.tensor_tensor_scan(out, d0, d1, initial, op0, op1),
                        reads=rd, writes=[out])

    def dma(self, eng, out, in_, key, final=False):
        if final:
            self.final_keys.add(key)
        return self.add(eng, lambda e: e.dma_start(out, in_), reads=[in_], writes=[out], dma_key=key)

    def emit(self):
        nc = self.nc
        instrs = self.instrs
        for ins in instrs:
            if ins.eng == "pe" and not ins.is_dma:
                ins.deps = {d for d in ins.deps if not (instrs[d].eng == "pe" and not instrs[d].is_dma)}
            for d in ins.deps:
                instrs[d].signal = True
        keys = []
        for ins in instrs:
            if ins.key not in keys:
                keys.append(ins.key)
        for ins in instrs:
            if ins.is_dma:
                ins.signal = True
        cnt = {k: 0 for k in keys}
        for ins in instrs:
            if ins.signal:
                cnt[ins.key] += 16 if ins.is_dma else 1
            ins.val = cnt[ins.key]
        used = [k for k in keys if cnt[k] > 0]
        self.sem_totals = {k: cnt[k] for k in used}
        import contextlib
        with contextlib.ExitStack() as st:
            sems = {k: st.enter_context(nc.semaphore("s_" + str(k))) for k in used}
            block = st.enter_context(nc.Block())
            per_eng = {e: [i for i in instrs if i.eng == e] for e in ENGS}
            nwaits = [0]

            def run(engobj, lst, is_last_sp=False):
                clock = {}
                for ins in lst:
                    need = {}
                    for d in ins.deps:
                        di = instrs[d]
                        if di.val > need.get(di.key, 0):
                            need[di.key] = di.val
                    for k, v in need.items():
                        if clock.get(k, 0) < v:
                            engobj.wait_ge(sems[k], v)
                            clock[k] = v
                            nwaits[0] += 1
                    bi = ins.fn(engobj)
                    if ins.signal:
                        bi.then_inc(sems[ins.key], 16 if ins.is_dma else 1)
                if is_last_sp:
                    for k in sorted(self.final_keys, key=str):
                        if k in sems:
                            engobj.wait_ge(sems[k], cnt[k])

            @block.tensor
            def _(e):
                run(e, per_eng["pe"])

            @block.scalar
            def _(e):
                run(e, per_eng["act"])

            @block.vector
            def _(e):
                run(e, per_eng["dve"])

            @block.gpsimd
            def _(e):
                run(e, per_eng["pool"])

            @block.sync
            def _(e):
                run(e, per_eng["sp"], True)
            self.nwaits = nwaits[0]

from concourse.bass_utils import run_bass_kernel_spmd
import ml_dtypes

S = 2048
D = 1024
DEPTH = 2
NCH = D // 128
NTT = S // 512
DFF = 2816
NFF = DFF // 128
EPS = 1e-6
GRP = 4
import os as _os
PULL_N = int(_os.environ.get('PULL_N', '0'))
PULL_S = int(_os.environ.get('PULL_S', '1'))
DMA_CAST = int(_os.environ.get('DMA_CAST', '1'))
NPRM = 224
NCST = 1536

C_FQ, C_FK, C_FV, C_FF = 0, 512, 1024, 1536
C_SB, C_SC, C_SV = 1544, 2056, 2568
C_DQ, C_DK, C_DV = 3080, 3592, 4104
C_DB, C_DA, C_DZ, C_G = 4616, 4620, 4624, 5136

P_GMIX, P_GFFN, P_GPLE = 0, 8, 16
P_FQG, P_FKG, P_BF, P_ALOG, P_DTB, P_DNG = 24, 25, 26, 27, 28, 29
P_SCW, P_DNW, P_FFW = 30, 42, 90

K_ID, K_M01, K_ONE, K_SEL, K_MPA, K_MPB = 0, 128, 384, 512, 1024, 1088


def _ffn_groups():
    gs = []
    j = 0
    while j < NFF:
        n = min(GRP, NFF - j)
        gs.append((j, n))
        j += n
    return gs


def wtile_index():
    idx = {}
    names = ["small"]
    names += [f"foxv{c}" for c in range(4)]
    names += [f"foxqk{h}" for h in range(8)]
    for j in range(4):
        names += [f"scb{j}", f"scc{j}", f"scv{j}"]
    for h in range(4):
        names += [f"dnq{h}", f"dnk{h}", f"dnv{h}", f"dnz{h}"]
    for dc in range(8):
        names += [f"g0_{dc}", f"g1_{dc}", f"g2_{dc}", f"brA{dc}", f"brB{dc}"]
    names += [f"wo{dc}" for dc in range(8)]
    for j in range(NFF):
        names += [f"upg{j}", f"upv{j}"]
    for gi, (j0, n) in enumerate(_ffn_groups()):
        names += [f"dn{gi}_{q}" for q in range(4)]
    for dc in range(8):
        names += [f"pg{dc}", f"pl{dc}"]
    for i, n in enumerate(names):
        idx[n] = i
    return idx


WIDX = wtile_index()
NT = len(WIDX)


def pack_weights(inp, l):
    W = np.zeros((NT, 128, 1024), np.float32)
    w_in = inp["w_in"][l]

    def kc(cols):
        n = cols.shape[1]
        return cols.reshape(8, 128, n).transpose(1, 0, 2).reshape(128, 8 * n)

    def put(name, arr):
        W[WIDX[name], :, :arr.shape[1]] = arr

    sm = np.zeros((1024, 96), np.float32)
    sm[:, 0:8] = w_in[:, C_FF:C_FF + 8]
    sm[:, 32:36] = w_in[:, C_DA:C_DA + 4]
    sm[:, 64:68] = w_in[:, C_DB:C_DB + 4]
    put("small", kc(sm))
    for c in range(4):
        put(f"foxv{c}", kc(w_in[:, C_FV + c * 128:C_FV + (c + 1) * 128]))
    for h in range(8):
        qk = np.concatenate([w_in[:, C_FQ + h * 64:C_FQ + (h + 1) * 64],
                             w_in[:, C_FK + h * 64:C_FK + (h + 1) * 64]], axis=1)
        put(f"foxqk{h}", kc(qk))
    for j in range(4):
        put(f"scb{j}", kc(w_in[:, C_SB + j * 128:C_SB + (j + 1) * 128]))
        put(f"scc{j}", kc(w_in[:, C_SC + j * 128:C_SC + (j + 1) * 128]))
        put(f"scv{j}", kc(w_in[:, C_SV + j * 128:C_SV + (j + 1) * 128]))
    for h in range(4):
        put(f"dnq{h}", kc(w_in[:, C_DQ + h * 128:C_DQ + (h + 1) * 128]))
        put(f"dnk{h}", kc(w_in[:, C_DK + h * 128:C_DK + (h + 1) * 128]))
        put(f"dnv{h}", kc(w_in[:, C_DV + h * 128:C_DV + (h + 1) * 128]))
        put(f"dnz{h}", kc(w_in[:, C_DZ + h * 128:C_DZ + (h + 1) * 128]))
    wb = inp["w_branch"][l]
    for dc in range(8):
        for b in range(3):
            put(f"g{b}_{dc}", kc(w_in[:, C_G + b * 1024 + dc * 128:C_G + b * 1024 + (dc + 1) * 128]))

        def br(b):
            a = wb[b][:, dc * 128:(dc + 1) * 128]
            return a.reshape(4, 128, 128).transpose(1, 0, 2).reshape(128, 512)
        put(f"brA{dc}", np.concatenate([br(0), br(1)], axis=1))
        put(f"brB{dc}", br(2))
        put(f"wo{dc}", kc(inp["w_o"][l][:, dc * 128:(dc + 1) * 128]))
        put(f"pg{dc}", kc(inp["w_ple_gate"][l][:, dc * 128:(dc + 1) * 128]))
        a = inp["w_ple"][l][:, dc * 128:(dc + 1) * 128]
        put(f"pl{dc}", a.reshape(2, 128, 128).transpose(1, 0, 2).reshape(128, 256))
    w_up = inp["w_up"][l]
    for j in range(NFF):
        put(f"upg{j}", kc(w_up[:, j * 128:(j + 1) * 128]))
        put(f"upv{j}", kc(w_up[:, DFF + j * 128:DFF + (j + 1) * 128]))
    w_dn = inp["w_down"][l]
    for gi, (j0, n) in enumerate(_ffn_groups()):
        for q in range(4):
            a = w_dn[j0 * 128:(j0 + n) * 128, q * 256:(q + 1) * 256]
            put(f"dn{gi}_{q}", a.reshape(n, 128, 256).transpose(1, 0, 2).reshape(128, n * 256))
    return W


def pack_params(inp, l):
    Pm = np.zeros((128, NPRM), np.float32)
    Pm[:, P_GMIX:P_GMIX + 8] = inp["g_mix"][l].reshape(8, 128).T
    Pm[:, P_GFFN:P_GFFN + 8] = inp["g_ffn"][l].reshape(8, 128).T
    Pm[:, P_GPLE:P_GPLE + 8] = inp["g_ple"][l].reshape(8, 128).T
    Pm[0:64, P_FQG] = inp["fox_q_gain"][l]
    Pm[64:128, P_FQG] = inp["fox_k_gain"][l]
    Pm[0:64, P_FKG] = inp["fox_k_gain"][l]
    Pm[0:8, P_BF] = inp["b_fox_f"][l]
    Pm[32:36, P_ALOG] = inp["dn_a_log"][l]
    Pm[32:36, P_DTB] = inp["dn_dt_bias"][l]
    Pm[:, P_DNG] = inp["dn_norm_gain"][l]
    Pm[:, P_SCW:P_SCW + 12] = inp["sc_conv_w"][l].reshape(3, 4, 128).transpose(2, 1, 0).reshape(128, 12)
    Pm[:, P_DNW:P_DNW + 48] = inp["dn_conv_w"][l].reshape(4, 12, 128).transpose(2, 1, 0).reshape(128, 48)
    Pm[:, P_FFW:P_FFW + 132] = inp["ffn_conv_w"][l].reshape(3, 44, 128).transpose(2, 1, 0).reshape(128, 132)
    return Pm


def make_consts():
    C = np.zeros((128, NCST), np.float32)
    r = np.arange(128)[:, None]
    c = np.arange(128)[None, :]
    C[:, K_ID:K_ID + 128] = (r == c)
    C[:, K_M01:K_M01 + 128] = (c >= r)
    C[:, K_M01 + 128:K_M01 + 256] = ((r // 64) == (c // 64))
    C[:, K_ONE:K_ONE + 128] = 1.0
    for h in range(4):
        C[32 + h, K_SEL + h * 128:K_SEL + (h + 1) * 128] = 1.0
    r6 = np.arange(64)[:, None]
    c6 = np.arange(64)[None, :]
    C[0:64, K_MPA:K_MPA + 64] = np.where(r6 > c6, 0.0, 1.0e4)
    C[0:64, K_MPB:K_MPB + 64] = np.where(c6 >= r6, 0.0, 1.0e4)
    return C


def build(depth=DEPTH, dbg=None, stop_after=None):
    nc = bass.Bass("TRN2", target_bir_lowering=False)
    x_in = nc.dram_tensor("x", [S, D], F32, kind="ExternalInput").ap()
    p_in = nc.dram_tensor("p", [DEPTH, S, 256], F32, kind="ExternalInput").ap()
    wst = nc.dram_tensor("wst", [DEPTH, NT, 128, 1024], F32, kind="ExternalInput").ap()
    prm_in = nc.dram_tensor("prm", [DEPTH, 128, NPRM], F32, kind="ExternalInput").ap()
    cst_in = nc.dram_tensor("cst", [128, NCST], F32, kind="ExternalInput").ap()
    out = nc.dram_tensor("out", [S, D], F32, kind="ExternalOutput").ap()
    scrx = nc.dram_tensor("scr_x", [128, NCH, S], F32).ap()
    dbg_outs = {}

    TOT = 211456
    SB = nc.alloc_sbuf_tensor("SB", [128, TOT // 4], F32)
    PS = nc.alloc_psum_tensor("PS", [128, 8 * 512], F32)
    P = Prog(nc)

    def V(off, shape, dt=F32, p0=0):
        es = mybir.dt.size(dt)
        n = _prod(shape[1:])
        assert off % 4 == 0 and off + n * es <= TOT, (off, shape)
        nw = (n * es + 3) // 4
        v = SB[p0:p0 + shape[0], off // 4: off // 4 + nw]
        if dt != F32:
            v = v.bitcast(dt)
        v = v[:, 0:n]
        if len(shape) == 3:
            v = v.rearrange("p (a b) -> p a b", a=shape[1])
        elif len(shape) == 4:
            v = v.rearrange("p (a b c) -> p a b c", a=shape[1], b=shape[2])
        return v

    def bank(b, parts=128, dt=F32, p0=0):
        v = PS[p0:p0 + parts, b * 512:(b + 1) * 512]
        if dt != F32:
            v = v.bitcast(dt)
        return v

    KB = 1024
    O_XT = 0
    O_HT = 64 * KB
    O_YT = 96 * KB
    O_WS = 144 * KB
    O_WB = 152 * KB
    O_CST = 158 * KB
    O_CBF = 164 * KB
    O_PRM = 165 * KB
    O_MISC = 166 * KB
    O_SMT = 167 * KB
    O_SCR = 175 * KB
    SCR_END = TOT

    xT = V(O_XT, [128, NCH, S], F32)
    hT = V(O_HT, [128, NCH, S], BF16)
    yT = V(O_YT, [128, 12, S], BF16)
    wstage = [V(O_WS + i * 4 * KB, [128, 1024], F32) for i in range(2)]
    wbf = [V(O_WB + i * 2 * KB, [128, 1024], BF16) for i in range(3)]
    cst = V(O_CST, [128, NCST], F32)
    cbf = V(O_CBF, [128, 512], BF16)
    prm = V(O_PRM, [128, NPRM], F32)
    misc = V(O_MISC, [128, 256], F32)
    smT = V(O_SMT, [128, S], F32)

    ident_f = cst[:, K_ID:K_ID + 128]
    ident_b = cbf[:, 0:128]
    mask01_b = cbf[:, 128:256]
    ones_b = cbf[:, 384:512]
    bd_b = cbf[:, 256:384]
    sel_f = cst[:, K_SEL:K_SEL + 512]
    mposA = cst[0:64, K_MPA:K_MPA + 64]
    mposB = cst[0:64, K_MPB:K_MPB + 64]
    epsc = misc[:, 0:1]
    onec = misc[:, 1:2]
    qgs = misc[:, 2:3]
    negb = misc[:, 3:4]
    negA = misc[:, 4:5]

    state = {"ws": 0, "wb": 0, "ps": 0}

    def wload(l, name, X):
        si = state["ws"] % 2
        bi = state["wb"] % 3
        state["ws"] += 1
        state["wb"] += 1
        if DMA_CAST:
            P.dma("pool", wbf[bi][:, 0:X], wst[l, WIDX[name], :, 0:X], f"wb{bi}")
            return wbf[bi][:, 0:X]
        P.dma("sp", wstage[si][:, 0:X], wst[l, WIDX[name], :, 0:X], f"ws{si}")
        P.copy("pool", wbf[bi][:, 0:X], wstage[si][:, 0:X])
        return wbf[bi][:, 0:X]

    def psb(n=4, base=0):
        k = ("ps", base, n)
        state[k] = state.get(k, 0) + 1
        return base + (state[k] - 1) % n

    def dump(name, ap, shape):
        if dbg is None or name not in dbg:
            return
        t = nc.dram_tensor("dbg_" + name, list(shape), ap.dtype, kind="ExternalOutput").ap()
        dbg_outs[name] = t
        P.dma("sp", t, ap, "dbgout", final=True)

    P.dma("sp", cst[:], cst_in, "cstin")
    P.copy("dve", cbf[:], cst[:, 0:512])
    P.memset("dve", epsc, EPS)
    P.memset("dve", onec, 1.0)

    def rmsnorm_to_hT(gcol0, o_scr):
        sq = [V(o_scr + i * KB, [128, 512], BF16) for i in range(2)]
        lnt = V(o_scr + 2 * KB, [128, 512], F32)
        rstd = V(o_scr + 4 * KB, [128, 512], F32)
        for tt in range(NTT):
            ts_ = slice(tt * 512, (tt + 1) * 512)
            pb = bank(psb(2, 6))
            for c in range(NCH):
                s_ = sq[c % 2]
                P.act(s_, xT[:, c, ts_], AF.Square)
                P.mm(pb, ones_b, s_, start=(c == 0), stop=(c == NCH - 1))
            P.act(lnt, pb, AF.Ln, bias=epsc, scale=1.0 / D)
            P.act(rstd, lnt, AF.Exp, scale=-0.5)
            for c in range(NCH):
                P.stt("dve", hT[:, c, ts_], xT[:, c, ts_], prm[:, gcol0 + c:gcol0 + c + 1], rstd,
                      ALU.mult, ALU.mult)

    def proj_fm(wt, M, m0, consumer, kchunks=NCH, rhs_of=None, tts=range(NTT), defer=False, nb=4):
        w3 = wt.rearrange("p (k c) -> p k c", k=kchunks)
        pend = None
        for tt in tts:
            ts_ = slice(tt * 512, (tt + 1) * 512)
            pb = bank(psb(nb, 0), M)
            for k in range(kchunks):
                rhs = hT[:, k, ts_] if rhs_of is None else rhs_of(k, ts_)
                P.mm(pb, w3[:, k, m0:m0 + M], rhs, start=(k == 0), stop=(k == kchunks - 1))
            if defer:
                if pend is not None:
                    consumer(*pend)
                pend = (tt, ts_, pb)
            else:
                consumer(tt, ts_, pb)
        if pend is not None:
            consumer(*pend)

    def layer(l):
        P.dma("sp", prm[:], prm_in[l], "prmin")
        P.ts("dve", qgs[0:64], prm[0:64, P_FQG:P_FQG + 1], 0.125, None, op0=ALU.mult)
        P.copy("dve", qgs[64:128], prm[64:128, P_FQG:P_FQG + 1])
        P.ts("dve", negb[0:8], prm[0:8, P_BF:P_BF + 1], -1.0, None, op0=ALU.mult)
        P.act(negA[32:36], prm[32:36, P_ALOG:P_ALOG + 1], AF.Exp)
        P.ts("dve", negA[32:36], negA[32:36], -1.0, None, op0=ALU.mult)

        rmsnorm_to_hT(P_GMIX, O_SCR)
        dump(f"h{l}", hT[:, :, :], [128, NCH, S])
        for c in range(NCH):
            P.dma("sp", scrx[:, c, :], xT[:, c, :], f"spill{c}")
        if stop_after == "norm1":
            return

        wt = wload(l, "small", 8 * 96)

        def small_cons(tt, ts_, pb):
            P.act(smT[0:8, ts_], pb[0:8, :], AF.Exp, bias=negb[0:8], scale=-1.0)
            P.act(smT[32:36, ts_], pb[32:36, :], AF.Exp, bias=prm[32:36, P_DTB:P_DTB + 1], scale=1.0)
            P.act(smT[64:68, ts_], pb[64:68, :], AF.Exp, scale=-1.0)
        proj_fm(wt, 96, 0, small_cons)
        o = O_XT
        cqs = V(o, [8, 3, S], BF16); o += 12 * KB
        vext = V(o, [128, 16, 8, 65], BF16); o += 17 * KB
        qa = [V(o + i * 8 * KB, [128, S], BF16) for i in range(2)]
        ka = [V(o + i * 8 * KB + 4 * KB, [128, S], BF16) for i in range(2)]
        o += 16 * KB
        ytok = V(o, [128, 16, 128], BF16); o += 4 * KB
        ptile = [V(o + i * KB, [128, 512], BF16) for i in range(3)]; o += 3 * KB
        sqh = [V(o + i * KB, [128, 512], BF16) for i in range(2)]; o += 2 * KB
        lnh = V(o, [128, 512], F32); o += 2 * KB
        rsh = V(o, [128, 512], F32); o += 2 * KB
        rcp = V(o, [128, 4], F32); o += 128
        assert o <= 64 * KB
        P.memset("dve", vext[:, :, :, 64:65], 1.0)
        for ct in range(4):
            wt = wload(l, f"foxv{ct}", 1024)
            w3 = wt.rearrange("p (k c) -> p k c", k=NCH)
            for g4 in range(4):
                pb = bank(psb(4, 0))
                for t4 in range(4):
                    tb = g4 * 4 + t4
                    for k in range(NCH):
                        P.mm(pb[:, t4 * 128:(t4 + 1) * 128], hT[:, k, tb * 128:(tb + 1) * 128], w3[:, k, :],
                             start=(k == 0), stop=(k == NCH - 1))
                P.copy("act", vext[:, g4 * 4:(g4 + 1) * 4, 2 * ct:2 * ct + 2, 0:64],
                       pb.rearrange("p (a b c) -> p a b c", a=4, b=2))
        P.act(smT[0:8, :], smT[0:8, :], AF.Ln, bias=onec[0:8], scale=1.0)
        P.act(smT[32:36, :], smT[32:36, :], AF.Ln, bias=onec[32:36], scale=1.0)
        P.act(smT[64:68, :], smT[64:68, :], AF.Ln, bias=onec[64:68], scale=1.0)
        P.act(smT[64:68, :], smT[64:68, :], AF.Exp, scale=-1.0)
        aux = V(O_SCR + 16 * KB, [128, S], F32)
        P.scan(aux[0:8, :], onec[0:8].to_broadcast([8, S]), smT[0:8, :], 0.0, ALU.mult, ALU.subtract)
        cqf = aux[0:8, :]
        P.ts("dve", smT[32:36, :], smT[32:36, :], negA[32:36], None, op0=ALU.mult)
        dump(f"dn_g{l}", smT[32:36, :], [4, S])
        dump(f"dn_beta{l}", smT[64:68, :], [4, S])
        P.scan(aux[32:36, :], onec[32:36].to_broadcast([4, S]), smT[32:36, :], 0.0, ALU.mult, ALU.add)
        a3 = aux[32:36, :].rearrange("p (n c) -> p n c", c=64)
        g3 = smT[32:36, :].rearrange("p (n c) -> p n c", c=64)
        gl = V(O_SCR, [128, 32], F32)
        P.copy("dve", gl[32:36, :], a3[:, :, 63])
        P.copy("dve", g3[:, 0, :], a3[:, 0, :])
        P.tt("dve", g3[:, 1:32, :], a3[:, 1:32, :], gl[32:36, 0:31].unsqueeze(2).to_broadcast([4, 31, 64]),
             ALU.subtract)
        dump(f"cq{l}", cqf, [8, S])
        dump(f"gcum{l}", smT[32:36, :], [4, S])

        cr = V(O_SCR + 8 * KB, [8, S], F32)
        P.copy("dve", cqs[:, 0, :], cqf)
        P.tt("dve", cr, cqf, cqs[:, 0, :], ALU.subtract)
        P.copy("dve", cqs[:, 1, :], cr)
        P.tt("dve", cr, cr, cqs[:, 1, :], ALU.subtract)
        P.copy("dve", cqs[:, 2, :], cr)
        def fox_proj(h):
            wt = wload(l, f"foxqk{h}", 1024)
            qa_h, ka_h = qa[h % 2], ka[h % 2]

            def qk_cons(tt, ts_, pb):
                s_ = sqh[tt % 2]
                P.act(s_, pb, AF.Square)
                p2 = bank(psb(2, 6))
                P.mm(p2, bd_b, s_)
                P.act(lnh, p2, AF.Ln, bias=epsc, scale=1.0 / 64)
                P.act(rsh, lnh, AF.Exp, scale=-0.5)
                P.stt("dve", qa_h[:, ts_], pb, qgs, rsh, ALU.mult, ALU.mult)
            proj_fm(wt, 128, 0, qk_cons, defer=True)
            P.dma("sp", ka_h[0:64, :], qa_h[64:128, :], f"kmov{h % 2}")
            P.memset("pool", ka_h[64:70, :], 1.0)
            P.dma("sp", ka_h[67:70, :], cqs[h:h + 1, :, :], f"augk{h % 2}")
            P.memset("pool", qa_h[64:70, :], -1.0)
            P.dma("sp", qa_h[64:67, :], cqs[h:h + 1, :, :], f"augq{h % 2}")

        fox_proj(0)
        for h in range(8):
            qa_h, ka_h = qa[h % 2], ka[h % 2]
            if h + 1 < 8:
                fox_proj(h + 1)
            tasks = [(qt, kb) for qt in range(4) for kb in range(4 * qt + 4)]
            obanks = {}

            def emit_S(qt, kb):
                n0 = max(kb * 128, qt * 512)
                ncols = (qt + 1) * 512 - n0
                pb = bank(psb(4, 0))
                P.mm(pb[:, 0:ncols], ka_h[0:70, kb * 128:(kb + 1) * 128], qa_h[0:70, n0:n0 + ncols])
                pt = ptile[state["ps"] % 3]
                state["ps"] += 1
                P.act(pt[:, 0:ncols], pb[:, 0:ncols], AF.Exp)
                if kb * 128 >= qt * 512:
                    P.tt("dve", pt[:, 0:128], pt[:, 0:128], mask01_b, ALU.mult)
                return pt, n0

            def emit_PV(qt, kb, pt, n0):
                ob = bank(4 + qt % 2)
                o4 = ob.rearrange("p (a b) -> p a b", a=4)
                for qb in range(n0 // 128, 4 * qt + 4):
                    c0 = qb * 128 - n0
                    P.mm(o4[:, qb - 4 * qt, 0:65], pt[:, c0:c0 + 128], vext[:, kb, h, :],
                         start=(kb == 0 and qb == 4 * qt), stop=(kb == qb))
                if kb == 4 * qt + 3:
                    P.add("dve", lambda e, o4=o4: e.reciprocal(rcp[:, :].unsqueeze(2), o4[:, :, 64:65]),
                          reads=[o4[:, :, 64:65]], writes=[rcp[:, :]])
                    P.tt("dve", ytok[:, 4 * qt:4 * qt + 4, (h % 2) * 64:(h % 2) * 64 + 64], o4[:, :, 0:64],
                         rcp[:, :].unsqueeze(2).to_broadcast([128, 4, 64]), ALU.mult)

            LA = 2
            pend = [emit_S(*tasks[i]) for i in range(LA)]
            for i, (qt, kb) in enumerate(tasks):
                if i + LA < len(tasks):
                    pend.append(emit_S(*tasks[i + LA]))
                emit_PV(qt, kb, *pend.pop(0))
            if h % 2 == 1:
                for half in range(2):
                    tp = bank(6 + half, 128, BF16)
                    for i in range(8):
                        qb = half * 8 + i
                        P.transpose(tp[:, i * 128:(i + 1) * 128], ytok[:, qb, :], ident_b)
                    P.copy("act", yT[:, h // 2, half * 1024:(half + 1) * 1024], tp)
        dump(f"y_fox{l}", yT[:, 0:4, :], [128, 4, S])
        if stop_after == "fox":
            return

        o = O_XT
        cv = V(o, [128, S + 2], F32); o += 8 * KB + 128
        acc = V(o, [128, S], F32); o += 8 * KB
        bsb = V(o, [128, S], F32); o += 8 * KB
        ctmp = [V(o + i * 2 * KB, [128, 512], F32) for i in range(2)]; o += 4 * KB
        P.memset("dve", cv[:, 0:2], 0.0)
        for j in range(4):
            wb_ = wload(l, f"scb{j}", 1024)
            proj_fm(wb_, 128, 0, lambda tt, ts_, pb: P.copy("act", bsb[:, ts_], pb))
            wc_ = wload(l, f"scc{j}", 1024)
            wv_ = wload(l, f"scv{j}", 1024)
            w3c = wc_.rearrange("p (k c) -> p k c", k=NCH)
            w3v = wv_.rearrange("p (k c) -> p k c", k=NCH)
            for tt in range(NTT):
                ts_ = slice(tt * 512, (tt + 1) * 512)
                pc = bank(psb(4, 0))
                for k in range(NCH):
                    P.mm(pc, w3c[:, k, :], hT[:, k, ts_], start=(k == 0), stop=(k == NCH - 1))
                P.copy("act", ctmp[tt % 2], pc)
                pv = bank(psb(4, 0))
                for k in range(NCH):
                    P.mm(pv, w3v[:, k, :], hT[:, k, ts_], start=(k == 0), stop=(k == NCH - 1))
                P.tt("dve", cv[:, 2 + tt * 512:2 + (tt + 1) * 512], pv, ctmp[tt % 2], ALU.mult)
            w0 = prm[:, P_SCW + j * 3:P_SCW + j * 3 + 1]
            w1 = prm[:, P_SCW + j * 3 + 1:P_SCW + j * 3 + 2]
            w2 = prm[:, P_SCW + j * 3 + 2:P_SCW + j * 3 + 3]
            P.act(acc, cv[:, 2:2 + S], AF.Copy, scale=w2)
            P.stt("dve", acc, cv[:, 1:1 + S], w1, acc, ALU.mult, ALU.add)
            P.stt("dve", acc, cv[:, 0:S], w0, acc, ALU.mult, ALU.add)
            P.tt("dve", yT[:, 4 + j, :], acc, bsb, ALU.mult)
        dump(f"y_sc{l}", yT[:, 4:8, :], [128, 4, S])
        if stop_after == "sc":
            return

        gtok = V(O_SCR, [64, 32, 4], F32)
        btok = V(O_SCR + 512, [64, 32, 4], F32)
        for src0, dst in ((32, gtok), (64, btok)):
            pb = bank(psb(2, 6), 64)
            p3 = pb[:, 0:128].rearrange("p (n h) -> p n h", h=4)
            for n in range(32):
                P.transpose(p3[:, n, :], smT[src0:src0 + 4, n * 64:(n + 1) * 64],
                            ident_f[src0:src0 + 4, src0:src0 + 4])
            P.copy("dve", dst, p3)
        o = O_XT
        raw = V(o, [128, S + 3], F32); o += 8 * KB + 128
        cacc = V(o, [128, S], F32); o += 8 * KB
        QT = V(o, [128, S], BF16); o += 4 * KB
        KT = V(o, [128, S], BF16); o += 4 * KB
        VT = V(o, [128, S], BF16); o += 4 * KB
        qdT = V(o, [128, S], BF16); o += 4 * KB
        nkcT = V(o, [128, 32, 64], BF16); o += 4 * KB
        Kg = V(o, [64, 32, 128], BF16); o += 8 * KB
        Kd = V(o, [64, 32, 128], BF16); o += 8 * KB
        Vb = V(o, [64, 32, 128], BF16); o += 8 * KB
        sqp = [V(o + i * KB, [128, 512], BF16) for i in range(2)]; o += 2 * KB
        assert o <= 64 * KB, o
        o = O_SCR + 1 * KB
        cm = [V(o + i * 4 * KB, [64, 32, 64], BF16) for i in range(5)]; o += 20 * KB
        Mm, MTm, PTm, qkT, Mn = cm
        MTn = V(o, [64, 32, 64], BF16); o += 4 * KB
        PTn = V(o, [64, 32, 64], BF16); o += 4 * KB
        assert o <= SCR_END, o
        GB = V(O_XT, [128, S], F32)
        Xm = V(O_XT + 8 * KB + 128, [64, 32, 64], F32)
        sq2 = [V(O_SCR + 29 * KB + i * KB, [128, 512], BF16) for i in range(2)]
        eg = misc[0:64, 8:40]; bgc = misc[0:64, 40:72]; edc = misc[0:64, 72:104]
        gtot = misc[:, 104:136]; nbt = misc[0:64, 136:168]

        def prep_gen(h):
            lnp = V(O_YT + (8 + h) * 4 * KB, [128, 512], F32)
            rsp = V(O_YT + (8 + h) * 4 * KB + 2 * KB, [128, 512], F32)
            for which, nm, dstT, scl in ((0, "dnq", QT, 128.0 ** -0.5), (1, "dnk", KT, 1.0), (2, "dnv", VT, None)):
                wt = wload(l, f"{nm}{h}", 1024)
                P.memset("dve", raw[:, 0:3], 0.0)
                w3 = wt.rearrange("p (k c) -> p k c", k=NCH)
                for tt in range(NTT):
                    ts_ = slice(tt * 512, (tt + 1) * 512)
                    pb = bank(psb(2, 6))
                    for k in range(NCH):
                        P.mm(pb, w3[:, k, :], hT[:, k, ts_], start=(k == 0), stop=(k == NCH - 1))
                    P.copy("act", raw[:, 3 + tt * 512:3 + (tt + 1) * 512], pb)
                    yield
                cw = P_DNW + (which * 4 + h) * 4
                P.act(cacc, raw[:, 3:3 + S], AF.Copy, scale=prm[:, cw + 3:cw + 4])
                yield
                for j in range(3):
                    P.stt("dve", cacc, raw[:, j:j + S], prm[:, cw + j:cw + j + 1], cacc, ALU.mult, ALU.add)
                    yield
                if scl is None:
                    P.act(dstT, cacc, AF.Silu)
                    yield
                else:
                    P.act(cacc, cacc, AF.Silu)
                    yield
                    for tt in range(NTT):
                        ts_ = slice(tt * 512, (tt + 1) * 512)
                        s_ = sqp[tt % 2]
                        P.act(s_, cacc[:, ts_], AF.Square)
                        p2 = bank(psb(2, 6))
                        P.mm(p2, ones_b, s_)
                        yield
                        P.act(lnp, p2, AF.Ln, bias=epsc, scale=1.0)
                        P.act(rsp, lnp, AF.Exp, scale=-0.5)
                        yield
                        P.stt("dve", dstT[:, ts_], cacc[:, ts_], scl, rsp, ALU.mult, ALU.mult)
                        yield

        gen_box = [None]

        def pull(k):
            g = gen_box[0]
            if g is None:
                return
            for _ in range(k):
                try:
                    next(g)
                except StopIteration:
                    gen_box[0] = None
                    return

        def drain():
            while gen_box[0] is not None:
                pull(8)

        gen_box[0] = prep_gen(0)
        drain()
        for h in range(4):
            if h == 0:
                dump(f"dn_q{l}", QT, [128, S]); dump(f"dn_k{l}", KT, [128, S]); dump(f"dn_v{l}", VT, [128, S])
            for tt in range(NTT):
                ts_ = slice(tt * 512, (tt + 1) * 512)
                pb = bank(psb(4, 0))
                P.mm(pb, sel_f[32:36, h * 128:(h + 1) * 128], smT[32:36, ts_])
                P.copy("act", GB[:, ts_], pb)
                egt = V(O_SCR + 25 * KB, [128, 512], F32)
                P.act(egt, pb, AF.Exp)
                P.tt("dve", qdT[:, ts_], QT[:, ts_], egt, ALU.mult)
            GB3 = GB.rearrange("p (n c) -> p n c", c=64)
            P.act(eg, gtok[:, :, h], AF.Exp)
            P.tt("dve", bgc, btok[:, :, h], eg, ALU.mult)
            P.tt("dve", edc, GB3[0:64, :, 63], gtok[:, :, h], ALU.subtract)
            P.act(edc, edc, AF.Exp)
            P.act(gtot, GB3[:, :, 63], AF.Exp)
            P.ts("dve", nbt, btok[:, :, h], -1.0, None, op0=ALU.mult)
            for g8 in range(4):
                for src, outs in ((KT, ((Kg, bgc), (Kd, edc))), (VT, ((Vb, btok[:, :, h]),))):
                    tp = bank(psb(2, 6), 64, BF16)
                    for i in range(8):
                        n = g8 * 8 + i
                        P.transpose(tp[:, i * 128:(i + 1) * 128], src[:, n * 64:(n + 1) * 64], ident_b)
                    t3 = tp.rearrange("p (n d) -> p n d", d=128)
                    for (dst, col) in outs:
                        P.tt("dve", dst[:, g8 * 8:(g8 + 1) * 8, :], t3,
                             col[:, g8 * 8:(g8 + 1) * 8].unsqueeze(2).to_broadcast([64, 8, 128]), ALU.mult)
            P.tt("dve", Xm, GB3[0:64, :, :], gtok[:, :, h].unsqueeze(2).to_broadcast([64, 32, 64]), ALU.subtract)
            DLn = V(O_SCR + 21 * KB, [64, 32, 64], F32)
            P.tt("dve", DLn, Xm, mposA.unsqueeze(1).to_broadcast([64, 32, 64]), ALU.add)
            P.act(DLn, DLn, AF.Exp, scale=-1.0)
            P.tt("dve", DLn, DLn, nbt[:, :].unsqueeze(2).to_broadcast([64, 32, 64]), ALU.mult)
            P.tt("dve", Xm, Xm, mposB.unsqueeze(1).to_broadcast([64, 32, 64]), ALU.subtract)
            P.act(Xm, Xm, AF.Exp)
            for g8 in range(4):
                pk = bank(psb(4, 0), 64)
                pq = bank(psb(4, 0), 64)
                for i in range(8):
                    n = g8 * 8 + i
                    cs = slice(n * 64, (n + 1) * 64)
                    P.mm(pk[:, i * 64:(i + 1) * 64], KT[:, cs], KT[:, cs])
                    P.mm(pq[:, i * 64:(i + 1) * 64], KT[:, cs], QT[:, cs])
                gs = slice(g8 * 8, (g8 + 1) * 8)
                P.tt("dve", Mm[:, gs, :], pk.rearrange("p (n c) -> p n c", c=64), DLn[:, gs, :], ALU.mult)
                P.tt("dve", qkT[:, gs, :], pq.rearrange("p (n c) -> p n c", c=64), Xm[:, gs, :], ALU.mult)
            for g8 in range(4):
                tp = bank(psb(2, 6), 64, BF16)
                for i in range(8):
                    n = g8 * 8 + i
                    P.transpose(tp[:, i * 64:(i + 1) * 64], Mm[:, n, :], ident_b[0:64, 0:64])
                P.copy("act", MTm[:, g8 * 8:(g8 + 1) * 8, :], tp[:, 0:512].rearrange("p (n c) -> p n c", c=64))
            P.tt("dve", PTm, MTm, ident_b[0:64, 0:64].unsqueeze(1).to_broadcast([64, 32, 64]), ALU.add)
            if h < 3:
                gen_box[0] = prep_gen(h + 1)
            Wc, WTc, PTc = Mm, MTm, PTm
            Wn, WTn, PTx = Mn, MTn, PTn
            for it in range(5):
                def stA(g8, it=it, Wc=Wc, WTc=WTc, Wn=Wn, WTn=WTn):
                    pw = bank(psb(6, 0), 64)
                    pwt = bank(psb(6, 0), 64) if it < 4 else None
                    for i in range(8):
                        n = g8 * 8 + i
                        P.mm(pw[:, i * 64:(i + 1) * 64], WTc[:, n, :], Wc[:, n, :])
                        if pwt is not None:
                            P.mm(pwt[:, i * 64:(i + 1) * 64], Wc[:, n, :], WTc[:, n, :])
                    gs = slice(g8 * 8, (g8 + 1) * 8)
                    P.copy("act", Wn[:, gs, :], pw.rearrange("p (n c) -> p n c", c=64))
                    if pwt is not None:
                        P.copy("dve", WTn[:, gs, :], pwt.rearrange("p (n c) -> p n c", c=64))

                def stB(g8, Wn=Wn, PTc=PTc, PTx=PTx):
                    pp = bank(psb(6, 0), 64)
                    for i in range(8):
                        n = g8 * 8 + i
                        P.mm(pp[:, i * 64:(i + 1) * 64], ident_b[0:64, 0:64], PTc[:, n, :], start=True, stop=False)
                        P.mm(pp[:, i * 64:(i + 1) * 64], Wn[:, n, :], PTc[:, n, :], start=False, stop=True)
                    gs = slice(g8 * 8, (g8 + 1) * 8)
                    P.copy("dve", PTx[:, gs, :], pp.rearrange("p (n c) -> p n c", c=64))
                for st_, g_ in ((stA, 0), (stA, 1), (stB, 0), (stA, 2), (stB, 1), (stA, 3), (stB, 2), (stB, 3)):
                    st_(g_)
                    pull(PULL_N)
                Wc, Wn = Wn, Wc
                WTc, WTn = WTn, WTc
                PTc, PTx = PTx, PTc
            TT = PTc
            for g8 in range(4):
                pb = bank(psb(4, 0))
                for i in range(8):
                    n = g8 * 8 + i
                    P.mm(pb[:, i * 64:(i + 1) * 64], Kg[:, n, :], TT[:, n, :])
                P.act(nkcT[:, g8 * 8:(g8 + 1) * 8, :], pb.rearrange("p (n c) -> p n c", c=64), AF.Copy, scale=-1.0)
            o3 = O_SCR + 29 * KB
            Sf = V(o3, [128, 128], F32)
            Sbb = [V(o3 + 512 + i * 256, [128, 128], BF16) for i in range(2)]
            vn = [V(o3 + 1024 + i * 256, [64, 128], BF16) for i in range(2)]
            oT = V(O_SCR + 1 * KB, [128, S], F32)
            P.memset("dve", Sf, 0.0)
            P.memset("dve", Sbb[0], 0.0)
            for n in range(32):
                sb_ = Sbb[n % 2]
                pv = bank(4, 64)[:, (n % 2) * 128:(n % 2) * 128 + 128]
                P.mm(pv, TT[:, n, :], Vb[:, n, :], start=True, stop=False)
                P.mm(pv, nkcT[:, n, :], sb_, start=False, stop=True)
                vn_ = vn[n % 2]
                P.copy("act", vn_, pv)
                if n % 8 == 0:
                    po = bank(psb(2, 2))
                pos = po[:, (n % 8) * 64:(n % 8 + 1) * 64]
                P.mm(pos, sb_, qdT[:, n * 64:(n + 1) * 64], start=True, stop=False)
                P.mm(pos, vn_, qkT[:, n, :], start=False, stop=True)
                pd = bank(5)[:, (n % 2) * 128:(n % 2) * 128 + 128]
                P.mm(pd, Kd[:, n, :], vn_)
                P.stt("dve", Sbb[(n + 1) % 2], Sf, gtot[:, n:n + 1], pd, ALU.mult, ALU.add)
                P.stt("dve", Sf, Sf, gtot[:, n:n + 1], pd, ALU.mult, ALU.add)
                if n % 8 == 7:
                    tt = n // 8
                    P.copy("act", oT[:, tt * 512:(tt + 1) * 512], po)
                pull(PULL_S)
            if h == 0:
                dump(f"o_dn{l}", oT, [128, S])
            drain()
            wz = wload(l, f"dnz{h}", 1024)

            for tt in range(NTT):
                ts_ = slice(tt * 512, (tt + 1) * 512)
                s_ = sq2[tt % 2]
                P.act(s_, oT[:, ts_], AF.Square)
                p2 = bank(psb(2, 6))
                P.mm(p2, ones_b, s_)
                lnt = V(O_SCR + 25 * KB, [128, 512], F32)
                rst = V(O_SCR + 27 * KB, [128, 512], F32)
                P.act(lnt, p2, AF.Ln, bias=epsc, scale=1.0 / 128)
                P.act(rst, lnt, AF.Exp, scale=-0.5)
                P.stt("dve", oT[:, ts_], oT[:, ts_], prm[:, P_DNG:P_DNG + 1], rst, ALU.mult, ALU.mult)

            def z_cons(tt, ts_, pb, h=h):
                zt = V(O_SCR + 25 * KB + (tt % 2) * 2 * KB, [128, 512], F32)
                P.act(zt, pb, AF.Silu)
                P.tt("dve", yT[:, 8 + h, ts_], oT[:, ts_], zt, ALU.mult)
            proj_fm(wz, 128, 0, z_cons)
        dump(f"y_dn{l}", yT[:, 8:12, :], [128, 4, S])
        if stop_after == "dn":
            return

        mT = V(O_SMT, [128, NCH, S], BF16)
        macc = V(O_SMT + 33 * KB, [128, 512], F32)
        mtmp = V(O_SMT + 35 * KB, [128, 512], F32)
        sgt = [[V(O_XT + (b * NTT + tt) * 2 * KB, [128, 512], F32) for tt in range(NTT)] for b in range(3)]
        for dc in range(NCH):
            for b in range(3):
                wgb = wload(l, f"g{b}_{dc}", 1024).rearrange("p (k c) -> p k c", k=NCH)
                for tt in range(NTT):
                    ts_ = slice(tt * 512, (tt + 1) * 512)
                    pg = bank(psb(6, 0))
                    for k in range(NCH):
                        P.mm(pg, wgb[:, k, :], hT[:, k, ts_], start=(k == 0), stop=(k == NCH - 1))
                    P.act(sgt[b][tt], pg, AF.Sigmoid)
            wA3 = wload(l, f"brA{dc}", 1024).rearrange("p (b c m) -> p b c m", b=2, c=4)
            wB3 = wload(l, f"brB{dc}", 512).rearrange("p (c m) -> p c m", c=4)
            for tt in range(NTT):
                ts_ = slice(tt * 512, (tt + 1) * 512)
                for b in range(3):
                    pbr = bank(psb(6, 0))
                    for c in range(4):
                        lw = wA3[:, b, c, :] if b < 2 else wB3[:, c, :]
                        P.mm(pbr, lw, yT[:, 4 * b + c, ts_], start=(c == 0), stop=(c == 3))
                    if b == 0:
                        P.tt("dve", macc, pbr, sgt[b][tt], ALU.mult)
                    elif b == 1:
                        P.tt("dve", mtmp, pbr, sgt[b][tt], ALU.mult)
                        P.tt("dve", macc, macc, mtmp, ALU.add)
                    else:
                        P.tt("dve", mtmp, pbr, sgt[b][tt], ALU.mult)
                        P.tt("dve", mT[:, dc, ts_], macc, mtmp, ALU.add)
        dump(f"merged{l}", mT[:, :, 0:1024], [128, NCH, 1024])
        for dc in range(NCH):
            wo = wload(l, f"wo{dc}", 1024)
            P.dma("sp", xT[:, dc, :], scrx[:, dc, :], f"unsp{dc}")

            def wo_cons(tt, ts_, pb, dc=dc):
                P.tt("dve", xT[:, dc, ts_], xT[:, dc, ts_], pb, ALU.add)
            proj_fm(wo, 128, 0, wo_cons, rhs_of=lambda k, ts_: mT[:, k, ts_], nb=6)
        dump(f"x_mix{l}", xT[:, :, :], [128, NCH, S])
        if stop_after == "mix":
            return

        rmsnorm_to_hT(P_GFFN, O_SCR)
        o = O_YT
        graw = V(o, [128, S + 2], F32); o += 8 * KB + 128
        vraw = V(o, [128, S + 2], F32); o += 8 * KB + 128
        gacs = [V(o, [128, S], F32), V(O_SMT, [128, S], F32)]; o += 8 * KB
        vacs = [V(o, [128, S], F32), V(O_SCR + 20 * KB, [128, S], F32)]; o += 8 * KB
        assert o <= O_WS
        aT0 = V(O_SCR + 0 * KB, [128, GRP, S], BF16)
        aT1 = V(O_YT + 33 * KB, [128, 3, S], BF16)
        aT1b = V(O_SCR + 16 * KB, [128, 1, S], BF16)
        P.memset("dve", graw[:, 0:2], 0.0)
        P.memset("dve", vraw[:, 0:2], 0.0)

        def a_slot(gi, jj):
            if gi % 2 == 0:
                return aT0[:, jj, :]
            return aT1[:, jj, :] if jj < 3 else aT1b[:, 0, :]

        def ffn_up(gi, j0, n):
                for jj in range(n):
                    j = j0 + jj
                    gac = gacs[j % 2]
                    vac = vacs[j % 2]
                    for nm, rawb, accb, c0 in (("upg", graw, gac, j), ("upv", vraw, vac, NFF + j)):
                        wt = wload(l, f"{nm}{j}", 1024)
                        w2c = prm[:, P_FFW + c0 * 3 + 2:P_FFW + c0 * 3 + 3]

                        def up_cons(tt, ts_, pb, rawb=rawb, accb=accb, w2c=w2c):
                            P.copy("act", rawb[:, 2 + tt * 512:2 + (tt + 1) * 512], pb)
                            P.act(accb[:, ts_], pb, AF.Copy, scale=w2c)
                        proj_fm(wt, 128, 0, up_cons, nb=6)
                        P.stt("dve", accb, rawb[:, 1:1 + S], prm[:, P_FFW + c0 * 3 + 1:P_FFW + c0 * 3 + 2], accb,
                              ALU.mult, ALU.add)
                        P.stt("dve", accb, rawb[:, 0:S], prm[:, P_FFW + c0 * 3:P_FFW + c0 * 3 + 1], accb,
                              ALU.mult, ALU.add)
                    P.act(gac, gac, AF.Silu)
                    P.tt("dve", a_slot(gi, jj), gac, vac, ALU.mult)

        def ffn_down(gi, j0, n):
                for q in range(4):
                    wd = wload(l, f"dn{gi}_{q}", n * 256)
                    wd3 = wd.rearrange("p (j c) -> p j c", j=n)
                    for dd in range(2):
                        dc = 2 * q + dd
                        for tt in range(NTT):
                            ts_ = slice(tt * 512, (tt + 1) * 512)
                            pb = bank(psb(6, 0))
                            for jj in range(n):
                                P.mm(pb, wd3[:, jj, dd * 128:(dd + 1) * 128], a_slot(gi, jj)[:, ts_],
                                     start=(jj == 0), stop=(jj == n - 1))
                            P.tt("dve", xT[:, dc, ts_], xT[:, dc, ts_], pb, ALU.add)

        groups = _ffn_groups()
        for gi, (j0, n) in enumerate(groups):
            ffn_up(gi, j0, n)
            if gi > 0:
                ffn_down(gi - 1, *groups[gi - 1])
        ffn_down(len(groups) - 1, *groups[-1])
        dump(f"x_ffn{l}", xT[:, :, :], [128, NCH, S])
        if stop_after == "ffn":
            return

        rmsnorm_to_hT(P_GPLE, O_SCR)
        ptok = V(O_YT, [128, 16, 256], F32)
        pT = V(O_YT + 16 * KB, [128, 2, S], BF16)
        sgp = [V(O_YT + 24 * KB + i * 2 * KB, [128, 512], F32) for i in range(2)]
        ptmp = V(O_YT + 28 * KB, [128, 512], F32)
        for q in range(4):
            P.dma("sp", ptok[:, q * 4:(q + 1) * 4, :],
                  p_in[l, q * 512:(q + 1) * 512, :].rearrange("(a p) c -> p a c", p=128), f"pin{q}")
        for c in range(2):
            for g4 in range(4):
                tp = bank(psb(2, 6))
                for i in range(4):
                    tb = g4 * 4 + i
                    P.transpose(tp[:, i * 128:(i + 1) * 128], ptok[:, tb, c * 128:(c + 1) * 128], ident_f)
                P.copy("act", pT[:, c, g4 * 512:(g4 + 1) * 512], tp)
        for dc in range(NCH):
            wpg = wload(l, f"pg{dc}", 1024)
            wpl = wload(l, f"pl{dc}", 256)
            wpl3 = wpl.rearrange("p (c m) -> p c m", c=2)

            def pg_cons(tt, ts_, pb, dc=dc, wpl3=wpl3):
                s_ = sgp[tt % 2]
                P.act(s_, pb, AF.Sigmoid)
                pp = bank(psb(2, 4))
                for c in range(2):
                    P.mm(pp, wpl3[:, c, :], pT[:, c, ts_], start=(c == 0), stop=(c == 1))
                P.tt("dve", ptmp, pp, s_, ALU.mult)
                P.tt("dve", xT[:, dc, ts_], xT[:, dc, ts_], ptmp, ALU.add)
            proj_fm(wpg, 128, 0, pg_cons, defer=True)
        dump(f"x_out{l}", xT[:, :, :], [128, NCH, S])

    xin = [V(O_YT + i * 4 * KB, [128, D], F32) for i in range(2)]
    for tb in range(16):
        xi = xin[tb % 2]
        P.dma("sp", xi, x_in[tb * 128:(tb + 1) * 128, :], f"xin{tb % 2}")
        for g2 in range(2):
            tp = bank(psb(2, 6))
            for i in range(4):
                c = g2 * 4 + i
                P.transpose(tp[:, i * 128:(i + 1) * 128], xi[:, c * 128:(c + 1) * 128], ident_f)
            eng = "act" if g2 == 0 else "dve"
            P.copy(eng, xT[:, g2 * 4:(g2 + 1) * 4, tb * 128:(tb + 1) * 128],
                   tp.rearrange("p (c t) -> p c t", c=4))

    for l in range(depth):
        layer(l)

    if stop_after is None:
        xo = [V(O_YT + i * 4 * KB, [128, D], F32) for i in range(2)]
        for tb in range(16):
            xo_ = xo[tb % 2]
            for g2 in range(2):
                tp = bank(psb(2, 6))
                for i in range(4):
                    c = g2 * 4 + i
                    P.transpose(tp[:, i * 128:(i + 1) * 128], xT[:, c, tb * 128:(tb + 1) * 128], ident_f)
                eng = "act" if g2 == 0 else "dve"
                P.copy(eng, xo_[:, g2 * 512:(g2 + 1) * 512], tp)
            P.dma("sp", out[tb * 128:(tb + 1) * 128, :], xo_, f"xout{tb % 2}", final=True)
    else:
        z = V(O_SCR, [128, 8], F32)
        P.memset("pool", z, 0.0)
        P.dma("sp", out[0:128, 0:8], z, "xout0", final=True)
    P.emit()
    return nc, dbg_outs, P


_CACHE = {}


def kernel(**inputs):
    inp = {k: np.asarray(v) for k, v in inputs.items()}
    if "nc" not in _CACHE:
        _CACHE["nc"] = build()[0]
    nc = _CACHE["nc"]
    wstream = np.stack([pack_weights(inp, l) for l in range(DEPTH)])
    prm = np.stack([pack_params(inp, l) for l in range(DEPTH)])
    cst = make_consts()
    in_maps = []
    for b in range(8):
        in_maps.append({"x": np.ascontiguousarray(inp["x"][b]),
                        "p": np.ascontiguousarray(inp["p"][:, b]),
                        "wst": wstream, "prm": prm, "cst": cst})
    res = run_bass_kernel_spmd(nc, in_maps, core_ids=list(range(8)))
    return np.stack([np.asarray(r["out"]) for r in res.results]).astype(np.float32)
```
